# Optimizing a Trainium2 kernel written in Bass

```python
import jax
import jax.numpy as jnp
from jax import lax
import numpy as np

D_MODEL = 2048
BATCH = 2
SEQ = 16384
DEPTH = 2

N_A_LAYERS = DEPTH // 2
N_B_LAYERS = DEPTH - N_A_LAYERS
RMS_EPS = 1e-6

RET_HEADS = 8
RET_QK_DIM = D_MODEL // RET_HEADS
RET_V_DIM = 2 * RET_QK_DIM
RET_CHUNK = 128
ROPE_BASE = 10000.0
RET_PROJ = 2 * RET_HEADS * RET_QK_DIM + 2 * RET_HEADS * RET_V_DIM

NSA_Q_HEADS = 16
NSA_KV_HEADS = 4
NSA_GROUP = NSA_Q_HEADS // NSA_KV_HEADS
NSA_HEAD_DIM = 128
N_BRANCH = 3
CMP_BLOCK = 32
CMP_STRIDE = 16
CMP_HIDDEN = 256
SLC_BLOCK = 64
SLC_TOPK = 16
WIN = 512
NSA_Q_BLOCK = 64
CMP_RATIO = CMP_BLOCK // CMP_STRIDE
SLC_RATIO = SLC_BLOCK // CMP_STRIDE
NSA_Q_PROJ = NSA_Q_HEADS * NSA_HEAD_DIM + N_BRANCH * NSA_Q_HEADS
NSA_KV_PROJ = 2 * N_BRANCH * NSA_KV_HEADS * NSA_HEAD_DIM
NEG = -1e30
SEL_FORCE = 1e30

D_FF = 4096
CONV_W = 3

kernel_name = "retention_nsa_yoco_convffn"


def rmsnorm(x, gain):
    xf = x.astype(jnp.float32)
    y = xf * lax.rsqrt(jnp.mean(xf * xf, axis=-1, keepdims=True) + RMS_EPS)
    return (y * gain.astype(jnp.float32)).astype(x.dtype)


def rotary(x, pos):
    half = x.shape[-1] // 2
    freqs = ROPE_BASE ** (-jnp.arange(half, dtype=jnp.float32) / half)
    ang = pos[:, None] * freqs[None, :]
    cos = jnp.cos(ang)[None, :, None, :]
    sin = jnp.sin(ang)[None, :, None, :]
    x1 = x[..., :half].astype(jnp.float32)
    x2 = x[..., half:].astype(jnp.float32)
    return jnp.concatenate([x1 * cos - x2 * sin, x1 * sin + x2 * cos], axis=-1).astype(x.dtype)


def retention(xn, w_in, gn_gain, w_out):
    B, S, _ = xn.shape
    H, dk, dv, C = RET_HEADS, RET_QK_DIM, RET_V_DIM, RET_CHUNK
    nC = S // C
    proj = xn @ w_in
    q, k, v, g = jnp.split(proj, [H * dk, 2 * H * dk, 2 * H * dk + H * dv], axis=-1)
    pos = jnp.arange(S, dtype=jnp.float32)
    q = rotary(q.reshape(B, S, H, dk), pos)
    k = rotary(k.reshape(B, S, H, dk), pos) * (dk ** -0.5)
    v = v.reshape(B, S, H, dv)

    def chunks(t):
        return t.reshape(B, nC, C, H, t.shape[-1]).transpose(0, 3, 1, 2, 4)

    qc, kc, vc = chunks(q), chunks(k), chunks(v)
    log_gamma = jnp.log(1.0 - 2.0 ** (-5.0 - jnp.arange(H, dtype=jnp.float32)))
    idx = jnp.arange(C, dtype=jnp.float32)
    diff = idx[:, None] - idx[None, :]
    decay_in = jnp.where(diff >= 0, jnp.exp(jnp.maximum(diff, 0.0)[None] * log_gamma[:, None, None]), 0.0)
    scores = jnp.einsum('bhnqd,bhnkd->bhnqk', qc, kc) * decay_in[None, :, None]
    inner = jnp.einsum('bhnqk,bhnke->bhnqe', scores, vc)

    q_decay = jnp.exp((idx[None, :] + 1.0) * log_gamma[:, None])
    k_decay = jnp.exp((C - 1.0 - idx[None, :]) * log_gamma[:, None])
    chunk_decay = jnp.exp(C * log_gamma)

    def step(state, inp):
        q_i, k_i, v_i = inp
        cross = jnp.einsum('bhqd,bhde->bhqe', q_i, state) * q_decay[None, :, :, None]
        state = state * chunk_decay[None, :, None, None] + jnp.einsum(
            'bhkd,bhke->bhde', k_i * k_decay[None, :, :, None], v_i)
        return state, cross

    state0 = jnp.zeros((B, H, dk, dv), jnp.float32)
    xs = (qc.transpose(2, 0, 1, 3, 4), kc.transpose(2, 0, 1, 3, 4), vc.transpose(2, 0, 1, 3, 4))
    _, cross = lax.scan(step, state0, xs)
    out = inner + cross.transpose(1, 2, 0, 3, 4)
    out = out.transpose(0, 2, 3, 1, 4).reshape(B, S, H, dv).astype(jnp.float32)
    mu = jnp.mean(out, axis=-1, keepdims=True)
    var = jnp.mean(jnp.square(out - mu), axis=-1, keepdims=True)
    out = ((out - mu) * lax.rsqrt(var + RMS_EPS)).reshape(B, S, H * dv) * gn_gain.astype(jnp.float32)
    y = jax.nn.silu(g.astype(jnp.float32)) * out
    return y.astype(xn.dtype) @ w_out


def conv_ffn(xn, w_in, conv_w, conv_b, w_out):
    S = xn.shape[1]
    u = xn @ w_in
    up = jnp.pad(u, ((0, 0), (CONV_W - 1, 0), (0, 0)))
    c = conv_b
    for tap in range(CONV_W):
        c = c + up[:, tap:tap + S] * conv_w[tap]
    a, b = jnp.split(c, 2, axis=-1)
    return (jax.nn.silu(a) * b) @ w_out


def compress_blocks(t, pe, w1, w2):
    B, S, H, d = t.shape
    n = S // CMP_STRIDE
    n_cmp = n - CMP_RATIO + 1
    ch = t.reshape(B, n, CMP_STRIDE, H, d)
    blocks = jnp.concatenate([ch[:, r:r + n_cmp] for r in range(CMP_RATIO)], axis=2)
    blocks = blocks + pe[None, None, :, None, :]
    blocks = blocks.transpose(0, 1, 3, 2, 4).reshape(B, n_cmp, H, CMP_BLOCK * d)
    return jax.nn.gelu(blocks @ w1) @ w2


def nsa_shared_kv(hn, w_kv, cmp_pe_k, cmp_w1_k, cmp_w2_k, cmp_pe_v, cmp_w1_v, cmp_w2_v):
    B, S, _ = hn.shape
    kv = (hn @ w_kv).reshape(B, S, 2 * N_BRANCH, NSA_KV_HEADS, NSA_HEAD_DIM)
    k_cmp = compress_blocks(kv[:, :, 0], cmp_pe_k, cmp_w1_k, cmp_w2_k)
    v_cmp = compress_blocks(kv[:, :, 1], cmp_pe_v, cmp_w1_v, cmp_w2_v)
    return (k_cmp, v_cmp, kv[:, :, 2], kv[:, :, 3], kv[:, :, 4], kv[:, :, 5])


def nsa_attention(xn, w_q, w_o, k_cmp, v_cmp, k_slc, v_slc, k_win, v_win):
    B, S, _ = xn.shape
    Hkv, G, d, Qb = NSA_KV_HEADS, NSA_GROUP, NSA_HEAD_DIM, NSA_Q_BLOCK
    proj = xn @ w_q
    q = proj[..., :NSA_Q_HEADS * d].reshape(B, S, Hkv, G, d) * (d ** -0.5)
    gates = jax.nn.sigmoid(proj[..., NSA_Q_HEADS * d:].astype(jnp.float32)).reshape(B, S, Hkv, G, N_BRANCH)
    n_cmp = k_cmp.shape[1]
    n_sel = S // SLC_BLOCK
    top_k = min(SLC_TOPK, n_sel)
    cmp_end = jnp.arange(n_cmp) * CMP_STRIDE + CMP_BLOCK - 1
    ks_blk = k_slc.reshape(B, n_sel, SLC_BLOCK, Hkv, d).transpose(0, 3, 1, 2, 4)
    vs_blk = v_slc.reshape(B, n_sel, SLC_BLOCK, Hkv, d).transpose(0, 3, 1, 2, 4)
    kw_pad = jnp.pad(k_win, ((0, 0), (WIN, 0), (0, 0), (0, 0)))
    vw_pad = jnp.pad(v_win, ((0, 0), (WIN, 0), (0, 0), (0, 0)))
    bi = jnp.arange(B)[:, None, None, None]
    hi = jnp.arange(Hkv)[None, None, :, None]
    j_sel = jnp.arange(n_sel)

    def block(start):
        qb = lax.dynamic_slice_in_dim(q, start, Qb, axis=1)
        gb = lax.dynamic_slice_in_dim(gates, start, Qb, axis=1)
        t = start + jnp.arange(Qb)
        s = jnp.einsum('bqhgd,bchd->bhgqc', qb, k_cmp).astype(jnp.float32)
        m_cmp = cmp_end[None, :] <= t[:, None]
        p_cmp = jax.nn.softmax(jnp.where(m_cmp, s, NEG), axis=-1) * m_cmp
        o_cmp = jnp.einsum('bhgqc,bchd->bqhgd', p_cmp.astype(v_cmp.dtype), v_cmp)
        imp = p_cmp.sum(axis=2)
        imp = jnp.pad(imp, ((0, 0), (0, 0), (0, 0), (CMP_RATIO - 1, CMP_RATIO - 1)))
        p_slc = imp[..., 0:SLC_RATIO * n_sel:SLC_RATIO]
        for r in range(1, SLC_RATIO + CMP_RATIO - 1):
            p_slc = p_slc + imp[..., r:r + SLC_RATIO * n_sel:SLC_RATIO]
        cur = t // SLC_BLOCK
        forced = (j_sel[None, :] == 0) | (j_sel[None, :] == cur[:, None]) | (j_sel[None, :] == cur[:, None] - 1)
        valid = j_sel[None, :] <= cur[:, None]
        sel_score = jnp.where(forced, SEL_FORCE, jnp.where(valid, p_slc, NEG))
        _, idx = lax.top_k(sel_score.transpose(0, 2, 1, 3), top_k)
        kg = ks_blk[bi, hi, idx].reshape(B, Qb, Hkv, top_k * SLC_BLOCK, d)
        vg = vs_blk[bi, hi, idx].reshape(B, Qb, Hkv, top_k * SLC_BLOCK, d)
        tok = (idx[..., None] * SLC_BLOCK + jnp.arange(SLC_BLOCK)).reshape(B, Qb, Hkv, top_k * SLC_BLOCK)
        m_slc = (tok <= t[None, :, None, None])[:, :, :, None, :]
        s = jnp.einsum('bqhgd,bqhkd->bqhgk', qb, kg).astype(jnp.float32)
        p = jax.nn.softmax(jnp.where(m_slc, s, NEG), axis=-1)
        o_slc = jnp.einsum('bqhgk,bqhkd->bqhgd', p.astype(vg.dtype), vg)
        kw = lax.dynamic_slice_in_dim(kw_pad, start, WIN + Qb, axis=1)
        vw = lax.dynamic_slice_in_dim(vw_pad, start, WIN + Qb, axis=1)
        pos = start - WIN + jnp.arange(WIN + Qb)
        m_win = (pos[None, :] <= t[:, None]) & (pos[None, :] > t[:, None] - WIN) & (pos[None, :] >= 0)
        s = jnp.einsum('bqhgd,bkhd->bhgqk', qb, kw).astype(jnp.float32)
        p = jax.nn.softmax(jnp.where(m_win, s, NEG), axis=-1)
        o_win = jnp.einsum('bhgqk,bkhd->bqhgd', p.astype(vw.dtype), vw)
        o = gb[..., 0:1] * o_cmp + gb[..., 1:2] * o_slc + gb[..., 2:3] * o_win
        return o.astype(xn.dtype)

    starts = jnp.arange(S // Qb) * Qb
    out = lax.map(block, starts)
    out = out.transpose(1, 0, 2, 3, 4, 5).reshape(B, S, NSA_Q_HEADS * d)
    return out @ w_o


def setup_inputs(seed: int = 0) -> dict:
    key = jax.random.key(seed)
    ks = jax.random.split(key, 24)
    f32 = jnp.float32
    d = NSA_HEAD_DIM

    def nrm(k, shape, scale):
        return jax.random.normal(k, shape, f32) * scale

    return {
        "x": nrm(ks[0], (BATCH, SEQ, D_MODEL), 1.0),
        "norm_mix_gain": 1.0 + nrm(ks[1], (DEPTH, D_MODEL), 0.02),
        "norm_ffn_gain": 1.0 + nrm(ks[2], (DEPTH, D_MODEL), 0.02),
        "ret_w_in": nrm(ks[3], (N_A_LAYERS, D_MODEL, RET_PROJ), D_MODEL ** -0.5),
        "ret_gn_gain": 1.0 + nrm(ks[4], (N_A_LAYERS, RET_HEADS * RET_V_DIM), 0.02),
        "ret_w_out": nrm(ks[5], (N_A_LAYERS, RET_HEADS * RET_V_DIM, D_MODEL), (RET_HEADS * RET_V_DIM) ** -0.5),
        "nsa_kv_norm_gain": 1.0 + nrm(ks[6], (D_MODEL,), 0.02),
        "nsa_w_kv": nrm(ks[7], (D_MODEL, NSA_KV_PROJ), D_MODEL ** -0.5),
        "cmp_pe_k": nrm(ks[8], (CMP_BLOCK, d), 0.1),
        "cmp_w1_k": nrm(ks[9], (CMP_BLOCK * d, CMP_HIDDEN), (CMP_BLOCK * d) ** -0.5),
        "cmp_w2_k": nrm(ks[10], (CMP_HIDDEN, d), CMP_HIDDEN ** -0.5),
        "cmp_pe_v": nrm(ks[11], (CMP_BLOCK, d), 0.1),
        "cmp_w1_v": nrm(ks[12], (CMP_BLOCK * d, CMP_HIDDEN), (CMP_BLOCK * d) ** -0.5),
        "cmp_w2_v": nrm(ks[13], (CMP_HIDDEN, d), CMP_HIDDEN ** -0.5),
        "nsa_w_q": nrm(ks[14], (N_B_LAYERS, D_MODEL, NSA_Q_PROJ), D_MODEL ** -0.5),
        "nsa_w_o": nrm(ks[15], (N_B_LAYERS, NSA_Q_HEADS * d, D_MODEL), (NSA_Q_HEADS * d) ** -0.5),
        "ffn_w_in": nrm(ks[16], (DEPTH, D_MODEL, 2 * D_FF), D_MODEL ** -0.5),
        "ffn_conv_w": nrm(ks[17], (DEPTH, CONV_W, 2 * D_FF), CONV_W ** -0.5),
        "ffn_conv_b": nrm(ks[18], (DEPTH, 2 * D_FF), 0.01),
        "ffn_w_out": nrm(ks[19], (DEPTH, D_FF, D_MODEL), D_FF ** -0.5),
        "final_norm_gain": 1.0 + nrm(ks[20], (D_MODEL,), 0.02),
    }


def reference(x, norm_mix_gain, norm_ffn_gain, ret_w_in, ret_gn_gain, ret_w_out, nsa_kv_norm_gain, nsa_w_kv,
              cmp_pe_k, cmp_w1_k, cmp_w2_k, cmp_pe_v, cmp_w1_v, cmp_w2_v, nsa_w_q, nsa_w_o,
              ffn_w_in, ffn_conv_w, ffn_conv_b, ffn_w_out, final_norm_gain):
    h = x
    shared = None
    for layer in range(DEPTH):
        xn = rmsnorm(h, norm_mix_gain[layer])
        if layer < N_A_LAYERS:
            h = h + retention(xn, ret_w_in[layer], ret_gn_gain[layer], ret_w_out[layer])
        else:
            if layer == N_A_LAYERS:
                shared = nsa_shared_kv(rmsnorm(h, nsa_kv_norm_gain), nsa_w_kv, cmp_pe_k, cmp_w1_k, cmp_w2_k,
                                       cmp_pe_v, cmp_w1_v, cmp_w2_v)
            b = layer - N_A_LAYERS
            h = h + nsa_attention(xn, nsa_w_q[b], nsa_w_o[b], *shared)
        h = h + conv_ffn(rmsnorm(h, norm_ffn_gain[layer]), ffn_w_in[layer], ffn_conv_w[layer],
                         ffn_conv_b[layer], ffn_w_out[layer])
    return rmsnorm(h, final_norm_gain)
```

```python
import numpy as np
from contextlib import ExitStack
import concourse.bass as bass
import concourse.mybir as mybir
from concourse.bass_utils import run_bass_kernel_spmd

F32 = mybir.dt.float32
BF16 = mybir.dt.bfloat16
AF = mybir.ActivationFunctionType
ALU = mybir.AluOpType
AX = mybir.AxisListType


class Eng:
    def __init__(self, fw, name, h, is_pe=False):
        self.fw = fw
        self.name = name
        self.h = h
        self.sem = fw.es.enter_context(fw.nc.semaphore("sem_" + name))
        self.count = 0
        self.pending = False
        self.waited = {}
        self.is_pe = is_pe

    def wait_tok(self, tok):
        if tok is None:
            return
        sem, val, eng = tok
        if eng is self and self.is_pe:
            return
        key = id(sem)
        if self.waited.get(key, 0) >= val:
            return
        if eng is not None and eng is not self and eng.count < val:
            eng.flush()
        if eng is self and self.count < val:
            self.flush()
        self.h.wait_ge(sem, val)
        self.waited[key] = val

    def flush(self):
        if self.pending:
            self.count += 1
            self.h.nop().then_inc(self.sem, 1)
            self.pending = False


class Buf:
    __slots__ = ("name", "w", "r")

    def __init__(self, name=""):
        self.name = name
        self.w = None
        self.r = []


class FW:
    def __init__(self, nc):
        self.nc = nc
        self.es = ExitStack()
        self.pe = Eng(self, "pe", nc.tensor, is_pe=True)
        self.dve = Eng(self, "dve", nc.vector)
        self.act = Eng(self, "act", nc.scalar)
        self.pool = Eng(self, "pool", nc.gpsimd)
        self.sp = Eng(self, "sp", nc.sync)
        self.engs = [self.pe, self.dve, self.act, self.pool, self.sp]
        self.dma_sems = []
        self.n_dma = 0
        self.NDMA = 6
        for q in range(2):
            lst = []
            for i in range(self.NDMA):
                s = self.es.enter_context(nc.semaphore(f"dsem{q}_{i}"))
                lst.append([s, 0])
            self.dma_sems.append(lst)
        self.dma_rr = [0, 0]
        self.out_toks = []

    def sbuf(self, name, shape, dt):
        return self.es.enter_context(self.nc.sbuf_tensor(name, shape, dt))

    def psum(self, name, shape, dt=F32):
        return self.es.enter_context(self.nc.psum_tensor(name, shape, dt))

    def op(self, eng, fn, reads=(), writes=(), inc=True):
        for b in reads:
            eng.wait_tok(b.w)
        for b in writes:
            eng.wait_tok(b.w)
            for t in b.r:
                eng.wait_tok(t)
        inst = fn(eng.h)
        if inc:
            eng.count += 1
            inst.then_inc(eng.sem, 1)
            eng.pending = False
            tok = (eng.sem, eng.count, eng)
        else:
            eng.pending = True
            tok = (eng.sem, eng.count + 1, eng)
        for b in reads:
            b.r.append(tok)
            if len(b.r) > 24:
                b.r = b.r[-24:]
        for b in writes:
            b.w = tok
            b.r = []
        return tok

    def dma(self, out, in_, reads=(), writes=(), q=0, is_output=False, **kw):
        eng = self.sp if q == 0 else self.pool
        slot = self.dma_sems[q][self.dma_rr[q] % self.NDMA]
        self.dma_rr[q] += 1
        sem, val = slot
        if val > 0:
            eng.wait_tok((sem, val, None))
        for b in reads:
            eng.wait_tok(b.w)
        for b in writes:
            eng.wait_tok(b.w)
            for t in b.r:
                eng.wait_tok(t)
        slot[1] = val + 16
        eng.h.dma_start(out=out, in_=in_, **kw).then_inc(sem, 16)
        tok = (sem, val + 16, None)
        for b in reads:
            b.r.append(tok)
        for b in writes:
            b.w = tok
            b.r = []
        if is_output:
            self.out_toks.append(tok)
        return tok

    def finish(self):
        for e in self.engs:
            e.flush()
        for q in range(2):
            for sem, val in self.dma_sems[q]:
                if val > 0:
                    self.sp.wait_tok((sem, val, None))
        self.es.close()


RMS_EPS = 1e-6


def build_stage_a(NTOK, SEQ):
    nc = bass.Bass("TRN2", target_bir_lowering=False)
    D = 2048
    KC = D // 128
    TT = 512
    xT = nc.dram_tensor("xT", [D, NTOK], F32, kind="ExternalInput").ap()
    gain = nc.dram_tensor("gain", [128, KC], F32, kind="ExternalInput").ap()
    wq = nc.dram_tensor("wq", [D, 256], F32, kind="ExternalInput").ap()
    wk = nc.dram_tensor("wk", [D, 256], F32, kind="ExternalInput").ap()
    wv = nc.dram_tensor("wv", [D, 512], F32, kind="ExternalInput").ap()
    wg = nc.dram_tensor("wg", [D, 512], F32, kind="ExternalInput").ap()
    cosT = nc.dram_tensor("cosT", [128, SEQ], F32, kind="ExternalInput").ap()
    sinT = nc.dram_tensor("sinT", [128, SEQ], F32, kind="ExternalInput").ap()
    decT = nc.dram_tensor("decT", [128, 128], F32, kind="ExternalInput").ap()
    qdec = nc.dram_tensor("qdec", [128, TT], F32, kind="ExternalInput").ap()
    kdec = nc.dram_tensor("kdec", [128, 1], F32, kind="ExternalInput").ap()
    cdec = nc.dram_tensor("cdec", [128, 1], F32, kind="ExternalInput").ap()
    gng = nc.dram_tensor("gng", [128, 512], F32, kind="ExternalInput").ap()
    ident = nc.dram_tensor("ident", [128, 128], F32, kind="ExternalInput").ap()
    y = nc.dram_tensor("y", [NTOK, 512], BF16, kind="ExternalOutput").ap()

    fw = FW(nc)
    S = fw.sbuf
    w_sb = S("w_sb", [128, KC, 1536], BF16)
    gain_sb = S("gain_sb", [128, KC], F32)
    decT_sb = S("decT_sb", [128, 128], F32)
    qdec_sb = S("qdec_sb", [128, TT], F32)
    kdec_sb = S("kdec_sb", [128, 1], F32)
    cdec_sb = S("cdec_sb", [128, 1], F32)
    gng_sb = S("gng_sb", [128, 512], F32)
    ident_sb = S("ident_sb", [128, 128], BF16)
    ones_sb = S("ones_sb", [128, 128], BF16)
    eps_sb = S("eps_sb", [128, 1], F32)
    HT = 256
    x_sb = [S(f"x_sb{i}", [128, KC, HT], F32) for i in range(2)]
    cs_sb = [S(f"cs_sb{i}", [128, 2, TT], F32) for i in range(2)]
    sq_sb = S("sq_sb", [128, KC, HT], BF16)
    xn_sbs = [S(f"xn_sb{i}", [128, KC, TT], BF16) for i in range(2)]
    rstd_sb = S("rstd_sb", [128, HT], F32)
    qT_sb = S("qT_sb", [128, 2, TT], BF16)
    kT_sb = S("kT_sb", [128, 2, TT], BF16)
    qdT_sb = S("qdT_sb", [128, 2, TT], BF16)
    tmp_sb = [S(f"tmp_sb{i}", [128, TT], F32) for i in range(4)]
    v_sb = [S(f"v_sb{i}", [128, 512], BF16) for i in range(2)]
    gs_sb = [S(f"gs_sb{i}", [128, 512], F32) for i in range(2)]
    pT_sb = [S(f"pT_sb{i}", [128, 128], BF16) for i in range(2)]
    kd_sb = [S(f"kd_sb{i}", [128, 256], BF16) for i in range(2)]
    st_sb = S("st_sb", [128, 2, 512], F32)
    stb_sb = S("stb_sb", [128, 2, 512], BF16)
    stat_sb = [S(f"stat_sb{i}", [128, 6], F32) for i in range(2)]
    mv_sb = [S(f"mv_sb{i}", [128, 2], F32) for i in range(2)]
    rs_sb = [S(f"rs_sb{i}", [128, 1], F32) for i in range(2)]
    t_sb = [S(f"t_sb{i}", [128, 512], F32) for i in range(2)]
    y_sb = [S(f"y_sb{i}", [128, 512], BF16) for i in range(2)]

    P = fw.psum
    ps_qk = [P(f"ps_qk{i}", [128, 512]) for i in range(2)]
    ps_vg = [P(f"ps_vg{i}", [128, 512]) for i in range(2)]
    ps_sc = P("ps_sc", [128, 512])
    ps_kt = P("ps_kt", [128, 1024], BF16)
    ps_out = P("ps_out", [128, 512])
    ps_su = P("ps_su", [128, 512])

    B = Buf
    b_w, b_const = B("w"), B("const")
    b_x = [B("x0"), B("x1")]
    b_cs = [B("cs0"), B("cs1")]
    b_sq, b_rstd, b_qT, b_kT, b_qdT = B(), B(), B(), B(), B()
    b_xns = [B(), B()]
    b_psss = B()
    b_tmp = [B() for _ in range(4)]
    b_v, b_gs, b_pT, b_kd = [B(), B()], [B(), B()], [B(), B()], [B(), B()]
    b_st, b_stb = B(), B()
    b_stat, b_mv, b_rs, b_t, b_y = [B(), B()], [B(), B()], [B(), B()], [B(), B()], [B(), B()]
    b_psqk, b_psvg = [B(), B()], [B(), B()]
    b_pssc, b_pskt, b_psout, b_pssu = B(), B(), B(), B()

    fw.dma(gain_sb[:], gain, writes=[b_const], q=0)
    fw.dma(decT_sb[:], decT, writes=[b_const], q=0)
    fw.dma(qdec_sb[:], qdec, writes=[b_const], q=0)
    fw.dma(kdec_sb[:], kdec, writes=[b_const], q=0)
    fw.dma(cdec_sb[:], cdec, writes=[b_const], q=0)
    fw.dma(gng_sb[:], gng, writes=[b_const], q=0)
    fw.dma(ident_sb[:], ident, writes=[b_const], q=1)
    off = 0
    for wsrc, n in ((wq, 256), (wk, 256), (wv, 512), (wg, 512)):
        for c0 in range(0, KC, 4):
            fw.dma(w_sb[:, c0:c0 + 4, off:off + n],
                   wsrc.rearrange("(c p) m -> p c m", p=128)[:, c0:c0 + 4, :], writes=[b_w], q=1)
        off += n
    fw.op(fw.dve, lambda e: e.memset(ones_sb[:], 1.0), writes=[b_const])
    fw.op(fw.dve, lambda e: e.memset(eps_sb[:], RMS_EPS), writes=[b_const])

    xT_v = xT.rearrange("(c p) t -> p c t", p=128)
    ntiles = NTOK // TT
    nhalves = NTOK // HT

    def load_half(hi):
        t0 = hi * HT
        bi = hi % 2
        fw.dma(x_sb[bi][:, 0:8, :], xT_v[:, 0:8, t0:t0 + HT], writes=[b_x[bi]], q=0)
        fw.dma(x_sb[bi][:, 8:16, :], xT_v[:, 8:16, t0:t0 + HT], writes=[b_x[bi]], q=0)

    def load_cs(ti):
        s0 = (ti * TT) % SEQ
        bi = ti % 2
        fw.dma(cs_sb[bi][:, 0, :], cosT[:, s0:s0 + TT], writes=[b_cs[bi]], q=0)
        fw.dma(cs_sb[bi][:, 1, :], sinT[:, s0:s0 + TT], writes=[b_cs[bi]], q=0)

    def rms_half(hi):
        bi = hi % 2
        ti = hi // 2
        hs = slice((hi % 2) * HT, (hi % 2 + 1) * HT)
        xs = x_sb[bi]
        xn = xn_sbs[ti % 2]
        b_xn = b_xns[ti % 2]
        fw.op(fw.act, lambda e: e.activation(out=sq_sb[:], in_=xs[:], func=AF.Square), reads=[b_x[bi]], writes=[b_sq])
        for c in range(KC):
            fw.op(fw.pe, lambda e: e.matmul(ps_sc[:, 256:512], ones_sb[:], sq_sb[:, c, :], start=(c == 0), stop=(c == KC - 1)),
                  reads=[b_sq, b_const], writes=[b_psss], inc=(c == KC - 1))
        fw.op(fw.act, lambda e: e.activation(out=rstd_sb[:], in_=ps_sc[:, 256:512], func=AF.Sqrt, bias=eps_sb[:], scale=1.0 / D),
              reads=[b_psss, b_const], writes=[b_rstd])
        fw.op(fw.dve, lambda e: e.reciprocal(out=rstd_sb[:], in_=rstd_sb[:]), reads=[b_rstd], writes=[b_rstd])
        for c in range(KC):
            fw.op(fw.dve, lambda e: e.scalar_tensor_tensor(out=xn[:, c, hs], in0=xs[:, c, :], scalar=gain_sb[:, c:c + 1],
                                                           in1=rstd_sb[:], op0=ALU.mult, op1=ALU.mult),
                  reads=[b_x[bi], b_rstd, b_const], writes=[b_xn])
        if hi + 2 < nhalves:
            load_half(hi + 2)

    load_half(0)
    load_half(1)
    load_cs(0)
    rms_half(0)
    rms_half(1)
    cidx = 0
    for ti in range(ntiles):
        bi = ti % 2
        t0 = ti * TT
        xn_sb = xn_sbs[ti % 2]
        b_xn = b_xns[ti % 2]
        if ti + 1 < ntiles:
            load_cs(ti + 1)
        for which, (dst, b_dst, woff, scl) in enumerate(((qT_sb, b_qT, 0, 1.0), (kT_sb, b_kT, 256, 1.0 / 16.0))):
            for m in range(2):
                for c in range(KC):
                    fw.op(fw.pe, lambda e: e.matmul(ps_qk[m][:], w_sb[:, c, woff + m * 128: woff + (m + 1) * 128], xn_sb[:, c, :],
                                                    start=(c == 0), stop=(c == KC - 1)),
                          reads=[b_w, b_xn], writes=[b_psqk[m]], inc=(c == KC - 1))
            cos_t, sin_t = cs_sb[bi][:, 0, :], cs_sb[bi][:, 1, :]
            D_ = fw.dve
            fw.op(D_, lambda e: e.scalar_tensor_tensor(out=tmp_sb[0][:], in0=ps_qk[0][:], scalar=scl, in1=cos_t, op0=ALU.mult, op1=ALU.mult),
                  reads=[b_psqk[0], b_cs[bi]], writes=[b_tmp[0]])
            fw.op(D_, lambda e: e.scalar_tensor_tensor(out=tmp_sb[1][:], in0=ps_qk[1][:], scalar=scl, in1=sin_t, op0=ALU.mult, op1=ALU.mult),
                  reads=[b_psqk[1], b_cs[bi]], writes=[b_tmp[1]])
            fw.op(D_, lambda e: e.scalar_tensor_tensor(out=tmp_sb[2][:], in0=ps_qk[0][:], scalar=scl, in1=sin_t, op0=ALU.mult, op1=ALU.mult),
                  reads=[b_psqk[0], b_cs[bi]], writes=[b_tmp[2]])
            fw.op(D_, lambda e: e.scalar_tensor_tensor(out=tmp_sb[3][:], in0=ps_qk[1][:], scalar=scl, in1=cos_t, op0=ALU.mult, op1=ALU.mult),
                  reads=[b_psqk[1], b_cs[bi]], writes=[b_tmp[3]])
            fw.op(fw.pool, lambda e: e.tensor_tensor(out=dst[:, 0, :], in0=tmp_sb[0][:], in1=tmp_sb[1][:], op=ALU.subtract),
                  reads=[b_tmp[0], b_tmp[1]], writes=[b_dst])
            fw.op(fw.pool, lambda e: e.tensor_tensor(out=dst[:, 1, :], in0=tmp_sb[2][:], in1=tmp_sb[3][:], op=ALU.add),
                  reads=[b_tmp[2], b_tmp[3]], writes=[b_dst])
        for m in range(2):
            fw.op(fw.pool, lambda e: e.tensor_tensor(out=qdT_sb[:, m, :], in0=qT_sb[:, m, :], in1=qdec_sb[:], op=ALU.mult),
                  reads=[b_qT, b_const], writes=[b_qdT])
        if ti + 1 < ntiles:
            rms_half(2 * (ti + 1))
            rms_half(2 * (ti + 1) + 1)
        for ch in range(TT // 128):
            tok0 = t0 + ch * 128
            cs = slice(ch * 128, (ch + 1) * 128)
            ci = cidx % 2
            cidx += 1
            if tok0 % SEQ == 0:
                fw.op(fw.dve, lambda e: e.memset(st_sb[:], 0.0), writes=[b_st])
                fw.op(fw.pool, lambda e: e.memset(stb_sb[:], 0.0), writes=[b_stb])
            for c in range(KC):
                fw.op(fw.pe, lambda e: e.matmul(ps_vg[0][:], xn_sb[:, c, cs], w_sb[:, c, 512:1024], start=(c == 0), stop=(c == KC - 1)),
                      reads=[b_w, b_xn], writes=[b_psvg[0]], inc=(c == KC - 1))
            fw.op(fw.act, lambda e: e.activation(out=v_sb[ci][:], in_=ps_vg[0][:], func=AF.Copy), reads=[b_psvg[0]], writes=[b_v[ci]])
            for c in range(KC):
                fw.op(fw.pe, lambda e: e.matmul(ps_vg[1][:], xn_sb[:, c, cs], w_sb[:, c, 1024:1536], start=(c == 0), stop=(c == KC - 1)),
                      reads=[b_w, b_xn], writes=[b_psvg[1]], inc=(c == KC - 1))
            fw.op(fw.act, lambda e: e.activation(out=gs_sb[ci][:], in_=ps_vg[1][:], func=AF.Silu), reads=[b_psvg[1]], writes=[b_gs[ci]])
            fw.op(fw.pool, lambda e: e.tensor_tensor(out=gs_sb[ci][:], in0=gs_sb[ci][:], in1=gng_sb[:], op=ALU.mult),
                  reads=[b_gs[ci], b_const], writes=[b_gs[ci]])
            for m in range(2):
                fw.op(fw.pe, lambda e: e.matmul(ps_sc[:, 0:128], kT_sb[:, m, cs], qT_sb[:, m, cs], start=(m == 0), stop=(m == 1)),
                      reads=[b_kT, b_qT], writes=[b_pssc], inc=(m == 1))
            fw.op(fw.dve, lambda e: e.tensor_tensor(out=pT_sb[ci][:], in0=ps_sc[:, 0:128], in1=decT_sb[:], op=ALU.mult),
                  reads=[b_pssc, b_const], writes=[b_pT[ci]])
            for m in range(2):
                fw.op(fw.pe, lambda e: e.transpose(ps_kt[:, m * 128:(m + 1) * 128], kT_sb[:, m, cs], ident_sb[:]),
                      reads=[b_kT, b_const], writes=[b_pskt], inc=(m == 1))
            fw.op(fw.dve, lambda e: e.tensor_scalar(out=kd_sb[ci][:], in0=ps_kt[:, 0:256], scalar1=kdec_sb[:], scalar2=None, op0=ALU.mult),
                  reads=[b_pskt, b_const], writes=[b_kd[ci]])
            fw.op(fw.pe, lambda e: e.matmul(ps_out[:], pT_sb[ci][:], v_sb[ci][:], start=True, stop=False),
                  reads=[b_pT[ci], b_v[ci]], writes=[b_psout], inc=False)
            for m in range(2):
                fw.op(fw.pe, lambda e: e.matmul(ps_out[:], qdT_sb[:, m, cs], stb_sb[:, m, :], start=False, stop=(m == 1)),
                      reads=[b_qdT, b_stb], writes=[b_psout], inc=(m == 1))
            for m in range(2):
                fw.op(fw.pe, lambda e: e.matmul(ps_su[:], kd_sb[ci][:, m * 128:(m + 1) * 128], v_sb[ci][:], start=True, stop=True),
                      reads=[b_kd[ci], b_v[ci]], writes=[b_pssu], inc=True)
                fw.op(fw.dve, lambda e: e.scalar_tensor_tensor(out=st_sb[:, m, :], in0=st_sb[:, m, :], scalar=cdec_sb[:], in1=ps_su[:], op0=ALU.mult, op1=ALU.add),
                      reads=[b_st, b_pssu, b_const], writes=[b_st])
            fw.op(fw.act, lambda e: e.activation(out=stb_sb[:], in_=st_sb[:], func=AF.Copy), reads=[b_st], writes=[b_stb])
            fw.op(fw.dve, lambda e: e.bn_stats(out=stat_sb[ci][:], in_=ps_out[:]), reads=[b_psout], writes=[b_stat[ci]])
            fw.op(fw.dve, lambda e: e.bn_aggr(out=mv_sb[ci][:], in_=stat_sb[ci][:]), reads=[b_stat[ci]], writes=[b_mv[ci]])
            fw.op(fw.act, lambda e: e.activation(out=rs_sb[ci][:], in_=mv_sb[ci][:, 1:2], func=AF.Sqrt, bias=eps_sb[:], scale=1.0),
                  reads=[b_mv[ci], b_const], writes=[b_rs[ci]])
            fw.op(fw.dve, lambda e: e.reciprocal(out=rs_sb[ci][:], in_=rs_sb[ci][:]), reads=[b_rs[ci]], writes=[b_rs[ci]])
            fw.op(fw.dve, lambda e: e.tensor_scalar(out=t_sb[ci][:], in0=ps_out[:], scalar1=mv_sb[ci][:, 0:1], scalar2=rs_sb[ci][:],
                                                    op0=ALU.subtract, op1=ALU.mult),
                  reads=[b_psout, b_mv[ci], b_rs[ci]], writes=[b_t[ci]])
            fw.op(fw.pool, lambda e: e.tensor_tensor(out=y_sb[ci][:], in0=t_sb[ci][:], in1=gs_sb[ci][:], op=ALU.mult),
                  reads=[b_t[ci], b_gs[ci]], writes=[b_y[ci]])
            fw.dma(y[tok0:tok0 + 128, :], y_sb[ci][:], reads=[b_y[ci]], q=0, is_output=True)
    fw.finish()
    return nc


def stage_a_consts(SEQ):
    H = 8
    half = 128
    freqs = (10000.0 ** (-np.arange(half, dtype=np.float32) / half)).astype(np.float32)
    pos = np.arange(SEQ, dtype=np.float32)
    ang = (freqs[:, None] * pos[None, :]).astype(np.float32)
    cosT = np.cos(ang).astype(np.float32)
    sinT = np.sin(ang).astype(np.float32)
    out = []
    idx = np.arange(128, dtype=np.float64)
    for h in range(H):
        lg = np.log(1.0 - 2.0 ** (-5.0 - h))
        diff = idx[None, :] - idx[:, None]
        decT = np.where(diff >= 0, np.exp(np.maximum(diff, 0) * lg), 0.0).astype(np.float32)
        qd = np.exp((idx + 1.0) * lg).astype(np.float32)
        kd = np.exp((127.0 - idx) * lg).astype(np.float32)
        cd = np.float32(np.exp(128.0 * lg))
        out.append(dict(
            cosT=cosT, sinT=sinT, decT=decT,
            qdec=np.ascontiguousarray(np.broadcast_to(np.tile(qd, 4)[None, :], (128, 512))).astype(np.float32),
            kdec=kd.reshape(128, 1).copy(), cdec=np.full((128, 1), cd, np.float32),
            ident=np.eye(128, dtype=np.float32)))
    return out


RMS_EPS = 1e-6
D = 2048
KC = 16
DFF = 4096


def build_stage_b(T, KA, variant, WDT=BF16):
    nc = bass.Bass("TRN2", target_bir_lowering=False)
    KAC = KA // 128
    TT = 512
    assert T % TT == 0
    di = lambda n, s, dt: nc.dram_tensor(n, s, dt, kind="ExternalInput").ap()
    do = lambda n, s, dt: nc.dram_tensor(n, s, dt, kind="ExternalOutput").ap()
    hT = di("hT", [D, 2 + T], F32)
    aT = di("aT", [KA, 2 + T], BF16)
    w_mix = di("w_mix", [16, 128, KAC * 128], WDT)
    w_in = di("w_in", [64, 128, KC * 128], WDT)
    w_out = di("w_out", [16, 128, 32 * 128], WDT)
    gains = di("gains", [128, 3, KC], F32)
    convw = di("convw", [128, 64, 4], F32)
    if variant == "mid":
        w_kv = di("w_kv", [24, 128, KC * 128], WDT)
        w_q = di("w_q", [17, 128, KC * 128], WDT)
        h2T = do("h2T", [D, T], F32)
        kvT = do("kvT", [24 * 128, T], BF16)
        qT = do("qT", [16 * 128, T], BF16)
        gT = do("gT", [128, T], F32)
    else:
        outT = do("outT", [D, T], F32)

    fw = FW(nc)
    S = fw.sbuf
    h_sb = S("h_sb", [128, KC, TT], F32)
    a_sb = S("a_sb", [128, 32, TT], BF16)
    xn_sb = S("xn_sb", [128, KC, TT], BF16)
    sq_sb = S("sq_sb", [128, KC, TT], BF16)
    act_sb = S("act_sb", [128, 32, TT], BF16)
    NW = 4
    w_sb = [S(f"w_sb{i}", [128, 32 * 128], WDT) for i in range(NW)]
    u_sb = [S(f"u_sb{i}", [128, 2 + TT], F32) for i in range(4)]
    c_sb = [S(f"c_sb{i}", [128, TT], F32) for i in range(4)]
    carry_sb = S("carry_sb", [128, 64, 2], F32)
    rstd_sb = S("rstd_sb", [128, TT], F32)
    gains_sb = S("gains_sb", [128, 3, KC], F32)
    convw_sb = S("convw_sb", [128, 64, 4], F32)
    ones_sb = S("ones_sb", [128, 128], BF16)
    eps_sb = S("eps_sb", [128, 1], F32)
    g_sb = S("g_sb", [128, TT], F32)

    P = fw.psum
    ps = [P(f"ps{i}", [128, 512]) for i in range(6)]
    ps_ss = P("ps_ss", [128, 512])
    B = Buf
    b_h, b_a, b_xn, b_sq, b_act, b_rstd, b_const, b_carry, b_g = B(), B(), B(), B(), B(), B(), B(), B(), B()
    b_w = [B() for _ in range(NW)]
    b_u = [B() for _ in range(4)]
    b_c = [B() for _ in range(4)]
    b_ps = [B() for _ in range(6)]
    b_psss = B()

    fw.dma(gains_sb[:], gains, writes=[b_const], q=0)
    fw.dma(convw_sb[:], convw, writes=[b_const], q=0)
    fw.op(fw.dve, lambda e: e.memset(ones_sb[:], 1.0), writes=[b_const])
    fw.op(fw.dve, lambda e: e.memset(eps_sb[:], RMS_EPS), writes=[b_const])

    hT_v = hT.rearrange("(c p) t -> p c t", p=128)
    aT_v = aT.rearrange("(c p) t -> p c t", p=128)
    wq_state = {"n": 0, "ps": 0}

    def load_w(src, m, kc):
        i = wq_state["n"] % NW
        wq_state["n"] += 1
        half = kc * 128 // 2
        fw.dma(w_sb[i][:, 0:half], src[m, :, 0:half], writes=[b_w[i]], q=0)
        fw.dma(w_sb[i][:, half:kc * 128], src[m, :, half:kc * 128], writes=[b_w[i]], q=0)
        return i

    def linear(src_w, M, kc, rhs_sb, b_rhs, n, epilogue, prefetch=2, mrows=None):
        loaded = []
        for m in range(min(prefetch, M)):
            loaded.append(load_w(src_w, m, kc))
        for m in range(M):
            if m + prefetch < M:
                loaded.append(load_w(src_w, m + prefetch, kc))
            wi = loaded[m]
            pi = wq_state["ps"] % 6
            wq_state["ps"] += 1
            for k in range(kc):
                fw.op(fw.pe, lambda e: e.matmul(ps[pi][:, 0:n], w_sb[wi][:, k * 128:(k + 1) * 128], rhs_sb[:, k, 0:n],
                                                start=(k == 0), stop=(k == kc - 1)),
                      reads=[b_w[wi], b_rhs], writes=[b_ps[pi]], inc=(k == kc - 1))
            epilogue(m, ps[pi], b_ps[pi])

    def rms(n, gi, dst_sb, b_dst, compute_rstd=True):
        if compute_rstd:
            fw.op(fw.act, lambda e: e.activation(out=sq_sb[:, :, 0:n], in_=h_sb[:, :, 0:n], func=AF.Square), reads=[b_h], writes=[b_sq])
            for c in range(KC):
                fw.op(fw.pe, lambda e: e.matmul(ps_ss[:, 0:n], ones_sb[:], sq_sb[:, c, 0:n], start=(c == 0), stop=(c == KC - 1)),
                      reads=[b_sq, b_const], writes=[b_psss], inc=(c == KC - 1))
            fw.op(fw.act, lambda e: e.activation(out=rstd_sb[:, 0:n], in_=ps_ss[:, 0:n], func=AF.Sqrt, bias=eps_sb[:], scale=1.0 / D),
                  reads=[b_psss, b_const], writes=[b_rstd])
            fw.op(fw.dve, lambda e: e.reciprocal(out=rstd_sb[:, 0:n], in_=rstd_sb[:, 0:n]), reads=[b_rstd], writes=[b_rstd])
        for c in range(KC):
            fw.op(fw.dve, lambda e: e.scalar_tensor_tensor(out=dst_sb[:, c, 0:n], in0=h_sb[:, c, 0:n], scalar=gains_sb[:, gi, c:c + 1],
                                                           in1=rstd_sb[:, 0:n], op0=ALU.mult, op1=ALU.mult),
                  reads=[b_h, b_rstd, b_const], writes=[b_dst])

    tiles = [(0, 2, True)] + [(2 + i * TT, TT, False) for i in range(T // TT)]
    for (c0, n, halo) in tiles:
        o0 = c0 - 2
        fw.dma(h_sb[:, 0:8, 0:n], hT_v[:, 0:8, c0:c0 + n], writes=[b_h], q=0)
        fw.dma(h_sb[:, 8:16, 0:n], hT_v[:, 8:16, c0:c0 + n], writes=[b_h], q=0)
        for k0 in range(0, KAC, 8):
            fw.dma(a_sb[:, k0:k0 + 8, 0:n], aT_v[:, k0:k0 + 8, c0:c0 + n], writes=[b_a], q=0)

        def ep1(m, p, bp):
            fw.op(fw.dve, lambda e: e.tensor_tensor(out=h_sb[:, m, 0:n], in0=h_sb[:, m, 0:n], in1=p[:, 0:n], op=ALU.add),
                  reads=[bp, b_h], writes=[b_h])
        linear(w_mix, 16, KAC, a_sb, b_a, n, ep1)
        rms(n, 0, xn_sb, b_xn)

        def ep3(mm, p, bp):
            j = mm // 2
            isb = mm % 2
            ch = j + 32 * isb
            ui = (mm % 4)
            U, bU = u_sb[ui], b_u[ui]
            C, bC = c_sb[ui], b_c[ui]
            fw.op(fw.pool, lambda e: e.tensor_copy(out=U[:, 0:2], in_=carry_sb[:, ch, :]), reads=[b_carry], writes=[bU])
            fw.op(fw.act, lambda e: e.activation(out=U[:, 2:2 + n], in_=p[:, 0:n], func=AF.Copy), reads=[bp], writes=[bU])
            fw.op(fw.pool, lambda e: e.tensor_copy(out=carry_sb[:, ch, :], in_=U[:, n:n + 2]), reads=[bU], writes=[b_carry])
            if halo:
                return
            cw = convw_sb
            fw.op(fw.dve, lambda e: e.tensor_scalar(out=C[:, 0:n], in0=U[:, 2:2 + n], scalar1=cw[:, ch, 2:3], scalar2=cw[:, ch, 3:4],
                                                    op0=ALU.mult, op1=ALU.add), reads=[bU, b_const], writes=[bC])
            fw.op(fw.dve, lambda e: e.scalar_tensor_tensor(out=C[:, 0:n], in0=U[:, 1:1 + n], scalar=cw[:, ch, 1:2], in1=C[:, 0:n],
                                                           op0=ALU.mult, op1=ALU.add), reads=[bU, bC, b_const], writes=[bC])
            fw.op(fw.dve, lambda e: e.scalar_tensor_tensor(out=C[:, 0:n], in0=U[:, 0:n], scalar=cw[:, ch, 0:1], in1=C[:, 0:n],
                                                           op0=ALU.mult, op1=ALU.add), reads=[bU, bC, b_const], writes=[bC])
            if isb == 0:
                fw.op(fw.act, lambda e: e.activation(out=C[:, 0:n], in_=C[:, 0:n], func=AF.Silu), reads=[bC], writes=[bC])
            else:
                Ca, bCa = c_sb[ui - 1], b_c[ui - 1]
                fw.op(fw.pool, lambda e: e.tensor_tensor(out=act_sb[:, j, 0:n], in0=Ca[:, 0:n], in1=C[:, 0:n], op=ALU.mult),
                      reads=[bC, bCa], writes=[b_act])

        class WIn:
            pass
        w_in_perm = w_in
        linear(w_in_perm, 64, KC, xn_sb, b_xn, n, ep3)
        if halo:
            continue

        linear(w_out, 16, 32, act_sb, b_act, n, ep1)

        if variant == "mid":
            fw.dma(h2T.rearrange("(c p) t -> p c t", p=128)[:, :, o0:o0 + n], h_sb[:, :, 0:n], reads=[b_h], q=0, is_output=True)
            rms(n, 1, xn_sb, b_xn)

            def ep_kv(m, p, bp):
                fw.op(fw.act, lambda e: e.activation(out=a_sb[:, m, 0:n], in_=p[:, 0:n], func=AF.Copy), reads=[bp], writes=[b_a])
            linear(w_kv, 24, KC, xn_sb, b_xn, n, ep_kv)
            fw.dma(kvT.rearrange("(c p) t -> p c t", p=128)[:, :, o0:o0 + n], a_sb[:, 0:24, 0:n], reads=[b_a], q=0, is_output=True)
            rms(n, 2, xn_sb, b_xn, compute_rstd=False)

            def ep_q(m, p, bp):
                if m < 16:
                    fw.op(fw.act, lambda e: e.activation(out=act_sb[:, m, 0:n], in_=p[:, 0:n], func=AF.Copy, scale=128.0 ** -0.5),
                          reads=[bp], writes=[b_act])
                else:
                    fw.op(fw.act, lambda e: e.activation(out=g_sb[:, 0:n], in_=p[:, 0:n], func=AF.Sigmoid), reads=[bp], writes=[b_g])
            linear(w_q, 17, KC, xn_sb, b_xn, n, ep_q)
            fw.dma(qT.rearrange("(c p) t -> p c t", p=128)[:, :, o0:o0 + n], act_sb[:, 0:16, 0:n], reads=[b_act], q=0, is_output=True)
            fw.dma(gT[:, o0:o0 + n], g_sb[:, 0:n], reads=[b_g], q=0, is_output=True)
        else:
            fw.op(fw.act, lambda e: e.activation(out=sq_sb[:, :, 0:n], in_=h_sb[:, :, 0:n], func=AF.Square), reads=[b_h], writes=[b_sq])
            for c in range(KC):
                fw.op(fw.pe, lambda e: e.matmul(ps_ss[:, 0:n], ones_sb[:], sq_sb[:, c, 0:n], start=(c == 0), stop=(c == KC - 1)),
                      reads=[b_sq, b_const], writes=[b_psss], inc=(c == KC - 1))
            fw.op(fw.act, lambda e: e.activation(out=rstd_sb[:, 0:n], in_=ps_ss[:, 0:n], func=AF.Sqrt, bias=eps_sb[:], scale=1.0 / D),
                  reads=[b_psss, b_const], writes=[b_rstd])
            fw.op(fw.dve, lambda e: e.reciprocal(out=rstd_sb[:, 0:n], in_=rstd_sb[:, 0:n]), reads=[b_rstd], writes=[b_rstd])
            for c in range(KC):
                fw.op(fw.dve, lambda e: e.scalar_tensor_tensor(out=h_sb[:, c, 0:n], in0=h_sb[:, c, 0:n], scalar=gains_sb[:, 1, c:c + 1],
                                                               in1=rstd_sb[:, 0:n], op0=ALU.mult, op1=ALU.mult),
                      reads=[b_h, b_rstd, b_const], writes=[b_h])
            fw.dma(outT.rearrange("(c p) t -> p c t", p=128)[:, :, o0:o0 + n], h_sb[:, :, 0:n], reads=[b_h], q=0, is_output=True)
    fw.finish()
    return nc


def lay_w(w, pad_to=None):
    K, M = w.shape
    if pad_to is not None and M < pad_to:
        w = np.concatenate([w, np.zeros((K, pad_to - M), w.dtype)], axis=1)
        M = pad_to
    return np.ascontiguousarray(w.reshape(K // 128, 128, M // 128, 128).transpose(2, 1, 0, 3)).reshape(M // 128, 128, K)


def lay_gain(g):
    return np.ascontiguousarray(g.reshape(16, 128).T)


NEGB = -30000.0


def build_stage_c(S):
    nc = bass.Bass("TRN2", target_bir_lowering=False)
    NQT = S // 128
    NC = S // 16
    NCT = (NC + 127) // 128
    NCP = NCT * 128
    NSEL = S // 64
    NBH = (NSEL + 127) // 128
    NBW = NBH * 128
    EW = min(NSEL, 128) * 64
    di = lambda n, s, dt: nc.dram_tensor(n, s, dt, kind="ExternalInput").ap()
    kcT = di("kcT", [128, S], BF16)
    vcT = di("vcT", [128, S], BF16)
    ksT = di("ksT", [128, S], BF16)
    kwT = di("kwT", [128, S], BF16)
    vs = di("vs", [S, 128], BF16)
    vw = di("vw", [S, 128], BF16)
    qT = di("qT", [NQT, 128, 512], BF16)
    gates = di("gates", [NQT, 128, 12], F32)
    w1 = [di("w1k", [128, 32, 256], F32), di("w1v", [128, 32, 256], F32)]
    peT = [di("peTk", [128, 32], F32), di("peTv", [128, 32], F32)]
    w2 = [di("w2k", [128, 2, 128], F32), di("w2v", [128, 2, 128], F32)]
    ident = di("ident", [128, 128], F32)
    ebig = di("ebig", [128, EW], F32)
    caus = di("caus", [128, 128], F32)
    low = di("low", [128, 128], F32)
    cposrow = di("cposrow", [128, NCP], F32)
    tcol = di("tcol", [128, NQT], F32)
    curcol = di("curcol", [128, NQT], F32)
    jrow = di("jrow", [128, NBW], F32)
    qrow = di("qrow", [128, 128], F32)
    cpq = di("cpq", [128, NCT, NQT], F32)
    o = nc.dram_tensor("o", [S, 512], BF16, kind="ExternalOutput").ap()

    fw = FW(nc)
    Sb = fw.sbuf
    B = Buf
    CBW = 16 * 512 + 32
    cbuf = Sb("cbuf", [128, CBW], BF16)
    w1_sb = Sb("w1_sb", [128, 32, 256], BF16)
    peT_sb = Sb("peT_sb", [128, 32], BF16)
    w2_sb = Sb("w2_sb", [128, 2, 128], BF16)
    hb_sb = Sb("hb_sb", [128, 2], F32)
    hx = [Sb(f"hx{i}", [128, 512], F32) for i in range(3)]
    hg_sb = Sb("hg_sb", [128, 2, 512], BF16)
    kcmpT = Sb("kcmpT", [128, NCP], BF16)
    vcmp = Sb("vcmp", [128, NCT, 129], BF16)
    ksT_sb = Sb("ksT_sb", [128, S], BF16)
    vs_sb = Sb("vs_sb", [128, NQT, 129], BF16)
    kw_sb = Sb("kw_sb", [128, 8, 128], BF16)
    vw_sb = Sb("vw_sb", [128, 8, 129], BF16)
    ident_sb = Sb("ident_sb", [128, 128], BF16)
    identf_sb = Sb("identf_sb", [128, 128], F32)
    ebig_sb = Sb("ebig_sb", [128, EW], BF16)
    caus_sb = Sb("caus_sb", [128, 128], BF16)
    low_sb = Sb("low_sb", [128, 128], BF16)
    cposrow_sb = Sb("cposrow_sb", [128, NCP], F32)
    tcol_sb = Sb("tcol_sb", [128, NQT], F32)
    curcol_sb = Sb("curcol_sb", [128, NQT], F32)
    jrow_sb = Sb("jrow_sb", [128, NBW], F32)
    qrow_sb = Sb("qrow_sb", [128, 128], F32)
    cpq_sb = Sb("cpq_sb", [128, NCT, NQT], F32)
    zero_sb = Sb("zero_sb", [128, 512], BF16)
    q_sb = [Sb(f"q_sb{i}", [128, 512], BF16) for i in range(2)]
    g_sb = [Sb(f"g_sb{i}", [128, 12], F32) for i in range(2)]
    bsel_sb = Sb("bsel_sb", [128, NCP], F32)
    s_sb = [Sb(f"s_sb{i}", [128, NCP], F32) for i in range(2)]
    e_sb = [Sb(f"e_sb{i}", [128, NCP], F32) for i in range(2)]
    sum_sb = [Sb(f"sum_sb{i}", [128, 1], F32) for i in range(2)]
    imp_sb = Sb("imp_sb", [128, NCP + 8], F32)
    psl_sb = Sb("psl_sb", [128, NBW], F32)
    d_sb = Sb("d_sb", [128, NBW], F32)
    val_sb = Sb("val_sb", [128, NBW], F32)
    fm_sb = Sb("fm_sb", [128, NBW], F32)
    sco_sb = Sb("sco_sb", [128, NBW], F32)
    wrk_sb = Sb("wrk_sb", [128, NBW], F32)
    m8_sb = Sb("m8_sb", [128, 16], F32)
    nsel_sb = Sb("nsel_sb", [128, NBW], F32)
    nselT_sb = Sb("nselT_sb", [128, NBH, 128], BF16)
    cb_sb = [Sb(f"cb_sb{i}", [128, 128], BF16) for i in range(2)]
    pT_sb = [Sb(f"pT_sb{i}", [128, 512], BF16) for i in range(3)]
    coef_sb = [Sb(f"coef_sb{i}", [128, 2], F32) for i in range(4)]
    otmp_sb = [Sb(f"otmp_sb{i}", [128, 2, 128], F32) for i in range(2)]
    oacc_sb = [Sb(f"oacc_sb{i}", [128, 4, 128], F32) for i in range(2)]
    oout_sb = [Sb(f"oout_sb{i}", [128, 512], BF16) for i in range(2)]

    P = fw.psum
    ps_s = [P(f"ps_s{i}", [128, 512]) for i in range(2)]
    ps_o = [P(f"ps_o{i}", [128, 2, 256]) for i in range(4)]
    ps_sel = [P(f"ps_sel{i}", [128, 512]) for i in range(2)]
    b_pss = [B(), B()]
    b_pso = [B() for _ in range(4)]
    b_pssel = [B(), B()]
    b_const, b_cbuf, b_w1, b_w2, b_hb, b_hg, b_kcmp, b_vcmp, b_ks, b_vs = [B() for _ in range(10)]
    b_hx = [B(), B(), B()]
    b_kw = [B() for _ in range(8)]
    b_vw = [B() for _ in range(8)]
    b_q, b_g = [B(), B()], [B(), B()]
    b_bsel, b_imp, b_psl, b_d, b_val, b_fm, b_sco, b_wrk, b_m8, b_nsel, b_nselT = [B() for _ in range(11)]
    b_s, b_e, b_sum = [B(), B()], [B(), B()], [B(), B()]
    b_cb = [B(), B()]
    b_pT = [B(), B(), B()]
    b_coef = [B() for _ in range(4)]
    b_otmp = [B(), B()]
    b_oacc, b_oout = [B(), B()], [B(), B()]

    for dst, src in ((ident_sb, ident), (ebig_sb, ebig), (caus_sb, caus), (low_sb, low)):
        fw.dma(dst[:], src, writes=[b_const], q=1)
    for dst, src in ((identf_sb, ident), (cposrow_sb, cposrow), (tcol_sb, tcol), (curcol_sb, curcol), (jrow_sb, jrow),
                     (qrow_sb, qrow), (cpq_sb, cpq)):
        fw.dma(dst[:], src, writes=[b_const], q=0)
    fw.op(fw.dve, lambda e: e.memset(zero_sb[:], 0.0), writes=[b_const])
    fw.op(fw.dve, lambda e: e.memset(vs_sb[:, :, 128:129], 1.0), writes=[b_vs])
    fw.op(fw.dve, lambda e: e.memset(vw_sb[:, :, 128:129], 1.0), writes=[b_vw[0]])
    fw.op(fw.dve, lambda e: e.memset(vcmp[:, :, 128:129], 1.0), writes=[b_vcmp])
    fw.op(fw.dve, lambda e: e.memset(imp_sb[:], 0.0), writes=[b_imp])
    for i in range(1, 8):
        b_vw[i].w = b_vw[0].w
    fw.dma(ksT_sb[:], ksT, writes=[b_ks], q=0)
    vs_v = vs.rearrange("(t p) d -> p t d", p=128)
    for t0 in range(0, NQT, 16):
        t1 = min(NQT, t0 + 16)
        fw.dma(vs_sb[:, t0:t1, 0:128], vs_v[:, t0:t1, :], writes=[b_vs], q=0)

    for kv in range(2):
        src = kcT if kv == 0 else vcT
        fw.dma(w1_sb[:, 0:16, :], w1[kv][:, 0:16, :], writes=[b_w1], q=1)
        fw.dma(w1_sb[:, 16:32, :], w1[kv][:, 16:32, :], writes=[b_w1], q=1)
        fw.dma(peT_sb[:], peT[kv], writes=[b_w1], q=1)
        fw.dma(w2_sb[:], w2[kv], writes=[b_w2], q=1)
        for mc in range(2):
            for p in range(32):
                fw.op(fw.pe, lambda e: e.matmul(ps_sel[0][:, mc:mc + 1], w1_sb[:, p, mc * 128:(mc + 1) * 128], peT_sb[:, p:p + 1],
                                                start=(p == 0), stop=(p == 31)),
                      reads=[b_w1], writes=[b_pssel[0]], inc=(p == 31))
        fw.op(fw.dve, lambda e: e.tensor_copy(out=hb_sb[:], in_=ps_sel[0][:, 0:2]), reads=[b_pssel[0]], writes=[b_hb])
        for cbk in range((NC + 511) // 512):
            nb = min(512, NC - 512 * cbk)
            tok0 = 8192 * cbk
            need = 16 * nb + 16
            avail = min(need, S - tok0)
            if avail < need:
                fw.op(fw.dve, lambda e: e.memset(cbuf[:, avail:need], 0.0), writes=[b_cbuf])
            fw.dma(cbuf[:, 0:avail], src[:, tok0:tok0 + avail], writes=[b_cbuf], q=0)
            for mc in range(2):
                pb = ps_sel[mc]
                for p in range(32):
                    fw.op(fw.pe, lambda e: e.matmul(pb[:, 0:nb], w1_sb[:, p, mc * 128:(mc + 1) * 128],
                                                    cbuf[:, p:p + 16 * nb].rearrange("d (i s) -> d i s", s=16)[:, :, 0],
                                                    start=(p == 0), stop=(p == 31)),
                          reads=[b_w1, b_cbuf], writes=[b_pssel[mc]], inc=(p == 31))
                X, X2, T3 = hx[0], hx[1], hx[2]
                fw.op(fw.act, lambda e: e.activation(out=X[:, 0:nb], in_=pb[:, 0:nb], func=AF.Identity, bias=hb_sb[:, mc:mc + 1], scale=1.0),
                      reads=[b_pssel[mc], b_hb], writes=[b_hx[0]])
                fw.op(fw.dve, lambda e: e.tensor_tensor(out=X2[:, 0:nb], in0=X[:, 0:nb], in1=X[:, 0:nb], op=ALU.mult),
                      reads=[b_hx[0]], writes=[b_hx[1]])
                fw.op(fw.dve, lambda e: e.tensor_scalar(out=X2[:, 0:nb], in0=X2[:, 0:nb], scalar1=0.044715, scalar2=1.0, op0=ALU.mult, op1=ALU.add),
                      reads=[b_hx[1]], writes=[b_hx[1]])
                fw.op(fw.dve, lambda e: e.tensor_tensor(out=X2[:, 0:nb], in0=X2[:, 0:nb], in1=X[:, 0:nb], op=ALU.mult),
                      reads=[b_hx[0], b_hx[1]], writes=[b_hx[1]])
                fw.op(fw.act, lambda e: e.activation(out=T3[:, 0:nb], in_=X2[:, 0:nb], func=AF.Sigmoid, scale=1.5957691216057308),
                      reads=[b_hx[1]], writes=[b_hx[2]])
                fw.op(fw.dve, lambda e: e.tensor_tensor(out=hg_sb[:, mc, 0:nb], in0=T3[:, 0:nb], in1=X[:, 0:nb], op=ALU.mult),
                      reads=[b_hx[0], b_hx[2]], writes=[b_hg])
            if kv == 0:
                for mc in range(2):
                    fw.op(fw.pe, lambda e: e.matmul(ps_sel[0][:, 0:nb], w2_sb[:, mc, :], hg_sb[:, mc, 0:nb], start=(mc == 0), stop=(mc == 1)),
                          reads=[b_w2, b_hg], writes=[b_pssel[0]], inc=(mc == 1))
                fw.op(fw.act, lambda e: e.activation(out=kcmpT[:, 512 * cbk:512 * cbk + nb], in_=ps_sel[0][:, 0:nb], func=AF.Copy),
                      reads=[b_pssel[0]], writes=[b_kcmp])
            else:
                for ct in range((nb + 127) // 128):
                    w_ = min(128, nb - ct * 128)
                    for mc in range(2):
                        fw.op(fw.pe, lambda e: e.matmul(ps_sel[0][0:w_, ct * 128:(ct + 1) * 128], hg_sb[:, mc, ct * 128:ct * 128 + w_], w2_sb[:, mc, :],
                                                        start=(mc == 0), stop=(mc == 1)),
                              reads=[b_w2, b_hg], writes=[b_pssel[0]], inc=(mc == 1))
                    fw.op(fw.act, lambda e: e.activation(out=vcmp[0:w_, 4 * cbk + ct, 0:128], in_=ps_sel[0][0:w_, ct * 128:(ct + 1) * 128], func=AF.Copy),
                          reads=[b_pssel[0]], writes=[b_vcmp])

    cnt = {"s": 0, "p": 0, "o": 0, "cb": 0}

    def load_q(qt):
        bi = qt % 2
        fw.dma(q_sb[bi][:], qT[qt], writes=[b_q[bi]], q=0)
        fw.dma(g_sb[bi][:], gates[qt], writes=[b_g[bi]], q=0)
        sl = qt % 8
        fw.dma(kw_sb[:, sl, :], kwT[:, qt * 128:(qt + 1) * 128], writes=[b_kw[sl]], q=0)
        fw.dma(vw_sb[:, sl, 0:128], vw[qt * 128:(qt + 1) * 128, :], writes=[b_vw[sl]], q=0)

    def attend(qt, tiles, br, first_branch):
        bi = qt % 2
        oset = cnt["o"] % 2
        cnt["o"] += 1
        po = [ps_o[2 * oset], ps_o[2 * oset + 1]]
        bpo = [b_pso[2 * oset], b_pso[2 * oset + 1]]
        for hb in range(2):
            fw.op(fw.pe, lambda e: e.matmul(po[hb][:, :, 0:129], zero_sb[:, 0:128], zero_sb[:, 0:258].rearrange("p (a b) -> p a b", a=2),
                                            start=True, stop=False, skip_group_check=True),
                  reads=[b_const], writes=[bpo[hb]], inc=False)
        nt = len(tiles)
        for ti, (kT_ap, b_k, va_ap, b_v, biases) in enumerate(tiles):
            si = cnt["s"] % 2
            cnt["s"] += 1
            pi = cnt["p"] % 3
            cnt["p"] += 1
            fw.op(fw.pe, lambda e: e.matmul(ps_s[si][:], kT_ap, q_sb[bi][:], start=True, stop=(len(biases) == 0)),
                  reads=[b_k, b_q[bi]], writes=[b_pss[si]], inc=(len(biases) == 0))
            for bj, (l_ap, r_ap, bufs) in enumerate(biases):
                last = bj == len(biases) - 1
                fw.op(fw.pe, lambda e: e.matmul(ps_s[si][:], l_ap, r_ap, start=False, stop=last),
                      reads=bufs, writes=[b_pss[si]], inc=last)
            fw.op(fw.act, lambda e: e.activation(out=pT_sb[pi][:], in_=ps_s[si][:], func=AF.Exp), reads=[b_pss[si]], writes=[b_pT[pi]])
            for g in range(4):
                lastmm = (ti == nt - 1)
                fw.op(fw.pe, lambda e: e.matmul(po[g // 2][:, g % 2, 0:129], pT_sb[pi][:, g * 128:(g + 1) * 128], va_ap,
                                                start=False, stop=lastmm, skip_group_check=True),
                      reads=[b_pT[pi], b_v], writes=[bpo[g // 2]], inc=(lastmm and g % 2 == 1) or (g == 3))
        for hb in range(2):
            ci = (cnt["o"] * 2 + hb) % 4
            cf, bcf = coef_sb[ci], b_coef[ci]
            fw.op(fw.dve, lambda e: e.tensor_scalar(out=cf[:], in0=po[hb][:, :, 128], scalar1=1e-30, scalar2=None, op0=ALU.max),
                  reads=[bpo[hb]], writes=[bcf])
            fw.op(fw.dve, lambda e: e.reciprocal(out=cf[:], in_=cf[:]), reads=[bcf], writes=[bcf])
            fw.op(fw.dve, lambda e: e.tensor_tensor(out=cf[:], in0=cf[:], in1=g_sb[bi][:, br * 4 + 2 * hb: br * 4 + 2 * hb + 2], op=ALU.mult),
                  reads=[bcf, b_g[bi]], writes=[bcf])
            dst = oacc_sb[bi][:, 2 * hb:2 * hb + 2, :]
            if first_branch:
                fw.op(fw.dve, lambda e: e.tensor_tensor(out=dst, in0=po[hb][:, :, 0:128], in1=cf[:].unsqueeze(2).to_broadcast([128, 2, 128]), op=ALU.mult),
                      reads=[bpo[hb], bcf], writes=[b_oacc[bi]])
            else:
                ot, bot = otmp_sb[hb], b_otmp[hb]
                fw.op(fw.dve, lambda e: e.tensor_tensor(out=ot[:], in0=po[hb][:, :, 0:128], in1=cf[:].unsqueeze(2).to_broadcast([128, 2, 128]), op=ALU.mult),
                      reads=[bpo[hb], bcf], writes=[bot])
                fw.op(fw.pool, lambda e: e.tensor_tensor(out=dst, in0=dst, in1=ot[:], op=ALU.add), reads=[bot, b_oacc[bi]], writes=[b_oacc[bi]])

    load_q(0)
    for qt in range(NQT):
        bi = qt % 2
        if qt + 1 < NQT:
            load_q(qt + 1)
        cmax = 8 * qt + 6
        ncw = min(NCP, ((cmax + 1 + 7) // 8) * 8)
        ncw = max(ncw, 8)
        nbw = min(NBW, max(16, ((2 * qt + 2 + 7) // 8) * 8))
        fw.op(fw.dve, lambda e: e.tensor_scalar(out=bsel_sb[:, 0:ncw], in0=cposrow_sb[:, 0:ncw], scalar1=tcol_sb[:, qt:qt + 1], scalar2=NEGB,
                                                op0=ALU.is_gt, op1=ALU.mult), reads=[b_const], writes=[b_bsel])
        for g in range(4):
            gi = g % 2
            for c0 in range(0, ncw, 512):
                c1 = min(ncw, c0 + 512)
                pb = ps_sel[c0 // 512]
                fw.op(fw.pe, lambda e: e.matmul(pb[:, 0:c1 - c0], q_sb[bi][:, g * 128:(g + 1) * 128], kcmpT[:, c0:c1], start=True, stop=True),
                      reads=[b_q[bi], b_kcmp], writes=[b_pssel[c0 // 512]], inc=True)
                fw.op(fw.dve, lambda e: e.tensor_tensor(out=s_sb[gi][:, c0:c1], in0=pb[:, 0:c1 - c0], in1=bsel_sb[:, c0:c1], op=ALU.add),
                      reads=[b_pssel[c0 // 512], b_bsel], writes=[b_s[gi]])
            fw.op(fw.act, lambda e: e.activation(out=e_sb[gi][:, 0:ncw], in_=s_sb[gi][:, 0:ncw], func=AF.Exp, accum_out=sum_sb[gi][:]),
                  reads=[b_s[gi]], writes=[b_e[gi], b_sum[gi]])
            fw.op(fw.dve, lambda e: e.tensor_scalar(out=sum_sb[gi][:], in0=sum_sb[gi][:], scalar1=1e-30, scalar2=None, op0=ALU.max),
                  reads=[b_sum[gi]], writes=[b_sum[gi]])
            fw.op(fw.dve, lambda e: e.reciprocal(out=sum_sb[gi][:], in_=sum_sb[gi][:]), reads=[b_sum[gi]], writes=[b_sum[gi]])
            if g == 0:
                fw.op(fw.dve, lambda e: e.tensor_scalar(out=imp_sb[:, 1:1 + ncw], in0=e_sb[gi][:, 0:ncw], scalar1=sum_sb[gi][:], scalar2=None, op0=ALU.mult),
                      reads=[b_e[gi], b_sum[gi]], writes=[b_imp])
            else:
                fw.op(fw.dve, lambda e: e.scalar_tensor_tensor(out=imp_sb[:, 1:1 + ncw], in0=e_sb[gi][:, 0:ncw], scalar=sum_sb[gi][:], in1=imp_sb[:, 1:1 + ncw],
                                                               op0=ALU.mult, op1=ALU.add), reads=[b_e[gi], b_sum[gi], b_imp], writes=[b_imp])
        if ncw < NCP:
            fw.op(fw.dve, lambda e: e.memset(imp_sb[:, 1 + ncw:min(NCP + 8, 1 + ncw + 8)], 0.0), writes=[b_imp])
        nbe = min(nbw, (ncw + 3) // 4)
        fw.op(fw.dve, lambda e: e.memset(psl_sb[:, 0:nbw], 0.0), writes=[b_psl])
        fw.op(fw.dve, lambda e: e.tensor_reduce(out=psl_sb[:, 0:nbe], in_=imp_sb[:, 0:4 * nbe].rearrange("p (a b) -> p a b", b=4), axis=AX.X, op=ALU.add),
              reads=[b_imp], writes=[b_psl])
        fw.op(fw.dve, lambda e: e.tensor_tensor(out=psl_sb[:, 0:nbe], in0=psl_sb[:, 0:nbe],
                                                in1=imp_sb[:, 4:4 + 4 * nbe].rearrange("p (a b) -> p a b", b=4)[:, :, 0], op=ALU.add),
              reads=[b_imp, b_psl], writes=[b_psl])
        fw.op(fw.dve, lambda e: e.tensor_scalar(out=d_sb[:, 0:nbw], in0=jrow_sb[:, 0:nbw], scalar1=curcol_sb[:, qt:qt + 1], scalar2=None, op0=ALU.subtract),
              reads=[b_const], writes=[b_d])
        fw.op(fw.dve, lambda e: e.tensor_scalar(out=val_sb[:, 0:nbw], in0=d_sb[:, 0:nbw], scalar1=0.0, scalar2=None, op0=ALU.is_le),
              reads=[b_d], writes=[b_val])
        fw.op(fw.dve, lambda e: e.scalar_tensor_tensor(out=fm_sb[:, 0:nbw], in0=d_sb[:, 0:nbw], scalar=-1.0, in1=val_sb[:, 0:nbw], op0=ALU.is_ge, op1=ALU.mult),
              reads=[b_d, b_val], writes=[b_fm])
        fw.op(fw.dve, lambda e: e.scalar_tensor_tensor(out=sco_sb[:, 0:nbw], in0=fm_sb[:, 0:nbw], scalar=100.0, in1=psl_sb[:, 0:nbw], op0=ALU.mult, op1=ALU.add),
              reads=[b_fm, b_psl], writes=[b_sco])
        fw.op(fw.dve, lambda e: e.tensor_scalar(out=sco_sb[:, 0:1], in0=sco_sb[:, 0:1], scalar1=100.0, scalar2=None, op0=ALU.add),
              reads=[b_sco], writes=[b_sco])
        fw.op(fw.dve, lambda e: e.max(out=m8_sb[:, 0:8], in_=sco_sb[:, 0:nbw]), reads=[b_sco], writes=[b_m8])
        fw.op(fw.dve, lambda e: e.match_replace(out=wrk_sb[:, 0:nbw], in_to_replace=m8_sb[:, 0:8], in_values=sco_sb[:, 0:nbw], imm_value=-1.0),
              reads=[b_sco, b_m8], writes=[b_wrk])
        fw.op(fw.dve, lambda e: e.max(out=m8_sb[:, 8:16], in_=wrk_sb[:, 0:nbw]), reads=[b_wrk, b_m8], writes=[b_m8])
        fw.op(fw.dve, lambda e: e.scalar_tensor_tensor(out=wrk_sb[:, 0:nbw], in0=sco_sb[:, 0:nbw], scalar=m8_sb[:, 15:16], in1=val_sb[:, 0:nbw],
                                                       op0=ALU.is_ge, op1=ALU.mult), reads=[b_sco, b_m8, b_val], writes=[b_wrk])
        if nbw < NBW:
            fw.op(fw.dve, lambda e: e.memset(nsel_sb[:, nbw:NBW], NEGB), writes=[b_nsel])
        fw.op(fw.dve, lambda e: e.tensor_scalar(out=nsel_sb[:, 0:nbw], in0=wrk_sb[:, 0:nbw], scalar1=1.0, scalar2=-NEGB, op0=ALU.subtract, op1=ALU.mult),
              reads=[b_wrk], writes=[b_nsel])
        nbh_used = (min(NSEL, 2 * qt + 2) + 127) // 128
        for hb in range(nbh_used):
            fw.op(fw.pe, lambda e: e.transpose(ps_sel[hb][:, 0:128], nsel_sb[:, hb * 128:(hb + 1) * 128], identf_sb[:]),
                  reads=[b_nsel, b_const], writes=[b_pssel[hb]], inc=True)
            fw.op(fw.act, lambda e: e.activation(out=nselT_sb[:, hb, :], in_=ps_sel[hb][:, 0:128], func=AF.Copy), reads=[b_pssel[hb]], writes=[b_nselT])

        tiles = []
        nct = cmax // 128 + 1
        for ct in range(min(nct, NCT)):
            biases = []
            if 16 * (128 * ct + 127) + 31 > 128 * qt or (ct == NCT - 1):
                ci = cnt["cb"] % 2
                cnt["cb"] += 1
                fw.op(fw.dve, lambda e: e.tensor_scalar(out=cb_sb[ci][:], in0=qrow_sb[:], scalar1=cpq_sb[:, ct, qt:qt + 1], scalar2=NEGB,
                                                        op0=ALU.is_lt, op1=ALU.mult), reads=[b_const], writes=[b_cb[ci]])
                biases.append((ident_sb[:], cb_sb[ci][:].unsqueeze(1).to_broadcast([128, 4, 128]), [b_const, b_cb[ci]]))
            tiles.append((kcmpT[:, ct * 128:(ct + 1) * 128], b_kcmp, vcmp[:, ct, :], b_vcmp, biases))
        attend(qt, tiles, 0, True)
        tiles = []
        for kt in range(qt + 1):
            hb = kt // 64
            ktp = kt % 64
            biases = [(ebig_sb[:, ktp * 128:(ktp + 1) * 128], nselT_sb[:, hb, :].unsqueeze(1).to_broadcast([128, 4, 128]), [b_const, b_nselT])]
            if kt == qt:
                biases.append((ident_sb[:], caus_sb[:].unsqueeze(1).to_broadcast([128, 4, 128]), [b_const]))
            tiles.append((ksT_sb[:, kt * 128:(kt + 1) * 128], b_ks, vs_sb[:, kt, :], b_vs, biases))
        attend(qt, tiles, 1, False)
        tiles = []
        for kt in range(max(0, qt - 4), qt + 1):
            sl = kt % 8
            biases = []
            if kt == qt:
                biases.append((ident_sb[:], caus_sb[:].unsqueeze(1).to_broadcast([128, 4, 128]), [b_const]))
            if kt == qt - 4:
                biases.append((ident_sb[:], low_sb[:].unsqueeze(1).to_broadcast([128, 4, 128]), [b_const]))
            tiles.append((kw_sb[:, sl, :], b_kw[sl], vw_sb[:, sl, :], b_vw[sl], biases))
        attend(qt, tiles, 2, False)
        fw.op(fw.act, lambda e: e.activation(out=oout_sb[bi][:], in_=oacc_sb[bi][:].rearrange("p a b -> p (a b)"), func=AF.Copy),
              reads=[b_oacc[bi]], writes=[b_oout[bi]])
        fw.dma(o[qt * 128:(qt + 1) * 128, :], oout_sb[bi][:], reads=[b_oout[bi]], q=0, is_output=True)
    fw.finish()
    return nc


def stage_c_consts(S):
    NQT = S // 128
    NC = S // 16
    NCT = (NC + 127) // 128
    NCP = NCT * 128
    NSEL = S // 64
    NBH = (NSEL + 127) // 128
    NBW = NBH * 128
    EW = min(NSEL, 128) * 64
    p = np.arange(128)
    c = np.arange(NCP)
    cpos = (16.0 * c + 31.0).astype(np.float32)
    cpos[NC - 1:] = 1e9
    kl = p[:, None]
    ql = p[None, :]
    cpq = np.zeros((128, NCT, NQT), np.float32)
    for ct in range(NCT):
        cpq[:, ct, :] = cpos[ct * 128 + p][:, None] - 128.0 * np.arange(NQT)[None, :]
    return dict(
        ident=np.eye(128, dtype=np.float32),
        ebig=(np.arange(EW)[None, :] // 64 == p[:, None]).astype(np.float32),
        caus=np.where(kl > ql, NEGB, 0.0).astype(np.float32),
        low=np.where(kl <= ql, NEGB, 0.0).astype(np.float32),
        cposrow=np.ascontiguousarray(np.broadcast_to(cpos[None, :], (128, NCP))),
        tcol=(128.0 * np.arange(NQT)[None, :] + p[:, None]).astype(np.float32),
        curcol=((128 * np.arange(NQT)[None, :] + p[:, None]) // 64).astype(np.float32),
        jrow=np.ascontiguousarray(np.broadcast_to(np.arange(NBW, dtype=np.float32)[None, :], (128, NBW))),
        qrow=np.ascontiguousarray(np.broadcast_to(np.arange(128, dtype=np.float32)[None, :], (128, 128))),
        cpq=cpq,
    )

import ml_dtypes

BF_NP = ml_dtypes.bfloat16
NCORES = 8
BATCH, SEQ, DM = 2, 16384, 2048
_CACHE = {}


def _prog(key, fn):
    if key not in _CACHE:
        _CACHE[key] = fn()
    return _CACHE[key]


def build_cast(ncols):
    nc = bass.Bass("TRN2", target_bir_lowering=False)
    src = nc.dram_tensor("src", [128, ncols], F32, kind="ExternalInput").ap()
    dst = nc.dram_tensor("dst", [128, ncols], BF16, kind="ExternalOutput").ap()
    fw = FW(nc)
    CH = 8192
    bufs = [fw.sbuf(f"cb{i}", [128, CH], BF16) for i in range(2)]
    bb = [Buf(), Buf()]
    for i, c0 in enumerate(range(0, ncols, CH)):
        c1 = min(ncols, c0 + CH)
        fw.dma(bufs[i % 2][:, 0:c1 - c0], src[:, c0:c1], writes=[bb[i % 2]], q=1, max_dma_last_dim=8192)
        fw.dma(dst[:, c0:c1], bufs[i % 2][:, 0:c1 - c0], reads=[bb[i % 2]], q=0, is_output=True)
    fw.finish()
    return nc


def _run(nc, in_maps):
    res = run_bass_kernel_spmd(nc, in_maps, core_ids=list(range(NCORES)))
    return res.results


def kernel(x, norm_mix_gain, norm_ffn_gain, ret_w_in, ret_gn_gain, ret_w_out, nsa_kv_norm_gain, nsa_w_kv,
           cmp_pe_k, cmp_w1_k, cmp_w2_k, cmp_pe_v, cmp_w1_v, cmp_w2_v, nsa_w_q, nsa_w_o,
           ffn_w_in, ffn_conv_w, ffn_conv_b, ffn_w_out, final_norm_gain):
    f32 = np.float32
    x = np.asarray(x, f32)
    NTOK = BATCH * SEQ
    T = NTOK // NCORES

    ncA = _prog("A", lambda: build_stage_a(NTOK, SEQ))
    ncB = _prog("Bmid", lambda: build_stage_b(T, 4096, "mid"))
    ncC = _prog("C", lambda: build_stage_c(SEQ))
    ncD = _prog("Bfin", lambda: build_stage_b(T, 2048, "final"))
    wlist = [("ret_w_out", np.asarray(ret_w_out[0], f32)), ("ffn_w_in0", np.asarray(ffn_w_in[0], f32)),
             ("ffn_w_in1", np.asarray(ffn_w_in[1], f32)), ("ffn_w_out0", np.asarray(ffn_w_out[0], f32)),
             ("ffn_w_out1", np.asarray(ffn_w_out[1], f32)), ("nsa_w_kv", np.asarray(nsa_w_kv, f32)),
             ("nsa_w_q", np.asarray(nsa_w_q[0], f32)), ("nsa_w_o", np.asarray(nsa_w_o[0], f32))]
    flat = np.concatenate([w.reshape(-1) for _, w in wlist])
    per = -(-flat.size // (NCORES * 128))
    per = -(-per // 64) * 64
    tot = per * NCORES * 128
    flat_p = np.zeros(tot, f32)
    flat_p[:flat.size] = flat
    flat_p = flat_p.reshape(NCORES, 128, per)
    nc0 = _prog(("cast", per), lambda: build_cast(per))
    r0 = _run(nc0, [{"src": flat_p[c]} for c in range(NCORES)])
    flat_b = np.concatenate([np.asarray(r["dst"]).reshape(-1) for r in r0])[:flat.size]
    wb = {}
    off = 0
    for name, w in wlist:
        wb[name] = flat_b[off:off + w.size].reshape(w.shape)
        off += w.size

    xT = np.ascontiguousarray(x.reshape(NTOK, DM).T)
    ncA = _prog("A", lambda: build_stage_a(NTOK, SEQ))
    cA = stage_a_consts(SEQ)
    w_in = np.asarray(ret_w_in[0], f32)
    H, dk, dv = 8, 256, 512
    gn = np.asarray(ret_gn_gain[0], f32)
    g_mix0 = lay_gain(np.asarray(norm_mix_gain[0], f32))
    maps = []
    for h in range(H):
        m = dict(cA[h])
        m["xT"] = xT
        m["gain"] = g_mix0
        m["wq"] = np.ascontiguousarray(w_in[:, h * dk:(h + 1) * dk])
        m["wk"] = np.ascontiguousarray(w_in[:, H * dk + h * dk: H * dk + (h + 1) * dk])
        m["wv"] = np.ascontiguousarray(w_in[:, 2 * H * dk + h * dv: 2 * H * dk + (h + 1) * dv])
        m["wg"] = np.ascontiguousarray(w_in[:, 2 * H * dk + H * dv + h * dv: 2 * H * dk + H * dv + (h + 1) * dv])
        m["gng"] = np.ascontiguousarray(np.broadcast_to(gn[h * dv:(h + 1) * dv][None, :], (128, 512)))
        maps.append(m)
    rA = _run(ncA, maps)
    yT = np.concatenate([np.asarray(r["y"]).T for r in rA], axis=0)
    del rA

    perm = np.stack([np.arange(32), np.arange(32) + 32], axis=1).reshape(-1)

    def permcols(w):
        return np.ascontiguousarray(w.reshape(w.shape[0], 64, 128)[:, perm, :].reshape(w.shape[0], 8192))

    def convw_of(layer):
        cw = np.asarray(ffn_conv_w[layer], f32)
        cb = np.asarray(ffn_conv_b[layer], f32)
        out = np.zeros((128, 64, 4), f32)
        out[:, :, 0:3] = cw.reshape(3, 64, 128).transpose(2, 1, 0)
        out[:, :, 3] = cb.reshape(64, 128).T
        return out

    def with_halo(fullT, c, dt):
        b, j = divmod(c, NCORES // BATCH)
        lo = b * SEQ + j * T
        out = np.zeros((fullT.shape[0], 2 + T), dt)
        out[:, 2:] = fullT[:, lo:lo + T]
        if j > 0:
            out[:, 0:2] = fullT[:, lo - 2:lo]
        return out

    ncB = _prog("Bmid", lambda: build_stage_b(T, 4096, "mid"))
    gains_mid = np.ascontiguousarray(np.stack([lay_gain(np.asarray(norm_ffn_gain[0], f32)), lay_gain(np.asarray(nsa_kv_norm_gain, f32)),
                                               lay_gain(np.asarray(norm_mix_gain[1], f32))], axis=1))
    common = dict(w_mix=lay_w(wb["ret_w_out"]), w_in=lay_w(permcols(wb["ffn_w_in0"])), w_out=lay_w(wb["ffn_w_out0"]),
                  gains=gains_mid, convw=convw_of(0), w_kv=lay_w(wb["nsa_w_kv"]), w_q=lay_w(wb["nsa_w_q"], pad_to=17 * 128))
    maps = []
    for c in range(NCORES):
        m = dict(common)
        m["hT"] = with_halo(xT, c, f32)
        m["aT"] = with_halo(yT, c, BF_NP)
        maps.append(m)
    rB = _run(ncB, maps)
    del yT, xT
    h2T = np.concatenate([r["h2T"] for r in rB], axis=1)
    kvT = np.concatenate([np.asarray(r["kvT"]) for r in rB], axis=1)
    qTf = np.concatenate([np.asarray(r["qT"]) for r in rB], axis=1)
    gTf = np.concatenate([r["gT"] for r in rB], axis=1)[:48]
    del rB

    ncC = _prog("C", lambda: build_stage_c(SEQ))
    cC = stage_c_consts(SEQ)
    NQT = SEQ // 128

    def w1lay(w):
        return np.ascontiguousarray(np.asarray(w, f32).reshape(32, 128, 256).transpose(1, 0, 2))

    def w2lay(w):
        return np.ascontiguousarray(np.asarray(w, f32).reshape(2, 128, 128).transpose(1, 0, 2))
    cw = dict(w1k=w1lay(cmp_w1_k), w1v=w1lay(cmp_w1_v), w2k=w2lay(cmp_w2_k), w2v=w2lay(cmp_w2_v),
              peTk=np.ascontiguousarray(np.asarray(cmp_pe_k, f32).T), peTv=np.ascontiguousarray(np.asarray(cmp_pe_v, f32).T))
    maps = []
    for b in range(BATCH):
        ts = slice(b * SEQ, (b + 1) * SEQ)
        for h in range(4):
            m = dict(cC)
            m.update(cw)
            row = lambda i: slice((i * 4 + h) * 128, (i * 4 + h + 1) * 128)
            m["kcT"] = np.ascontiguousarray(kvT[row(0), ts])
            m["vcT"] = np.ascontiguousarray(kvT[row(1), ts])
            m["ksT"] = np.ascontiguousarray(kvT[row(2), ts])
            m["vs"] = np.ascontiguousarray(kvT[row(3), ts].T)
            m["kwT"] = np.ascontiguousarray(kvT[row(4), ts])
            m["vw"] = np.ascontiguousarray(kvT[row(5), ts].T)
            qq = qTf[h * 512:(h + 1) * 512, ts].reshape(4, 128, NQT, 128)
            m["qT"] = np.ascontiguousarray(qq.transpose(2, 1, 0, 3)).reshape(NQT, 128, 512)
            gg = gTf[h * 12:(h + 1) * 12, ts].reshape(4, 3, NQT, 128)
            m["gates"] = np.ascontiguousarray(gg.transpose(2, 3, 1, 0)).reshape(NQT, 128, 12)
            maps.append(m)
    rC = _run(ncC, maps)
    del kvT, qTf, gTf
    oT = np.zeros((2048, NTOK), BF_NP)
    for b in range(BATCH):
        for h in range(4):
            oT[h * 512:(h + 1) * 512, b * SEQ:(b + 1) * SEQ] = np.asarray(rC[b * 4 + h]["o"]).T
    del rC

    ncD = _prog("Bfin", lambda: build_stage_b(T, 2048, "final"))
    gains_fin = np.ascontiguousarray(np.stack([lay_gain(np.asarray(norm_ffn_gain[1], f32)), lay_gain(np.asarray(final_norm_gain, f32)),
                                               lay_gain(np.asarray(final_norm_gain, f32))], axis=1))
    common = dict(w_mix=lay_w(wb["nsa_w_o"]), w_in=lay_w(permcols(wb["ffn_w_in1"])), w_out=lay_w(wb["ffn_w_out1"]),
                  gains=gains_fin, convw=convw_of(1))
    maps = []
    for c in range(NCORES):
        m = dict(common)
        m["hT"] = with_halo(h2T, c, f32)
        m["aT"] = with_halo(oT, c, BF_NP)
        maps.append(m)
    rD = _run(ncD, maps)
    outT = np.concatenate([r["outT"] for r in rD], axis=1)
    return np.ascontiguousarray(outT.T).reshape(BATCH, SEQ, DM).astype(f32)
```

```python
import numpy as np
from contextlib import ExitStack
import concourse.bass as bass
import concourse.mybir as mybir
from concourse.bass_utils import run_bass_kernel_spmd

F32 = mybir.dt.float32
BF16 = mybir.dt.bfloat16
AF = mybir.ActivationFunctionType
ALU = mybir.AluOpType
AX = mybir.AxisListType


class Eng:
    def __init__(self, fw, name, h, is_pe=False):
        self.fw = fw
        self.name = name
        self.h = h
        self.sem = fw.es.enter_context(fw.nc.semaphore("sem_" + name))
        self.count = 0
        self.pending = False
        self.waited = {}
        self.is_pe = is_pe

    def wait_tok(self, tok):
        if tok is None:
            return
        sem, val, eng = tok
        if eng is self and self.is_pe:
            return
        key = id(sem)
        if self.waited.get(key, 0) >= val:
            return
        if eng is not None and eng is not self and eng.count < val:
            eng.flush()
        if eng is self and self.count < val:
            self.flush()
        self.h.wait_ge(sem, val)
        self.waited[key] = val

    def flush(self):
        if self.pending:
            self.count += 1
            self.h.nop().then_inc(self.sem, 1)
            self.pending = False


class Buf:
    __slots__ = ("name", "w", "r")

    def __init__(self, name=""):
        self.name = name
        self.w = None
        self.r = []


class FW:
    def __init__(self, nc):
        self.nc = nc
        self.es = ExitStack()
        self.pe = Eng(self, "pe", nc.tensor, is_pe=True)
        self.dve = Eng(self, "dve", nc.vector)
        self.act = Eng(self, "act", nc.scalar)
        self.pool = Eng(self, "pool", nc.gpsimd)
        self.sp = Eng(self, "sp", nc.sync)
        self.engs = [self.pe, self.dve, self.act, self.pool, self.sp]
        self.dma_sems = []
        self.n_dma = 0
        self.NDMA = 6
        for q in range(2):
            lst = []
            for i in range(self.NDMA):
                s = self.es.enter_context(nc.semaphore(f"dsem{q}_{i}"))
                lst.append([s, 0])
            self.dma_sems.append(lst)
        self.dma_rr = [0, 0]
        self.out_toks = []

    def sbuf(self, name, shape, dt):
        return self.es.enter_context(self.nc.sbuf_tensor(name, shape, dt))

    def psum(self, name, shape, dt=F32):
        return self.es.enter_context(self.nc.psum_tensor(name, shape, dt))

    def op(self, eng, fn, reads=(), writes=(), inc=True):
        for b in reads:
            eng.wait_tok(b.w)
        for b in writes:
            eng.wait_tok(b.w)
            for t in b.r:
                eng.wait_tok(t)
        inst = fn(eng.h)
        if inc:
            eng.count += 1
            inst.then_inc(eng.sem, 1)
            eng.pending = False
            tok = (eng.sem, eng.count, eng)
        else:
            eng.pending = True
            tok = (eng.sem, eng.count + 1, eng)
        for b in reads:
            b.r.append(tok)
            if len(b.r) > 24:
                b.r = b.r[-24:]
        for b in writes:
            b.w = tok
            b.r = []
        return tok

    def dma(self, out, in_, reads=(), writes=(), q=0, is_output=False, **kw):
        eng = self.sp if q == 0 else self.pool
        slot = self.dma_sems[q][self.dma_rr[q] % self.NDMA]
        self.dma_rr[q] += 1
        sem, val = slot
        if val > 0:
            eng.wait_tok((sem, val, None))
        for b in reads:
            eng.wait_tok(b.w)
        for b in writes:
            eng.wait_tok(b.w)
            for t in b.r:
                eng.wait_tok(t)
        slot[1] = val + 16
        eng.h.dma_start(out=out, in_=in_, **kw).then_inc(sem, 16)
        tok = (sem, val + 16, None)
        for b in reads:
            b.r.append(tok)
        for b in writes:
            b.w = tok
            b.r = []
        if is_output:
            self.out_toks.append(tok)
        return tok

    def finish(self):
        for e in self.engs:
            e.flush()
        for q in range(2):
            for sem, val in self.dma_sems[q]:
                if val > 0:
                    self.sp.wait_tok((sem, val, None))
        self.es.close()


RMS_EPS = 1e-6


def build_stage_a(NTOK, SEQ):
    nc = bass.Bass("TRN2", target_bir_lowering=False)
    D = 2048
    KC = D // 128
    TT = 512
    xT = nc.dram_tensor("xT", [D, NTOK], F32, kind="ExternalInput").ap()
    gain = nc.dram_tensor("gain", [128, KC], F32, kind="ExternalInput").ap()
    wq = nc.dram_tensor("wq", [D, 256], F32, kind="ExternalInput").ap()
    wk = nc.dram_tensor("wk", [D, 256], F32, kind="ExternalInput").ap()
    wv = nc.dram_tensor("wv", [D, 512], F32, kind="ExternalInput").ap()
    wg = nc.dram_tensor("wg", [D, 512], F32, kind="ExternalInput").ap()
    cosT = nc.dram_tensor("cosT", [128, SEQ], F32, kind="ExternalInput").ap()
    sinT = nc.dram_tensor("sinT", [128, SEQ], F32, kind="ExternalInput").ap()
    decT = nc.dram_tensor("decT", [128, 128], F32, kind="ExternalInput").ap()
    qdec = nc.dram_tensor("qdec", [128, TT], F32, kind="ExternalInput").ap()
    kdec = nc.dram_tensor("kdec", [128, 1], F32, kind="ExternalInput").ap()
    cdec = nc.dram_tensor("cdec", [128, 1], F32, kind="ExternalInput").ap()
    gng = nc.dram_tensor("gng", [128, 512], F32, kind="ExternalInput").ap()
    ident = nc.dram_tensor("ident", [128, 128], F32, kind="ExternalInput").ap()
    y = nc.dram_tensor("y", [NTOK, 512], BF16, kind="ExternalOutput").ap()

    fw = FW(nc)
    S = fw.sbuf
    w_sb = S("w_sb", [128, KC, 1536], BF16)
    gain_sb = S("gain_sb", [128, KC], F32)
    decT_sb = S("decT_sb", [128, 128], F32)
    qdec_sb = S("qdec_sb", [128, TT], F32)
    kdec_sb = S("kdec_sb", [128, 1], F32)
    cdec_sb = S("cdec_sb", [128, 1], F32)
    gng_sb = S("gng_sb", [128, 512], F32)
    ident_sb = S("ident_sb", [128, 128], BF16)
    ones_sb = S("ones_sb", [128, 128], BF16)
    eps_sb = S("eps_sb", [128, 1], F32)
    HT = 256
    x_sb = [S(f"x_sb{i}", [128, KC, HT], F32) for i in range(2)]
    cs_sb = [S(f"cs_sb{i}", [128, 2, TT], F32) for i in range(2)]
    sq_sb = S("sq_sb", [128, KC, HT], BF16)
    xn_sbs = [S(f"xn_sb{i}", [128, KC, TT], BF16) for i in range(2)]
    rstd_sb = S("rstd_sb", [128, HT], F32)
    qT_sb = S("qT_sb", [128, 2, TT], BF16)
    kT_sb = S("kT_sb", [128, 2, TT], BF16)
    qdT_sb = S("qdT_sb", [128, 2, TT], BF16)
    tmp_sb = [S(f"tmp_sb{i}", [128, TT], F32) for i in range(4)]
    v_sb = [S(f"v_sb{i}", [128, 512], BF16) for i in range(2)]
    gs_sb = [S(f"gs_sb{i}", [128, 512], F32) for i in range(2)]
    pT_sb = [S(f"pT_sb{i}", [128, 128], BF16) for i in range(2)]
    kd_sb = [S(f"kd_sb{i}", [128, 256], BF16) for i in range(2)]
    st_sb = S("st_sb", [128, 2, 512], F32)
    stb_sb = S("stb_sb", [128, 2, 512], BF16)
    stat_sb = [S(f"stat_sb{i}", [128, 6], F32) for i in range(2)]
    mv_sb = [S(f"mv_sb{i}", [128, 2], F32) for i in range(2)]
    rs_sb = [S(f"rs_sb{i}", [128, 1], F32) for i in range(2)]
    t_sb = [S(f"t_sb{i}", [128, 512], F32) for i in range(2)]
    y_sb = [S(f"y_sb{i}", [128, 512], BF16) for i in range(2)]

    P = fw.psum
    ps_qk = [P(f"ps_qk{i}", [128, 512]) for i in range(2)]
    ps_vg = [P(f"ps_vg{i}", [128, 512]) for i in range(2)]
    ps_sc = P("ps_sc", [128, 512])
    ps_kt = P("ps_kt", [128, 1024], BF16)
    ps_out = P("ps_out", [128, 512])
    ps_su = P("ps_su", [128, 512])

    B = Buf
    b_w, b_const = B("w"), B("const")
    b_x = [B("x0"), B("x1")]
    b_cs = [B("cs0"), B("cs1")]
    b_sq, b_rstd, b_qT, b_kT, b_qdT = B(), B(), B(), B(), B()
    b_xns = [B(), B()]
    b_psss = B()
    b_tmp = [B() for _ in range(4)]
    b_v, b_gs, b_pT, b_kd = [B(), B()], [B(), B()], [B(), B()], [B(), B()]
    b_st, b_stb = B(), B()
    b_stat, b_mv, b_rs, b_t, b_y = [B(), B()], [B(), B()], [B(), B()], [B(), B()], [B(), B()]
    b_psqk, b_psvg = [B(), B()], [B(), B()]
    b_pssc, b_pskt, b_psout, b_pssu = B(), B(), B(), B()

    fw.dma(gain_sb[:], gain, writes=[b_const], q=0)
    fw.dma(decT_sb[:], decT, writes=[b_const], q=0)
    fw.dma(qdec_sb[:], qdec, writes=[b_const], q=0)
    fw.dma(kdec_sb[:], kdec, writes=[b_const], q=0)
    fw.dma(cdec_sb[:], cdec, writes=[b_const], q=0)
    fw.dma(gng_sb[:], gng, writes=[b_const], q=0)
    fw.dma(ident_sb[:], ident, writes=[b_const], q=1)
    off = 0
    for wsrc, n in ((wq, 256), (wk, 256), (wv, 512), (wg, 512)):
        for c0 in range(0, KC, 4):
            fw.dma(w_sb[:, c0:c0 + 4, off:off + n],
                   wsrc.rearrange("(c p) m -> p c m", p=128)[:, c0:c0 + 4, :], writes=[b_w], q=1)
        off += n
    fw.op(fw.dve, lambda e: e.memset(ones_sb[:], 1.0), writes=[b_const])
    fw.op(fw.dve, lambda e: e.memset(eps_sb[:], RMS_EPS), writes=[b_const])

    xT_v = xT.rearrange("(c p) t -> p c t", p=128)
    ntiles = NTOK // TT
    nhalves = NTOK // HT

    def load_half(hi):
        t0 = hi * HT
        bi = hi % 2
        fw.dma(x_sb[bi][:, 0:8, :], xT_v[:, 0:8, t0:t0 + HT], writes=[b_x[bi]], q=0)
        fw.dma(x_sb[bi][:, 8:16, :], xT_v[:, 8:16, t0:t0 + HT], writes=[b_x[bi]], q=0)

    def load_cs(ti):
        s0 = (ti * TT) % SEQ
        bi = ti % 2
        fw.dma(cs_sb[bi][:, 0, :], cosT[:, s0:s0 + TT], writes=[b_cs[bi]], q=0)
        fw.dma(cs_sb[bi][:, 1, :], sinT[:, s0:s0 + TT], writes=[b_cs[bi]], q=0)

    def rms_half(hi):
        bi = hi % 2
        ti = hi // 2
        hs = slice((hi % 2) * HT, (hi % 2 + 1) * HT)
        xs = x_sb[bi]
        xn = xn_sbs[ti % 2]
        b_xn = b_xns[ti % 2]
        fw.op(fw.act, lambda e: e.activation(out=sq_sb[:], in_=xs[:], func=AF.Square), reads=[b_x[bi]], writes=[b_sq])
        for c in range(KC):
            fw.op(fw.pe, lambda e: e.matmul(ps_sc[:, 256:512], ones_sb[:], sq_sb[:, c, :], start=(c == 0), stop=(c == KC - 1)),
                  reads=[b_sq, b_const], writes=[b_psss], inc=(c == KC - 1))
        fw.op(fw.act, lambda e: e.activation(out=rstd_sb[:], in_=ps_sc[:, 256:512], func=AF.Sqrt, bias=eps_sb[:], scale=1.0 / D),
              reads=[b_psss, b_const], writes=[b_rstd])
        fw.op(fw.dve, lambda e: e.reciprocal(out=rstd_sb[:], in_=rstd_sb[:]), reads=[b_rstd], writes=[b_rstd])
        for c in range(KC):
            fw.op(fw.dve, lambda e: e.scalar_tensor_tensor(out=xn[:, c, hs], in0=xs[:, c, :], scalar=gain_sb[:, c:c + 1],
                                                           in1=rstd_sb[:], op0=ALU.mult, op1=ALU.mult),
                  reads=[b_x[bi], b_rstd, b_const], writes=[b_xn])
        if hi + 2 < nhalves:
            load_half(hi + 2)

    load_half(0)
    load_half(1)
    load_cs(0)
    rms_half(0)
    rms_half(1)
    cidx = 0
    for ti in range(ntiles):
        bi = ti % 2
        t0 = ti * TT
        xn_sb = xn_sbs[ti % 2]
        b_xn = b_xns[ti % 2]
        if ti + 1 < ntiles:
            load_cs(ti + 1)
        for which, (dst, b_dst, woff, scl) in enumerate(((qT_sb, b_qT, 0, 1.0), (kT_sb, b_kT, 256, 1.0 / 16.0))):
            for m in range(2):
                for c in range(KC):
                    fw.op(fw.pe, lambda e: e.matmul(ps_qk[m][:], w_sb[:, c, woff + m * 128: woff + (m + 1) * 128], xn_sb[:, c, :],
                                                    start=(c == 0), stop=(c == KC - 1)),
                          reads=[b_w, b_xn], writes=[b_psqk[m]], inc=(c == KC - 1))
            cos_t, sin_t = cs_sb[bi][:, 0, :], cs_sb[bi][:, 1, :]
            D_ = fw.dve
            fw.op(D_, lambda e: e.scalar_tensor_tensor(out=tmp_sb[0][:], in0=ps_qk[0][:], scalar=scl, in1=cos_t, op0=ALU.mult, op1=ALU.mult),
                  reads=[b_psqk[0], b_cs[bi]], writes=[b_tmp[0]])
            fw.op(D_, lambda e: e.scalar_tensor_tensor(out=tmp_sb[1][:], in0=ps_qk[1][:], scalar=scl, in1=sin_t, op0=ALU.mult, op1=ALU.mult),
                  reads=[b_psqk[1], b_cs[bi]], writes=[b_tmp[1]])
            fw.op(D_, lambda e: e.scalar_tensor_tensor(out=tmp_sb[2][:], in0=ps_qk[0][:], scalar=scl, in1=sin_t, op0=ALU.mult, op1=ALU.mult),
                  reads=[b_psqk[0], b_cs[bi]], writes=[b_tmp[2]])
            fw.op(D_, lambda e: e.scalar_tensor_tensor(out=tmp_sb[3][:], in0=ps_qk[1][:], scalar=scl, in1=cos_t, op0=ALU.mult, op1=ALU.mult),
                  reads=[b_psqk[1], b_cs[bi]], writes=[b_tmp[3]])
            fw.op(fw.pool, lambda e: e.tensor_tensor(out=dst[:, 0, :], in0=tmp_sb[0][:], in1=tmp_sb[1][:], op=ALU.subtract),
                  reads=[b_tmp[0], b_tmp[1]], writes=[b_dst])
            fw.op(fw.pool, lambda e: e.tensor_tensor(out=dst[:, 1, :], in0=tmp_sb[2][:], in1=tmp_sb[3][:], op=ALU.add),
                  reads=[b_tmp[2], b_tmp[3]], writes=[b_dst])
        for m in range(2):
            fw.op(fw.pool, lambda e: e.tensor_tensor(out=qdT_sb[:, m, :], in0=qT_sb[:, m, :], in1=qdec_sb[:], op=ALU.mult),
                  reads=[b_qT, b_const], writes=[b_qdT])
        if ti + 1 < ntiles:
            rms_half(2 * (ti + 1))
            rms_half(2 * (ti + 1) + 1)
        for ch in range(TT // 128):
            tok0 = t0 + ch * 128
            cs = slice(ch * 128, (ch + 1) * 128)
            ci = cidx % 2
            cidx += 1
            if tok0 % SEQ == 0:
                fw.op(fw.dve, lambda e: e.memset(st_sb[:], 0.0), writes=[b_st])
                fw.op(fw.pool, lambda e: e.memset(stb_sb[:], 0.0), writes=[b_stb])
            for c in range(KC):
                fw.op(fw.pe, lambda e: e.matmul(ps_vg[0][:], xn_sb[:, c, cs], w_sb[:, c, 512:1024], start=(c == 0), stop=(c == KC - 1)),
                      reads=[b_w, b_xn], writes=[b_psvg[0]], inc=(c == KC - 1))
            fw.op(fw.act, lambda e: e.activation(out=v_sb[ci][:], in_=ps_vg[0][:], func=AF.Copy), reads=[b_psvg[0]], writes=[b_v[ci]])
            for c in range(KC):
                fw.op(fw.pe, lambda e: e.matmul(ps_vg[1][:], xn_sb[:, c, cs], w_sb[:, c, 1024:1536], start=(c == 0), stop=(c == KC - 1)),
                      reads=[b_w, b_xn], writes=[b_psvg[1]], inc=(c == KC - 1))
            fw.op(fw.act, lambda e: e.activation(out=gs_sb[ci][:], in_=ps_vg[1][:], func=AF.Silu), reads=[b_psvg[1]], writes=[b_gs[ci]])
            fw.op(fw.pool, lambda e: e.tensor_tensor(out=gs_sb[ci][:], in0=gs_sb[ci][:], in1=gng_sb[:], op=ALU.mult),
                  reads=[b_gs[ci], b_const], writes=[b_gs[ci]])
            for m in range(2):
                fw.op(fw.pe, lambda e: e.matmul(ps_sc[:, 0:128], kT_sb[:, m, cs], qT_sb[:, m, cs], start=(m == 0), stop=(m == 1)),
                      reads=[b_kT, b_qT], writes=[b_pssc], inc=(m == 1))
            fw.op(fw.dve, lambda e: e.tensor_tensor(out=pT_sb[ci][:], in0=ps_sc[:, 0:128], in1=decT_sb[:], op=ALU.mult),
                  reads=[b_pssc, b_const], writes=[b_pT[ci]])
            for m in range(2):
                fw.op(fw.pe, lambda e: e.transpose(ps_kt[:, m * 128:(m + 1) * 128], kT_sb[:, m, cs], ident_sb[:]),
                      reads=[b_kT, b_const], writes=[b_pskt], inc=(m == 1))
            fw.op(fw.dve, lambda e: e.tensor_scalar(out=kd_sb[ci][:], in0=ps_kt[:, 0:256], scalar1=kdec_sb[:], scalar2=None, op0=ALU.mult),
                  reads=[b_pskt, b_const], writes=[b_kd[ci]])
            fw.op(fw.pe, lambda e: e.matmul(ps_out[:], pT_sb[ci][:], v_sb[ci][:], start=True, stop=False),
                  reads=[b_pT[ci], b_v[ci]], writes=[b_psout], inc=False)
            for m in range(2):
                fw.op(fw.pe, lambda e: e.matmul(ps_out[:], qdT_sb[:, m, cs], stb_sb[:, m, :], start=False, stop=(m == 1)),
                      reads=[b_qdT, b_stb], writes=[b_psout], inc=(m == 1))
            for m in range(2):
                fw.op(fw.pe, lambda e: e.matmul(ps_su[:], kd_sb[ci][:, m * 128:(m + 1) * 128], v_sb[ci][:], start=True, stop=True),
                      reads=[b_kd[ci], b_v[ci]], writes=[b_pssu], inc=True)
                fw.op(fw.dve, lambda e: e.scalar_tensor_tensor(out=st_sb[:, m, :], in0=st_sb[:, m, :], scalar=cdec_sb[:], in1=ps_su[:], op0=ALU.mult, op1=ALU.add),
                      reads=[b_st, b_pssu, b_const], writes=[b_st])
            fw.op(fw.act, lambda e: e.activation(out=stb_sb[:], in_=st_sb[:], func=AF.Copy), reads=[b_st], writes=[b_stb])
            fw.op(fw.dve, lambda e: e.bn_stats(out=stat_sb[ci][:], in_=ps_out[:]), reads=[b_psout], writes=[b_stat[ci]])
            fw.op(fw.dve, lambda e: e.bn_aggr(out=mv_sb[ci][:], in_=stat_sb[ci][:]), reads=[b_stat[ci]], writes=[b_mv[ci]])
            fw.op(fw.act, lambda e: e.activation(out=rs_sb[ci][:], in_=mv_sb[ci][:, 1:2], func=AF.Sqrt, bias=eps_sb[:], scale=1.0),
                  reads=[b_mv[ci], b_const], writes=[b_rs[ci]])
            fw.op(fw.dve, lambda e: e.reciprocal(out=rs_sb[ci][:], in_=rs_sb[ci][:]), reads=[b_rs[ci]], writes=[b_rs[ci]])
            fw.op(fw.dve, lambda e: e.tensor_scalar(out=t_sb[ci][:], in0=ps_out[:], scalar1=mv_sb[ci][:, 0:1], scalar2=rs_sb[ci][:],
                                                    op0=ALU.subtract, op1=ALU.mult),
                  reads=[b_psout, b_mv[ci], b_rs[ci]], writes=[b_t[ci]])
            fw.op(fw.pool, lambda e: e.tensor_tensor(out=y_sb[ci][:], in0=t_sb[ci][:], in1=gs_sb[ci][:], op=ALU.mult),
                  reads=[b_t[ci], b_gs[ci]], writes=[b_y[ci]])
            fw.dma(y[tok0:tok0 + 128, :], y_sb[ci][:], reads=[b_y[ci]], q=0, is_output=True)
    fw.finish()
    return nc


def stage_a_consts(SEQ):
    H = 8
    half = 128
    freqs = (10000.0 ** (-np.arange(half, dtype=np.float32) / half)).astype(np.float32)
    pos = np.arange(SEQ, dtype=np.float32)
    ang = (freqs[:, None] * pos[None, :]).astype(np.float32)
    cosT = np.cos(ang).astype(np.float32)
    sinT = np.sin(ang).astype(np.float32)
    out = []
    idx = np.arange(128, dtype=np.float64)
    for h in range(H):
        lg = np.log(1.0 - 2.0 ** (-5.0 - h))
        diff = idx[None, :] - idx[:, None]
        decT = np.where(diff >= 0, np.exp(np.maximum(diff, 0) * lg), 0.0).astype(np.float32)
        qd = np.exp((idx + 1.0) * lg).astype(np.float32)
        kd = np.exp((127.0 - idx) * lg).astype(np.float32)
        cd = np.float32(np.exp(128.0 * lg))
        out.append(dict(
            cosT=cosT, sinT=sinT, decT=decT,
            qdec=np.ascontiguousarray(np.broadcast_to(np.tile(qd, 4)[None, :], (128, 512))).astype(np.float32),
            kdec=kd.reshape(128, 1).copy(), cdec=np.full((128, 1), cd, np.float32),
            ident=np.eye(128, dtype=np.float32)))
    return out


RMS_EPS = 1e-6
D = 2048
KC = 16
DFF = 4096


def build_stage_b(T, KA, variant, WDT=BF16):
    nc = bass.Bass("TRN2", target_bir_lowering=False)
    KAC = KA // 128
    TT = 512
    assert T % TT == 0
    di = lambda n, s, dt: nc.dram_tensor(n, s, dt, kind="ExternalInput").ap()
    do = lambda n, s, dt: nc.dram_tensor(n, s, dt, kind="ExternalOutput").ap()
    hT = di("hT", [D, 2 + T], F32)
    aT = di("aT", [KA, 2 + T], BF16)
    w_mix = di("w_mix", [16, 128, KAC * 128], WDT)
    w_in = di("w_in", [64, 128, KC * 128], WDT)
    w_out = di("w_out", [16, 128, 32 * 128], WDT)
    gains = di("gains", [128, 3, KC], F32)
    convw = di("convw", [128, 64, 4], F32)
    if variant == "mid":
        w_kv = di("w_kv", [24, 128, KC * 128], WDT)
        w_q = di("w_q", [17, 128, KC * 128], WDT)
        h2T = do("h2T", [D, T], F32)
        kvT = do("kvT", [24 * 128, T], BF16)
        qT = do("qT", [16 * 128, T], BF16)
        gT = do("gT", [128, T], F32)
    else:
        outT = do("outT", [D, T], F32)

    fw = FW(nc)
    S = fw.sbuf
    h_sb = S("h_sb", [128, KC, TT], F32)
    a_sb = S("a_sb", [128, 32, TT], BF16)
    xn_sb = S("xn_sb", [128, KC, TT], BF16)
    sq_sb = S("sq_sb", [128, KC, TT], BF16)
    act_sb = S("act_sb", [128, 32, TT], BF16)
    NW = 4
    w_sb = [S(f"w_sb{i}", [128, 32 * 128], WDT) for i in range(NW)]
    u_sb = [S(f"u_sb{i}", [128, 2 + TT], F32) for i in range(4)]
    c_sb = [S(f"c_sb{i}", [128, TT], F32) for i in range(4)]
    carry_sb = S("carry_sb", [128, 64, 2], F32)
    rstd_sb = S("rstd_sb", [128, TT], F32)
    gains_sb = S("gains_sb", [128, 3, KC], F32)
    convw_sb = S("convw_sb", [128, 64, 4], F32)
    ones_sb = S("ones_sb", [128, 128], BF16)
    eps_sb = S("eps_sb", [128, 1], F32)
    g_sb = S("g_sb", [128, TT], F32)

    P = fw.psum
    ps = [P(f"ps{i}", [128, 512]) for i in range(6)]
    ps_ss = P("ps_ss", [128, 512])
    B = Buf
    b_h, b_a, b_xn, b_sq, b_act, b_rstd, b_const, b_carry, b_g = B(), B(), B(), B(), B(), B(), B(), B(), B()
    b_w = [B() for _ in range(NW)]
    b_u = [B() for _ in range(4)]
    b_c = [B() for _ in range(4)]
    b_ps = [B() for _ in range(6)]
    b_psss = B()

    fw.dma(gains_sb[:], gains, writes=[b_const], q=0)
    fw.dma(convw_sb[:], convw, writes=[b_const], q=0)
    fw.op(fw.dve, lambda e: e.memset(ones_sb[:], 1.0), writes=[b_const])
    fw.op(fw.dve, lambda e: e.memset(eps_sb[:], RMS_EPS), writes=[b_const])

    hT_v = hT.rearrange("(c p) t -> p c t", p=128)
    aT_v = aT.rearrange("(c p) t -> p c t", p=128)
    wq_state = {"n": 0, "ps": 0}

    def load_w(src, m, kc):
        i = wq_state["n"] % NW
        wq_state["n"] += 1
        half = kc * 128 // 2
        fw.dma(w_sb[i][:, 0:half], src[m, :, 0:half], writes=[b_w[i]], q=0)
        fw.dma(w_sb[i][:, half:kc * 128], src[m, :, half:kc * 128], writes=[b_w[i]], q=0)
        return i

    def linear(src_w, M, kc, rhs_sb, b_rhs, n, epilogue, prefetch=2, mrows=None):
        loaded = []
        for m in range(min(prefetch, M)):
            loaded.append(load_w(src_w, m, kc))
        for m in range(M):
            if m + prefetch < M:
                loaded.append(load_w(src_w, m + prefetch, kc))
            wi = loaded[m]
            pi = wq_state["ps"] % 6
            wq_state["ps"] += 1
            for k in range(kc):
                fw.op(fw.pe, lambda e: e.matmul(ps[pi][:, 0:n], w_sb[wi][:, k * 128:(k + 1) * 128], rhs_sb[:, k, 0:n],
                                                start=(k == 0), stop=(k == kc - 1)),
                      reads=[b_w[wi], b_rhs], writes=[b_ps[pi]], inc=(k == kc - 1))
            epilogue(m, ps[pi], b_ps[pi])

    def rms(n, gi, dst_sb, b_dst, compute_rstd=True):
        if compute_rstd:
            fw.op(fw.act, lambda e: e.activation(out=sq_sb[:, :, 0:n], in_=h_sb[:, :, 0:n], func=AF.Square), reads=[b_h], writes=[b_sq])
            for c in range(KC):
                fw.op(fw.pe, lambda e: e.matmul(ps_ss[:, 0:n], ones_sb[:], sq_sb[:, c, 0:n], start=(c == 0), stop=(c == KC - 1)),
                      reads=[b_sq, b_const], writes=[b_psss], inc=(c == KC - 1))
            fw.op(fw.act, lambda e: e.activation(out=rstd_sb[:, 0:n], in_=ps_ss[:, 0:n], func=AF.Sqrt, bias=eps_sb[:], scale=1.0 / D),
                  reads=[b_psss, b_const], writes=[b_rstd])
            fw.op(fw.dve, lambda e: e.reciprocal(out=rstd_sb[:, 0:n], in_=rstd_sb[:, 0:n]), reads=[b_rstd], writes=[b_rstd])
        for c in range(KC):
            fw.op(fw.dve, lambda e: e.scalar_tensor_tensor(out=dst_sb[:, c, 0:n], in0=h_sb[:, c, 0:n], scalar=gains_sb[:, gi, c:c + 1],
                                                           in1=rstd_sb[:, 0:n], op0=ALU.mult, op1=ALU.mult),
                  reads=[b_h, b_rstd, b_const], writes=[b_dst])

    tiles = [(0, 2, True)] + [(2 + i * TT, TT, False) for i in range(T // TT)]
    for (c0, n, halo) in tiles:
        o0 = c0 - 2
        fw.dma(h_sb[:, 0:8, 0:n], hT_v[:, 0:8, c0:c0 + n], writes=[b_h], q=0)
        fw.dma(h_sb[:, 8:16, 0:n], hT_v[:, 8:16, c0:c0 + n], writes=[b_h], q=0)
        for k0 in range(0, KAC, 8):
            fw.dma(a_sb[:, k0:k0 + 8, 0:n], aT_v[:, k0:k0 + 8, c0:c0 + n], writes=[b_a], q=0)

        def ep1(m, p, bp):
            fw.op(fw.dve, lambda e: e.tensor_tensor(out=h_sb[:, m, 0:n], in0=h_sb[:, m, 0:n], in1=p[:, 0:n], op=ALU.add),
                  reads=[bp, b_h], writes=[b_h])
        linear(w_mix, 16, KAC, a_sb, b_a, n, ep1)
        rms(n, 0, xn_sb, b_xn)

        def ep3(mm, p, bp):
            j = mm // 2
            isb = mm % 2
            ch = j + 32 * isb
            ui = (mm % 4)
            U, bU = u_sb[ui], b_u[ui]
            C, bC = c_sb[ui], b_c[ui]
            fw.op(fw.pool, lambda e: e.tensor_copy(out=U[:, 0:2], in_=carry_sb[:, ch, :]), reads=[b_carry], writes=[bU])
            fw.op(fw.act, lambda e: e.activation(out=U[:, 2:2 + n], in_=p[:, 0:n], func=AF.Copy), reads=[bp], writes=[bU])
            fw.op(fw.pool, lambda e: e.tensor_copy(out=carry_sb[:, ch, :], in_=U[:, n:n + 2]), reads=[bU], writes=[b_carry])
            if halo:
                return
            cw = convw_sb
            fw.op(fw.dve, lambda e: e.tensor_scalar(out=C[:, 0:n], in0=U[:, 2:2 + n], scalar1=cw[:, ch, 2:3], scalar2=cw[:, ch, 3:4],
                                                    op0=ALU.mult, op1=ALU.add), reads=[bU, b_const], writes=[bC])
            fw.op(fw.dve, lambda e: e.scalar_tensor_tensor(out=C[:, 0:n], in0=U[:, 1:1 + n], scalar=cw[:, ch, 1:2], in1=C[:, 0:n],
                                                           op0=ALU.mult, op1=ALU.add), reads=[bU, bC, b_const], writes=[bC])
            fw.op(fw.dve, lambda e: e.scalar_tensor_tensor(out=C[:, 0:n], in0=U[:, 0:n], scalar=cw[:, ch, 0:1], in1=C[:, 0:n],
                                                           op0=ALU.mult, op1=ALU.add), reads=[bU, bC, b_const], writes=[bC])
            if isb == 0:
                fw.op(fw.act, lambda e: e.activation(out=C[:, 0:n], in_=C[:, 0:n], func=AF.Silu), reads=[bC], writes=[bC])
            else:
                Ca, bCa = c_sb[ui - 1], b_c[ui - 1]
                fw.op(fw.pool, lambda e: e.tensor_tensor(out=act_sb[:, j, 0:n], in0=Ca[:, 0:n], in1=C[:, 0:n], op=ALU.mult),
                      reads=[bC, bCa], writes=[b_act])

        class WIn:
            pass
        w_in_perm = w_in
        linear(w_in_perm, 64, KC, xn_sb, b_xn, n, ep3)
        if halo:
            continue

        linear(w_out, 16, 32, act_sb, b_act, n, ep1)

        if variant == "mid":
            fw.dma(h2T.rearrange("(c p) t -> p c t", p=128)[:, :, o0:o0 + n], h_sb[:, :, 0:n], reads=[b_h], q=0, is_output=True)
            rms(n, 1, xn_sb, b_xn)

            def ep_kv(m, p, bp):
                fw.op(fw.act, lambda e: e.activation(out=a_sb[:, m, 0:n], in_=p[:, 0:n], func=AF.Copy), reads=[bp], writes=[b_a])
            linear(w_kv, 24, KC, xn_sb, b_xn, n, ep_kv)
            fw.dma(kvT.rearrange("(c p) t -> p c t", p=128)[:, :, o0:o0 + n], a_sb[:, 0:24, 0:n], reads=[b_a], q=0, is_output=True)
            rms(n, 2, xn_sb, b_xn, compute_rstd=False)

            def ep_q(m, p, bp):
                if m < 16:
                    fw.op(fw.act, lambda e: e.activation(out=act_sb[:, m, 0:n], in_=p[:, 0:n], func=AF.Copy, scale=128.0 ** -0.5),
                          reads=[bp], writes=[b_act])
                else:
                    fw.op(fw.act, lambda e: e.activation(out=g_sb[:, 0:n], in_=p[:, 0:n], func=AF.Sigmoid), reads=[bp], writes=[b_g])
            linear(w_q, 17, KC, xn_sb, b_xn, n, ep_q)
            fw.dma(qT.rearrange("(c p) t -> p c t", p=128)[:, :, o0:o0 + n], act_sb[:, 0:16, 0:n], reads=[b_act], q=0, is_output=True)
            fw.dma(gT[:, o0:o0 + n], g_sb[:, 0:n], reads=[b_g], q=0, is_output=True)
        else:
            fw.op(fw.act, lambda e: e.activation(out=sq_sb[:, :, 0:n], in_=h_sb[:, :, 0:n], func=AF.Square), reads=[b_h], writes=[b_sq])
            for c in range(KC):
                fw.op(fw.pe, lambda e: e.matmul(ps_ss[:, 0:n], ones_sb[:], sq_sb[:, c, 0:n], start=(c == 0), stop=(c == KC - 1)),
                      reads=[b_sq, b_const], writes=[b_psss], inc=(c == KC - 1))
            fw.op(fw.act, lambda e: e.activation(out=rstd_sb[:, 0:n], in_=ps_ss[:, 0:n], func=AF.Sqrt, bias=eps_sb[:], scale=1.0 / D),
                  reads=[b_psss, b_const], writes=[b_rstd])
            fw.op(fw.dve, lambda e: e.reciprocal(out=rstd_sb[:, 0:n], in_=rstd_sb[:, 0:n]), reads=[b_rstd], writes=[b_rstd])
            for c in range(KC):
                fw.op(fw.dve, lambda e: e.scalar_tensor_tensor(out=h_sb[:, c, 0:n], in0=h_sb[:, c, 0:n], scalar=gains_sb[:, 1, c:c + 1],
                                                               in1=rstd_sb[:, 0:n], op0=ALU.mult, op1=ALU.mult),
                      reads=[b_h, b_rstd, b_const], writes=[b_h])
            fw.dma(outT.rearrange("(c p) t -> p c t", p=128)[:, :, o0:o0 + n], h_sb[:, :, 0:n], reads=[b_h], q=0, is_output=True)
    fw.finish()
    return nc


def lay_w(w, pad_to=None):
    K, M = w.shape
    if pad_to is not None and M < pad_to:
        w = np.concatenate([w, np.zeros((K, pad_to - M), w.dtype)], axis=1)
        M = pad_to
    return np.ascontiguousarray(w.reshape(K // 128, 128, M // 128, 128).transpose(2, 1, 0, 3)).reshape(M // 128, 128, K)


def lay_gain(g):
    return np.ascontiguousarray(g.reshape(16, 128).T)


NEGB = -30000.0


def build_stage_c(S):
    nc = bass.Bass("TRN2", target_bir_lowering=False)
    NQT = S // 128
    NC = S // 16
    NCT = (NC + 127) // 128
    NCP = NCT * 128
    NSEL = S // 64
    NBH = (NSEL + 127) // 128
    NBW = NBH * 128
    EW = min(NSEL, 128) * 64
    di = lambda n, s, dt: nc.dram_tensor(n, s, dt, kind="ExternalInput").ap()
    kcT = di("kcT", [128, S], BF16)
    vcT = di("vcT", [128, S], BF16)
    ksT = di("ksT", [128, S], BF16)
    kwT = di("kwT", [128, S], BF16)
    vs = di("vs", [S, 128], BF16)
    vw = di("vw", [S, 128], BF16)
    qT = di("qT", [NQT, 128, 512], BF16)
    gates = di("gates", [NQT, 128, 12], F32)
    w1 = [di("w1k", [128, 32, 256], F32), di("w1v", [128, 32, 256], F32)]
    peT = [di("peTk", [128, 32], F32), di("peTv", [128, 32], F32)]
    w2 = [di("w2k", [128, 2, 128], F32), di("w2v", [128, 2, 128], F32)]
    ident = di("ident", [128, 128], F32)
    ebig = di("ebig", [128, EW], F32)
    caus = di("caus", [128, 128], F32)
    low = di("low", [128, 128], F32)
    cposrow = di("cposrow", [128, NCP], F32)
    tcol = di("tcol", [128, NQT], F32)
    curcol = di("curcol", [128, NQT], F32)
    jrow = di("jrow", [128, NBW], F32)
    qrow = di("qrow", [128, 128], F32)
    cpq = di("cpq", [128, NCT, NQT], F32)
    o = nc.dram_tensor("o", [S, 512], BF16, kind="ExternalOutput").ap()

    fw = FW(nc)
    Sb = fw.sbuf
    B = Buf
    CBW = 16 * 512 + 32
    cbuf = Sb("cbuf", [128, CBW], BF16)
    w1_sb = Sb("w1_sb", [128, 32, 256], BF16)
    peT_sb = Sb("peT_sb", [128, 32], BF16)
    w2_sb = Sb("w2_sb", [128, 2, 128], BF16)
    hb_sb = Sb("hb_sb", [128, 2], F32)
    hx = [Sb(f"hx{i}", [128, 512], F32) for i in range(3)]
    hg_sb = Sb("hg_sb", [128, 2, 512], BF16)
    kcmpT = Sb("kcmpT", [128, NCP], BF16)
    vcmp = Sb("vcmp", [128, NCT, 129], BF16)
    ksT_sb = Sb("ksT_sb", [128, S], BF16)
    vs_sb = Sb("vs_sb", [128, NQT, 129], BF16)
    kw_sb = Sb("kw_sb", [128, 8, 128], BF16)
    vw_sb = Sb("vw_sb", [128, 8, 129], BF16)
    ident_sb = Sb("ident_sb", [128, 128], BF16)
    identf_sb = Sb("identf_sb", [128, 128], F32)
    ebig_sb = Sb("ebig_sb", [128, EW], BF16)
    caus_sb = Sb("caus_sb", [128, 128], BF16)
    low_sb = Sb("low_sb", [128, 128], BF16)
    cposrow_sb = Sb("cposrow_sb", [128, NCP], F32)
    tcol_sb = Sb("tcol_sb", [128, NQT], F32)
    curcol_sb = Sb("curcol_sb", [128, NQT], F32)
    jrow_sb = Sb("jrow_sb", [128, NBW], F32)
    qrow_sb = Sb("qrow_sb", [128, 128], F32)
    cpq_sb = Sb("cpq_sb", [128, NCT, NQT], F32)
    zero_sb = Sb("zero_sb", [128, 512], BF16)
    q_sb = [Sb(f"q_sb{i}", [128, 512], BF16) for i in range(2)]
    g_sb = [Sb(f"g_sb{i}", [128, 12], F32) for i in range(2)]
    bsel_sb = Sb("bsel_sb", [128, NCP], F32)
    s_sb = [Sb(f"s_sb{i}", [128, NCP], F32) for i in range(2)]
    e_sb = [Sb(f"e_sb{i}", [128, NCP], F32) for i in range(2)]
    sum_sb = [Sb(f"sum_sb{i}", [128, 1], F32) for i in range(2)]
    imp_sb = Sb("imp_sb", [128, NCP + 8], F32)
    psl_sb = Sb("psl_sb", [128, NBW], F32)
    d_sb = Sb("d_sb", [128, NBW], F32)
    val_sb = Sb("val_sb", [128, NBW], F32)
    fm_sb = Sb("fm_sb", [128, NBW], F32)
    sco_sb = Sb("sco_sb", [128, NBW], F32)
    wrk_sb = Sb("wrk_sb", [128, NBW], F32)
    m8_sb = Sb("m8_sb", [128, 16], F32)
    nsel_sb = Sb("nsel_sb", [128, NBW], F32)
    nselT_sbs = [Sb(f"nselT_sb{i}", [128, NBH, 128], BF16) for i in range(2)]
    cb_sb = [Sb(f"cb_sb{i}", [128, 128], BF16) for i in range(2)]
    pT_sb = [Sb(f"pT_sb{i}", [128, 512], BF16) for i in range(3)]
    coef_sb = [Sb(f"coef_sb{i}", [128, 2], F32) for i in range(4)]
    otmp_sb = [Sb(f"otmp_sb{i}", [128, 2, 128], F32) for i in range(2)]
    oacc_sb = [Sb(f"oacc_sb{i}", [128, 4, 128], F32) for i in range(2)]
    oout_sb = [Sb(f"oout_sb{i}", [128, 512], BF16) for i in range(2)]

    P = fw.psum
    ps_s = [P(f"ps_s{i}", [128, 512]) for i in range(2)]
    ps_o = [P(f"ps_o{i}", [128, 2, 256]) for i in range(4)]
    ps_sel = [P(f"ps_sel{i}", [128, 512]) for i in range(2)]
    b_pss = [B(), B()]
    b_pso = [B() for _ in range(4)]
    b_pssel = [B(), B()]
    b_const, b_cbuf, b_w1, b_w2, b_hb, b_hg, b_kcmp, b_vcmp, b_ks, b_vs = [B() for _ in range(10)]
    b_hx = [B(), B(), B()]
    b_kw = [B() for _ in range(8)]
    b_vw = [B() for _ in range(8)]
    b_q, b_g = [B(), B()], [B(), B()]
    b_bsel, b_imp, b_psl, b_d, b_val, b_fm, b_sco, b_wrk, b_m8, b_nsel = [B() for _ in range(10)]
    b_nselTs = [B(), B()]
    b_s, b_e, b_sum = [B(), B()], [B(), B()], [B(), B()]
    b_cb = [B(), B()]
    b_pT = [B(), B(), B()]
    b_coef = [B() for _ in range(4)]
    b_otmp = [B(), B()]
    b_oacc, b_oout = [B(), B()], [B(), B()]

    for dst, src in ((ident_sb, ident), (ebig_sb, ebig), (caus_sb, caus), (low_sb, low)):
        fw.dma(dst[:], src, writes=[b_const], q=1)
    for dst, src in ((identf_sb, ident), (cposrow_sb, cposrow), (tcol_sb, tcol), (curcol_sb, curcol), (jrow_sb, jrow),
                     (qrow_sb, qrow), (cpq_sb, cpq)):
        fw.dma(dst[:], src, writes=[b_const], q=0)
    fw.op(fw.dve, lambda e: e.memset(zero_sb[:], 0.0), writes=[b_const])
    fw.op(fw.dve, lambda e: e.memset(vs_sb[:, :, 128:129], 1.0), writes=[b_vs])
    fw.op(fw.dve, lambda e: e.memset(vw_sb[:, :, 128:129], 1.0), writes=[b_vw[0]])
    fw.op(fw.dve, lambda e: e.memset(vcmp[:, :, 128:129], 1.0), writes=[b_vcmp])
    fw.op(fw.dve, lambda e: e.memset(imp_sb[:], 0.0), writes=[b_imp])
    for i in range(1, 8):
        b_vw[i].w = b_vw[0].w
    fw.dma(ksT_sb[:], ksT, writes=[b_ks], q=0)
    vs_v = vs.rearrange("(t p) d -> p t d", p=128)
    for t0 in range(0, NQT, 16):
        t1 = min(NQT, t0 + 16)
        fw.dma(vs_sb[:, t0:t1, 0:128], vs_v[:, t0:t1, :], writes=[b_vs], q=0)

    for kv in range(2):
        src = kcT if kv == 0 else vcT
        fw.dma(w1_sb[:, 0:16, :], w1[kv][:, 0:16, :], writes=[b_w1], q=1)
        fw.dma(w1_sb[:, 16:32, :], w1[kv][:, 16:32, :], writes=[b_w1], q=1)
        fw.dma(peT_sb[:], peT[kv], writes=[b_w1], q=1)
        fw.dma(w2_sb[:], w2[kv], writes=[b_w2], q=1)
        for mc in range(2):
            for p in range(32):
                fw.op(fw.pe, lambda e: e.matmul(ps_sel[0][:, mc:mc + 1], w1_sb[:, p, mc * 128:(mc + 1) * 128], peT_sb[:, p:p + 1],
                                                start=(p == 0), stop=(p == 31)),
                      reads=[b_w1], writes=[b_pssel[0]], inc=(p == 31))
        fw.op(fw.dve, lambda e: e.tensor_copy(out=hb_sb[:], in_=ps_sel[0][:, 0:2]), reads=[b_pssel[0]], writes=[b_hb])
        for cbk in range((NC + 511) // 512):
            nb = min(512, NC - 512 * cbk)
            tok0 = 8192 * cbk
            need = 16 * nb + 16
            avail = min(need, S - tok0)
            if avail < need:
                fw.op(fw.dve, lambda e: e.memset(cbuf[:, avail:need], 0.0), writes=[b_cbuf])
            fw.dma(cbuf[:, 0:avail], src[:, tok0:tok0 + avail], writes=[b_cbuf], q=0)
            for mc in range(2):
                pb = ps_sel[mc]
                for p in range(32):
                    fw.op(fw.pe, lambda e: e.matmul(pb[:, 0:nb], w1_sb[:, p, mc * 128:(mc + 1) * 128],
                                                    cbuf[:, p:p + 16 * nb].rearrange("d (i s) -> d i s", s=16)[:, :, 0],
                                                    start=(p == 0), stop=(p == 31)),
                          reads=[b_w1, b_cbuf], writes=[b_pssel[mc]], inc=(p == 31))
                X, X2, T3 = hx[0], hx[1], hx[2]
                fw.op(fw.act, lambda e: e.activation(out=X[:, 0:nb], in_=pb[:, 0:nb], func=AF.Identity, bias=hb_sb[:, mc:mc + 1], scale=1.0),
                      reads=[b_pssel[mc], b_hb], writes=[b_hx[0]])
                fw.op(fw.dve, lambda e: e.tensor_tensor(out=X2[:, 0:nb], in0=X[:, 0:nb], in1=X[:, 0:nb], op=ALU.mult),
                      reads=[b_hx[0]], writes=[b_hx[1]])
                fw.op(fw.dve, lambda e: e.tensor_scalar(out=X2[:, 0:nb], in0=X2[:, 0:nb], scalar1=0.044715, scalar2=1.0, op0=ALU.mult, op1=ALU.add),
                      reads=[b_hx[1]], writes=[b_hx[1]])
                fw.op(fw.dve, lambda e: e.tensor_tensor(out=X2[:, 0:nb], in0=X2[:, 0:nb], in1=X[:, 0:nb], op=ALU.mult),
                      reads=[b_hx[0], b_hx[1]], writes=[b_hx[1]])
                fw.op(fw.act, lambda e: e.activation(out=T3[:, 0:nb], in_=X2[:, 0:nb], func=AF.Sigmoid, scale=1.5957691216057308),
                      reads=[b_hx[1]], writes=[b_hx[2]])
                fw.op(fw.dve, lambda e: e.tensor_tensor(out=hg_sb[:, mc, 0:nb], in0=T3[:, 0:nb], in1=X[:, 0:nb], op=ALU.mult),
                      reads=[b_hx[0], b_hx[2]], writes=[b_hg])
            if kv == 0:
                for mc in range(2):
                    fw.op(fw.pe, lambda e: e.matmul(ps_sel[0][:, 0:nb], w2_sb[:, mc, :], hg_sb[:, mc, 0:nb], start=(mc == 0), stop=(mc == 1)),
                          reads=[b_w2, b_hg], writes=[b_pssel[0]], inc=(mc == 1))
                fw.op(fw.act, lambda e: e.activation(out=kcmpT[:, 512 * cbk:512 * cbk + nb], in_=ps_sel[0][:, 0:nb], func=AF.Copy),
                      reads=[b_pssel[0]], writes=[b_kcmp])
            else:
                for ct in range((nb + 127) // 128):
                    w_ = min(128, nb - ct * 128)
                    for mc in range(2):
                        fw.op(fw.pe, lambda e: e.matmul(ps_sel[0][0:w_, ct * 128:(ct + 1) * 128], hg_sb[:, mc, ct * 128:ct * 128 + w_], w2_sb[:, mc, :],
                                                        start=(mc == 0), stop=(mc == 1)),
                              reads=[b_w2, b_hg], writes=[b_pssel[0]], inc=(mc == 1))
                    fw.op(fw.act, lambda e: e.activation(out=vcmp[0:w_, 4 * cbk + ct, 0:128], in_=ps_sel[0][0:w_, ct * 128:(ct + 1) * 128], func=AF.Copy),
                          reads=[b_pssel[0]], writes=[b_vcmp])

    cnt = {"s": 0, "p": 0, "o": 0, "cb": 0}

    def load_q(qt):
        bi = qt % 2
        fw.dma(q_sb[bi][:], qT[qt], writes=[b_q[bi]], q=0)
        fw.dma(g_sb[bi][:], gates[qt], writes=[b_g[bi]], q=0)
        sl = qt % 8
        fw.dma(kw_sb[:, sl, :], kwT[:, qt * 128:(qt + 1) * 128], writes=[b_kw[sl]], q=0)
        fw.dma(vw_sb[:, sl, 0:128], vw[qt * 128:(qt + 1) * 128, :], writes=[b_vw[sl]], q=0)

    def attend(qt, tiles, br, first_branch, between=None):
        bi = qt % 2
        oset = cnt["o"] % 2
        cnt["o"] += 1
        po = [ps_o[2 * oset], ps_o[2 * oset + 1]]
        bpo = [b_pso[2 * oset], b_pso[2 * oset + 1]]
        for hb in range(2):
            fw.op(fw.pe, lambda e: e.matmul(po[hb][:, :, 0:129], zero_sb[:, 0:128], zero_sb[:, 0:258].rearrange("p (a b) -> p a b", a=2),
                                            start=True, stop=False, skip_group_check=True),
                  reads=[b_const], writes=[bpo[hb]], inc=False)
        nt = len(tiles)
        slots = []

        def emit_s(ti):
            kT_ap, b_k, va_ap, b_v, biases = tiles[ti]
            si = cnt["s"] % 2
            cnt["s"] += 1
            fw.op(fw.pe, lambda e: e.matmul(ps_s[si][:], kT_ap, q_sb[bi][:], start=True, stop=(len(biases) == 0)),
                  reads=[b_k, b_q[bi]], writes=[b_pss[si]], inc=(len(biases) == 0))
            for bj, (l_ap, r_ap, bufs) in enumerate(biases):
                last = bj == len(biases) - 1
                fw.op(fw.pe, lambda e: e.matmul(ps_s[si][:], l_ap, r_ap, start=False, stop=last),
                      reads=bufs, writes=[b_pss[si]], inc=last)
            slots.append(si)

        emit_s(0)
        for ti in range(nt):
            kT_ap, b_k, va_ap, b_v, biases = tiles[ti]
            si = slots[ti]
            pi = cnt["p"] % 3
            cnt["p"] += 1
            fw.op(fw.act, lambda e: e.activation(out=pT_sb[pi][:], in_=ps_s[si][:], func=AF.Exp), reads=[b_pss[si]], writes=[b_pT[pi]])
            if ti + 1 < nt:
                emit_s(ti + 1)
            for g in range(4):
                lastmm = (ti == nt - 1)
                fw.op(fw.pe, lambda e: e.matmul(po[g // 2][:, g % 2, 0:129], pT_sb[pi][:, g * 128:(g + 1) * 128], va_ap,
                                                start=False, stop=lastmm, skip_group_check=True),
                      reads=[b_pT[pi], b_v], writes=[bpo[g // 2]], inc=(lastmm and g % 2 == 1) or (g == 3))
            if between is not None:
                between(ti)
        for hb in range(2):
            ci = (cnt["o"] * 2 + hb) % 4
            cf, bcf = coef_sb[ci], b_coef[ci]
            fw.op(fw.dve, lambda e: e.tensor_scalar(out=cf[:], in0=po[hb][:, :, 128], scalar1=1e-30, scalar2=None, op0=ALU.max),
                  reads=[bpo[hb]], writes=[bcf])
            fw.op(fw.dve, lambda e: e.reciprocal(out=cf[:], in_=cf[:]), reads=[bcf], writes=[bcf])
            fw.op(fw.dve, lambda e: e.tensor_tensor(out=cf[:], in0=cf[:], in1=g_sb[bi][:, br * 4 + 2 * hb: br * 4 + 2 * hb + 2], op=ALU.mult),
                  reads=[bcf, b_g[bi]], writes=[bcf])
            dst = oacc_sb[bi][:, 2 * hb:2 * hb + 2, :]
            if first_branch:
                fw.op(fw.dve, lambda e: e.tensor_tensor(out=dst, in0=po[hb][:, :, 0:128], in1=cf[:].unsqueeze(2).to_broadcast([128, 2, 128]), op=ALU.mult),
                      reads=[bpo[hb], bcf], writes=[b_oacc[bi]])
            else:
                ot, bot = otmp_sb[hb], b_otmp[hb]
                fw.op(fw.dve, lambda e: e.tensor_tensor(out=ot[:], in0=po[hb][:, :, 0:128], in1=cf[:].unsqueeze(2).to_broadcast([128, 2, 128]), op=ALU.mult),
                      reads=[bpo[hb], bcf], writes=[bot])
                fw.op(fw.pool, lambda e: e.tensor_tensor(out=dst, in0=dst, in1=ot[:], op=ALU.add), reads=[bot, b_oacc[bi]], writes=[b_oacc[bi]])

    def sel_gen(qt):
        bi = qt % 2
        nselT_sb = nselT_sbs[qt % 2]
        b_nselT = b_nselTs[qt % 2]
        cmax = 8 * qt + 6
        ncw = min(NCP, ((cmax + 1 + 7) // 8) * 8)
        ncw = max(ncw, 8)
        nbw = min(NBW, max(16, ((2 * qt + 2 + 7) // 8) * 8))
        fw.op(fw.dve, lambda e: e.tensor_scalar(out=bsel_sb[:, 0:ncw], in0=cposrow_sb[:, 0:ncw], scalar1=tcol_sb[:, qt:qt + 1], scalar2=NEGB,
                                                op0=ALU.is_gt, op1=ALU.mult), reads=[b_const], writes=[b_bsel])
        for g in range(4):
            gi = g % 2
            yield
            for c0 in range(0, ncw, 512):
                c1 = min(ncw, c0 + 512)
                pb = ps_sel[c0 // 512]
                fw.op(fw.pe, lambda e: e.matmul(pb[:, 0:c1 - c0], q_sb[bi][:, g * 128:(g + 1) * 128], kcmpT[:, c0:c1], start=True, stop=True),
                      reads=[b_q[bi], b_kcmp], writes=[b_pssel[c0 // 512]], inc=True)
                fw.op(fw.dve, lambda e: e.tensor_tensor(out=s_sb[gi][:, c0:c1], in0=pb[:, 0:c1 - c0], in1=bsel_sb[:, c0:c1], op=ALU.add),
                      reads=[b_pssel[c0 // 512], b_bsel], writes=[b_s[gi]])
            fw.op(fw.act, lambda e: e.activation(out=e_sb[gi][:, 0:ncw], in_=s_sb[gi][:, 0:ncw], func=AF.Exp, accum_out=sum_sb[gi][:]),
                  reads=[b_s[gi]], writes=[b_e[gi], b_sum[gi]])
            fw.op(fw.dve, lambda e: e.tensor_scalar(out=sum_sb[gi][:], in0=sum_sb[gi][:], scalar1=1e-30, scalar2=None, op0=ALU.max),
                  reads=[b_sum[gi]], writes=[b_sum[gi]])
            fw.op(fw.dve, lambda e: e.reciprocal(out=sum_sb[gi][:], in_=sum_sb[gi][:]), reads=[b_sum[gi]], writes=[b_sum[gi]])
            if g == 0:
                fw.op(fw.dve, lambda e: e.tensor_scalar(out=imp_sb[:, 1:1 + ncw], in0=e_sb[gi][:, 0:ncw], scalar1=sum_sb[gi][:], scalar2=None, op0=ALU.mult),
                      reads=[b_e[gi], b_sum[gi]], writes=[b_imp])
            else:
                fw.op(fw.dve, lambda e: e.scalar_tensor_tensor(out=imp_sb[:, 1:1 + ncw], in0=e_sb[gi][:, 0:ncw], scalar=sum_sb[gi][:], in1=imp_sb[:, 1:1 + ncw],
                                                               op0=ALU.mult, op1=ALU.add), reads=[b_e[gi], b_sum[gi], b_imp], writes=[b_imp])
        if ncw < NCP:
            fw.op(fw.dve, lambda e: e.memset(imp_sb[:, 1 + ncw:min(NCP + 8, 1 + ncw + 8)], 0.0), writes=[b_imp])
        nbe = min(nbw, (ncw + 3) // 4)
        fw.op(fw.dve, lambda e: e.memset(psl_sb[:, 0:nbw], 0.0), writes=[b_psl])
        fw.op(fw.dve, lambda e: e.tensor_reduce(out=psl_sb[:, 0:nbe], in_=imp_sb[:, 0:4 * nbe].rearrange("p (a b) -> p a b", b=4), axis=AX.X, op=ALU.add),
              reads=[b_imp], writes=[b_psl])
        fw.op(fw.dve, lambda e: e.tensor_tensor(out=psl_sb[:, 0:nbe], in0=psl_sb[:, 0:nbe],
                                                in1=imp_sb[:, 4:4 + 4 * nbe].rearrange("p (a b) -> p a b", b=4)[:, :, 0], op=ALU.add),
              reads=[b_imp, b_psl], writes=[b_psl])
        fw.op(fw.dve, lambda e: e.tensor_scalar(out=d_sb[:, 0:nbw], in0=jrow_sb[:, 0:nbw], scalar1=curcol_sb[:, qt:qt + 1], scalar2=None, op0=ALU.subtract),
              reads=[b_const], writes=[b_d])
        fw.op(fw.dve, lambda e: e.tensor_scalar(out=val_sb[:, 0:nbw], in0=d_sb[:, 0:nbw], scalar1=0.0, scalar2=None, op0=ALU.is_le),
              reads=[b_d], writes=[b_val])
        fw.op(fw.dve, lambda e: e.scalar_tensor_tensor(out=fm_sb[:, 0:nbw], in0=d_sb[:, 0:nbw], scalar=-1.0, in1=val_sb[:, 0:nbw], op0=ALU.is_ge, op1=ALU.mult),
              reads=[b_d, b_val], writes=[b_fm])
        fw.op(fw.dve, lambda e: e.scalar_tensor_tensor(out=sco_sb[:, 0:nbw], in0=fm_sb[:, 0:nbw], scalar=100.0, in1=psl_sb[:, 0:nbw], op0=ALU.mult, op1=ALU.add),
              reads=[b_fm, b_psl], writes=[b_sco])
        fw.op(fw.dve, lambda e: e.tensor_scalar(out=sco_sb[:, 0:1], in0=sco_sb[:, 0:1], scalar1=100.0, scalar2=None, op0=ALU.add),
              reads=[b_sco], writes=[b_sco])
        fw.op(fw.dve, lambda e: e.max(out=m8_sb[:, 0:8], in_=sco_sb[:, 0:nbw]), reads=[b_sco], writes=[b_m8])
        fw.op(fw.dve, lambda e: e.match_replace(out=wrk_sb[:, 0:nbw], in_to_replace=m8_sb[:, 0:8], in_values=sco_sb[:, 0:nbw], imm_value=-1.0),
              reads=[b_sco, b_m8], writes=[b_wrk])
        fw.op(fw.dve, lambda e: e.max(out=m8_sb[:, 8:16], in_=wrk_sb[:, 0:nbw]), reads=[b_wrk, b_m8], writes=[b_m8])
        fw.op(fw.dve, lambda e: e.scalar_tensor_tensor(out=wrk_sb[:, 0:nbw], in0=sco_sb[:, 0:nbw], scalar=m8_sb[:, 15:16], in1=val_sb[:, 0:nbw],
                                                       op0=ALU.is_ge, op1=ALU.mult), reads=[b_sco, b_m8, b_val], writes=[b_wrk])
        if nbw < NBW:
            fw.op(fw.dve, lambda e: e.memset(nsel_sb[:, nbw:NBW], NEGB), writes=[b_nsel])
        fw.op(fw.dve, lambda e: e.tensor_scalar(out=nsel_sb[:, 0:nbw], in0=wrk_sb[:, 0:nbw], scalar1=1.0, scalar2=-NEGB, op0=ALU.subtract, op1=ALU.mult),
              reads=[b_wrk], writes=[b_nsel])
        for _ in range(6):
            yield
        nbh_used = (min(NSEL, 2 * qt + 2) + 127) // 128
        for hb in range(nbh_used):
            fw.op(fw.pe, lambda e: e.transpose(ps_sel[hb][:, 0:128], nsel_sb[:, hb * 128:(hb + 1) * 128], identf_sb[:]),
                  reads=[b_nsel, b_const], writes=[b_pssel[hb]], inc=True)
            fw.op(fw.act, lambda e: e.activation(out=nselT_sb[:, hb, :], in_=ps_sel[hb][:, 0:128], func=AF.Copy), reads=[b_pssel[hb]], writes=[b_nselT])


    load_q(0)
    for _ in sel_gen(0):
        pass
    for qt in range(NQT):
        bi = qt % 2
        if qt + 1 < NQT:
            load_q(qt + 1)
        tiles = []
        cmax = 8 * qt + 6
        nct = cmax // 128 + 1
        for ct in range(min(nct, NCT)):
            biases = []
            if 16 * (128 * ct + 127) + 31 > 128 * qt or (ct == NCT - 1):
                ci = cnt["cb"] % 2
                cnt["cb"] += 1
                fw.op(fw.dve, lambda e: e.tensor_scalar(out=cb_sb[ci][:], in0=qrow_sb[:], scalar1=cpq_sb[:, ct, qt:qt + 1], scalar2=NEGB,
                                                        op0=ALU.is_lt, op1=ALU.mult), reads=[b_const], writes=[b_cb[ci]])
                biases.append((ident_sb[:], cb_sb[ci][:].unsqueeze(1).to_broadcast([128, 4, 128]), [b_const, b_cb[ci]]))
            tiles.append((kcmpT[:, ct * 128:(ct + 1) * 128], b_kcmp, vcmp[:, ct, :], b_vcmp, biases))
        attend(qt, tiles, 0, True)
        tiles = []
        for kt in range(qt + 1):
            hb = kt // 64
            ktp = kt % 64
            biases = [(ebig_sb[:, ktp * 128:(ktp + 1) * 128], nselT_sbs[qt % 2][:, hb, :].unsqueeze(1).to_broadcast([128, 4, 128]), [b_const, b_nselTs[qt % 2]])]
            if kt == qt:
                biases.append((ident_sb[:], caus_sb[:].unsqueeze(1).to_broadcast([128, 4, 128]), [b_const]))
            tiles.append((ksT_sb[:, kt * 128:(kt + 1) * 128], b_ks, vs_sb[:, kt, :], b_vs, biases))
        gen = sel_gen(qt + 1) if qt + 1 < NQT else iter(())

        def between(ti, gen=gen):
            if ti % 2 == 1:
                next(gen, None)
        attend(qt, tiles, 1, False, between=between)
        for _ in gen:
            pass
        tiles = []
        for kt in range(max(0, qt - 4), qt + 1):
            sl = kt % 8
            biases = []
            if kt == qt:
                biases.append((ident_sb[:], caus_sb[:].unsqueeze(1).to_broadcast([128, 4, 128]), [b_const]))
            if kt == qt - 4:
                biases.append((ident_sb[:], low_sb[:].unsqueeze(1).to_broadcast([128, 4, 128]), [b_const]))
            tiles.append((kw_sb[:, sl, :], b_kw[sl], vw_sb[:, sl, :], b_vw[sl], biases))
        attend(qt, tiles, 2, False)
        fw.op(fw.act, lambda e: e.activation(out=oout_sb[bi][:], in_=oacc_sb[bi][:].rearrange("p a b -> p (a b)"), func=AF.Copy),
              reads=[b_oacc[bi]], writes=[b_oout[bi]])
        fw.dma(o[qt * 128:(qt + 1) * 128, :], oout_sb[bi][:], reads=[b_oout[bi]], q=0, is_output=True)
    fw.finish()
    return nc


def stage_c_consts(S):
    NQT = S // 128
    NC = S // 16
    NCT = (NC + 127) // 128
    NCP = NCT * 128
    NSEL = S // 64
    NBH = (NSEL + 127) // 128
    NBW = NBH * 128
    EW = min(NSEL, 128) * 64
    p = np.arange(128)
    c = np.arange(NCP)
    cpos = (16.0 * c + 31.0).astype(np.float32)
    cpos[NC - 1:] = 1e9
    kl = p[:, None]
    ql = p[None, :]
    cpq = np.zeros((128, NCT, NQT), np.float32)
    for ct in range(NCT):
        cpq[:, ct, :] = cpos[ct * 128 + p][:, None] - 128.0 * np.arange(NQT)[None, :]
    return dict(
        ident=np.eye(128, dtype=np.float32),
        ebig=(np.arange(EW)[None, :] // 64 == p[:, None]).astype(np.float32),
        caus=np.where(kl > ql, NEGB, 0.0).astype(np.float32),
        low=np.where(kl <= ql, NEGB, 0.0).astype(np.float32),
        cposrow=np.ascontiguousarray(np.broadcast_to(cpos[None, :], (128, NCP))),
        tcol=(128.0 * np.arange(NQT)[None, :] + p[:, None]).astype(np.float32),
        curcol=((128 * np.arange(NQT)[None, :] + p[:, None]) // 64).astype(np.float32),
        jrow=np.ascontiguousarray(np.broadcast_to(np.arange(NBW, dtype=np.float32)[None, :], (128, NBW))),
        qrow=np.ascontiguousarray(np.broadcast_to(np.arange(128, dtype=np.float32)[None, :], (128, 128))),
        cpq=cpq,
    )

import ml_dtypes

BF_NP = ml_dtypes.bfloat16
NCORES = 8
BATCH, SEQ, DM = 2, 16384, 2048
_CACHE = {}


def _prog(key, fn):
    if key not in _CACHE:
        _CACHE[key] = fn()
    return _CACHE[key]


def build_cast(ncols):
    nc = bass.Bass("TRN2", target_bir_lowering=False)
    src = nc.dram_tensor("src", [128, ncols], F32, kind="ExternalInput").ap()
    dst = nc.dram_tensor("dst", [128, ncols], BF16, kind="ExternalOutput").ap()
    fw = FW(nc)
    CH = 8192
    bufs = [fw.sbuf(f"cb{i}", [128, CH], BF16) for i in range(2)]
    bb = [Buf(), Buf()]
    for i, c0 in enumerate(range(0, ncols, CH)):
        c1 = min(ncols, c0 + CH)
        fw.dma(bufs[i % 2][:, 0:c1 - c0], src[:, c0:c1], writes=[bb[i % 2]], q=1, max_dma_last_dim=8192)
        fw.dma(dst[:, c0:c1], bufs[i % 2][:, 0:c1 - c0], reads=[bb[i % 2]], q=0, is_output=True)
    fw.finish()
    return nc


def _run(nc, in_maps):
    res = run_bass_kernel_spmd(nc, in_maps, core_ids=list(range(NCORES)))
    return res.results


def kernel(x, norm_mix_gain, norm_ffn_gain, ret_w_in, ret_gn_gain, ret_w_out, nsa_kv_norm_gain, nsa_w_kv,
           cmp_pe_k, cmp_w1_k, cmp_w2_k, cmp_pe_v, cmp_w1_v, cmp_w2_v, nsa_w_q, nsa_w_o,
           ffn_w_in, ffn_conv_w, ffn_conv_b, ffn_w_out, final_norm_gain):
    f32 = np.float32
    x = np.asarray(x, f32)
    NTOK = BATCH * SEQ
    T = NTOK // NCORES

    ncA = _prog("A", lambda: build_stage_a(NTOK, SEQ))
    ncB = _prog("Bmid", lambda: build_stage_b(T, 4096, "mid"))
    ncC = _prog("C", lambda: build_stage_c(SEQ))
    ncD = _prog("Bfin", lambda: build_stage_b(T, 2048, "final"))
    wlist = [("ret_w_out", np.asarray(ret_w_out[0], f32)), ("ffn_w_in0", np.asarray(ffn_w_in[0], f32)),
             ("ffn_w_in1", np.asarray(ffn_w_in[1], f32)), ("ffn_w_out0", np.asarray(ffn_w_out[0], f32)),
             ("ffn_w_out1", np.asarray(ffn_w_out[1], f32)), ("nsa_w_kv", np.asarray(nsa_w_kv, f32)),
             ("nsa_w_q", np.asarray(nsa_w_q[0], f32)), ("nsa_w_o", np.asarray(nsa_w_o[0], f32))]
    flat = np.concatenate([w.reshape(-1) for _, w in wlist])
    per = -(-flat.size // (NCORES * 128))
    per = -(-per // 64) * 64
    tot = per * NCORES * 128
    flat_p = np.zeros(tot, f32)
    flat_p[:flat.size] = flat
    flat_p = flat_p.reshape(NCORES, 128, per)
    nc0 = _prog(("cast", per), lambda: build_cast(per))
    r0 = _run(nc0, [{"src": flat_p[c]} for c in range(NCORES)])
    flat_b = np.concatenate([np.asarray(r["dst"]).reshape(-1) for r in r0])[:flat.size]
    wb = {}
    off = 0
    for name, w in wlist:
        wb[name] = flat_b[off:off + w.size].reshape(w.shape)
        off += w.size

    xT = np.ascontiguousarray(x.reshape(NTOK, DM).T)
    ncA = _prog("A", lambda: build_stage_a(NTOK, SEQ))
    cA = stage_a_consts(SEQ)
    w_in = np.asarray(ret_w_in[0], f32)
    H, dk, dv = 8, 256, 512
    gn = np.asarray(ret_gn_gain[0], f32)
    g_mix0 = lay_gain(np.asarray(norm_mix_gain[0], f32))
    maps = []
    for h in range(H):
        m = dict(cA[h])
        m["xT"] = xT
        m["gain"] = g_mix0
        m["wq"] = np.ascontiguousarray(w_in[:, h * dk:(h + 1) * dk])
        m["wk"] = np.ascontiguousarray(w_in[:, H * dk + h * dk: H * dk + (h + 1) * dk])
        m["wv"] = np.ascontiguousarray(w_in[:, 2 * H * dk + h * dv: 2 * H * dk + (h + 1) * dv])
        m["wg"] = np.ascontiguousarray(w_in[:, 2 * H * dk + H * dv + h * dv: 2 * H * dk + H * dv + (h + 1) * dv])
        m["gng"] = np.ascontiguousarray(np.broadcast_to(gn[h * dv:(h + 1) * dv][None, :], (128, 512)))
        maps.append(m)
    rA = _run(ncA, maps)
    yT = np.concatenate([np.asarray(r["y"]).T for r in rA], axis=0)
    del rA

    perm = np.stack([np.arange(32), np.arange(32) + 32], axis=1).reshape(-1)

    def permcols(w):
        return np.ascontiguousarray(w.reshape(w.shape[0], 64, 128)[:, perm, :].reshape(w.shape[0], 8192))

    def convw_of(layer):
        cw = np.asarray(ffn_conv_w[layer], f32)
        cb = np.asarray(ffn_conv_b[layer], f32)
        out = np.zeros((128, 64, 4), f32)
        out[:, :, 0:3] = cw.reshape(3, 64, 128).transpose(2, 1, 0)
        out[:, :, 3] = cb.reshape(64, 128).T
        return out

    def with_halo(fullT, c, dt):
        b, j = divmod(c, NCORES // BATCH)
        lo = b * SEQ + j * T
        out = np.zeros((fullT.shape[0], 2 + T), dt)
        out[:, 2:] = fullT[:, lo:lo + T]
        if j > 0:
            out[:, 0:2] = fullT[:, lo - 2:lo]
        return out

    ncB = _prog("Bmid", lambda: build_stage_b(T, 4096, "mid"))
    gains_mid = np.ascontiguousarray(np.stack([lay_gain(np.asarray(norm_ffn_gain[0], f32)), lay_gain(np.asarray(nsa_kv_norm_gain, f32)),
                                               lay_gain(np.asarray(norm_mix_gain[1], f32))], axis=1))
    common = dict(w_mix=lay_w(wb["ret_w_out"]), w_in=lay_w(permcols(wb["ffn_w_in0"])), w_out=lay_w(wb["ffn_w_out0"]),
                  gains=gains_mid, convw=convw_of(0), w_kv=lay_w(wb["nsa_w_kv"]), w_q=lay_w(wb["nsa_w_q"], pad_to=17 * 128))
    maps = []
    for c in range(NCORES):
        m = dict(common)
        m["hT"] = with_halo(xT, c, f32)
        m["aT"] = with_halo(yT, c, BF_NP)
        maps.append(m)
    rB = _run(ncB, maps)
    del yT, xT
    h2T = np.concatenate([r["h2T"] for r in rB], axis=1)
    kvT = np.concatenate([np.asarray(r["kvT"]) for r in rB], axis=1)
    qTf = np.concatenate([np.asarray(r["qT"]) for r in rB], axis=1)
    gTf = np.concatenate([r["gT"] for r in rB], axis=1)[:48]
    del rB

    ncC = _prog("C", lambda: build_stage_c(SEQ))
    cC = stage_c_consts(SEQ)
    NQT = SEQ // 128

    def w1lay(w):
        return np.ascontiguousarray(np.asarray(w, f32).reshape(32, 128, 256).transpose(1, 0, 2))

    def w2lay(w):
        return np.ascontiguousarray(np.asarray(w, f32).reshape(2, 128, 128).transpose(1, 0, 2))
    cw = dict(w1k=w1lay(cmp_w1_k), w1v=w1lay(cmp_w1_v), w2k=w2lay(cmp_w2_k), w2v=w2lay(cmp_w2_v),
              peTk=np.ascontiguousarray(np.asarray(cmp_pe_k, f32).T), peTv=np.ascontiguousarray(np.asarray(cmp_pe_v, f32).T))
    maps = []
    for b in range(BATCH):
        ts = slice(b * SEQ, (b + 1) * SEQ)
        for h in range(4):
            m = dict(cC)
            m.update(cw)
            row = lambda i: slice((i * 4 + h) * 128, (i * 4 + h + 1) * 128)
            m["kcT"] = np.ascontiguousarray(kvT[row(0), ts])
            m["vcT"] = np.ascontiguousarray(kvT[row(1), ts])
            m["ksT"] = np.ascontiguousarray(kvT[row(2), ts])
            m["vs"] = np.ascontiguousarray(kvT[row(3), ts].T)
            m["kwT"] = np.ascontiguousarray(kvT[row(4), ts])
            m["vw"] = np.ascontiguousarray(kvT[row(5), ts].T)
            qq = qTf[h * 512:(h + 1) * 512, ts].reshape(4, 128, NQT, 128)
            m["qT"] = np.ascontiguousarray(qq.transpose(2, 1, 0, 3)).reshape(NQT, 128, 512)
            gg = gTf[h * 12:(h + 1) * 12, ts].reshape(4, 3, NQT, 128)
            m["gates"] = np.ascontiguousarray(gg.transpose(2, 3, 1, 0)).reshape(NQT, 128, 12)
            maps.append(m)
    rC = _run(ncC, maps)
    del kvT, qTf, gTf
    oT = np.zeros((2048, NTOK), BF_NP)
    for b in range(BATCH):
        for h in range(4):
            oT[h * 512:(h + 1) * 512, b * SEQ:(b + 1) * SEQ] = np.asarray(rC[b * 4 + h]["o"]).T
    del rC

    ncD = _prog("Bfin", lambda: build_stage_b(T, 2048, "final"))
    gains_fin = np.ascontiguousarray(np.stack([lay_gain(np.asarray(norm_ffn_gain[1], f32)), lay_gain(np.asarray(final_norm_gain, f32)),
                                               lay_gain(np.asarray(final_norm_gain, f32))], axis=1))
    common = dict(w_mix=lay_w(wb["nsa_w_o"]), w_in=lay_w(permcols(wb["ffn_w_in1"])), w_out=lay_w(wb["ffn_w_out1"]),
                  gains=gains_fin, convw=convw_of(1))
    maps = []
    for c in range(NCORES):
        m = dict(common)
        m["hT"] = with_halo(h2T, c, f32)
        m["aT"] = with_halo(oT, c, BF_NP)
        maps.append(m)
    rD = _run(ncD, maps)
    outT = np.concatenate([r["outT"] for r in rD], axis=1)
    return np.ascontiguousarray(outT.T).reshape(BATCH, SEQ, DM).astype(f32)
```

```python
import numpy as np
from contextlib import ExitStack
import concourse.bass as bass
import concourse.mybir as mybir
from concourse.bass_utils import run_bass_kernel_spmd

F32 = mybir.dt.float32
BF16 = mybir.dt.bfloat16
AF = mybir.ActivationFunctionType
ALU = mybir.AluOpType
AX = mybir.AxisListType


class Eng:
    def __init__(self, fw, name, h, is_pe=False):
        self.fw = fw
        self.name = name
        self.h = h
        self.sem = fw.es.enter_context(fw.nc.semaphore("sem_" + name))
        self.count = 0
        self.pending = False
        self.waited = {}
        self.is_pe = is_pe

    def wait_tok(self, tok):
        if tok is None:
            return
        sem, val, eng = tok
        if eng is self and self.is_pe:
            return
        key = id(sem)
        if self.waited.get(key, 0) >= val:
            return
        if eng is not None and eng is not self and eng.count < val:
            eng.flush()
        if eng is self and self.count < val:
            self.flush()
        self.h.wait_ge(sem, val)
        self.waited[key] = val

    def flush(self):
        if self.pending:
            self.count += 1
            self.h.nop().then_inc(self.sem, 1)
            self.pending = False


class Buf:
    __slots__ = ("name", "w", "r")

    def __init__(self, name=""):
        self.name = name
        self.w = None
        self.r = []


class FW:
    def __init__(self, nc):
        self.nc = nc
        self.es = ExitStack()
        self.pe = Eng(self, "pe", nc.tensor, is_pe=True)
        self.dve = Eng(self, "dve", nc.vector)
        self.act = Eng(self, "act", nc.scalar)
        self.pool = Eng(self, "pool", nc.gpsimd)
        self.sp = Eng(self, "sp", nc.sync)
        self.engs = [self.pe, self.dve, self.act, self.pool, self.sp]
        self.dma_sems = []
        self.n_dma = 0
        self.NDMA = 6
        for q in range(2):
            lst = []
            for i in range(self.NDMA):
                s = self.es.enter_context(nc.semaphore(f"dsem{q}_{i}"))
                lst.append([s, 0])
            self.dma_sems.append(lst)
        self.dma_rr = [0, 0]
        self.out_toks = []

    def sbuf(self, name, shape, dt):
        return self.es.enter_context(self.nc.sbuf_tensor(name, shape, dt))

    def psum(self, name, shape, dt=F32):
        return self.es.enter_context(self.nc.psum_tensor(name, shape, dt))

    def op(self, eng, fn, reads=(), writes=(), inc=True):
        for b in reads:
            eng.wait_tok(b.w)
        for b in writes:
            eng.wait_tok(b.w)
            for t in b.r:
                eng.wait_tok(t)
        inst = fn(eng.h)
        if inc:
            eng.count += 1
            inst.then_inc(eng.sem, 1)
            eng.pending = False
            tok = (eng.sem, eng.count, eng)
        else:
            eng.pending = True
            tok = (eng.sem, eng.count + 1, eng)
        for b in reads:
            b.r.append(tok)
            if len(b.r) > 24:
                b.r = b.r[-24:]
        for b in writes:
            b.w = tok
            b.r = []
        return tok

    def dma(self, out, in_, reads=(), writes=(), q=0, is_output=False, **kw):
        eng = self.sp if q == 0 else self.pool
        slot = self.dma_sems[q][self.dma_rr[q] % self.NDMA]
        self.dma_rr[q] += 1
        sem, val = slot
        if val > 0:
            eng.wait_tok((sem, val, None))
        for b in reads:
            eng.wait_tok(b.w)
        for b in writes:
            eng.wait_tok(b.w)
            for t in b.r:
                eng.wait_tok(t)
        slot[1] = val + 16
        eng.h.dma_start(out=out, in_=in_, **kw).then_inc(sem, 16)
        tok = (sem, val + 16, None)
        for b in reads:
            b.r.append(tok)
        for b in writes:
            b.w = tok
            b.r = []
        if is_output:
            self.out_toks.append(tok)
        return tok

    def finish(self):
        for e in self.engs:
            e.flush()
        for q in range(2):
            for sem, val in self.dma_sems[q]:
                if val > 0:
                    self.sp.wait_tok((sem, val, None))
        self.es.close()


RMS_EPS = 1e-6


def build_stage_a(NTOK, SEQ):
    nc = bass.Bass("TRN2", target_bir_lowering=False)
    D = 2048
    KC = D // 128
    TT = 512
    xT = nc.dram_tensor("xT", [D, NTOK], F32, kind="ExternalInput").ap()
    gain = nc.dram_tensor("gain", [128, KC], F32, kind="ExternalInput").ap()
    wq = nc.dram_tensor("wq", [D, 256], F32, kind="ExternalInput").ap()
    wk = nc.dram_tensor("wk", [D, 256], F32, kind="ExternalInput").ap()
    wv = nc.dram_tensor("wv", [D, 512], F32, kind="ExternalInput").ap()
    wg = nc.dram_tensor("wg", [D, 512], F32, kind="ExternalInput").ap()
    cosT = nc.dram_tensor("cosT", [128, SEQ], F32, kind="ExternalInput").ap()
    sinT = nc.dram_tensor("sinT", [128, SEQ], F32, kind="ExternalInput").ap()
    decT = nc.dram_tensor("decT", [128, 128], F32, kind="ExternalInput").ap()
    qdec = nc.dram_tensor("qdec", [128, TT], F32, kind="ExternalInput").ap()
    kdec = nc.dram_tensor("kdec", [128, 1], F32, kind="ExternalInput").ap()
    cdec = nc.dram_tensor("cdec", [128, 1], F32, kind="ExternalInput").ap()
    gng = nc.dram_tensor("gng", [128, 512], F32, kind="ExternalInput").ap()
    ident = nc.dram_tensor("ident", [128, 128], F32, kind="ExternalInput").ap()
    y = nc.dram_tensor("y", [NTOK, 512], BF16, kind="ExternalOutput").ap()

    fw = FW(nc)
    S = fw.sbuf
    w_sb = S("w_sb", [128, KC, 1536], BF16)
    gain_sb = S("gain_sb", [128, KC], F32)
    decT_sb = S("decT_sb", [128, 128], F32)
    qdec_sb = S("qdec_sb", [128, TT], F32)
    kdec_sb = S("kdec_sb", [128, 1], F32)
    cdec_sb = S("cdec_sb", [128, 1], F32)
    gng_sb = S("gng_sb", [128, 512], F32)
    ident_sb = S("ident_sb", [128, 128], BF16)
    ones_sb = S("ones_sb", [128, 128], BF16)
    eps_sb = S("eps_sb", [128, 1], F32)
    HT = 256
    x_sb = [S(f"x_sb{i}", [128, KC, HT], F32) for i in range(2)]
    cs_sb = [S(f"cs_sb{i}", [128, 2, TT], F32) for i in range(2)]
    sq_sb = S("sq_sb", [128, KC, HT], BF16)
    xn_sbs = [S(f"xn_sb{i}", [128, KC, TT], BF16) for i in range(2)]
    rstd_sb = S("rstd_sb", [128, HT], F32)
    qT_sb = S("qT_sb", [128, 2, TT], BF16)
    kT_sb = S("kT_sb", [128, 2, TT], BF16)
    qdT_sb = S("qdT_sb", [128, 2, TT], BF16)
    tmp_sb = [S(f"tmp_sb{i}", [128, TT], F32) for i in range(4)]
    v_sb = [S(f"v_sb{i}", [128, 512], BF16) for i in range(2)]
    gs_sb = [S(f"gs_sb{i}", [128, 512], F32) for i in range(2)]
    pT_sb = [S(f"pT_sb{i}", [128, 128], BF16) for i in range(2)]
    kd_sb = [S(f"kd_sb{i}", [128, 256], BF16) for i in range(2)]
    st_sb = S("st_sb", [128, 2, 512], F32)
    stb_sb = S("stb_sb", [128, 2, 512], BF16)
    stat_sb = [S(f"stat_sb{i}", [128, 6], F32) for i in range(2)]
    mv_sb = [S(f"mv_sb{i}", [128, 2], F32) for i in range(2)]
    rs_sb = [S(f"rs_sb{i}", [128, 1], F32) for i in range(2)]
    t_sb = [S(f"t_sb{i}", [128, 512], F32) for i in range(2)]
    y_sb = [S(f"y_sb{i}", [128, 512], BF16) for i in range(2)]

    P = fw.psum
    ps_qk = [P(f"ps_qk{i}", [128, 512]) for i in range(2)]
    ps_vg = [P(f"ps_vg{i}", [128, 512]) for i in range(2)]
    ps_sc = P("ps_sc", [128, 512])
    ps_kt = P("ps_kt", [128, 1024], BF16)
    ps_out = P("ps_out", [128, 512])
    ps_su = P("ps_su", [128, 512])

    B = Buf
    b_w, b_const = B("w"), B("const")
    b_x = [B("x0"), B("x1")]
    b_cs = [B("cs0"), B("cs1")]
    b_sq, b_rstd, b_qT, b_kT, b_qdT = B(), B(), B(), B(), B()
    b_xns = [B(), B()]
    b_psss = B()
    b_tmp = [B() for _ in range(4)]
    b_v, b_gs, b_pT, b_kd = [B(), B()], [B(), B()], [B(), B()], [B(), B()]
    b_st, b_stb = B(), B()
    b_stat, b_mv, b_rs, b_t, b_y = [B(), B()], [B(), B()], [B(), B()], [B(), B()], [B(), B()]
    b_psqk, b_psvg = [B(), B()], [B(), B()]
    b_pssc, b_pskt, b_psout, b_pssu = B(), B(), B(), B()

    fw.dma(gain_sb[:], gain, writes=[b_const], q=0)
    fw.dma(decT_sb[:], decT, writes=[b_const], q=0)
    fw.dma(qdec_sb[:], qdec, writes=[b_const], q=0)
    fw.dma(kdec_sb[:], kdec, writes=[b_const], q=0)
    fw.dma(cdec_sb[:], cdec, writes=[b_const], q=0)
    fw.dma(gng_sb[:], gng, writes=[b_const], q=0)
    fw.dma(ident_sb[:], ident, writes=[b_const], q=1)
    off = 0
    for wsrc, n in ((wq, 256), (wk, 256), (wv, 512), (wg, 512)):
        for c0 in range(0, KC, 4):
            fw.dma(w_sb[:, c0:c0 + 4, off:off + n],
                   wsrc.rearrange("(c p) m -> p c m", p=128)[:, c0:c0 + 4, :], writes=[b_w], q=1)
        off += n
    fw.op(fw.dve, lambda e: e.memset(ones_sb[:], 1.0), writes=[b_const])
    fw.op(fw.dve, lambda e: e.memset(eps_sb[:], RMS_EPS), writes=[b_const])

    xT_v = xT.rearrange("(c p) t -> p c t", p=128)
    ntiles = NTOK // TT
    nhalves = NTOK // HT

    def load_half(hi):
        t0 = hi * HT
        bi = hi % 2
        fw.dma(x_sb[bi][:, 0:8, :], xT_v[:, 0:8, t0:t0 + HT], writes=[b_x[bi]], q=0)
        fw.dma(x_sb[bi][:, 8:16, :], xT_v[:, 8:16, t0:t0 + HT], writes=[b_x[bi]], q=0)

    def load_cs(ti):
        s0 = (ti * TT) % SEQ
        bi = ti % 2
        fw.dma(cs_sb[bi][:, 0, :], cosT[:, s0:s0 + TT], writes=[b_cs[bi]], q=0)
        fw.dma(cs_sb[bi][:, 1, :], sinT[:, s0:s0 + TT], writes=[b_cs[bi]], q=0)

    def rms_half(hi):
        bi = hi % 2
        ti = hi // 2
        hs = slice((hi % 2) * HT, (hi % 2 + 1) * HT)
        xs = x_sb[bi]
        xn = xn_sbs[ti % 2]
        b_xn = b_xns[ti % 2]
        fw.op(fw.act, lambda e: e.activation(out=sq_sb[:], in_=xs[:], func=AF.Square), reads=[b_x[bi]], writes=[b_sq])
        for c in range(KC):
            fw.op(fw.pe, lambda e: e.matmul(ps_sc[:, 256:512], ones_sb[:], sq_sb[:, c, :], start=(c == 0), stop=(c == KC - 1)),
                  reads=[b_sq, b_const], writes=[b_psss], inc=(c == KC - 1))
        fw.op(fw.act, lambda e: e.activation(out=rstd_sb[:], in_=ps_sc[:, 256:512], func=AF.Sqrt, bias=eps_sb[:], scale=1.0 / D),
              reads=[b_psss, b_const], writes=[b_rstd])
        fw.op(fw.dve, lambda e: e.reciprocal(out=rstd_sb[:], in_=rstd_sb[:]), reads=[b_rstd], writes=[b_rstd])
        for c in range(KC):
            fw.op(fw.dve, lambda e: e.scalar_tensor_tensor(out=xn[:, c, hs], in0=xs[:, c, :], scalar=gain_sb[:, c:c + 1],
                                                           in1=rstd_sb[:], op0=ALU.mult, op1=ALU.mult),
                  reads=[b_x[bi], b_rstd, b_const], writes=[b_xn])
        if hi + 2 < nhalves:
            load_half(hi + 2)

    load_half(0)
    load_half(1)
    load_cs(0)
    rms_half(0)
    rms_half(1)
    cidx = 0
    for ti in range(ntiles):
        bi = ti % 2
        t0 = ti * TT
        xn_sb = xn_sbs[ti % 2]
        b_xn = b_xns[ti % 2]
        if ti + 1 < ntiles:
            load_cs(ti + 1)
        for which, (dst, b_dst, woff, scl) in enumerate(((qT_sb, b_qT, 0, 1.0), (kT_sb, b_kT, 256, 1.0 / 16.0))):
            for m in range(2):
                for c in range(KC):
                    fw.op(fw.pe, lambda e: e.matmul(ps_qk[m][:], w_sb[:, c, woff + m * 128: woff + (m + 1) * 128], xn_sb[:, c, :],
                                                    start=(c == 0), stop=(c == KC - 1)),
                          reads=[b_w, b_xn], writes=[b_psqk[m]], inc=(c == KC - 1))
            cos_t, sin_t = cs_sb[bi][:, 0, :], cs_sb[bi][:, 1, :]
            D_ = fw.dve
            fw.op(D_, lambda e: e.scalar_tensor_tensor(out=tmp_sb[0][:], in0=ps_qk[0][:], scalar=scl, in1=cos_t, op0=ALU.mult, op1=ALU.mult),
                  reads=[b_psqk[0], b_cs[bi]], writes=[b_tmp[0]])
            fw.op(D_, lambda e: e.scalar_tensor_tensor(out=tmp_sb[1][:], in0=ps_qk[1][:], scalar=scl, in1=sin_t, op0=ALU.mult, op1=ALU.mult),
                  reads=[b_psqk[1], b_cs[bi]], writes=[b_tmp[1]])
            fw.op(D_, lambda e: e.scalar_tensor_tensor(out=tmp_sb[2][:], in0=ps_qk[0][:], scalar=scl, in1=sin_t, op0=ALU.mult, op1=ALU.mult),
                  reads=[b_psqk[0], b_cs[bi]], writes=[b_tmp[2]])
            fw.op(D_, lambda e: e.scalar_tensor_tensor(out=tmp_sb[3][:], in0=ps_qk[1][:], scalar=scl, in1=cos_t, op0=ALU.mult, op1=ALU.mult),
                  reads=[b_psqk[1], b_cs[bi]], writes=[b_tmp[3]])
            fw.op(fw.pool, lambda e: e.tensor_tensor(out=dst[:, 0, :], in0=tmp_sb[0][:], in1=tmp_sb[1][:], op=ALU.subtract),
                  reads=[b_tmp[0], b_tmp[1]], writes=[b_dst])
            fw.op(fw.pool, lambda e: e.tensor_tensor(out=dst[:, 1, :], in0=tmp_sb[2][:], in1=tmp_sb[3][:], op=ALU.add),
                  reads=[b_tmp[2], b_tmp[3]], writes=[b_dst])
        for m in range(2):
            fw.op(fw.pool, lambda e: e.tensor_tensor(out=qdT_sb[:, m, :], in0=qT_sb[:, m, :], in1=qdec_sb[:], op=ALU.mult),
                  reads=[b_qT, b_const], writes=[b_qdT])
        if ti + 1 < ntiles:
            rms_half(2 * (ti + 1))
            rms_half(2 * (ti + 1) + 1)
        for ch in range(TT // 128):
            tok0 = t0 + ch * 128
            cs = slice(ch * 128, (ch + 1) * 128)
            ci = cidx % 2
            cidx += 1
            if tok0 % SEQ == 0:
                fw.op(fw.dve, lambda e: e.memset(st_sb[:], 0.0), writes=[b_st])
                fw.op(fw.pool, lambda e: e.memset(stb_sb[:], 0.0), writes=[b_stb])
            for c in range(KC):
                fw.op(fw.pe, lambda e: e.matmul(ps_vg[0][:], xn_sb[:, c, cs], w_sb[:, c, 512:1024], start=(c == 0), stop=(c == KC - 1)),
                      reads=[b_w, b_xn], writes=[b_psvg[0]], inc=(c == KC - 1))
            fw.op(fw.act, lambda e: e.activation(out=v_sb[ci][:], in_=ps_vg[0][:], func=AF.Copy), reads=[b_psvg[0]], writes=[b_v[ci]])
            for c in range(KC):
                fw.op(fw.pe, lambda e: e.matmul(ps_vg[1][:], xn_sb[:, c, cs], w_sb[:, c, 1024:1536], start=(c == 0), stop=(c == KC - 1)),
                      reads=[b_w, b_xn], writes=[b_psvg[1]], inc=(c == KC - 1))
            fw.op(fw.act, lambda e: e.activation(out=gs_sb[ci][:], in_=ps_vg[1][:], func=AF.Silu), reads=[b_psvg[1]], writes=[b_gs[ci]])
            fw.op(fw.pool, lambda e: e.tensor_tensor(out=gs_sb[ci][:], in0=gs_sb[ci][:], in1=gng_sb[:], op=ALU.mult),
                  reads=[b_gs[ci], b_const], writes=[b_gs[ci]])
            for m in range(2):
                fw.op(fw.pe, lambda e: e.matmul(ps_sc[:, 0:128], kT_sb[:, m, cs], qT_sb[:, m, cs], start=(m == 0), stop=(m == 1)),
                      reads=[b_kT, b_qT], writes=[b_pssc], inc=(m == 1))
            fw.op(fw.dve, lambda e: e.tensor_tensor(out=pT_sb[ci][:], in0=ps_sc[:, 0:128], in1=decT_sb[:], op=ALU.mult),
                  reads=[b_pssc, b_const], writes=[b_pT[ci]])
            for m in range(2):
                fw.op(fw.pe, lambda e: e.transpose(ps_kt[:, m * 128:(m + 1) * 128], kT_sb[:, m, cs], ident_sb[:]),
                      reads=[b_kT, b_const], writes=[b_pskt], inc=(m == 1))
            fw.op(fw.dve, lambda e: e.tensor_scalar(out=kd_sb[ci][:], in0=ps_kt[:, 0:256], scalar1=kdec_sb[:], scalar2=None, op0=ALU.mult),
                  reads=[b_pskt, b_const], writes=[b_kd[ci]])
            fw.op(fw.pe, lambda e: e.matmul(ps_out[:], pT_sb[ci][:], v_sb[ci][:], start=True, stop=False),
                  reads=[b_pT[ci], b_v[ci]], writes=[b_psout], inc=False)
            for m in range(2):
                fw.op(fw.pe, lambda e: e.matmul(ps_out[:], qdT_sb[:, m, cs], stb_sb[:, m, :], start=False, stop=(m == 1)),
                      reads=[b_qdT, b_stb], writes=[b_psout], inc=(m == 1))
            for m in range(2):
                fw.op(fw.pe, lambda e: e.matmul(ps_su[:], kd_sb[ci][:, m * 128:(m + 1) * 128], v_sb[ci][:], start=True, stop=True),
                      reads=[b_kd[ci], b_v[ci]], writes=[b_pssu], inc=True)
                fw.op(fw.dve, lambda e: e.scalar_tensor_tensor(out=st_sb[:, m, :], in0=st_sb[:, m, :], scalar=cdec_sb[:], in1=ps_su[:], op0=ALU.mult, op1=ALU.add),
                      reads=[b_st, b_pssu, b_const], writes=[b_st])
            fw.op(fw.act, lambda e: e.activation(out=stb_sb[:], in_=st_sb[:], func=AF.Copy), reads=[b_st], writes=[b_stb])
            fw.op(fw.dve, lambda e: e.bn_stats(out=stat_sb[ci][:], in_=ps_out[:]), reads=[b_psout], writes=[b_stat[ci]])
            fw.op(fw.dve, lambda e: e.bn_aggr(out=mv_sb[ci][:], in_=stat_sb[ci][:]), reads=[b_stat[ci]], writes=[b_mv[ci]])
            fw.op(fw.act, lambda e: e.activation(out=rs_sb[ci][:], in_=mv_sb[ci][:, 1:2], func=AF.Sqrt, bias=eps_sb[:], scale=1.0),
                  reads=[b_mv[ci], b_const], writes=[b_rs[ci]])
            fw.op(fw.dve, lambda e: e.reciprocal(out=rs_sb[ci][:], in_=rs_sb[ci][:]), reads=[b_rs[ci]], writes=[b_rs[ci]])
            fw.op(fw.dve, lambda e: e.tensor_scalar(out=t_sb[ci][:], in0=ps_out[:], scalar1=mv_sb[ci][:, 0:1], scalar2=rs_sb[ci][:],
                                                    op0=ALU.subtract, op1=ALU.mult),
                  reads=[b_psout, b_mv[ci], b_rs[ci]], writes=[b_t[ci]])
            fw.op(fw.pool, lambda e: e.tensor_tensor(out=y_sb[ci][:], in0=t_sb[ci][:], in1=gs_sb[ci][:], op=ALU.mult),
                  reads=[b_t[ci], b_gs[ci]], writes=[b_y[ci]])
            fw.dma(y[tok0:tok0 + 128, :], y_sb[ci][:], reads=[b_y[ci]], q=0, is_output=True)
    fw.finish()
    return nc


def stage_a_consts(SEQ):
    H = 8
    half = 128
    freqs = (10000.0 ** (-np.arange(half, dtype=np.float32) / half)).astype(np.float32)
    pos = np.arange(SEQ, dtype=np.float32)
    ang = (freqs[:, None] * pos[None, :]).astype(np.float32)
    cosT = np.cos(ang).astype(np.float32)
    sinT = np.sin(ang).astype(np.float32)
    out = []
    idx = np.arange(128, dtype=np.float64)
    for h in range(H):
        lg = np.log(1.0 - 2.0 ** (-5.0 - h))
        diff = idx[None, :] - idx[:, None]
        decT = np.where(diff >= 0, np.exp(np.maximum(diff, 0) * lg), 0.0).astype(np.float32)
        qd = np.exp((idx + 1.0) * lg).astype(np.float32)
        kd = np.exp((127.0 - idx) * lg).astype(np.float32)
        cd = np.float32(np.exp(128.0 * lg))
        out.append(dict(
            cosT=cosT, sinT=sinT, decT=decT,
            qdec=np.ascontiguousarray(np.broadcast_to(np.tile(qd, 4)[None, :], (128, 512))).astype(np.float32),
            kdec=kd.reshape(128, 1).copy(), cdec=np.full((128, 1), cd, np.float32),
            ident=np.eye(128, dtype=np.float32)))
    return out


RMS_EPS = 1e-6
D = 2048
KC = 16
DFF = 4096


def build_stage_b(T, KA, variant, WDT=BF16):
    nc = bass.Bass("TRN2", target_bir_lowering=False)
    KAC = KA // 128
    TT = 512
    assert T % TT == 0
    di = lambda n, s, dt: nc.dram_tensor(n, s, dt, kind="ExternalInput").ap()
    do = lambda n, s, dt: nc.dram_tensor(n, s, dt, kind="ExternalOutput").ap()
    hT = di("hT", [D, 2 + T], F32)
    aT = di("aT", [KA, 2 + T], BF16)
    w_mix = di("w_mix", [16, 128, KAC * 128], WDT)
    w_in = di("w_in", [64, 128, KC * 128], WDT)
    w_out = di("w_out", [16, 128, 32 * 128], WDT)
    gains = di("gains", [128, 3, KC], F32)
    convw = di("convw", [128, 64, 4], F32)
    if variant == "mid":
        w_kv = di("w_kv", [24, 128, KC * 128], WDT)
        w_q = di("w_q", [17, 128, KC * 128], WDT)
        h2T = do("h2T", [D, T], F32)
        kvT = do("kvT", [24 * 128, T], BF16)
        qT = do("qT", [16 * 128, T], BF16)
        gT = do("gT", [128, T], F32)
    else:
        outT = do("outT", [D, T], F32)

    fw = FW(nc)
    S = fw.sbuf
    h_sb = S("h_sb", [128, KC, TT], F32)
    a_sb = S("a_sb", [128, 32, TT], BF16)
    xn_sb = S("xn_sb", [128, KC, TT], BF16)
    sq_sb = S("sq_sb", [128, KC, TT], BF16)
    act_sb = S("act_sb", [128, 32, TT], BF16)
    NW = 4
    w_sb = [S(f"w_sb{i}", [128, 32 * 128], WDT) for i in range(NW)]
    u_sb = [S(f"u_sb{i}", [128, 2 + TT], F32) for i in range(4)]
    c_sb = [S(f"c_sb{i}", [128, TT], F32) for i in range(4)]
    carry_sb = S("carry_sb", [128, 64, 2], F32)
    rstd_sb = S("rstd_sb", [128, TT], F32)
    gains_sb = S("gains_sb", [128, 3, KC], F32)
    convw_sb = S("convw_sb", [128, 64, 4], F32)
    ones_sb = S("ones_sb", [128, 128], BF16)
    eps_sb = S("eps_sb", [128, 1], F32)
    g_sb = S("g_sb", [128, TT], F32)

    P = fw.psum
    ps = [P(f"ps{i}", [128, 512]) for i in range(6)]
    ps_ss = P("ps_ss", [128, 512])
    B = Buf
    b_h, b_a, b_xn, b_sq, b_act, b_rstd, b_const, b_carry, b_g = B(), B(), B(), B(), B(), B(), B(), B(), B()
    b_w = [B() for _ in range(NW)]
    b_u = [B() for _ in range(4)]
    b_c = [B() for _ in range(4)]
    b_ps = [B() for _ in range(6)]
    b_psss = B()

    fw.dma(gains_sb[:], gains, writes=[b_const], q=0)
    fw.dma(convw_sb[:], convw, writes=[b_const], q=0)
    fw.op(fw.dve, lambda e: e.memset(ones_sb[:], 1.0), writes=[b_const])
    fw.op(fw.dve, lambda e: e.memset(eps_sb[:], RMS_EPS), writes=[b_const])

    hT_v = hT.rearrange("(c p) t -> p c t", p=128)
    aT_v = aT.rearrange("(c p) t -> p c t", p=128)
    wq_state = {"n": 0, "ps": 0}

    def load_w(src, m, kc):
        i = wq_state["n"] % NW
        wq_state["n"] += 1
        half = kc * 128 // 2
        fw.dma(w_sb[i][:, 0:half], src[m, :, 0:half], writes=[b_w[i]], q=0)
        fw.dma(w_sb[i][:, half:kc * 128], src[m, :, half:kc * 128], writes=[b_w[i]], q=0)
        return i

    def linear(src_w, M, kc, rhs_sb, b_rhs, n, epilogue, prefetch=2, mrows=None):
        loaded = []
        for m in range(min(prefetch, M)):
            loaded.append(load_w(src_w, m, kc))
        for m in range(M):
            if m + prefetch < M:
                loaded.append(load_w(src_w, m + prefetch, kc))
            wi = loaded[m]
            pi = wq_state["ps"] % 6
            wq_state["ps"] += 1
            for k in range(kc):
                fw.op(fw.pe, lambda e: e.matmul(ps[pi][:, 0:n], w_sb[wi][:, k * 128:(k + 1) * 128], rhs_sb[:, k, 0:n],
                                                start=(k == 0), stop=(k == kc - 1)),
                      reads=[b_w[wi], b_rhs], writes=[b_ps[pi]], inc=(k == kc - 1))
            epilogue(m, ps[pi], b_ps[pi])

    def rms(n, gi, dst_sb, b_dst, compute_rstd=True):
        if compute_rstd:
            fw.op(fw.act, lambda e: e.activation(out=sq_sb[:, :, 0:n], in_=h_sb[:, :, 0:n], func=AF.Square), reads=[b_h], writes=[b_sq])
            for c in range(KC):
                fw.op(fw.pe, lambda e: e.matmul(ps_ss[:, 0:n], ones_sb[:], sq_sb[:, c, 0:n], start=(c == 0), stop=(c == KC - 1)),
                      reads=[b_sq, b_const], writes=[b_psss], inc=(c == KC - 1))
            fw.op(fw.act, lambda e: e.activation(out=rstd_sb[:, 0:n], in_=ps_ss[:, 0:n], func=AF.Sqrt, bias=eps_sb[:], scale=1.0 / D),
                  reads=[b_psss, b_const], writes=[b_rstd])
            fw.op(fw.dve, lambda e: e.reciprocal(out=rstd_sb[:, 0:n], in_=rstd_sb[:, 0:n]), reads=[b_rstd], writes=[b_rstd])
        for c in range(KC):
            fw.op(fw.dve, lambda e: e.scalar_tensor_tensor(out=dst_sb[:, c, 0:n], in0=h_sb[:, c, 0:n], scalar=gains_sb[:, gi, c:c + 1],
                                                           in1=rstd_sb[:, 0:n], op0=ALU.mult, op1=ALU.mult),
                  reads=[b_h, b_rstd, b_const], writes=[b_dst])

    tiles = [(0, 2, True)] + [(2 + i * TT, TT, False) for i in range(T // TT)]
    for (c0, n, halo) in tiles:
        o0 = c0 - 2
        fw.dma(h_sb[:, 0:8, 0:n], hT_v[:, 0:8, c0:c0 + n], writes=[b_h], q=0)
        fw.dma(h_sb[:, 8:16, 0:n], hT_v[:, 8:16, c0:c0 + n], writes=[b_h], q=0)
        for k0 in range(0, KAC, 8):
            fw.dma(a_sb[:, k0:k0 + 8, 0:n], aT_v[:, k0:k0 + 8, c0:c0 + n], writes=[b_a], q=0)

        def ep1(m, p, bp):
            fw.op(fw.dve, lambda e: e.tensor_tensor(out=h_sb[:, m, 0:n], in0=h_sb[:, m, 0:n], in1=p[:, 0:n], op=ALU.add),
                  reads=[bp, b_h], writes=[b_h])
        linear(w_mix, 16, KAC, a_sb, b_a, n, ep1)
        rms(n, 0, xn_sb, b_xn)

        def ep3(mm, p, bp):
            j = mm // 2
            isb = mm % 2
            ch = j + 32 * isb
            ui = (mm % 4)
            U, bU = u_sb[ui], b_u[ui]
            C, bC = c_sb[ui], b_c[ui]
            fw.op(fw.pool, lambda e: e.tensor_copy(out=U[:, 0:2], in_=carry_sb[:, ch, :]), reads=[b_carry], writes=[bU])
            fw.op(fw.act, lambda e: e.activation(out=U[:, 2:2 + n], in_=p[:, 0:n], func=AF.Copy), reads=[bp], writes=[bU])
            fw.op(fw.pool, lambda e: e.tensor_copy(out=carry_sb[:, ch, :], in_=U[:, n:n + 2]), reads=[bU], writes=[b_carry])
            if halo:
                return
            cw = convw_sb
            fw.op(fw.dve, lambda e: e.tensor_scalar(out=C[:, 0:n], in0=U[:, 2:2 + n], scalar1=cw[:, ch, 2:3], scalar2=cw[:, ch, 3:4],
                                                    op0=ALU.mult, op1=ALU.add), reads=[bU, b_const], writes=[bC])
            fw.op(fw.dve, lambda e: e.scalar_tensor_tensor(out=C[:, 0:n], in0=U[:, 1:1 + n], scalar=cw[:, ch, 1:2], in1=C[:, 0:n],
                                                           op0=ALU.mult, op1=ALU.add), reads=[bU, bC, b_const], writes=[bC])
            fw.op(fw.dve, lambda e: e.scalar_tensor_tensor(out=C[:, 0:n], in0=U[:, 0:n], scalar=cw[:, ch, 0:1], in1=C[:, 0:n],
                                                           op0=ALU.mult, op1=ALU.add), reads=[bU, bC, b_const], writes=[bC])
            if isb == 0:
                fw.op(fw.act, lambda e: e.activation(out=C[:, 0:n], in_=C[:, 0:n], func=AF.Silu), reads=[bC], writes=[bC])
            else:
                Ca, bCa = c_sb[ui - 1], b_c[ui - 1]
                fw.op(fw.pool, lambda e: e.tensor_tensor(out=act_sb[:, j, 0:n], in0=Ca[:, 0:n], in1=C[:, 0:n], op=ALU.mult),
                      reads=[bC, bCa], writes=[b_act])

        class WIn:
            pass
        w_in_perm = w_in
        linear(w_in_perm, 64, KC, xn_sb, b_xn, n, ep3)
        if halo:
            continue

        linear(w_out, 16, 32, act_sb, b_act, n, ep1)

        if variant == "mid":
            fw.dma(h2T.rearrange("(c p) t -> p c t", p=128)[:, :, o0:o0 + n], h_sb[:, :, 0:n], reads=[b_h], q=0, is_output=True)
            rms(n, 1, xn_sb, b_xn)

            def ep_kv(m, p, bp):
                fw.op(fw.act, lambda e: e.activation(out=a_sb[:, m, 0:n], in_=p[:, 0:n], func=AF.Copy), reads=[bp], writes=[b_a])
            linear(w_kv, 24, KC, xn_sb, b_xn, n, ep_kv)
            fw.dma(kvT.rearrange("(c p) t -> p c t", p=128)[:, :, o0:o0 + n], a_sb[:, 0:24, 0:n], reads=[b_a], q=0, is_output=True)
            rms(n, 2, xn_sb, b_xn, compute_rstd=False)

            def ep_q(m, p, bp):
                if m < 16:
                    fw.op(fw.act, lambda e: e.activation(out=act_sb[:, m, 0:n], in_=p[:, 0:n], func=AF.Copy, scale=128.0 ** -0.5),
                          reads=[bp], writes=[b_act])
                else:
                    fw.op(fw.act, lambda e: e.activation(out=g_sb[:, 0:n], in_=p[:, 0:n], func=AF.Sigmoid), reads=[bp], writes=[b_g])
            linear(w_q, 17, KC, xn_sb, b_xn, n, ep_q)
            fw.dma(qT.rearrange("(c p) t -> p c t", p=128)[:, :, o0:o0 + n], act_sb[:, 0:16, 0:n], reads=[b_act], q=0, is_output=True)
            fw.dma(gT[:, o0:o0 + n], g_sb[:, 0:n], reads=[b_g], q=0, is_output=True)
        else:
            fw.op(fw.act, lambda e: e.activation(out=sq_sb[:, :, 0:n], in_=h_sb[:, :, 0:n], func=AF.Square), reads=[b_h], writes=[b_sq])
            for c in range(KC):
                fw.op(fw.pe, lambda e: e.matmul(ps_ss[:, 0:n], ones_sb[:], sq_sb[:, c, 0:n], start=(c == 0), stop=(c == KC - 1)),
                      reads=[b_sq, b_const], writes=[b_psss], inc=(c == KC - 1))
            fw.op(fw.act, lambda e: e.activation(out=rstd_sb[:, 0:n], in_=ps_ss[:, 0:n], func=AF.Sqrt, bias=eps_sb[:], scale=1.0 / D),
                  reads=[b_psss, b_const], writes=[b_rstd])
            fw.op(fw.dve, lambda e: e.reciprocal(out=rstd_sb[:, 0:n], in_=rstd_sb[:, 0:n]), reads=[b_rstd], writes=[b_rstd])
            for c in range(KC):
                fw.op(fw.dve, lambda e: e.scalar_tensor_tensor(out=h_sb[:, c, 0:n], in0=h_sb[:, c, 0:n], scalar=gains_sb[:, 1, c:c + 1],
                                                               in1=rstd_sb[:, 0:n], op0=ALU.mult, op1=ALU.mult),
                      reads=[b_h, b_rstd, b_const], writes=[b_h])
            fw.dma(outT.rearrange("(c p) t -> p c t", p=128)[:, :, o0:o0 + n], h_sb[:, :, 0:n], reads=[b_h], q=0, is_output=True)
    fw.finish()
    return nc


def lay_w(w, pad_to=None):
    K, M = w.shape
    if pad_to is not None and M < pad_to:
        w = np.concatenate([w, np.zeros((K, pad_to - M), w.dtype)], axis=1)
        M = pad_to
    return np.ascontiguousarray(w.reshape(K // 128, 128, M // 128, 128).transpose(2, 1, 0, 3)).reshape(M // 128, 128, K)


def lay_gain(g):
    return np.ascontiguousarray(g.reshape(16, 128).T)


NEGB = -30000.0


def build_stage_c(S):
    nc = bass.Bass("TRN2", target_bir_lowering=False)
    NQT = S // 128
    NC = S // 16
    NCT = (NC + 127) // 128
    NCP = NCT * 128
    NSEL = S // 64
    NBH = (NSEL + 127) // 128
    NBW = NBH * 128
    EW = min(NSEL, 128) * 64
    di = lambda n, s, dt: nc.dram_tensor(n, s, dt, kind="ExternalInput").ap()
    kcT = di("kcT", [128, S], BF16)
    vcT = di("vcT", [128, S], BF16)
    ksT = di("ksT", [128, S], BF16)
    kwT = di("kwT", [128, S], BF16)
    vs = di("vs", [S, 128], BF16)
    vw = di("vw", [S, 128], BF16)
    qT = di("qT", [NQT, 128, 512], BF16)
    gates = di("gates", [NQT, 128, 12], F32)
    w1 = [di("w1k", [128, 32, 256], F32), di("w1v", [128, 32, 256], F32)]
    peT = [di("peTk", [128, 32], F32), di("peTv", [128, 32], F32)]
    w2 = [di("w2k", [128, 2, 128], F32), di("w2v", [128, 2, 128], F32)]
    ident = di("ident", [128, 128], F32)
    ebig = di("ebig", [128, EW], F32)
    caus = di("caus", [128, 128], F32)
    low = di("low", [128, 128], F32)
    cposrow = di("cposrow", [128, NCP], F32)
    tcol = di("tcol", [128, NQT], F32)
    curcol = di("curcol", [128, NQT], F32)
    jrow = di("jrow", [128, NBW], F32)
    qrow = di("qrow", [128, 128], F32)
    cpq = di("cpq", [128, NCT, NQT], F32)
    o = nc.dram_tensor("o", [S, 512], BF16, kind="ExternalOutput").ap()

    fw = FW(nc)
    Sb = fw.sbuf
    B = Buf
    CBW = 16 * 512 + 32
    cbuf = Sb("cbuf", [128, CBW], BF16)
    w1_sb = Sb("w1_sb", [128, 32, 256], BF16)
    peT_sb = Sb("peT_sb", [128, 32], BF16)
    w2_sb = Sb("w2_sb", [128, 2, 128], BF16)
    hb_sb = Sb("hb_sb", [128, 2], F32)
    hx = [Sb(f"hx{i}", [128, 512], F32) for i in range(3)]
    hg_sb = Sb("hg_sb", [128, 2, 512], BF16)
    kcmpT = Sb("kcmpT", [128, NCP], BF16)
    vcmp = Sb("vcmp", [128, NCT, 129], BF16)
    ksT_sb = Sb("ksT_sb", [128, S], BF16)
    vs_sb = Sb("vs_sb", [128, NQT, 129], BF16)
    kw_sb = Sb("kw_sb", [128, 8, 128], BF16)
    vw_sb = Sb("vw_sb", [128, 8, 129], BF16)
    ident_sb = Sb("ident_sb", [128, 128], BF16)
    identf_sb = Sb("identf_sb", [128, 128], F32)
    ebig_sb = Sb("ebig_sb", [128, EW], BF16)
    caus_sb = Sb("caus_sb", [128, 128], BF16)
    low_sb = Sb("low_sb", [128, 128], BF16)
    cposrow_sb = Sb("cposrow_sb", [128, NCP], F32)
    tcol_sb = Sb("tcol_sb", [128, NQT], F32)
    curcol_sb = Sb("curcol_sb", [128, NQT], F32)
    jrow_sb = Sb("jrow_sb", [128, NBW], F32)
    qrow_sb = Sb("qrow_sb", [128, 128], F32)
    cpq_sb = Sb("cpq_sb", [128, NCT, NQT], F32)
    zero_sb = Sb("zero_sb", [128, 512], BF16)
    q_sb = [Sb(f"q_sb{i}", [128, 512], BF16) for i in range(2)]
    g_sb = [Sb(f"g_sb{i}", [128, 12], F32) for i in range(2)]
    bsel_sb = Sb("bsel_sb", [128, NCP], F32)
    s_sb = [Sb(f"s_sb{i}", [128, NCP], F32) for i in range(2)]
    e_sb = [Sb(f"e_sb{i}", [128, NCP], F32) for i in range(2)]
    sum_sb = [Sb(f"sum_sb{i}", [128, 1], F32) for i in range(2)]
    imp_sb = Sb("imp_sb", [128, NCP + 8], F32)
    psl_sb = Sb("psl_sb", [128, NBW], F32)
    d_sb = Sb("d_sb", [128, NBW], F32)
    val_sb = Sb("val_sb", [128, NBW], F32)
    fm_sb = Sb("fm_sb", [128, NBW], F32)
    sco_sb = Sb("sco_sb", [128, NBW], F32)
    wrk_sb = Sb("wrk_sb", [128, NBW], F32)
    m8_sb = Sb("m8_sb", [128, 16], F32)
    nsel_sb = Sb("nsel_sb", [128, NBW], F32)
    nselT_sbs = [Sb(f"nselT_sb{i}", [128, NBH, 128], BF16) for i in range(2)]
    cb_sb = [Sb(f"cb_sb{i}", [128, 128], BF16) for i in range(2)]
    pT_sb = [Sb(f"pT_sb{i}", [128, 512], BF16) for i in range(3)]
    coef_sb = [Sb(f"coef_sb{i}", [128, 2], F32) for i in range(4)]
    otmp_sb = [Sb(f"otmp_sb{i}", [128, 2, 128], F32) for i in range(2)]
    oacc_sb = [Sb(f"oacc_sb{i}", [128, 4, 128], F32) for i in range(2)]
    oout_sb = [Sb(f"oout_sb{i}", [128, 512], BF16) for i in range(2)]

    P = fw.psum
    ps_s = [P(f"ps_s{i}", [128, 512]) for i in range(3)]
    ps_o = [P(f"ps_o{i}", [128, 2, 256]) for i in range(4)]
    ps_sel = [P("ps_sel0", [128, 512])]
    ps_sel = [ps_sel[0], ps_sel[0]]
    b_pss = [B(), B(), B()]
    b_pso = [B() for _ in range(4)]
    b_pssel = [B()]
    b_pssel = [b_pssel[0], b_pssel[0]]
    b_const, b_cbuf, b_w1, b_w2, b_hb, b_hg, b_kcmp, b_vcmp, b_ks, b_vs = [B() for _ in range(10)]
    b_hx = [B(), B(), B()]
    b_kw = [B() for _ in range(8)]
    b_vw = [B() for _ in range(8)]
    b_q, b_g = [B(), B()], [B(), B()]
    b_bsel, b_imp, b_psl, b_d, b_val, b_fm, b_sco, b_wrk, b_m8, b_nsel = [B() for _ in range(10)]
    b_nselTs = [B(), B()]
    b_s, b_e, b_sum = [B(), B()], [B(), B()], [B(), B()]
    b_cb = [B(), B()]
    b_pT = [B(), B(), B()]
    b_coef = [B() for _ in range(4)]
    b_otmp = [B(), B()]
    b_oacc, b_oout = [B(), B()], [B(), B()]

    for dst, src in ((ident_sb, ident), (ebig_sb, ebig), (caus_sb, caus), (low_sb, low)):
        fw.dma(dst[:], src, writes=[b_const], q=1)
    for dst, src in ((identf_sb, ident), (cposrow_sb, cposrow), (tcol_sb, tcol), (curcol_sb, curcol), (jrow_sb, jrow),
                     (qrow_sb, qrow), (cpq_sb, cpq)):
        fw.dma(dst[:], src, writes=[b_const], q=0)
    fw.op(fw.dve, lambda e: e.memset(zero_sb[:], 0.0), writes=[b_const])
    fw.op(fw.dve, lambda e: e.memset(vs_sb[:, :, 128:129], 1.0), writes=[b_vs])
    fw.op(fw.dve, lambda e: e.memset(vw_sb[:, :, 128:129], 1.0), writes=[b_vw[0]])
    fw.op(fw.dve, lambda e: e.memset(vcmp[:, :, 128:129], 1.0), writes=[b_vcmp])
    fw.op(fw.dve, lambda e: e.memset(imp_sb[:], 0.0), writes=[b_imp])
    for i in range(1, 8):
        b_vw[i].w = b_vw[0].w
    fw.dma(ksT_sb[:], ksT, writes=[b_ks], q=0)
    vs_v = vs.rearrange("(t p) d -> p t d", p=128)
    for t0 in range(0, NQT, 16):
        t1 = min(NQT, t0 + 16)
        fw.dma(vs_sb[:, t0:t1, 0:128], vs_v[:, t0:t1, :], writes=[b_vs], q=0)

    for kv in range(2):
        src = kcT if kv == 0 else vcT
        fw.dma(w1_sb[:, 0:16, :], w1[kv][:, 0:16, :], writes=[b_w1], q=1)
        fw.dma(w1_sb[:, 16:32, :], w1[kv][:, 16:32, :], writes=[b_w1], q=1)
        fw.dma(peT_sb[:], peT[kv], writes=[b_w1], q=1)
        fw.dma(w2_sb[:], w2[kv], writes=[b_w2], q=1)
        for mc in range(2):
            for p in range(32):
                fw.op(fw.pe, lambda e: e.matmul(ps_sel[0][:, mc:mc + 1], w1_sb[:, p, mc * 128:(mc + 1) * 128], peT_sb[:, p:p + 1],
                                                start=(p == 0), stop=(p == 31)),
                      reads=[b_w1], writes=[b_pssel[0]], inc=(p == 31))
        fw.op(fw.dve, lambda e: e.tensor_copy(out=hb_sb[:], in_=ps_sel[0][:, 0:2]), reads=[b_pssel[0]], writes=[b_hb])
        for cbk in range((NC + 511) // 512):
            nb = min(512, NC - 512 * cbk)
            tok0 = 8192 * cbk
            need = 16 * nb + 16
            avail = min(need, S - tok0)
            if avail < need:
                fw.op(fw.dve, lambda e: e.memset(cbuf[:, avail:need], 0.0), writes=[b_cbuf])
            fw.dma(cbuf[:, 0:avail], src[:, tok0:tok0 + avail], writes=[b_cbuf], q=0)
            for mc in range(2):
                pb = ps_s[mc]
                for p in range(32):
                    fw.op(fw.pe, lambda e: e.matmul(pb[:, 0:nb], w1_sb[:, p, mc * 128:(mc + 1) * 128],
                                                    cbuf[:, p:p + 16 * nb].rearrange("d (i s) -> d i s", s=16)[:, :, 0],
                                                    start=(p == 0), stop=(p == 31)),
                          reads=[b_w1, b_cbuf], writes=[b_pss[mc]], inc=(p == 31))
                X, X2, T3 = hx[0], hx[1], hx[2]
                fw.op(fw.act, lambda e: e.activation(out=X[:, 0:nb], in_=pb[:, 0:nb], func=AF.Identity, bias=hb_sb[:, mc:mc + 1], scale=1.0),
                      reads=[b_pss[mc], b_hb], writes=[b_hx[0]])
                fw.op(fw.dve, lambda e: e.tensor_tensor(out=X2[:, 0:nb], in0=X[:, 0:nb], in1=X[:, 0:nb], op=ALU.mult),
                      reads=[b_hx[0]], writes=[b_hx[1]])
                fw.op(fw.dve, lambda e: e.tensor_scalar(out=X2[:, 0:nb], in0=X2[:, 0:nb], scalar1=0.044715, scalar2=1.0, op0=ALU.mult, op1=ALU.add),
                      reads=[b_hx[1]], writes=[b_hx[1]])
                fw.op(fw.dve, lambda e: e.tensor_tensor(out=X2[:, 0:nb], in0=X2[:, 0:nb], in1=X[:, 0:nb], op=ALU.mult),
                      reads=[b_hx[0], b_hx[1]], writes=[b_hx[1]])
                fw.op(fw.act, lambda e: e.activation(out=T3[:, 0:nb], in_=X2[:, 0:nb], func=AF.Sigmoid, scale=1.5957691216057308),
                      reads=[b_hx[1]], writes=[b_hx[2]])
                fw.op(fw.dve, lambda e: e.tensor_tensor(out=hg_sb[:, mc, 0:nb], in0=T3[:, 0:nb], in1=X[:, 0:nb], op=ALU.mult),
                      reads=[b_hx[0], b_hx[2]], writes=[b_hg])
            if kv == 0:
                for mc in range(2):
                    fw.op(fw.pe, lambda e: e.matmul(ps_sel[0][:, 0:nb], w2_sb[:, mc, :], hg_sb[:, mc, 0:nb], start=(mc == 0), stop=(mc == 1)),
                          reads=[b_w2, b_hg], writes=[b_pssel[0]], inc=(mc == 1))
                fw.op(fw.act, lambda e: e.activation(out=kcmpT[:, 512 * cbk:512 * cbk + nb], in_=ps_sel[0][:, 0:nb], func=AF.Copy),
                      reads=[b_pssel[0]], writes=[b_kcmp])
            else:
                for ct in range((nb + 127) // 128):
                    w_ = min(128, nb - ct * 128)
                    for mc in range(2):
                        fw.op(fw.pe, lambda e: e.matmul(ps_sel[0][0:w_, ct * 128:(ct + 1) * 128], hg_sb[:, mc, ct * 128:ct * 128 + w_], w2_sb[:, mc, :],
                                                        start=(mc == 0), stop=(mc == 1)),
                              reads=[b_w2, b_hg], writes=[b_pssel[0]], inc=(mc == 1))
                    fw.op(fw.act, lambda e: e.activation(out=vcmp[0:w_, 4 * cbk + ct, 0:128], in_=ps_sel[0][0:w_, ct * 128:(ct + 1) * 128], func=AF.Copy),
                          reads=[b_pssel[0]], writes=[b_vcmp])

    cnt = {"s": 0, "p": 0, "o": 0, "cb": 0}

    def load_q(qt):
        bi = qt % 2
        fw.dma(q_sb[bi][:], qT[qt], writes=[b_q[bi]], q=0)
        fw.dma(g_sb[bi][:], gates[qt], writes=[b_g[bi]], q=0)
        sl = qt % 8
        fw.dma(kw_sb[:, sl, :], kwT[:, qt * 128:(qt + 1) * 128], writes=[b_kw[sl]], q=0)
        fw.dma(vw_sb[:, sl, 0:128], vw[qt * 128:(qt + 1) * 128, :], writes=[b_vw[sl]], q=0)

    def attend(qt, tiles, br, first_branch, between=None):
        bi = qt % 2
        oset = cnt["o"] % 2
        cnt["o"] += 1
        po = [ps_o[2 * oset], ps_o[2 * oset + 1]]
        bpo = [b_pso[2 * oset], b_pso[2 * oset + 1]]
        for hb in range(2):
            fw.op(fw.pe, lambda e: e.matmul(po[hb][:, :, 0:129], zero_sb[:, 0:128], zero_sb[:, 0:258].rearrange("p (a b) -> p a b", a=2),
                                            start=True, stop=False, skip_group_check=True),
                  reads=[b_const], writes=[bpo[hb]], inc=False)
        nt = len(tiles)
        slots = []

        def emit_s(ti):
            kT_ap, b_k, va_ap, b_v, biases = tiles[ti]
            si = cnt["s"] % 3
            cnt["s"] += 1
            fw.op(fw.pe, lambda e: e.matmul(ps_s[si][:], kT_ap, q_sb[bi][:], start=True, stop=(len(biases) == 0)),
                  reads=[b_k, b_q[bi]], writes=[b_pss[si]], inc=(len(biases) == 0))
            for bj, (l_ap, r_ap, bufs) in enumerate(biases):
                last = bj == len(biases) - 1
                fw.op(fw.pe, lambda e: e.matmul(ps_s[si][:], l_ap, r_ap, start=False, stop=last),
                      reads=bufs, writes=[b_pss[si]], inc=last)
            slots.append(si)

        emit_s(0)
        if nt > 1:
            emit_s(1)
        for ti in range(nt):
            kT_ap, b_k, va_ap, b_v, biases = tiles[ti]
            si = slots[ti]
            pi = cnt["p"] % 3
            cnt["p"] += 1
            fw.op(fw.act, lambda e: e.activation(out=pT_sb[pi][:], in_=ps_s[si][:], func=AF.Exp), reads=[b_pss[si]], writes=[b_pT[pi]])
            if ti + 2 < nt:
                emit_s(ti + 2)
            for g in range(4):
                lastmm = (ti == nt - 1)
                fw.op(fw.pe, lambda e: e.matmul(po[g // 2][:, g % 2, 0:129], pT_sb[pi][:, g * 128:(g + 1) * 128], va_ap,
                                                start=False, stop=lastmm, skip_group_check=True),
                      reads=[b_pT[pi], b_v], writes=[bpo[g // 2]], inc=(lastmm and g % 2 == 1) or (g == 3))
            if between is not None:
                between(ti)
        for hb in range(2):
            ci = (cnt["o"] * 2 + hb) % 4
            cf, bcf = coef_sb[ci], b_coef[ci]
            fw.op(fw.dve, lambda e: e.tensor_scalar(out=cf[:], in0=po[hb][:, :, 128], scalar1=1e-30, scalar2=None, op0=ALU.max),
                  reads=[bpo[hb]], writes=[bcf])
            fw.op(fw.dve, lambda e: e.reciprocal(out=cf[:], in_=cf[:]), reads=[bcf], writes=[bcf])
            fw.op(fw.dve, lambda e: e.tensor_tensor(out=cf[:], in0=cf[:], in1=g_sb[bi][:, br * 4 + 2 * hb: br * 4 + 2 * hb + 2], op=ALU.mult),
                  reads=[bcf, b_g[bi]], writes=[bcf])
            dst = oacc_sb[bi][:, 2 * hb:2 * hb + 2, :]
            if first_branch:
                fw.op(fw.dve, lambda e: e.tensor_tensor(out=dst, in0=po[hb][:, :, 0:128], in1=cf[:].unsqueeze(2).to_broadcast([128, 2, 128]), op=ALU.mult),
                      reads=[bpo[hb], bcf], writes=[b_oacc[bi]])
            else:
                ot, bot = otmp_sb[hb], b_otmp[hb]
                fw.op(fw.dve, lambda e: e.tensor_tensor(out=ot[:], in0=po[hb][:, :, 0:128], in1=cf[:].unsqueeze(2).to_broadcast([128, 2, 128]), op=ALU.mult),
                      reads=[bpo[hb], bcf], writes=[bot])
                fw.op(fw.pool, lambda e: e.tensor_tensor(out=dst, in0=dst, in1=ot[:], op=ALU.add), reads=[bot, b_oacc[bi]], writes=[b_oacc[bi]])

    def sel_gen(qt):
        bi = qt % 2
        nselT_sb = nselT_sbs[qt % 2]
        b_nselT = b_nselTs[qt % 2]
        cmax = 8 * qt + 6
        ncw = min(NCP, ((cmax + 1 + 7) // 8) * 8)
        ncw = max(ncw, 8)
        nbw = min(NBW, max(16, ((2 * qt + 2 + 7) // 8) * 8))
        fw.op(fw.dve, lambda e: e.tensor_scalar(out=bsel_sb[:, 0:ncw], in0=cposrow_sb[:, 0:ncw], scalar1=tcol_sb[:, qt:qt + 1], scalar2=NEGB,
                                                op0=ALU.is_gt, op1=ALU.mult), reads=[b_const], writes=[b_bsel])
        for g in range(4):
            gi = g % 2
            yield
            for c0 in range(0, ncw, 512):
                if c0 > 0:
                    yield
                c1 = min(ncw, c0 + 512)
                pb = ps_sel[c0 // 512]
                fw.op(fw.pe, lambda e: e.matmul(pb[:, 0:c1 - c0], q_sb[bi][:, g * 128:(g + 1) * 128], kcmpT[:, c0:c1], start=True, stop=True),
                      reads=[b_q[bi], b_kcmp], writes=[b_pssel[c0 // 512]], inc=True)
                fw.op(fw.dve, lambda e: e.tensor_tensor(out=s_sb[gi][:, c0:c1], in0=pb[:, 0:c1 - c0], in1=bsel_sb[:, c0:c1], op=ALU.add),
                      reads=[b_pssel[c0 // 512], b_bsel], writes=[b_s[gi]])
            fw.op(fw.act, lambda e: e.activation(out=e_sb[gi][:, 0:ncw], in_=s_sb[gi][:, 0:ncw], func=AF.Exp, accum_out=sum_sb[gi][:]),
                  reads=[b_s[gi]], writes=[b_e[gi], b_sum[gi]])
            fw.op(fw.dve, lambda e: e.tensor_scalar(out=sum_sb[gi][:], in0=sum_sb[gi][:], scalar1=1e-30, scalar2=None, op0=ALU.max),
                  reads=[b_sum[gi]], writes=[b_sum[gi]])
            fw.op(fw.dve, lambda e: e.reciprocal(out=sum_sb[gi][:], in_=sum_sb[gi][:]), reads=[b_sum[gi]], writes=[b_sum[gi]])
            if g == 0:
                fw.op(fw.dve, lambda e: e.tensor_scalar(out=imp_sb[:, 1:1 + ncw], in0=e_sb[gi][:, 0:ncw], scalar1=sum_sb[gi][:], scalar2=None, op0=ALU.mult),
                      reads=[b_e[gi], b_sum[gi]], writes=[b_imp])
            else:
                fw.op(fw.dve, lambda e: e.scalar_tensor_tensor(out=imp_sb[:, 1:1 + ncw], in0=e_sb[gi][:, 0:ncw], scalar=sum_sb[gi][:], in1=imp_sb[:, 1:1 + ncw],
                                                               op0=ALU.mult, op1=ALU.add), reads=[b_e[gi], b_sum[gi], b_imp], writes=[b_imp])
        if ncw < NCP:
            fw.op(fw.dve, lambda e: e.memset(imp_sb[:, 1 + ncw:min(NCP + 8, 1 + ncw + 8)], 0.0), writes=[b_imp])
        nbe = min(nbw, (ncw + 3) // 4)
        fw.op(fw.dve, lambda e: e.memset(psl_sb[:, 0:nbw], 0.0), writes=[b_psl])
        fw.op(fw.dve, lambda e: e.tensor_reduce(out=psl_sb[:, 0:nbe], in_=imp_sb[:, 0:4 * nbe].rearrange("p (a b) -> p a b", b=4), axis=AX.X, op=ALU.add),
              reads=[b_imp], writes=[b_psl])
        fw.op(fw.dve, lambda e: e.tensor_tensor(out=psl_sb[:, 0:nbe], in0=psl_sb[:, 0:nbe],
                                                in1=imp_sb[:, 4:4 + 4 * nbe].rearrange("p (a b) -> p a b", b=4)[:, :, 0], op=ALU.add),
              reads=[b_imp, b_psl], writes=[b_psl])
        fw.op(fw.dve, lambda e: e.tensor_scalar(out=d_sb[:, 0:nbw], in0=jrow_sb[:, 0:nbw], scalar1=curcol_sb[:, qt:qt + 1], scalar2=None, op0=ALU.subtract),
              reads=[b_const], writes=[b_d])
        fw.op(fw.dve, lambda e: e.tensor_scalar(out=val_sb[:, 0:nbw], in0=d_sb[:, 0:nbw], scalar1=0.0, scalar2=None, op0=ALU.is_le),
              reads=[b_d], writes=[b_val])
        fw.op(fw.dve, lambda e: e.scalar_tensor_tensor(out=fm_sb[:, 0:nbw], in0=d_sb[:, 0:nbw], scalar=-1.0, in1=val_sb[:, 0:nbw], op0=ALU.is_ge, op1=ALU.mult),
              reads=[b_d, b_val], writes=[b_fm])
        fw.op(fw.dve, lambda e: e.scalar_tensor_tensor(out=sco_sb[:, 0:nbw], in0=fm_sb[:, 0:nbw], scalar=100.0, in1=psl_sb[:, 0:nbw], op0=ALU.mult, op1=ALU.add),
              reads=[b_fm, b_psl], writes=[b_sco])
        fw.op(fw.dve, lambda e: e.tensor_scalar(out=sco_sb[:, 0:1], in0=sco_sb[:, 0:1], scalar1=100.0, scalar2=None, op0=ALU.add),
              reads=[b_sco], writes=[b_sco])
        fw.op(fw.dve, lambda e: e.max(out=m8_sb[:, 0:8], in_=sco_sb[:, 0:nbw]), reads=[b_sco], writes=[b_m8])
        fw.op(fw.dve, lambda e: e.match_replace(out=wrk_sb[:, 0:nbw], in_to_replace=m8_sb[:, 0:8], in_values=sco_sb[:, 0:nbw], imm_value=-1.0),
              reads=[b_sco, b_m8], writes=[b_wrk])
        fw.op(fw.dve, lambda e: e.max(out=m8_sb[:, 8:16], in_=wrk_sb[:, 0:nbw]), reads=[b_wrk, b_m8], writes=[b_m8])
        fw.op(fw.dve, lambda e: e.scalar_tensor_tensor(out=wrk_sb[:, 0:nbw], in0=sco_sb[:, 0:nbw], scalar=m8_sb[:, 15:16], in1=val_sb[:, 0:nbw],
                                                       op0=ALU.is_ge, op1=ALU.mult), reads=[b_sco, b_m8, b_val], writes=[b_wrk])
        if nbw < NBW:
            fw.op(fw.dve, lambda e: e.memset(nsel_sb[:, nbw:NBW], NEGB), writes=[b_nsel])
        fw.op(fw.dve, lambda e: e.tensor_scalar(out=nsel_sb[:, 0:nbw], in0=wrk_sb[:, 0:nbw], scalar1=1.0, scalar2=-NEGB, op0=ALU.subtract, op1=ALU.mult),
              reads=[b_wrk], writes=[b_nsel])
        for _ in range(6):
            yield
        nbh_used = (min(NSEL, 2 * qt + 2) + 127) // 128
        for hb in range(nbh_used):
            if hb > 0:
                yield
            fw.op(fw.pe, lambda e: e.transpose(ps_sel[hb][:, 0:128], nsel_sb[:, hb * 128:(hb + 1) * 128], identf_sb[:]),
                  reads=[b_nsel, b_const], writes=[b_pssel[hb]], inc=True)
            fw.op(fw.act, lambda e: e.activation(out=nselT_sb[:, hb, :], in_=ps_sel[hb][:, 0:128], func=AF.Copy), reads=[b_pssel[hb]], writes=[b_nselT])


    load_q(0)
    for _ in sel_gen(0):
        pass
    for qt in range(NQT):
        bi = qt % 2
        if qt + 1 < NQT:
            load_q(qt + 1)
        tiles = []
        cmax = 8 * qt + 6
        nct = cmax // 128 + 1
        for ct in range(min(nct, NCT)):
            biases = []
            if 16 * (128 * ct + 127) + 31 > 128 * qt or (ct == NCT - 1):
                ci = cnt["cb"] % 2
                cnt["cb"] += 1
                fw.op(fw.dve, lambda e: e.tensor_scalar(out=cb_sb[ci][:], in0=qrow_sb[:], scalar1=cpq_sb[:, ct, qt:qt + 1], scalar2=NEGB,
                                                        op0=ALU.is_lt, op1=ALU.mult), reads=[b_const], writes=[b_cb[ci]])
                biases.append((ident_sb[:], cb_sb[ci][:].unsqueeze(1).to_broadcast([128, 4, 128]), [b_const, b_cb[ci]]))
            tiles.append((kcmpT[:, ct * 128:(ct + 1) * 128], b_kcmp, vcmp[:, ct, :], b_vcmp, biases))
        attend(qt, tiles, 0, True)
        tiles = []
        for kt in range(qt + 1):
            hb = kt // 64
            ktp = kt % 64
            biases = [(ebig_sb[:, ktp * 128:(ktp + 1) * 128], nselT_sbs[qt % 2][:, hb, :].unsqueeze(1).to_broadcast([128, 4, 128]), [b_const, b_nselTs[qt % 2]])]
            if kt == qt:
                biases.append((ident_sb[:], caus_sb[:].unsqueeze(1).to_broadcast([128, 4, 128]), [b_const]))
            tiles.append((ksT_sb[:, kt * 128:(kt + 1) * 128], b_ks, vs_sb[:, kt, :], b_vs, biases))
        gen = sel_gen(qt + 1) if qt + 1 < NQT else iter(())

        def between(ti, gen=gen):
            if ti % 2 == 1:
                next(gen, None)
        attend(qt, tiles, 1, False, between=between)
        for _ in gen:
            pass
        tiles = []
        for kt in range(max(0, qt - 4), qt + 1):
            sl = kt % 8
            biases = []
            if kt == qt:
                biases.append((ident_sb[:], caus_sb[:].unsqueeze(1).to_broadcast([128, 4, 128]), [b_const]))
            if kt == qt - 4:
                biases.append((ident_sb[:], low_sb[:].unsqueeze(1).to_broadcast([128, 4, 128]), [b_const]))
            tiles.append((kw_sb[:, sl, :], b_kw[sl], vw_sb[:, sl, :], b_vw[sl], biases))
        attend(qt, tiles, 2, False)
        fw.op(fw.act, lambda e: e.activation(out=oout_sb[bi][:], in_=oacc_sb[bi][:].rearrange("p a b -> p (a b)"), func=AF.Copy),
              reads=[b_oacc[bi]], writes=[b_oout[bi]])
        fw.dma(o[qt * 128:(qt + 1) * 128, :], oout_sb[bi][:], reads=[b_oout[bi]], q=0, is_output=True)
    fw.finish()
    return nc


def stage_c_consts(S):
    NQT = S // 128
    NC = S // 16
    NCT = (NC + 127) // 128
    NCP = NCT * 128
    NSEL = S // 64
    NBH = (NSEL + 127) // 128
    NBW = NBH * 128
    EW = min(NSEL, 128) * 64
    p = np.arange(128)
    c = np.arange(NCP)
    cpos = (16.0 * c + 31.0).astype(np.float32)
    cpos[NC - 1:] = 1e9
    kl = p[:, None]
    ql = p[None, :]
    cpq = np.zeros((128, NCT, NQT), np.float32)
    for ct in range(NCT):
        cpq[:, ct, :] = cpos[ct * 128 + p][:, None] - 128.0 * np.arange(NQT)[None, :]
    return dict(
        ident=np.eye(128, dtype=np.float32),
        ebig=(np.arange(EW)[None, :] // 64 == p[:, None]).astype(np.float32),
        caus=np.where(kl > ql, NEGB, 0.0).astype(np.float32),
        low=np.where(kl <= ql, NEGB, 0.0).astype(np.float32),
        cposrow=np.ascontiguousarray(np.broadcast_to(cpos[None, :], (128, NCP))),
        tcol=(128.0 * np.arange(NQT)[None, :] + p[:, None]).astype(np.float32),
        curcol=((128 * np.arange(NQT)[None, :] + p[:, None]) // 64).astype(np.float32),
        jrow=np.ascontiguousarray(np.broadcast_to(np.arange(NBW, dtype=np.float32)[None, :], (128, NBW))),
        qrow=np.ascontiguousarray(np.broadcast_to(np.arange(128, dtype=np.float32)[None, :], (128, 128))),
        cpq=cpq,
    )

import ml_dtypes

BF_NP = ml_dtypes.bfloat16
NCORES = 8
BATCH, SEQ, DM = 2, 16384, 2048
_CACHE = {}


def _prog(key, fn):
    if key not in _CACHE:
        _CACHE[key] = fn()
    return _CACHE[key]


def build_cast(ncols):
    nc = bass.Bass("TRN2", target_bir_lowering=False)
    src = nc.dram_tensor("src", [128, ncols], F32, kind="ExternalInput").ap()
    dst = nc.dram_tensor("dst", [128, ncols], BF16, kind="ExternalOutput").ap()
    fw = FW(nc)
    CH = 8192
    bufs = [fw.sbuf(f"cb{i}", [128, CH], BF16) for i in range(2)]
    bb = [Buf(), Buf()]
    for i, c0 in enumerate(range(0, ncols, CH)):
        c1 = min(ncols, c0 + CH)
        fw.dma(bufs[i % 2][:, 0:c1 - c0], src[:, c0:c1], writes=[bb[i % 2]], q=1, max_dma_last_dim=8192)
        fw.dma(dst[:, c0:c1], bufs[i % 2][:, 0:c1 - c0], reads=[bb[i % 2]], q=0, is_output=True)
    fw.finish()
    return nc


def _run(nc, in_maps):
    res = run_bass_kernel_spmd(nc, in_maps, core_ids=list(range(NCORES)))
    return res.results


def kernel(x, norm_mix_gain, norm_ffn_gain, ret_w_in, ret_gn_gain, ret_w_out, nsa_kv_norm_gain, nsa_w_kv,
           cmp_pe_k, cmp_w1_k, cmp_w2_k, cmp_pe_v, cmp_w1_v, cmp_w2_v, nsa_w_q, nsa_w_o,
           ffn_w_in, ffn_conv_w, ffn_conv_b, ffn_w_out, final_norm_gain):
    f32 = np.float32
    x = np.asarray(x, f32)
    NTOK = BATCH * SEQ
    T = NTOK // NCORES

    ncA = _prog("A", lambda: build_stage_a(NTOK, SEQ))
    ncB = _prog("Bmid", lambda: build_stage_b(T, 4096, "mid"))
    ncC = _prog("C", lambda: build_stage_c(SEQ))
    ncD = _prog("Bfin", lambda: build_stage_b(T, 2048, "final"))
    wlist = [("ret_w_out", np.asarray(ret_w_out[0], f32)), ("ffn_w_in0", np.asarray(ffn_w_in[0], f32)),
             ("ffn_w_in1", np.asarray(ffn_w_in[1], f32)), ("ffn_w_out0", np.asarray(ffn_w_out[0], f32)),
             ("ffn_w_out1", np.asarray(ffn_w_out[1], f32)), ("nsa_w_kv", np.asarray(nsa_w_kv, f32)),
             ("nsa_w_q", np.asarray(nsa_w_q[0], f32)), ("nsa_w_o", np.asarray(nsa_w_o[0], f32))]
    flat = np.concatenate([w.reshape(-1) for _, w in wlist])
    per = -(-flat.size // (NCORES * 128))
    per = -(-per // 64) * 64
    tot = per * NCORES * 128
    flat_p = np.zeros(tot, f32)
    flat_p[:flat.size] = flat
    flat_p = flat_p.reshape(NCORES, 128, per)
    nc0 = _prog(("cast", per), lambda: build_cast(per))
    r0 = _run(nc0, [{"src": flat_p[c]} for c in range(NCORES)])
    flat_b = np.concatenate([np.asarray(r["dst"]).reshape(-1) for r in r0])[:flat.size]
    wb = {}
    off = 0
    for name, w in wlist:
        wb[name] = flat_b[off:off + w.size].reshape(w.shape)
        off += w.size

    xT = np.ascontiguousarray(x.reshape(NTOK, DM).T)
    ncA = _prog("A", lambda: build_stage_a(NTOK, SEQ))
    cA = stage_a_consts(SEQ)
    w_in = np.asarray(ret_w_in[0], f32)
    H, dk, dv = 8, 256, 512
    gn = np.asarray(ret_gn_gain[0], f32)
    g_mix0 = lay_gain(np.asarray(norm_mix_gain[0], f32))
    maps = []
    for h in range(H):
        m = dict(cA[h])
        m["xT"] = xT
        m["gain"] = g_mix0
        m["wq"] = np.ascontiguousarray(w_in[:, h * dk:(h + 1) * dk])
        m["wk"] = np.ascontiguousarray(w_in[:, H * dk + h * dk: H * dk + (h + 1) * dk])
        m["wv"] = np.ascontiguousarray(w_in[:, 2 * H * dk + h * dv: 2 * H * dk + (h + 1) * dv])
        m["wg"] = np.ascontiguousarray(w_in[:, 2 * H * dk + H * dv + h * dv: 2 * H * dk + H * dv + (h + 1) * dv])
        m["gng"] = np.ascontiguousarray(np.broadcast_to(gn[h * dv:(h + 1) * dv][None, :], (128, 512)))
        maps.append(m)
    rA = _run(ncA, maps)
    yT = np.concatenate([np.asarray(r["y"]).T for r in rA], axis=0)
    del rA

    perm = np.stack([np.arange(32), np.arange(32) + 32], axis=1).reshape(-1)

    def permcols(w):
        return np.ascontiguousarray(w.reshape(w.shape[0], 64, 128)[:, perm, :].reshape(w.shape[0], 8192))

    def convw_of(layer):
        cw = np.asarray(ffn_conv_w[layer], f32)
        cb = np.asarray(ffn_conv_b[layer], f32)
        out = np.zeros((128, 64, 4), f32)
        out[:, :, 0:3] = cw.reshape(3, 64, 128).transpose(2, 1, 0)
        out[:, :, 3] = cb.reshape(64, 128).T
        return out

    def with_halo(fullT, c, dt):
        b, j = divmod(c, NCORES // BATCH)
        lo = b * SEQ + j * T
        out = np.zeros((fullT.shape[0], 2 + T), dt)
        out[:, 2:] = fullT[:, lo:lo + T]
        if j > 0:
            out[:, 0:2] = fullT[:, lo - 2:lo]
        return out

    ncB = _prog("Bmid", lambda: build_stage_b(T, 4096, "mid"))
    gains_mid = np.ascontiguousarray(np.stack([lay_gain(np.asarray(norm_ffn_gain[0], f32)), lay_gain(np.asarray(nsa_kv_norm_gain, f32)),
                                               lay_gain(np.asarray(norm_mix_gain[1], f32))], axis=1))
    common = dict(w_mix=lay_w(wb["ret_w_out"]), w_in=lay_w(permcols(wb["ffn_w_in0"])), w_out=lay_w(wb["ffn_w_out0"]),
                  gains=gains_mid, convw=convw_of(0), w_kv=lay_w(wb["nsa_w_kv"]), w_q=lay_w(wb["nsa_w_q"], pad_to=17 * 128))
    maps = []
    for c in range(NCORES):
        m = dict(common)
        m["hT"] = with_halo(xT, c, f32)
        m["aT"] = with_halo(yT, c, BF_NP)
        maps.append(m)
    rB = _run(ncB, maps)
    del yT, xT
    h2T = np.concatenate([r["h2T"] for r in rB], axis=1)
    kvT = np.concatenate([np.asarray(r["kvT"]) for r in rB], axis=1)
    qTf = np.concatenate([np.asarray(r["qT"]) for r in rB], axis=1)
    gTf = np.concatenate([r["gT"] for r in rB], axis=1)[:48]
    del rB

    ncC = _prog("C", lambda: build_stage_c(SEQ))
    cC = stage_c_consts(SEQ)
    NQT = SEQ // 128

    def w1lay(w):
        return np.ascontiguousarray(np.asarray(w, f32).reshape(32, 128, 256).transpose(1, 0, 2))

    def w2lay(w):
        return np.ascontiguousarray(np.asarray(w, f32).reshape(2, 128, 128).transpose(1, 0, 2))
    cw = dict(w1k=w1lay(cmp_w1_k), w1v=w1lay(cmp_w1_v), w2k=w2lay(cmp_w2_k), w2v=w2lay(cmp_w2_v),
              peTk=np.ascontiguousarray(np.asarray(cmp_pe_k, f32).T), peTv=np.ascontiguousarray(np.asarray(cmp_pe_v, f32).T))
    maps = []
    for b in range(BATCH):
        ts = slice(b * SEQ, (b + 1) * SEQ)
        for h in range(4):
            m = dict(cC)
            m.update(cw)
            row = lambda i: slice((i * 4 + h) * 128, (i * 4 + h + 1) * 128)
            m["kcT"] = np.ascontiguousarray(kvT[row(0), ts])
            m["vcT"] = np.ascontiguousarray(kvT[row(1), ts])
            m["ksT"] = np.ascontiguousarray(kvT[row(2), ts])
            m["vs"] = np.ascontiguousarray(kvT[row(3), ts].T)
            m["kwT"] = np.ascontiguousarray(kvT[row(4), ts])
            m["vw"] = np.ascontiguousarray(kvT[row(5), ts].T)
            qq = qTf[h * 512:(h + 1) * 512, ts].reshape(4, 128, NQT, 128)
            m["qT"] = np.ascontiguousarray(qq.transpose(2, 1, 0, 3)).reshape(NQT, 128, 512)
            gg = gTf[h * 12:(h + 1) * 12, ts].reshape(4, 3, NQT, 128)
            m["gates"] = np.ascontiguousarray(gg.transpose(2, 3, 1, 0)).reshape(NQT, 128, 12)
            maps.append(m)
    rC = _run(ncC, maps)
    del kvT, qTf, gTf
    oT = np.zeros((2048, NTOK), BF_NP)
    for b in range(BATCH):
        for h in range(4):
            oT[h * 512:(h + 1) * 512, b * SEQ:(b + 1) * SEQ] = np.asarray(rC[b * 4 + h]["o"]).T
    del rC

    ncD = _prog("Bfin", lambda: build_stage_b(T, 2048, "final"))
    gains_fin = np.ascontiguousarray(np.stack([lay_gain(np.asarray(norm_ffn_gain[1], f32)), lay_gain(np.asarray(final_norm_gain, f32)),
                                               lay_gain(np.asarray(final_norm_gain, f32))], axis=1))
    common = dict(w_mix=lay_w(wb["nsa_w_o"]), w_in=lay_w(permcols(wb["ffn_w_in1"])), w_out=lay_w(wb["ffn_w_out1"]),
                  gains=gains_fin, convw=convw_of(1))
    maps = []
    for c in range(NCORES):
        m = dict(common)
        m["hT"] = with_halo(h2T, c, f32)
        m["aT"] = with_halo(oT, c, BF_NP)
        maps.append(m)
    rD = _run(ncD, maps)
    outT = np.concatenate([r["outT"] for r in rD], axis=1)
    return np.ascontiguousarray(outT.T).reshape(BATCH, SEQ, DM).astype(f32)
```

```python
import numpy as np
from contextlib import ExitStack
import concourse.bass as bass
import concourse.mybir as mybir
from concourse.bass_utils import run_bass_kernel_spmd

F32 = mybir.dt.float32
BF16 = mybir.dt.bfloat16
AF = mybir.ActivationFunctionType
ALU = mybir.AluOpType
AX = mybir.AxisListType


class Eng:
    def __init__(self, fw, name, h, is_pe=False):
        self.fw = fw
        self.name = name
        self.h = h
        self.sem = fw.es.enter_context(fw.nc.semaphore("sem_" + name))
        self.count = 0
        self.pending = False
        self.waited = {}
        self.is_pe = is_pe

    def wait_tok(self, tok):
        if tok is None:
            return
        sem, val, eng = tok
        if eng is self and self.is_pe:
            return
        key = id(sem)
        if self.waited.get(key, 0) >= val:
            return
        if eng is not None and eng is not self and eng.count < val:
            eng.flush()
        if eng is self and self.count < val:
            self.flush()
        self.h.wait_ge(sem, val)
        self.waited[key] = val

    def flush(self):
        if self.pending:
            self.count += 1
            self.h.nop().then_inc(self.sem, 1)
            self.pending = False


class Buf:
    __slots__ = ("name", "w", "r")

    def __init__(self, name=""):
        self.name = name
        self.w = None
        self.r = []


class FW:
    def __init__(self, nc):
        self.nc = nc
        self.es = ExitStack()
        self.pe = Eng(self, "pe", nc.tensor, is_pe=True)
        self.dve = Eng(self, "dve", nc.vector)
        self.act = Eng(self, "act", nc.scalar)
        self.pool = Eng(self, "pool", nc.gpsimd)
        self.sp = Eng(self, "sp", nc.sync)
        self.engs = [self.pe, self.dve, self.act, self.pool, self.sp]
        self.dma_sems = []
        self.n_dma = 0
        self.NDMA = 6
        for q in range(2):
            lst = []
            for i in range(self.NDMA):
                s = self.es.enter_context(nc.semaphore(f"dsem{q}_{i}"))
                lst.append([s, 0])
            self.dma_sems.append(lst)
        self.dma_rr = [0, 0]
        self.out_toks = []

    def sbuf(self, name, shape, dt):
        return self.es.enter_context(self.nc.sbuf_tensor(name, shape, dt))

    def psum(self, name, shape, dt=F32):
        return self.es.enter_context(self.nc.psum_tensor(name, shape, dt))

    def op(self, eng, fn, reads=(), writes=(), inc=True):
        for b in reads:
            eng.wait_tok(b.w)
        for b in writes:
            eng.wait_tok(b.w)
            for t in b.r:
                eng.wait_tok(t)
        inst = fn(eng.h)
        if inc:
            eng.count += 1
            inst.then_inc(eng.sem, 1)
            eng.pending = False
            tok = (eng.sem, eng.count, eng)
        else:
            eng.pending = True
            tok = (eng.sem, eng.count + 1, eng)
        for b in reads:
            b.r.append(tok)
            if len(b.r) > 24:
                b.r = b.r[-24:]
        for b in writes:
            b.w = tok
            b.r = []
        return tok

    def dma(self, out, in_, reads=(), writes=(), q=0, is_output=False, **kw):
        eng = self.sp if q == 0 else self.pool
        slot = self.dma_sems[q][self.dma_rr[q] % self.NDMA]
        self.dma_rr[q] += 1
        sem, val = slot
        if val > 0:
            eng.wait_tok((sem, val, None))
        for b in reads:
            eng.wait_tok(b.w)
        for b in writes:
            eng.wait_tok(b.w)
            for t in b.r:
                eng.wait_tok(t)
        slot[1] = val + 16
        eng.h.dma_start(out=out, in_=in_, **kw).then_inc(sem, 16)
        tok = (sem, val + 16, None)
        for b in reads:
            b.r.append(tok)
        for b in writes:
            b.w = tok
            b.r = []
        if is_output:
            self.out_toks.append(tok)
        return tok

    def finish(self):
        for e in self.engs:
            e.flush()
        for q in range(2):
            for sem, val in self.dma_sems[q]:
                if val > 0:
                    self.sp.wait_tok((sem, val, None))
        self.es.close()


RMS_EPS = 1e-6


def build_stage_a(NTOK, SEQ, cast_cols=0):
    nc = bass.Bass("TRN2", target_bir_lowering=False)
    D = 2048
    KC = D // 128
    TT = 512
    xT = nc.dram_tensor("xT", [D, NTOK], F32, kind="ExternalInput").ap()
    gain = nc.dram_tensor("gain", [128, KC], F32, kind="ExternalInput").ap()
    wq = nc.dram_tensor("wq", [D, 256], F32, kind="ExternalInput").ap()
    wk = nc.dram_tensor("wk", [D, 256], F32, kind="ExternalInput").ap()
    wv = nc.dram_tensor("wv", [D, 512], F32, kind="ExternalInput").ap()
    wg = nc.dram_tensor("wg", [D, 512], F32, kind="ExternalInput").ap()
    cosT = nc.dram_tensor("cosT", [128, SEQ], F32, kind="ExternalInput").ap()
    sinT = nc.dram_tensor("sinT", [128, SEQ], F32, kind="ExternalInput").ap()
    decT = nc.dram_tensor("decT", [128, 128], F32, kind="ExternalInput").ap()
    qdec = nc.dram_tensor("qdec", [128, TT], F32, kind="ExternalInput").ap()
    kdec = nc.dram_tensor("kdec", [128, 1], F32, kind="ExternalInput").ap()
    cdec = nc.dram_tensor("cdec", [128, 1], F32, kind="ExternalInput").ap()
    gng = nc.dram_tensor("gng", [128, 512], F32, kind="ExternalInput").ap()
    ident = nc.dram_tensor("ident", [128, 128], F32, kind="ExternalInput").ap()
    y = nc.dram_tensor("y", [NTOK, 512], BF16, kind="ExternalOutput").ap()
    if cast_cols:
        csrc = nc.dram_tensor("csrc", [128, cast_cols], F32, kind="ExternalInput").ap()
        cdst = nc.dram_tensor("cdst", [128, cast_cols], BF16, kind="ExternalOutput").ap()

    fw = FW(nc)
    S = fw.sbuf
    w_sb = S("w_sb", [128, KC, 1536], BF16)
    gain_sb = S("gain_sb", [128, KC], F32)
    decT_sb = S("decT_sb", [128, 128], F32)
    qdec_sb = S("qdec_sb", [128, TT], F32)
    kdec_sb = S("kdec_sb", [128, 1], F32)
    cdec_sb = S("cdec_sb", [128, 1], F32)
    gng_sb = S("gng_sb", [128, 512], F32)
    ident_sb = S("ident_sb", [128, 128], BF16)
    ones_sb = S("ones_sb", [128, 128], BF16)
    eps_sb = S("eps_sb", [128, 1], F32)
    HT = 256
    x_sb = [S(f"x_sb{i}", [128, KC, HT], F32) for i in range(2)]
    cs_sb = [S(f"cs_sb{i}", [128, 2, TT], F32) for i in range(2)]
    sq_sb = S("sq_sb", [128, KC, HT], BF16)
    xn_sbs = [S(f"xn_sb{i}", [128, KC, TT], BF16) for i in range(2)]
    rstd_sb = S("rstd_sb", [128, HT], F32)
    qT_sb = S("qT_sb", [128, 2, TT], BF16)
    kT_sb = S("kT_sb", [128, 2, TT], BF16)
    qdT_sb = S("qdT_sb", [128, 2, TT], BF16)
    tmp_sb = [S(f"tmp_sb{i}", [128, TT], F32) for i in range(4)]
    v_sb = [S(f"v_sb{i}", [128, 512], BF16) for i in range(2)]
    gs_sb = [S(f"gs_sb{i}", [128, 512], F32) for i in range(2)]
    pT_sb = [S(f"pT_sb{i}", [128, 128], BF16) for i in range(2)]
    kd_sb = [S(f"kd_sb{i}", [128, 256], BF16) for i in range(2)]
    st_sb = S("st_sb", [128, 2, 512], F32)
    stb_sb = S("stb_sb", [128, 2, 512], BF16)
    stat_sb = [S(f"stat_sb{i}", [128, 6], F32) for i in range(2)]
    mv_sb = [S(f"mv_sb{i}", [128, 2], F32) for i in range(2)]
    rs_sb = [S(f"rs_sb{i}", [128, 1], F32) for i in range(2)]
    t_sb = [S(f"t_sb{i}", [128, 512], F32) for i in range(2)]
    y_sb = [S(f"y_sb{i}", [128, 512], BF16) for i in range(2)]

    P = fw.psum
    ps_qk = [P(f"ps_qk{i}", [128, 512]) for i in range(2)]
    ps_vg = [P(f"ps_vg{i}", [128, 512]) for i in range(2)]
    ps_sc = P("ps_sc", [128, 512])
    ps_kt = P("ps_kt", [128, 1024], BF16)
    ps_out = P("ps_out", [128, 512])
    ps_su = P("ps_su", [128, 512])

    B = Buf
    b_w, b_const = B("w"), B("const")
    b_x = [B("x0"), B("x1")]
    b_cs = [B("cs0"), B("cs1")]
    b_sq, b_rstd, b_qT, b_kT, b_qdT = B(), B(), B(), B(), B()
    b_xns = [B(), B()]
    b_psss = B()
    b_tmp = [B() for _ in range(4)]
    b_v, b_gs, b_pT, b_kd = [B(), B()], [B(), B()], [B(), B()], [B(), B()]
    b_st, b_stb = B(), B()
    b_stat, b_mv, b_rs, b_t, b_y = [B(), B()], [B(), B()], [B(), B()], [B(), B()], [B(), B()]
    b_psqk, b_psvg = [B(), B()], [B(), B()]
    b_pssc, b_pskt, b_psout, b_pssu = B(), B(), B(), B()

    fw.dma(gain_sb[:], gain, writes=[b_const], q=0)
    fw.dma(decT_sb[:], decT, writes=[b_const], q=0)
    fw.dma(qdec_sb[:], qdec, writes=[b_const], q=0)
    fw.dma(kdec_sb[:], kdec, writes=[b_const], q=0)
    fw.dma(cdec_sb[:], cdec, writes=[b_const], q=0)
    fw.dma(gng_sb[:], gng, writes=[b_const], q=0)
    fw.dma(ident_sb[:], ident, writes=[b_const], q=1)
    off = 0
    for wsrc, n in ((wq, 256), (wk, 256), (wv, 512), (wg, 512)):
        for c0 in range(0, KC, 4):
            fw.dma(w_sb[:, c0:c0 + 4, off:off + n],
                   wsrc.rearrange("(c p) m -> p c m", p=128)[:, c0:c0 + 4, :], writes=[b_w], q=1)
        off += n
    fw.op(fw.dve, lambda e: e.memset(ones_sb[:], 1.0), writes=[b_const])
    fw.op(fw.dve, lambda e: e.memset(eps_sb[:], RMS_EPS), writes=[b_const])

    xT_v = xT.rearrange("(c p) t -> p c t", p=128)
    ntiles = NTOK // TT
    nhalves = NTOK // HT

    def load_half(hi):
        t0 = hi * HT
        bi = hi % 2
        fw.dma(x_sb[bi][:, 0:8, :], xT_v[:, 0:8, t0:t0 + HT], writes=[b_x[bi]], q=0)
        fw.dma(x_sb[bi][:, 8:16, :], xT_v[:, 8:16, t0:t0 + HT], writes=[b_x[bi]], q=0)

    def load_cs(ti):
        s0 = (ti * TT) % SEQ
        bi = ti % 2
        fw.dma(cs_sb[bi][:, 0, :], cosT[:, s0:s0 + TT], writes=[b_cs[bi]], q=0)
        fw.dma(cs_sb[bi][:, 1, :], sinT[:, s0:s0 + TT], writes=[b_cs[bi]], q=0)

    def rms_half(hi):
        bi = hi % 2
        ti = hi // 2
        hs = slice((hi % 2) * HT, (hi % 2 + 1) * HT)
        xs = x_sb[bi]
        xn = xn_sbs[ti % 2]
        b_xn = b_xns[ti % 2]
        fw.op(fw.act, lambda e: e.activation(out=sq_sb[:], in_=xs[:], func=AF.Square), reads=[b_x[bi]], writes=[b_sq])
        for c in range(KC):
            fw.op(fw.pe, lambda e: e.matmul(ps_sc[:, 256:512], ones_sb[:], sq_sb[:, c, :], start=(c == 0), stop=(c == KC - 1)),
                  reads=[b_sq, b_const], writes=[b_psss], inc=(c == KC - 1))
        fw.op(fw.act, lambda e: e.activation(out=rstd_sb[:], in_=ps_sc[:, 256:512], func=AF.Sqrt, bias=eps_sb[:], scale=1.0 / D),
              reads=[b_psss, b_const], writes=[b_rstd])
        fw.op(fw.dve, lambda e: e.reciprocal(out=rstd_sb[:], in_=rstd_sb[:]), reads=[b_rstd], writes=[b_rstd])
        for c in range(KC):
            fw.op(fw.dve, lambda e: e.scalar_tensor_tensor(out=xn[:, c, hs], in0=xs[:, c, :], scalar=gain_sb[:, c:c + 1],
                                                           in1=rstd_sb[:], op0=ALU.mult, op1=ALU.mult),
                  reads=[b_x[bi], b_rstd, b_const], writes=[b_xn])
        if hi + 2 < nhalves:
            load_half(hi + 2)

    load_half(0)
    load_half(1)
    load_cs(0)
    if cast_cols:
        CCH = 2048
        cst = [S(f"cst{i}", [128, CCH], BF16) for i in range(2)]
        b_cst = [B(), B()]
        for i, c0 in enumerate(range(0, cast_cols, CCH)):
            c1 = min(cast_cols, c0 + CCH)
            fw.dma(cst[i % 2][:, 0:c1 - c0], csrc[:, c0:c1], writes=[b_cst[i % 2]], q=1, max_dma_last_dim=8192)
            fw.dma(cdst[:, c0:c1], cst[i % 2][:, 0:c1 - c0], reads=[b_cst[i % 2]], q=0, is_output=True)
    rms_half(0)
    rms_half(1)
    cidx = 0
    for ti in range(ntiles):
        bi = ti % 2
        t0 = ti * TT
        xn_sb = xn_sbs[ti % 2]
        b_xn = b_xns[ti % 2]
        if ti + 1 < ntiles:
            load_cs(ti + 1)
        for which, (dst, b_dst, woff, scl) in enumerate(((qT_sb, b_qT, 0, 1.0), (kT_sb, b_kT, 256, 1.0 / 16.0))):
            for m in range(2):
                for c in range(KC):
                    fw.op(fw.pe, lambda e: e.matmul(ps_qk[m][:], w_sb[:, c, woff + m * 128: woff + (m + 1) * 128], xn_sb[:, c, :],
                                                    start=(c == 0), stop=(c == KC - 1)),
                          reads=[b_w, b_xn], writes=[b_psqk[m]], inc=(c == KC - 1))
            cos_t, sin_t = cs_sb[bi][:, 0, :], cs_sb[bi][:, 1, :]
            D_ = fw.dve
            fw.op(D_, lambda e: e.scalar_tensor_tensor(out=tmp_sb[0][:], in0=ps_qk[0][:], scalar=scl, in1=cos_t, op0=ALU.mult, op1=ALU.mult),
                  reads=[b_psqk[0], b_cs[bi]], writes=[b_tmp[0]])
            fw.op(D_, lambda e: e.scalar_tensor_tensor(out=tmp_sb[1][:], in0=ps_qk[1][:], scalar=scl, in1=sin_t, op0=ALU.mult, op1=ALU.mult),
                  reads=[b_psqk[1], b_cs[bi]], writes=[b_tmp[1]])
            fw.op(D_, lambda e: e.scalar_tensor_tensor(out=tmp_sb[2][:], in0=ps_qk[0][:], scalar=scl, in1=sin_t, op0=ALU.mult, op1=ALU.mult),
                  reads=[b_psqk[0], b_cs[bi]], writes=[b_tmp[2]])
            fw.op(D_, lambda e: e.scalar_tensor_tensor(out=tmp_sb[3][:], in0=ps_qk[1][:], scalar=scl, in1=cos_t, op0=ALU.mult, op1=ALU.mult),
                  reads=[b_psqk[1], b_cs[bi]], writes=[b_tmp[3]])
            fw.op(fw.pool, lambda e: e.tensor_tensor(out=dst[:, 0, :], in0=tmp_sb[0][:], in1=tmp_sb[1][:], op=ALU.subtract),
                  reads=[b_tmp[0], b_tmp[1]], writes=[b_dst])
            fw.op(fw.pool, lambda e: e.tensor_tensor(out=dst[:, 1, :], in0=tmp_sb[2][:], in1=tmp_sb[3][:], op=ALU.add),
                  reads=[b_tmp[2], b_tmp[3]], writes=[b_dst])
        for m in range(2):
            fw.op(fw.pool, lambda e: e.tensor_tensor(out=qdT_sb[:, m, :], in0=qT_sb[:, m, :], in1=qdec_sb[:], op=ALU.mult),
                  reads=[b_qT, b_const], writes=[b_qdT])
        if ti + 1 < ntiles:
            rms_half(2 * (ti + 1))
            rms_half(2 * (ti + 1) + 1)
        for ch in range(TT // 128):
            tok0 = t0 + ch * 128
            cs = slice(ch * 128, (ch + 1) * 128)
            ci = cidx % 2
            cidx += 1
            if tok0 % SEQ == 0:
                fw.op(fw.dve, lambda e: e.memset(st_sb[:], 0.0), writes=[b_st])
                fw.op(fw.pool, lambda e: e.memset(stb_sb[:], 0.0), writes=[b_stb])
            for c in range(KC):
                fw.op(fw.pe, lambda e: e.matmul(ps_vg[0][:], xn_sb[:, c, cs], w_sb[:, c, 512:1024], start=(c == 0), stop=(c == KC - 1)),
                      reads=[b_w, b_xn], writes=[b_psvg[0]], inc=(c == KC - 1))
            fw.op(fw.act, lambda e: e.activation(out=v_sb[ci][:], in_=ps_vg[0][:], func=AF.Copy), reads=[b_psvg[0]], writes=[b_v[ci]])
            for c in range(KC):
                fw.op(fw.pe, lambda e: e.matmul(ps_vg[1][:], xn_sb[:, c, cs], w_sb[:, c, 1024:1536], start=(c == 0), stop=(c == KC - 1)),
                      reads=[b_w, b_xn], writes=[b_psvg[1]], inc=(c == KC - 1))
            fw.op(fw.act, lambda e: e.activation(out=gs_sb[ci][:], in_=ps_vg[1][:], func=AF.Silu), reads=[b_psvg[1]], writes=[b_gs[ci]])
            fw.op(fw.pool, lambda e: e.tensor_tensor(out=gs_sb[ci][:], in0=gs_sb[ci][:], in1=gng_sb[:], op=ALU.mult),
                  reads=[b_gs[ci], b_const], writes=[b_gs[ci]])
            for m in range(2):
                fw.op(fw.pe, lambda e: e.matmul(ps_sc[:, 0:128], kT_sb[:, m, cs], qT_sb[:, m, cs], start=(m == 0), stop=(m == 1)),
                      reads=[b_kT, b_qT], writes=[b_pssc], inc=(m == 1))
            fw.op(fw.dve, lambda e: e.tensor_tensor(out=pT_sb[ci][:], in0=ps_sc[:, 0:128], in1=decT_sb[:], op=ALU.mult),
                  reads=[b_pssc, b_const], writes=[b_pT[ci]])
            for m in range(2):
                fw.op(fw.pe, lambda e: e.transpose(ps_kt[:, m * 128:(m + 1) * 128], kT_sb[:, m, cs], ident_sb[:]),
                      reads=[b_kT, b_const], writes=[b_pskt], inc=(m == 1))
            fw.op(fw.dve, lambda e: e.tensor_scalar(out=kd_sb[ci][:], in0=ps_kt[:, 0:256], scalar1=kdec_sb[:], scalar2=None, op0=ALU.mult),
                  reads=[b_pskt, b_const], writes=[b_kd[ci]])
            fw.op(fw.pe, lambda e: e.matmul(ps_out[:], pT_sb[ci][:], v_sb[ci][:], start=True, stop=False),
                  reads=[b_pT[ci], b_v[ci]], writes=[b_psout], inc=False)
            for m in range(2):
                fw.op(fw.pe, lambda e: e.matmul(ps_out[:], qdT_sb[:, m, cs], stb_sb[:, m, :], start=False, stop=(m == 1)),
                      reads=[b_qdT, b_stb], writes=[b_psout], inc=(m == 1))
            for m in range(2):
                fw.op(fw.pe, lambda e: e.matmul(ps_su[:], kd_sb[ci][:, m * 128:(m + 1) * 128], v_sb[ci][:], start=True, stop=True),
                      reads=[b_kd[ci], b_v[ci]], writes=[b_pssu], inc=True)
                fw.op(fw.dve, lambda e: e.scalar_tensor_tensor(out=st_sb[:, m, :], in0=st_sb[:, m, :], scalar=cdec_sb[:], in1=ps_su[:], op0=ALU.mult, op1=ALU.add),
                      reads=[b_st, b_pssu, b_const], writes=[b_st])
            fw.op(fw.act, lambda e: e.activation(out=stb_sb[:], in_=st_sb[:], func=AF.Copy), reads=[b_st], writes=[b_stb])
            fw.op(fw.dve, lambda e: e.bn_stats(out=stat_sb[ci][:], in_=ps_out[:]), reads=[b_psout], writes=[b_stat[ci]])
            fw.op(fw.dve, lambda e: e.bn_aggr(out=mv_sb[ci][:], in_=stat_sb[ci][:]), reads=[b_stat[ci]], writes=[b_mv[ci]])
            fw.op(fw.act, lambda e: e.activation(out=rs_sb[ci][:], in_=mv_sb[ci][:, 1:2], func=AF.Sqrt, bias=eps_sb[:], scale=1.0),
                  reads=[b_mv[ci], b_const], writes=[b_rs[ci]])
            fw.op(fw.dve, lambda e: e.reciprocal(out=rs_sb[ci][:], in_=rs_sb[ci][:]), reads=[b_rs[ci]], writes=[b_rs[ci]])
            fw.op(fw.dve, lambda e: e.tensor_scalar(out=t_sb[ci][:], in0=ps_out[:], scalar1=mv_sb[ci][:, 0:1], scalar2=rs_sb[ci][:],
                                                    op0=ALU.subtract, op1=ALU.mult),
                  reads=[b_psout, b_mv[ci], b_rs[ci]], writes=[b_t[ci]])
            fw.op(fw.pool, lambda e: e.tensor_tensor(out=y_sb[ci][:], in0=t_sb[ci][:], in1=gs_sb[ci][:], op=ALU.mult),
                  reads=[b_t[ci], b_gs[ci]], writes=[b_y[ci]])
            fw.dma(y[tok0:tok0 + 128, :], y_sb[ci][:], reads=[b_y[ci]], q=0, is_output=True)
    fw.finish()
    return nc


def stage_a_consts(SEQ):
    H = 8
    half = 128
    freqs = (10000.0 ** (-np.arange(half, dtype=np.float32) / half)).astype(np.float32)
    pos = np.arange(SEQ, dtype=np.float32)
    ang = (freqs[:, None] * pos[None, :]).astype(np.float32)
    cosT = np.cos(ang).astype(np.float32)
    sinT = np.sin(ang).astype(np.float32)
    out = []
    idx = np.arange(128, dtype=np.float64)
    for h in range(H):
        lg = np.log(1.0 - 2.0 ** (-5.0 - h))
        diff = idx[None, :] - idx[:, None]
        decT = np.where(diff >= 0, np.exp(np.maximum(diff, 0) * lg), 0.0).astype(np.float32)
        qd = np.exp((idx + 1.0) * lg).astype(np.float32)
        kd = np.exp((127.0 - idx) * lg).astype(np.float32)
        cd = np.float32(np.exp(128.0 * lg))
        out.append(dict(
            cosT=cosT, sinT=sinT, decT=decT,
            qdec=np.ascontiguousarray(np.broadcast_to(np.tile(qd, 4)[None, :], (128, 512))).astype(np.float32),
            kdec=kd.reshape(128, 1).copy(), cdec=np.full((128, 1), cd, np.float32),
            ident=np.eye(128, dtype=np.float32)))
    return out


RMS_EPS = 1e-6
D = 2048
KC = 16
DFF = 4096


def build_stage_b(T, KA, variant, WDT=BF16):
    nc = bass.Bass("TRN2", target_bir_lowering=False)
    KAC = KA // 128
    TT = 512
    assert T % TT == 0
    di = lambda n, s, dt: nc.dram_tensor(n, s, dt, kind="ExternalInput").ap()
    do = lambda n, s, dt: nc.dram_tensor(n, s, dt, kind="ExternalOutput").ap()
    hT = di("hT", [D, 2 + T], F32)
    aT = di("aT", [KA, 2 + T], BF16)
    w_mix = di("w_mix", [16, 128, KAC * 128], WDT)
    w_in = di("w_in", [64, 128, KC * 128], WDT)
    w_out = di("w_out", [16, 128, 32 * 128], WDT)
    gains = di("gains", [128, 3, KC], F32)
    convw = di("convw", [128, 64, 4], F32)
    if variant == "mid":
        w_kv = di("w_kv", [24, 128, KC * 128], WDT)
        w_q = di("w_q", [17, 128, KC * 128], WDT)
        h2T = do("h2T", [D, T], F32)
        kvT = do("kvT", [24 * 128, T], BF16)
        qT = do("qT", [16 * 128, T], BF16)
        gT = do("gT", [128, T], F32)
    else:
        outT = do("outT", [D, T], F32)

    fw = FW(nc)
    S = fw.sbuf
    h_sb = S("h_sb", [128, KC, TT], F32)
    a_sb = S("a_sb", [128, 32, TT], BF16)
    xn_sb = S("xn_sb", [128, KC, TT], BF16)
    sq_sb = S("sq_sb", [128, KC, TT], BF16)
    act_sb = S("act_sb", [128, 32, TT], BF16)
    NW = 4
    w_sb = [S(f"w_sb{i}", [128, 32 * 128], WDT) for i in range(NW)]
    u_sb = [S(f"u_sb{i}", [128, 2 + TT], F32) for i in range(4)]
    c_sb = [S(f"c_sb{i}", [128, TT], F32) for i in range(4)]
    carry_sb = S("carry_sb", [128, 64, 2], F32)
    rstd_sb = S("rstd_sb", [128, TT], F32)
    gains_sb = S("gains_sb", [128, 3, KC], F32)
    convw_sb = S("convw_sb", [128, 64, 4], F32)
    ones_sb = S("ones_sb", [128, 128], BF16)
    eps_sb = S("eps_sb", [128, 1], F32)
    g_sb = S("g_sb", [128, TT], F32)

    P = fw.psum
    ps = [P(f"ps{i}", [128, 512]) for i in range(6)]
    ps_ss = P("ps_ss", [128, 512])
    B = Buf
    b_h, b_a, b_xn, b_sq, b_act, b_rstd, b_const, b_carry, b_g = B(), B(), B(), B(), B(), B(), B(), B(), B()
    b_w = [B() for _ in range(NW)]
    b_u = [B() for _ in range(4)]
    b_c = [B() for _ in range(4)]
    b_ps = [B() for _ in range(6)]
    b_psss = B()

    fw.dma(gains_sb[:], gains, writes=[b_const], q=0)
    fw.dma(convw_sb[:], convw, writes=[b_const], q=0)
    fw.op(fw.dve, lambda e: e.memset(ones_sb[:], 1.0), writes=[b_const])
    fw.op(fw.dve, lambda e: e.memset(eps_sb[:], RMS_EPS), writes=[b_const])

    hT_v = hT.rearrange("(c p) t -> p c t", p=128)
    aT_v = aT.rearrange("(c p) t -> p c t", p=128)
    wq_state = {"n": 0, "ps": 0}

    def load_w(src, m, kc):
        i = wq_state["n"] % NW
        wq_state["n"] += 1
        half = kc * 128 // 2
        fw.dma(w_sb[i][:, 0:half], src[m, :, 0:half], writes=[b_w[i]], q=0)
        fw.dma(w_sb[i][:, half:kc * 128], src[m, :, half:kc * 128], writes=[b_w[i]], q=0)
        return i

    def linear(src_w, M, kc, rhs_sb, b_rhs, n, epilogue, prefetch=2, mrows=None):
        loaded = []
        for m in range(min(prefetch, M)):
            loaded.append(load_w(src_w, m, kc))
        for m in range(M):
            if m + prefetch < M:
                loaded.append(load_w(src_w, m + prefetch, kc))
            wi = loaded[m]
            pi = wq_state["ps"] % 6
            wq_state["ps"] += 1
            for k in range(kc):
                fw.op(fw.pe, lambda e: e.matmul(ps[pi][:, 0:n], w_sb[wi][:, k * 128:(k + 1) * 128], rhs_sb[:, k, 0:n],
                                                start=(k == 0), stop=(k == kc - 1)),
                      reads=[b_w[wi], b_rhs], writes=[b_ps[pi]], inc=(k == kc - 1))
            epilogue(m, ps[pi], b_ps[pi])

    def rms(n, gi, dst_sb, b_dst, compute_rstd=True):
        if compute_rstd:
            fw.op(fw.act, lambda e: e.activation(out=sq_sb[:, :, 0:n], in_=h_sb[:, :, 0:n], func=AF.Square), reads=[b_h], writes=[b_sq])
            for c in range(KC):
                fw.op(fw.pe, lambda e: e.matmul(ps_ss[:, 0:n], ones_sb[:], sq_sb[:, c, 0:n], start=(c == 0), stop=(c == KC - 1)),
                      reads=[b_sq, b_const], writes=[b_psss], inc=(c == KC - 1))
            fw.op(fw.act, lambda e: e.activation(out=rstd_sb[:, 0:n], in_=ps_ss[:, 0:n], func=AF.Sqrt, bias=eps_sb[:], scale=1.0 / D),
                  reads=[b_psss, b_const], writes=[b_rstd])
            fw.op(fw.dve, lambda e: e.reciprocal(out=rstd_sb[:, 0:n], in_=rstd_sb[:, 0:n]), reads=[b_rstd], writes=[b_rstd])
        for c in range(KC):
            fw.op(fw.dve, lambda e: e.scalar_tensor_tensor(out=dst_sb[:, c, 0:n], in0=h_sb[:, c, 0:n], scalar=gains_sb[:, gi, c:c + 1],
                                                           in1=rstd_sb[:, 0:n], op0=ALU.mult, op1=ALU.mult),
                  reads=[b_h, b_rstd, b_const], writes=[b_dst])

    tiles = [(0, 2, True)] + [(2 + i * TT, TT, False) for i in range(T // TT)]
    for (c0, n, halo) in tiles:
        o0 = c0 - 2
        fw.dma(h_sb[:, 0:8, 0:n], hT_v[:, 0:8, c0:c0 + n], writes=[b_h], q=0)
        fw.dma(h_sb[:, 8:16, 0:n], hT_v[:, 8:16, c0:c0 + n], writes=[b_h], q=0)
        for k0 in range(0, KAC, 8):
            fw.dma(a_sb[:, k0:k0 + 8, 0:n], aT_v[:, k0:k0 + 8, c0:c0 + n], writes=[b_a], q=0)

        def ep1(m, p, bp):
            fw.op(fw.dve, lambda e: e.tensor_tensor(out=h_sb[:, m, 0:n], in0=h_sb[:, m, 0:n], in1=p[:, 0:n], op=ALU.add),
                  reads=[bp, b_h], writes=[b_h])
        linear(w_mix, 16, KAC, a_sb, b_a, n, ep1)
        rms(n, 0, xn_sb, b_xn)

        def ep3(mm, p, bp):
            j = mm // 2
            isb = mm % 2
            ch = j + 32 * isb
            ui = (mm % 4)
            U, bU = u_sb[ui], b_u[ui]
            C, bC = c_sb[ui], b_c[ui]
            fw.op(fw.pool, lambda e: e.tensor_copy(out=U[:, 0:2], in_=carry_sb[:, ch, :]), reads=[b_carry], writes=[bU])
            fw.op(fw.act, lambda e: e.activation(out=U[:, 2:2 + n], in_=p[:, 0:n], func=AF.Copy), reads=[bp], writes=[bU])
            fw.op(fw.pool, lambda e: e.tensor_copy(out=carry_sb[:, ch, :], in_=U[:, n:n + 2]), reads=[bU], writes=[b_carry])
            if halo:
                return
            cw = convw_sb
            fw.op(fw.dve, lambda e: e.tensor_scalar(out=C[:, 0:n], in0=U[:, 2:2 + n], scalar1=cw[:, ch, 2:3], scalar2=cw[:, ch, 3:4],
                                                    op0=ALU.mult, op1=ALU.add), reads=[bU, b_const], writes=[bC])
            fw.op(fw.dve, lambda e: e.scalar_tensor_tensor(out=C[:, 0:n], in0=U[:, 1:1 + n], scalar=cw[:, ch, 1:2], in1=C[:, 0:n],
                                                           op0=ALU.mult, op1=ALU.add), reads=[bU, bC, b_const], writes=[bC])
            fw.op(fw.dve, lambda e: e.scalar_tensor_tensor(out=C[:, 0:n], in0=U[:, 0:n], scalar=cw[:, ch, 0:1], in1=C[:, 0:n],
                                                           op0=ALU.mult, op1=ALU.add), reads=[bU, bC, b_const], writes=[bC])
            if isb == 0:
                fw.op(fw.act, lambda e: e.activation(out=C[:, 0:n], in_=C[:, 0:n], func=AF.Silu), reads=[bC], writes=[bC])
            else:
                Ca, bCa = c_sb[ui - 1], b_c[ui - 1]
                fw.op(fw.pool, lambda e: e.tensor_tensor(out=act_sb[:, j, 0:n], in0=Ca[:, 0:n], in1=C[:, 0:n], op=ALU.mult),
                      reads=[bC, bCa], writes=[b_act])

        class WIn:
            pass
        w_in_perm = w_in
        linear(w_in_perm, 64, KC, xn_sb, b_xn, n, ep3)
        if halo:
            continue

        linear(w_out, 16, 32, act_sb, b_act, n, ep1)

        if variant == "mid":
            fw.dma(h2T.rearrange("(c p) t -> p c t", p=128)[:, :, o0:o0 + n], h_sb[:, :, 0:n], reads=[b_h], q=0, is_output=True)
            rms(n, 1, xn_sb, b_xn)

            def ep_kv(m, p, bp):
                fw.op(fw.act, lambda e: e.activation(out=a_sb[:, m, 0:n], in_=p[:, 0:n], func=AF.Copy), reads=[bp], writes=[b_a])
            linear(w_kv, 24, KC, xn_sb, b_xn, n, ep_kv)
            fw.dma(kvT.rearrange("(c p) t -> p c t", p=128)[:, :, o0:o0 + n], a_sb[:, 0:24, 0:n], reads=[b_a], q=0, is_output=True)
            rms(n, 2, xn_sb, b_xn, compute_rstd=False)

            def ep_q(m, p, bp):
                if m < 16:
                    fw.op(fw.act, lambda e: e.activation(out=act_sb[:, m, 0:n], in_=p[:, 0:n], func=AF.Copy, scale=128.0 ** -0.5),
                          reads=[bp], writes=[b_act])
                else:
                    fw.op(fw.act, lambda e: e.activation(out=g_sb[:, 0:n], in_=p[:, 0:n], func=AF.Sigmoid), reads=[bp], writes=[b_g])
            linear(w_q, 17, KC, xn_sb, b_xn, n, ep_q)
            fw.dma(qT.rearrange("(c p) t -> p c t", p=128)[:, :, o0:o0 + n], act_sb[:, 0:16, 0:n], reads=[b_act], q=0, is_output=True)
            fw.dma(gT[:, o0:o0 + n], g_sb[:, 0:n], reads=[b_g], q=0, is_output=True)
        else:
            fw.op(fw.act, lambda e: e.activation(out=sq_sb[:, :, 0:n], in_=h_sb[:, :, 0:n], func=AF.Square), reads=[b_h], writes=[b_sq])
            for c in range(KC):
                fw.op(fw.pe, lambda e: e.matmul(ps_ss[:, 0:n], ones_sb[:], sq_sb[:, c, 0:n], start=(c == 0), stop=(c == KC - 1)),
                      reads=[b_sq, b_const], writes=[b_psss], inc=(c == KC - 1))
            fw.op(fw.act, lambda e: e.activation(out=rstd_sb[:, 0:n], in_=ps_ss[:, 0:n], func=AF.Sqrt, bias=eps_sb[:], scale=1.0 / D),
                  reads=[b_psss, b_const], writes=[b_rstd])
            fw.op(fw.dve, lambda e: e.reciprocal(out=rstd_sb[:, 0:n], in_=rstd_sb[:, 0:n]), reads=[b_rstd], writes=[b_rstd])
            for c in range(KC):
                fw.op(fw.dve, lambda e: e.scalar_tensor_tensor(out=h_sb[:, c, 0:n], in0=h_sb[:, c, 0:n], scalar=gains_sb[:, 1, c:c + 1],
                                                               in1=rstd_sb[:, 0:n], op0=ALU.mult, op1=ALU.mult),
                      reads=[b_h, b_rstd, b_const], writes=[b_h])
            fw.dma(outT.rearrange("(c p) t -> p c t", p=128)[:, :, o0:o0 + n], h_sb[:, :, 0:n], reads=[b_h], q=0, is_output=True)
    fw.finish()
    return nc


def lay_w(w, pad_to=None):
    K, M = w.shape
    if pad_to is not None and M < pad_to:
        w = np.concatenate([w, np.zeros((K, pad_to - M), w.dtype)], axis=1)
        M = pad_to
    return np.ascontiguousarray(w.reshape(K // 128, 128, M // 128, 128).transpose(2, 1, 0, 3)).reshape(M // 128, 128, K)


def lay_gain(g):
    return np.ascontiguousarray(g.reshape(16, 128).T)


NEGB = -30000.0


def build_stage_c(S):
    nc = bass.Bass("TRN2", target_bir_lowering=False)
    NQT = S // 128
    NC = S // 16
    NCT = (NC + 127) // 128
    NCP = NCT * 128
    NSEL = S // 64
    NBH = (NSEL + 127) // 128
    NBW = NBH * 128
    EW = min(NSEL, 128) * 64
    di = lambda n, s, dt: nc.dram_tensor(n, s, dt, kind="ExternalInput").ap()
    kcT = di("kcT", [128, S], BF16)
    vcT = di("vcT", [128, S], BF16)
    ksT = di("ksT", [128, S], BF16)
    kwT = di("kwT", [128, S], BF16)
    vs = di("vs", [S, 128], BF16)
    vw = di("vw", [S, 128], BF16)
    qT = di("qT", [NQT, 128, 512], BF16)
    gates = di("gates", [NQT, 128, 12], F32)
    w1 = [di("w1k", [128, 32, 256], F32), di("w1v", [128, 32, 256], F32)]
    peT = [di("peTk", [128, 32], F32), di("peTv", [128, 32], F32)]
    w2 = [di("w2k", [128, 2, 128], F32), di("w2v", [128, 2, 128], F32)]
    ident = di("ident", [128, 128], F32)
    ebig = di("ebig", [128, EW], F32)
    caus = di("caus", [128, 128], F32)
    low = di("low", [128, 128], F32)
    cposrow = di("cposrow", [128, NCP], F32)
    tcol = di("tcol", [128, NQT], F32)
    curcol = di("curcol", [128, NQT], F32)
    jrow = di("jrow", [128, NBW], F32)
    qrow = di("qrow", [128, 128], F32)
    cpq = di("cpq", [128, NCT, NQT], F32)
    o = nc.dram_tensor("o", [S, 512], BF16, kind="ExternalOutput").ap()

    fw = FW(nc)
    Sb = fw.sbuf
    B = Buf
    CBW = 16 * 512 + 32
    cbuf = Sb("cbuf", [128, CBW], BF16)
    w1_sb = Sb("w1_sb", [128, 32, 256], BF16)
    peT_sb = Sb("peT_sb", [128, 32], BF16)
    w2_sb = Sb("w2_sb", [128, 2, 128], BF16)
    hb_sb = Sb("hb_sb", [128, 2], F32)
    hx = [Sb(f"hx{i}", [128, 512], F32) for i in range(3)]
    hg_sb = Sb("hg_sb", [128, 2, 512], BF16)
    kcmpT = Sb("kcmpT", [128, NCP], BF16)
    vcmp = Sb("vcmp", [128, NCT, 129], BF16)
    ksT_sb = Sb("ksT_sb", [128, S], BF16)
    vs_sb = Sb("vs_sb", [128, NQT, 129], BF16)
    kw_sb = Sb("kw_sb", [128, 8, 128], BF16)
    vw_sb = Sb("vw_sb", [128, 8, 129], BF16)
    ident_sb = Sb("ident_sb", [128, 128], BF16)
    identf_sb = Sb("identf_sb", [128, 128], F32)
    ebig_sb = Sb("ebig_sb", [128, EW], BF16)
    caus_sb = Sb("caus_sb", [128, 128], BF16)
    low_sb = Sb("low_sb", [128, 128], BF16)
    cposrow_sb = Sb("cposrow_sb", [128, NCP], F32)
    tcol_sb = Sb("tcol_sb", [128, NQT], F32)
    curcol_sb = Sb("curcol_sb", [128, NQT], F32)
    jrow_sb = Sb("jrow_sb", [128, NBW], F32)
    qrow_sb = Sb("qrow_sb", [128, 128], F32)
    cpq_sb = Sb("cpq_sb", [128, NCT, NQT], F32)
    zero_sb = Sb("zero_sb", [128, 512], BF16)
    q_sb = [Sb(f"q_sb{i}", [128, 512], BF16) for i in range(2)]
    g_sb = [Sb(f"g_sb{i}", [128, 12], F32) for i in range(2)]
    bsel_sb = Sb("bsel_sb", [128, NCP], F32)
    s_sb = [Sb(f"s_sb{i}", [128, NCP], F32) for i in range(2)]
    e_sb = [Sb(f"e_sb{i}", [128, NCP], F32) for i in range(2)]
    sum_sb = [Sb(f"sum_sb{i}", [128, 1], F32) for i in range(2)]
    imp_sb = Sb("imp_sb", [128, NCP + 8], F32)
    psl_sb = Sb("psl_sb", [128, NBW], F32)
    d_sb = Sb("d_sb", [128, NBW], F32)
    val_sb = Sb("val_sb", [128, NBW], F32)
    fm_sb = Sb("fm_sb", [128, NBW], F32)
    sco_sb = Sb("sco_sb", [128, NBW], F32)
    wrk_sb = Sb("wrk_sb", [128, NBW], F32)
    m8_sb = Sb("m8_sb", [128, 16], F32)
    nsel_sb = Sb("nsel_sb", [128, NBW], F32)
    nselT_sbs = [Sb(f"nselT_sb{i}", [128, NBH, 128], BF16) for i in range(2)]
    cb_sb = [Sb(f"cb_sb{i}", [128, 128], BF16) for i in range(2)]
    pT_sb = [Sb(f"pT_sb{i}", [128, 512], BF16) for i in range(3)]
    coef_sb = [Sb(f"coef_sb{i}", [128, 2], F32) for i in range(4)]
    otmp_sb = [Sb(f"otmp_sb{i}", [128, 2, 128], F32) for i in range(2)]
    oacc_sb = [Sb(f"oacc_sb{i}", [128, 4, 128], F32) for i in range(2)]
    oout_sb = [Sb(f"oout_sb{i}", [128, 512], BF16) for i in range(2)]

    P = fw.psum
    ps_s = [P(f"ps_s{i}", [128, 512]) for i in range(3)]
    ps_o = [P(f"ps_o{i}", [128, 2, 256]) for i in range(4)]
    ps_sel = [P("ps_sel0", [128, 512])]
    ps_sel = [ps_sel[0], ps_sel[0]]
    b_pss = [B(), B(), B()]
    b_pso = [B() for _ in range(4)]
    b_pssel = [B()]
    b_pssel = [b_pssel[0], b_pssel[0]]
    b_const, b_cbuf, b_w1, b_w2, b_hb, b_hg, b_kcmp, b_vcmp, b_ks, b_vs = [B() for _ in range(10)]
    b_hx = [B(), B(), B()]
    b_kw = [B() for _ in range(8)]
    b_vw = [B() for _ in range(8)]
    b_q, b_g = [B(), B()], [B(), B()]
    b_bsel, b_imp, b_psl, b_d, b_val, b_fm, b_sco, b_wrk, b_m8, b_nsel = [B() for _ in range(10)]
    b_nselTs = [B(), B()]
    b_s, b_e, b_sum = [B(), B()], [B(), B()], [B(), B()]
    b_cb = [B(), B()]
    b_pT = [B(), B(), B()]
    b_coef = [B() for _ in range(4)]
    b_otmp = [B(), B()]
    b_oacc, b_oout = [B(), B()], [B(), B()]

    for dst, src in ((ident_sb, ident), (ebig_sb, ebig), (caus_sb, caus), (low_sb, low)):
        fw.dma(dst[:], src, writes=[b_const], q=1)
    for dst, src in ((identf_sb, ident), (cposrow_sb, cposrow), (tcol_sb, tcol), (curcol_sb, curcol), (jrow_sb, jrow),
                     (qrow_sb, qrow), (cpq_sb, cpq)):
        fw.dma(dst[:], src, writes=[b_const], q=0)
    fw.op(fw.dve, lambda e: e.memset(zero_sb[:], 0.0), writes=[b_const])
    fw.op(fw.dve, lambda e: e.memset(vs_sb[:, :, 128:129], 1.0), writes=[b_vs])
    fw.op(fw.dve, lambda e: e.memset(vw_sb[:, :, 128:129], 1.0), writes=[b_vw[0]])
    fw.op(fw.dve, lambda e: e.memset(vcmp[:, :, 128:129], 1.0), writes=[b_vcmp])
    fw.op(fw.dve, lambda e: e.memset(imp_sb[:], 0.0), writes=[b_imp])
    for i in range(1, 8):
        b_vw[i].w = b_vw[0].w
    fw.dma(ksT_sb[:], ksT, writes=[b_ks], q=0)
    vs_v = vs.rearrange("(t p) d -> p t d", p=128)
    for t0 in range(0, NQT, 16):
        t1 = min(NQT, t0 + 16)
        fw.dma(vs_sb[:, t0:t1, 0:128], vs_v[:, t0:t1, :], writes=[b_vs], q=0)

    for kv in range(2):
        src = kcT if kv == 0 else vcT
        fw.dma(w1_sb[:, 0:16, :], w1[kv][:, 0:16, :], writes=[b_w1], q=1)
        fw.dma(w1_sb[:, 16:32, :], w1[kv][:, 16:32, :], writes=[b_w1], q=1)
        fw.dma(peT_sb[:], peT[kv], writes=[b_w1], q=1)
        fw.dma(w2_sb[:], w2[kv], writes=[b_w2], q=1)
        for mc in range(2):
            for p in range(32):
                fw.op(fw.pe, lambda e: e.matmul(ps_sel[0][:, mc:mc + 1], w1_sb[:, p, mc * 128:(mc + 1) * 128], peT_sb[:, p:p + 1],
                                                start=(p == 0), stop=(p == 31)),
                      reads=[b_w1], writes=[b_pssel[0]], inc=(p == 31))
        fw.op(fw.dve, lambda e: e.tensor_copy(out=hb_sb[:], in_=ps_sel[0][:, 0:2]), reads=[b_pssel[0]], writes=[b_hb])
        for cbk in range((NC + 511) // 512):
            nb = min(512, NC - 512 * cbk)
            tok0 = 8192 * cbk
            need = 16 * nb + 16
            avail = min(need, S - tok0)
            if avail < need:
                fw.op(fw.dve, lambda e: e.memset(cbuf[:, avail:need], 0.0), writes=[b_cbuf])
            fw.dma(cbuf[:, 0:avail], src[:, tok0:tok0 + avail], writes=[b_cbuf], q=0)
            for mc in range(2):
                pb = ps_s[mc]
                for p in range(32):
                    fw.op(fw.pe, lambda e: e.matmul(pb[:, 0:nb], w1_sb[:, p, mc * 128:(mc + 1) * 128],
                                                    cbuf[:, p:p + 16 * nb].rearrange("d (i s) -> d i s", s=16)[:, :, 0],
                                                    start=(p == 0), stop=(p == 31)),
                          reads=[b_w1, b_cbuf], writes=[b_pss[mc]], inc=(p == 31))
                X, X2, T3 = hx[0], hx[1], hx[2]
                fw.op(fw.act, lambda e: e.activation(out=X[:, 0:nb], in_=pb[:, 0:nb], func=AF.Identity, bias=hb_sb[:, mc:mc + 1], scale=1.0),
                      reads=[b_pss[mc], b_hb], writes=[b_hx[0]])
                fw.op(fw.dve, lambda e: e.tensor_tensor(out=X2[:, 0:nb], in0=X[:, 0:nb], in1=X[:, 0:nb], op=ALU.mult),
                      reads=[b_hx[0]], writes=[b_hx[1]])
                fw.op(fw.dve, lambda e: e.tensor_scalar(out=X2[:, 0:nb], in0=X2[:, 0:nb], scalar1=0.044715, scalar2=1.0, op0=ALU.mult, op1=ALU.add),
                      reads=[b_hx[1]], writes=[b_hx[1]])
                fw.op(fw.dve, lambda e: e.tensor_tensor(out=X2[:, 0:nb], in0=X2[:, 0:nb], in1=X[:, 0:nb], op=ALU.mult),
                      reads=[b_hx[0], b_hx[1]], writes=[b_hx[1]])
                fw.op(fw.act, lambda e: e.activation(out=T3[:, 0:nb], in_=X2[:, 0:nb], func=AF.Sigmoid, scale=1.5957691216057308),
                      reads=[b_hx[1]], writes=[b_hx[2]])
                fw.op(fw.dve, lambda e: e.tensor_tensor(out=hg_sb[:, mc, 0:nb], in0=T3[:, 0:nb], in1=X[:, 0:nb], op=ALU.mult),
                      reads=[b_hx[0], b_hx[2]], writes=[b_hg])
            if kv == 0:
                for mc in range(2):
                    fw.op(fw.pe, lambda e: e.matmul(ps_sel[0][:, 0:nb], w2_sb[:, mc, :], hg_sb[:, mc, 0:nb], start=(mc == 0), stop=(mc == 1)),
                          reads=[b_w2, b_hg], writes=[b_pssel[0]], inc=(mc == 1))
                fw.op(fw.act, lambda e: e.activation(out=kcmpT[:, 512 * cbk:512 * cbk + nb], in_=ps_sel[0][:, 0:nb], func=AF.Copy),
                      reads=[b_pssel[0]], writes=[b_kcmp])
            else:
                for ct in range((nb + 127) // 128):
                    w_ = min(128, nb - ct * 128)
                    for mc in range(2):
                        fw.op(fw.pe, lambda e: e.matmul(ps_sel[0][0:w_, ct * 128:(ct + 1) * 128], hg_sb[:, mc, ct * 128:ct * 128 + w_], w2_sb[:, mc, :],
                                                        start=(mc == 0), stop=(mc == 1)),
                              reads=[b_w2, b_hg], writes=[b_pssel[0]], inc=(mc == 1))
                    fw.op(fw.act, lambda e: e.activation(out=vcmp[0:w_, 4 * cbk + ct, 0:128], in_=ps_sel[0][0:w_, ct * 128:(ct + 1) * 128], func=AF.Copy),
                          reads=[b_pssel[0]], writes=[b_vcmp])

    cnt = {"s": 0, "p": 0, "o": 0, "cb": 0}

    def load_q(qt):
        bi = qt % 2
        fw.dma(q_sb[bi][:], qT[qt], writes=[b_q[bi]], q=0)
        fw.dma(g_sb[bi][:], gates[qt], writes=[b_g[bi]], q=0)
        sl = qt % 8
        fw.dma(kw_sb[:, sl, :], kwT[:, qt * 128:(qt + 1) * 128], writes=[b_kw[sl]], q=0)
        fw.dma(vw_sb[:, sl, 0:128], vw[qt * 128:(qt + 1) * 128, :], writes=[b_vw[sl]], q=0)

    def attend(qt, tiles, br, first_branch, between=None):
        bi = qt % 2
        oset = cnt["o"] % 2
        cnt["o"] += 1
        po = [ps_o[2 * oset], ps_o[2 * oset + 1]]
        bpo = [b_pso[2 * oset], b_pso[2 * oset + 1]]
        for hb in range(2):
            fw.op(fw.pe, lambda e: e.matmul(po[hb][:, :, 0:129], zero_sb[:, 0:128], zero_sb[:, 0:258].rearrange("p (a b) -> p a b", a=2),
                                            start=True, stop=False, skip_group_check=True),
                  reads=[b_const], writes=[bpo[hb]], inc=False)
        nt = len(tiles)
        slots = []

        def emit_s(ti):
            kT_ap, b_k, va_ap, b_v, biases = tiles[ti]
            si = cnt["s"] % 3
            cnt["s"] += 1
            fw.op(fw.pe, lambda e: e.matmul(ps_s[si][:], kT_ap, q_sb[bi][:], start=True, stop=(len(biases) == 0)),
                  reads=[b_k, b_q[bi]], writes=[b_pss[si]], inc=(len(biases) == 0))
            for bj, (l_ap, r_ap, bufs) in enumerate(biases):
                last = bj == len(biases) - 1
                fw.op(fw.pe, lambda e: e.matmul(ps_s[si][:], l_ap, r_ap, start=False, stop=last),
                      reads=bufs, writes=[b_pss[si]], inc=last)
            slots.append(si)

        emit_s(0)
        if nt > 1:
            emit_s(1)
        for ti in range(nt):
            kT_ap, b_k, va_ap, b_v, biases = tiles[ti]
            si = slots[ti]
            pi = cnt["p"] % 3
            cnt["p"] += 1
            fw.op(fw.act, lambda e: e.activation(out=pT_sb[pi][:], in_=ps_s[si][:], func=AF.Exp), reads=[b_pss[si]], writes=[b_pT[pi]])
            if ti + 2 < nt:
                emit_s(ti + 2)
            for g in range(4):
                lastmm = (ti == nt - 1)
                fw.op(fw.pe, lambda e: e.matmul(po[g // 2][:, g % 2, 0:129], pT_sb[pi][:, g * 128:(g + 1) * 128], va_ap,
                                                start=False, stop=lastmm, skip_group_check=True),
                      reads=[b_pT[pi], b_v], writes=[bpo[g // 2]], inc=(lastmm and g % 2 == 1) or (g == 3))
            if between is not None:
                between(ti)
        for hb in range(2):
            ci = (cnt["o"] * 2 + hb) % 4
            cf, bcf = coef_sb[ci], b_coef[ci]
            fw.op(fw.dve, lambda e: e.tensor_scalar(out=cf[:], in0=po[hb][:, :, 128], scalar1=1e-30, scalar2=None, op0=ALU.max),
                  reads=[bpo[hb]], writes=[bcf])
            fw.op(fw.dve, lambda e: e.reciprocal(out=cf[:], in_=cf[:]), reads=[bcf], writes=[bcf])
            fw.op(fw.dve, lambda e: e.tensor_tensor(out=cf[:], in0=cf[:], in1=g_sb[bi][:, br * 4 + 2 * hb: br * 4 + 2 * hb + 2], op=ALU.mult),
                  reads=[bcf, b_g[bi]], writes=[bcf])
            dst = oacc_sb[bi][:, 2 * hb:2 * hb + 2, :]
            if first_branch:
                fw.op(fw.dve, lambda e: e.tensor_tensor(out=dst, in0=po[hb][:, :, 0:128], in1=cf[:].unsqueeze(2).to_broadcast([128, 2, 128]), op=ALU.mult),
                      reads=[bpo[hb], bcf], writes=[b_oacc[bi]])
            else:
                ot, bot = otmp_sb[hb], b_otmp[hb]
                fw.op(fw.dve, lambda e: e.tensor_tensor(out=ot[:], in0=po[hb][:, :, 0:128], in1=cf[:].unsqueeze(2).to_broadcast([128, 2, 128]), op=ALU.mult),
                      reads=[bpo[hb], bcf], writes=[bot])
                fw.op(fw.pool, lambda e: e.tensor_tensor(out=dst, in0=dst, in1=ot[:], op=ALU.add), reads=[bot, b_oacc[bi]], writes=[b_oacc[bi]])

    def sel_gen(qt):
        bi = qt % 2
        nselT_sb = nselT_sbs[qt % 2]
        b_nselT = b_nselTs[qt % 2]
        cmax = 8 * qt + 6
        ncw = min(NCP, ((cmax + 1 + 7) // 8) * 8)
        ncw = max(ncw, 8)
        nbw = min(NBW, max(16, ((2 * qt + 2 + 7) // 8) * 8))
        fw.op(fw.dve, lambda e: e.tensor_scalar(out=bsel_sb[:, 0:ncw], in0=cposrow_sb[:, 0:ncw], scalar1=tcol_sb[:, qt:qt + 1], scalar2=NEGB,
                                                op0=ALU.is_gt, op1=ALU.mult), reads=[b_const], writes=[b_bsel])
        for g in range(4):
            gi = g % 2
            yield
            for c0 in range(0, ncw, 512):
                if c0 > 0:
                    yield
                c1 = min(ncw, c0 + 512)
                pb = ps_sel[c0 // 512]
                fw.op(fw.pe, lambda e: e.matmul(pb[:, 0:c1 - c0], q_sb[bi][:, g * 128:(g + 1) * 128], kcmpT[:, c0:c1], start=True, stop=True),
                      reads=[b_q[bi], b_kcmp], writes=[b_pssel[c0 // 512]], inc=True)
                fw.op(fw.dve, lambda e: e.tensor_tensor(out=s_sb[gi][:, c0:c1], in0=pb[:, 0:c1 - c0], in1=bsel_sb[:, c0:c1], op=ALU.add),
                      reads=[b_pssel[c0 // 512], b_bsel], writes=[b_s[gi]])
            fw.op(fw.act, lambda e: e.activation(out=e_sb[gi][:, 0:ncw], in_=s_sb[gi][:, 0:ncw], func=AF.Exp, accum_out=sum_sb[gi][:]),
                  reads=[b_s[gi]], writes=[b_e[gi], b_sum[gi]])
            fw.op(fw.dve, lambda e: e.tensor_scalar(out=sum_sb[gi][:], in0=sum_sb[gi][:], scalar1=1e-30, scalar2=None, op0=ALU.max),
                  reads=[b_sum[gi]], writes=[b_sum[gi]])
            fw.op(fw.dve, lambda e: e.reciprocal(out=sum_sb[gi][:], in_=sum_sb[gi][:]), reads=[b_sum[gi]], writes=[b_sum[gi]])
            if g == 0:
                fw.op(fw.dve, lambda e: e.tensor_scalar(out=imp_sb[:, 1:1 + ncw], in0=e_sb[gi][:, 0:ncw], scalar1=sum_sb[gi][:], scalar2=None, op0=ALU.mult),
                      reads=[b_e[gi], b_sum[gi]], writes=[b_imp])
            else:
                fw.op(fw.dve, lambda e: e.scalar_tensor_tensor(out=imp_sb[:, 1:1 + ncw], in0=e_sb[gi][:, 0:ncw], scalar=sum_sb[gi][:], in1=imp_sb[:, 1:1 + ncw],
                                                               op0=ALU.mult, op1=ALU.add), reads=[b_e[gi], b_sum[gi], b_imp], writes=[b_imp])
        if ncw < NCP:
            fw.op(fw.dve, lambda e: e.memset(imp_sb[:, 1 + ncw:min(NCP + 8, 1 + ncw + 8)], 0.0), writes=[b_imp])
        nbe = min(nbw, (ncw + 3) // 4)
        fw.op(fw.dve, lambda e: e.memset(psl_sb[:, 0:nbw], 0.0), writes=[b_psl])
        fw.op(fw.dve, lambda e: e.tensor_reduce(out=psl_sb[:, 0:nbe], in_=imp_sb[:, 0:4 * nbe].rearrange("p (a b) -> p a b", b=4), axis=AX.X, op=ALU.add),
              reads=[b_imp], writes=[b_psl])
        fw.op(fw.dve, lambda e: e.tensor_tensor(out=psl_sb[:, 0:nbe], in0=psl_sb[:, 0:nbe],
                                                in1=imp_sb[:, 4:4 + 4 * nbe].rearrange("p (a b) -> p a b", b=4)[:, :, 0], op=ALU.add),
              reads=[b_imp, b_psl], writes=[b_psl])
        fw.op(fw.dve, lambda e: e.tensor_scalar(out=d_sb[:, 0:nbw], in0=jrow_sb[:, 0:nbw], scalar1=curcol_sb[:, qt:qt + 1], scalar2=None, op0=ALU.subtract),
              reads=[b_const], writes=[b_d])
        fw.op(fw.dve, lambda e: e.tensor_scalar(out=val_sb[:, 0:nbw], in0=d_sb[:, 0:nbw], scalar1=0.0, scalar2=None, op0=ALU.is_le),
              reads=[b_d], writes=[b_val])
        fw.op(fw.dve, lambda e: e.scalar_tensor_tensor(out=fm_sb[:, 0:nbw], in0=d_sb[:, 0:nbw], scalar=-1.0, in1=val_sb[:, 0:nbw], op0=ALU.is_ge, op1=ALU.mult),
              reads=[b_d, b_val], writes=[b_fm])
        fw.op(fw.dve, lambda e: e.scalar_tensor_tensor(out=sco_sb[:, 0:nbw], in0=fm_sb[:, 0:nbw], scalar=100.0, in1=psl_sb[:, 0:nbw], op0=ALU.mult, op1=ALU.add),
              reads=[b_fm, b_psl], writes=[b_sco])
        fw.op(fw.dve, lambda e: e.tensor_scalar(out=sco_sb[:, 0:1], in0=sco_sb[:, 0:1], scalar1=100.0, scalar2=None, op0=ALU.add),
              reads=[b_sco], writes=[b_sco])
        fw.op(fw.dve, lambda e: e.max(out=m8_sb[:, 0:8], in_=sco_sb[:, 0:nbw]), reads=[b_sco], writes=[b_m8])
        fw.op(fw.dve, lambda e: e.match_replace(out=wrk_sb[:, 0:nbw], in_to_replace=m8_sb[:, 0:8], in_values=sco_sb[:, 0:nbw], imm_value=-1.0),
              reads=[b_sco, b_m8], writes=[b_wrk])
        fw.op(fw.dve, lambda e: e.max(out=m8_sb[:, 8:16], in_=wrk_sb[:, 0:nbw]), reads=[b_wrk, b_m8], writes=[b_m8])
        fw.op(fw.dve, lambda e: e.scalar_tensor_tensor(out=wrk_sb[:, 0:nbw], in0=sco_sb[:, 0:nbw], scalar=m8_sb[:, 15:16], in1=val_sb[:, 0:nbw],
                                                       op0=ALU.is_ge, op1=ALU.mult), reads=[b_sco, b_m8, b_val], writes=[b_wrk])
        if nbw < NBW:
            fw.op(fw.dve, lambda e: e.memset(nsel_sb[:, nbw:NBW], NEGB), writes=[b_nsel])
        fw.op(fw.dve, lambda e: e.tensor_scalar(out=nsel_sb[:, 0:nbw], in0=wrk_sb[:, 0:nbw], scalar1=1.0, scalar2=-NEGB, op0=ALU.subtract, op1=ALU.mult),
              reads=[b_wrk], writes=[b_nsel])
        for _ in range(6):
            yield
        nbh_used = (min(NSEL, 2 * qt + 2) + 127) // 128
        for hb in range(nbh_used):
            if hb > 0:
                yield
            fw.op(fw.pe, lambda e: e.transpose(ps_sel[hb][:, 0:128], nsel_sb[:, hb * 128:(hb + 1) * 128], identf_sb[:]),
                  reads=[b_nsel, b_const], writes=[b_pssel[hb]], inc=True)
            fw.op(fw.act, lambda e: e.activation(out=nselT_sb[:, hb, :], in_=ps_sel[hb][:, 0:128], func=AF.Copy), reads=[b_pssel[hb]], writes=[b_nselT])


    load_q(0)
    for _ in sel_gen(0):
        pass
    for qt in range(NQT):
        bi = qt % 2
        if qt + 1 < NQT:
            load_q(qt + 1)
        tiles = []
        cmax = 8 * qt + 6
        nct = cmax // 128 + 1
        for ct in range(min(nct, NCT)):
            biases = []
            if 16 * (128 * ct + 127) + 31 > 128 * qt or (ct == NCT - 1):
                ci = cnt["cb"] % 2
                cnt["cb"] += 1
                fw.op(fw.dve, lambda e: e.tensor_scalar(out=cb_sb[ci][:], in0=qrow_sb[:], scalar1=cpq_sb[:, ct, qt:qt + 1], scalar2=NEGB,
                                                        op0=ALU.is_lt, op1=ALU.mult), reads=[b_const], writes=[b_cb[ci]])
                biases.append((ident_sb[:], cb_sb[ci][:].unsqueeze(1).to_broadcast([128, 4, 128]), [b_const, b_cb[ci]]))
            tiles.append((kcmpT[:, ct * 128:(ct + 1) * 128], b_kcmp, vcmp[:, ct, :], b_vcmp, biases))
        attend(qt, tiles, 0, True)
        tiles = []
        for kt in range(qt + 1):
            hb = kt // 64
            ktp = kt % 64
            biases = [(ebig_sb[:, ktp * 128:(ktp + 1) * 128], nselT_sbs[qt % 2][:, hb, :].unsqueeze(1).to_broadcast([128, 4, 128]), [b_const, b_nselTs[qt % 2]])]
            if kt == qt:
                biases.append((ident_sb[:], caus_sb[:].unsqueeze(1).to_broadcast([128, 4, 128]), [b_const]))
            tiles.append((ksT_sb[:, kt * 128:(kt + 1) * 128], b_ks, vs_sb[:, kt, :], b_vs, biases))
        gen = sel_gen(qt + 1) if qt + 1 < NQT else iter(())

        def between(ti, gen=gen):
            if ti % 2 == 1:
                next(gen, None)
        attend(qt, tiles, 1, False, between=between)
        for _ in gen:
            pass
        tiles = []
        for kt in range(max(0, qt - 4), qt + 1):
            sl = kt % 8
            biases = []
            if kt == qt:
                biases.append((ident_sb[:], caus_sb[:].unsqueeze(1).to_broadcast([128, 4, 128]), [b_const]))
            if kt == qt - 4:
                biases.append((ident_sb[:], low_sb[:].unsqueeze(1).to_broadcast([128, 4, 128]), [b_const]))
            tiles.append((kw_sb[:, sl, :], b_kw[sl], vw_sb[:, sl, :], b_vw[sl], biases))
        attend(qt, tiles, 2, False)
        fw.op(fw.act, lambda e: e.activation(out=oout_sb[bi][:], in_=oacc_sb[bi][:].rearrange("p a b -> p (a b)"), func=AF.Copy),
              reads=[b_oacc[bi]], writes=[b_oout[bi]])
        fw.dma(o[qt * 128:(qt + 1) * 128, :], oout_sb[bi][:], reads=[b_oout[bi]], q=0, is_output=True)
    fw.finish()
    return nc


def stage_c_consts(S):
    NQT = S // 128
    NC = S // 16
    NCT = (NC + 127) // 128
    NCP = NCT * 128
    NSEL = S // 64
    NBH = (NSEL + 127) // 128
    NBW = NBH * 128
    EW = min(NSEL, 128) * 64
    p = np.arange(128)
    c = np.arange(NCP)
    cpos = (16.0 * c + 31.0).astype(np.float32)
    cpos[NC - 1:] = 1e9
    kl = p[:, None]
    ql = p[None, :]
    cpq = np.zeros((128, NCT, NQT), np.float32)
    for ct in range(NCT):
        cpq[:, ct, :] = cpos[ct * 128 + p][:, None] - 128.0 * np.arange(NQT)[None, :]
    return dict(
        ident=np.eye(128, dtype=np.float32),
        ebig=(np.arange(EW)[None, :] // 64 == p[:, None]).astype(np.float32),
        caus=np.where(kl > ql, NEGB, 0.0).astype(np.float32),
        low=np.where(kl <= ql, NEGB, 0.0).astype(np.float32),
        cposrow=np.ascontiguousarray(np.broadcast_to(cpos[None, :], (128, NCP))),
        tcol=(128.0 * np.arange(NQT)[None, :] + p[:, None]).astype(np.float32),
        curcol=((128 * np.arange(NQT)[None, :] + p[:, None]) // 64).astype(np.float32),
        jrow=np.ascontiguousarray(np.broadcast_to(np.arange(NBW, dtype=np.float32)[None, :], (128, NBW))),
        qrow=np.ascontiguousarray(np.broadcast_to(np.arange(128, dtype=np.float32)[None, :], (128, 128))),
        cpq=cpq,
    )

import ml_dtypes

BF_NP = ml_dtypes.bfloat16
NCORES = 8
BATCH, SEQ, DM = 2, 16384, 2048
_CACHE = {}


def _prog(key, fn):
    if key not in _CACHE:
        _CACHE[key] = fn()
    return _CACHE[key]


def build_cast(ncols):
    nc = bass.Bass("TRN2", target_bir_lowering=False)
    src = nc.dram_tensor("src", [128, ncols], F32, kind="ExternalInput").ap()
    dst = nc.dram_tensor("dst", [128, ncols], BF16, kind="ExternalOutput").ap()
    fw = FW(nc)
    CH = 8192
    bufs = [fw.sbuf(f"cb{i}", [128, CH], BF16) for i in range(2)]
    bb = [Buf(), Buf()]
    for i, c0 in enumerate(range(0, ncols, CH)):
        c1 = min(ncols, c0 + CH)
        fw.dma(bufs[i % 2][:, 0:c1 - c0], src[:, c0:c1], writes=[bb[i % 2]], q=1, max_dma_last_dim=8192)
        fw.dma(dst[:, c0:c1], bufs[i % 2][:, 0:c1 - c0], reads=[bb[i % 2]], q=0, is_output=True)
    fw.finish()
    return nc


def _run(nc, in_maps):
    res = run_bass_kernel_spmd(nc, in_maps, core_ids=list(range(NCORES)))
    return res.results


def kernel(x, norm_mix_gain, norm_ffn_gain, ret_w_in, ret_gn_gain, ret_w_out, nsa_kv_norm_gain, nsa_w_kv,
           cmp_pe_k, cmp_w1_k, cmp_w2_k, cmp_pe_v, cmp_w1_v, cmp_w2_v, nsa_w_q, nsa_w_o,
           ffn_w_in, ffn_conv_w, ffn_conv_b, ffn_w_out, final_norm_gain):
    f32 = np.float32
    x = np.asarray(x, f32)
    NTOK = BATCH * SEQ
    T = NTOK // NCORES

    ncB = _prog("Bmid", lambda: build_stage_b(T, 4096, "mid"))
    ncC = _prog("C", lambda: build_stage_c(SEQ))
    ncD = _prog("Bfin", lambda: build_stage_b(T, 2048, "final"))
    wlist = [("ret_w_out", np.asarray(ret_w_out[0], f32)), ("ffn_w_in0", np.asarray(ffn_w_in[0], f32)),
             ("ffn_w_in1", np.asarray(ffn_w_in[1], f32)), ("ffn_w_out0", np.asarray(ffn_w_out[0], f32)),
             ("ffn_w_out1", np.asarray(ffn_w_out[1], f32)), ("nsa_w_kv", np.asarray(nsa_w_kv, f32)),
             ("nsa_w_q", np.asarray(nsa_w_q[0], f32)), ("nsa_w_o", np.asarray(nsa_w_o[0], f32))]
    flat = np.concatenate([w.reshape(-1) for _, w in wlist])
    per = -(-flat.size // (NCORES * 128))
    per = -(-per // 64) * 64
    tot = per * NCORES * 128
    flat_p = np.zeros(tot, f32)
    flat_p[:flat.size] = flat
    flat_p = flat_p.reshape(NCORES, 128, per)

    xT = np.ascontiguousarray(x.reshape(NTOK, DM).T)
    ncA = _prog(("A", per), lambda: build_stage_a(NTOK, SEQ, cast_cols=per))
    cA = stage_a_consts(SEQ)
    w_in = np.asarray(ret_w_in[0], f32)
    H, dk, dv = 8, 256, 512
    gn = np.asarray(ret_gn_gain[0], f32)
    g_mix0 = lay_gain(np.asarray(norm_mix_gain[0], f32))
    maps = []
    for h in range(H):
        m = dict(cA[h])
        m["xT"] = xT
        m["gain"] = g_mix0
        m["wq"] = np.ascontiguousarray(w_in[:, h * dk:(h + 1) * dk])
        m["wk"] = np.ascontiguousarray(w_in[:, H * dk + h * dk: H * dk + (h + 1) * dk])
        m["wv"] = np.ascontiguousarray(w_in[:, 2 * H * dk + h * dv: 2 * H * dk + (h + 1) * dv])
        m["wg"] = np.ascontiguousarray(w_in[:, 2 * H * dk + H * dv + h * dv: 2 * H * dk + H * dv + (h + 1) * dv])
        m["gng"] = np.ascontiguousarray(np.broadcast_to(gn[h * dv:(h + 1) * dv][None, :], (128, 512)))
        m["csrc"] = flat_p[h]
        maps.append(m)
    rA = _run(ncA, maps)
    flat_b = np.concatenate([np.asarray(r["cdst"]).reshape(-1) for r in rA])[:flat.size]
    wb = {}
    off = 0
    for name, w in wlist:
        wb[name] = flat_b[off:off + w.size].reshape(w.shape)
        off += w.size
    yT = np.concatenate([np.asarray(r["y"]).T for r in rA], axis=0)
    del rA

    perm = np.stack([np.arange(32), np.arange(32) + 32], axis=1).reshape(-1)

    def permcols(w):
        return np.ascontiguousarray(w.reshape(w.shape[0], 64, 128)[:, perm, :].reshape(w.shape[0], 8192))

    def convw_of(layer):
        cw = np.asarray(ffn_conv_w[layer], f32)
        cb = np.asarray(ffn_conv_b[layer], f32)
        out = np.zeros((128, 64, 4), f32)
        out[:, :, 0:3] = cw.reshape(3, 64, 128).transpose(2, 1, 0)
        out[:, :, 3] = cb.reshape(64, 128).T
        return out

    def with_halo(fullT, c, dt):
        b, j = divmod(c, NCORES // BATCH)
        lo = b * SEQ + j * T
        out = np.zeros((fullT.shape[0], 2 + T), dt)
        out[:, 2:] = fullT[:, lo:lo + T]
        if j > 0:
            out[:, 0:2] = fullT[:, lo - 2:lo]
        return out

    ncB = _prog("Bmid", lambda: build_stage_b(T, 4096, "mid"))
    gains_mid = np.ascontiguousarray(np.stack([lay_gain(np.asarray(norm_ffn_gain[0], f32)), lay_gain(np.asarray(nsa_kv_norm_gain, f32)),
                                               lay_gain(np.asarray(norm_mix_gain[1], f32))], axis=1))
    common = dict(w_mix=lay_w(wb["ret_w_out"]), w_in=lay_w(permcols(wb["ffn_w_in0"])), w_out=lay_w(wb["ffn_w_out0"]),
                  gains=gains_mid, convw=convw_of(0), w_kv=lay_w(wb["nsa_w_kv"]), w_q=lay_w(wb["nsa_w_q"], pad_to=17 * 128))
    maps = []
    for c in range(NCORES):
        m = dict(common)
        m["hT"] = with_halo(xT, c, f32)
        m["aT"] = with_halo(yT, c, BF_NP)
        maps.append(m)
    rB = _run(ncB, maps)
    del yT, xT
    h2T = np.concatenate([r["h2T"] for r in rB], axis=1)
    kvT = np.concatenate([np.asarray(r["kvT"]) for r in rB], axis=1)
    qTf = np.concatenate([np.asarray(r["qT"]) for r in rB], axis=1)
    gTf = np.concatenate([r["gT"] for r in rB], axis=1)[:48]
    del rB

    ncC = _prog("C", lambda: build_stage_c(SEQ))
    cC = stage_c_consts(SEQ)
    NQT = SEQ // 128

    def w1lay(w):
        return np.ascontiguousarray(np.asarray(w, f32).reshape(32, 128, 256).transpose(1, 0, 2))

    def w2lay(w):
        return np.ascontiguousarray(np.asarray(w, f32).reshape(2, 128, 128).transpose(1, 0, 2))
    cw = dict(w1k=w1lay(cmp_w1_k), w1v=w1lay(cmp_w1_v), w2k=w2lay(cmp_w2_k), w2v=w2lay(cmp_w2_v),
              peTk=np.ascontiguousarray(np.asarray(cmp_pe_k, f32).T), peTv=np.ascontiguousarray(np.asarray(cmp_pe_v, f32).T))
    maps = []
    for b in range(BATCH):
        ts = slice(b * SEQ, (b + 1) * SEQ)
        for h in range(4):
            m = dict(cC)
            m.update(cw)
            row = lambda i: slice((i * 4 + h) * 128, (i * 4 + h + 1) * 128)
            m["kcT"] = np.ascontiguousarray(kvT[row(0), ts])
            m["vcT"] = np.ascontiguousarray(kvT[row(1), ts])
            m["ksT"] = np.ascontiguousarray(kvT[row(2), ts])
            m["vs"] = np.ascontiguousarray(kvT[row(3), ts].T)
            m["kwT"] = np.ascontiguousarray(kvT[row(4), ts])
            m["vw"] = np.ascontiguousarray(kvT[row(5), ts].T)
            qq = qTf[h * 512:(h + 1) * 512, ts].reshape(4, 128, NQT, 128)
            m["qT"] = np.ascontiguousarray(qq.transpose(2, 1, 0, 3)).reshape(NQT, 128, 512)
            gg = gTf[h * 12:(h + 1) * 12, ts].reshape(4, 3, NQT, 128)
            m["gates"] = np.ascontiguousarray(gg.transpose(2, 3, 1, 0)).reshape(NQT, 128, 12)
            maps.append(m)
    rC = _run(ncC, maps)
    del kvT, qTf, gTf
    oT = np.zeros((2048, NTOK), BF_NP)
    for b in range(BATCH):
        for h in range(4):
            oT[h * 512:(h + 1) * 512, b * SEQ:(b + 1) * SEQ] = np.asarray(rC[b * 4 + h]["o"]).T
    del rC

    ncD = _prog("Bfin", lambda: build_stage_b(T, 2048, "final"))
    gains_fin = np.ascontiguousarray(np.stack([lay_gain(np.asarray(norm_ffn_gain[1], f32)), lay_gain(np.asarray(final_norm_gain, f32)),
                                               lay_gain(np.asarray(final_norm_gain, f32))], axis=1))
    common = dict(w_mix=lay_w(wb["nsa_w_o"]), w_in=lay_w(permcols(wb["ffn_w_in1"])), w_out=lay_w(wb["ffn_w_out1"]),
                  gains=gains_fin, convw=convw_of(1))
    maps = []
    for c in range(NCORES):
        m = dict(common)
        m["hT"] = with_halo(h2T, c, f32)
        m["aT"] = with_halo(oT, c, BF_NP)
        maps.append(m)
    rD = _run(ncD, maps)
    outT = np.concatenate([r["outT"] for r in rD], axis=1)
    return np.ascontiguousarray(outT.T).reshape(BATCH, SEQ, DM).astype(f32)
```

```python
import numpy as np
from contextlib import ExitStack
import concourse.bass as bass
import concourse.mybir as mybir
from concourse.bass_utils import run_bass_kernel_spmd

F32 = mybir.dt.float32
BF16 = mybir.dt.bfloat16
AF = mybir.ActivationFunctionType
ALU = mybir.AluOpType
AX = mybir.AxisListType


class Eng:
    def __init__(self, fw, name, h, is_pe=False):
        self.fw = fw
        self.name = name
        self.h = h
        self.sem = fw.es.enter_context(fw.nc.semaphore("sem_" + name))
        self.count = 0
        self.pending = False
        self.waited = {}
        self.is_pe = is_pe

    def wait_tok(self, tok):
        if tok is None:
            return
        sem, val, eng = tok
        if eng is self and self.is_pe:
            return
        key = id(sem)
        if self.waited.get(key, 0) >= val:
            return
        if eng is not None and eng is not self and eng.count < val:
            eng.flush()
        if eng is self and self.count < val:
            self.flush()
        self.h.wait_ge(sem, val)
        self.waited[key] = val

    def flush(self):
        if self.pending:
            self.count += 1
            self.h.nop().then_inc(self.sem, 1)
            self.pending = False


class Buf:
    __slots__ = ("name", "w", "r")

    def __init__(self, name=""):
        self.name = name
        self.w = None
        self.r = []


class FW:
    def __init__(self, nc):
        self.nc = nc
        self.es = ExitStack()
        self.pe = Eng(self, "pe", nc.tensor, is_pe=True)
        self.dve = Eng(self, "dve", nc.vector)
        self.act = Eng(self, "act", nc.scalar)
        self.pool = Eng(self, "pool", nc.gpsimd)
        self.sp = Eng(self, "sp", nc.sync)
        self.engs = [self.pe, self.dve, self.act, self.pool, self.sp]
        self.dma_sems = []
        self.n_dma = 0
        self.NDMA = 6
        for q in range(2):
            lst = []
            for i in range(self.NDMA):
                s = self.es.enter_context(nc.semaphore(f"dsem{q}_{i}"))
                lst.append([s, 0])
            self.dma_sems.append(lst)
        self.dma_rr = [0, 0]
        self.out_toks = []

    def sbuf(self, name, shape, dt):
        return self.es.enter_context(self.nc.sbuf_tensor(name, shape, dt))

    def psum(self, name, shape, dt=F32):
        return self.es.enter_context(self.nc.psum_tensor(name, shape, dt))

    def op(self, eng, fn, reads=(), writes=(), inc=True):
        for b in reads:
            eng.wait_tok(b.w)
        for b in writes:
            eng.wait_tok(b.w)
            for t in b.r:
                eng.wait_tok(t)
        inst = fn(eng.h)
        if inc:
            eng.count += 1
            inst.then_inc(eng.sem, 1)
            eng.pending = False
            tok = (eng.sem, eng.count, eng)
        else:
            eng.pending = True
            tok = (eng.sem, eng.count + 1, eng)
        for b in reads:
            b.r.append(tok)
            if len(b.r) > 24:
                b.r = b.r[-24:]
        for b in writes:
            b.w = tok
            b.r = []
        return tok

    def dma(self, out, in_, reads=(), writes=(), q=0, is_output=False, **kw):
        eng = self.sp if q == 0 else self.pool
        slot = self.dma_sems[q][self.dma_rr[q] % self.NDMA]
        self.dma_rr[q] += 1
        sem, val = slot
        if val > 0:
            eng.wait_tok((sem, val, None))
        for b in reads:
            eng.wait_tok(b.w)
        for b in writes:
            eng.wait_tok(b.w)
            for t in b.r:
                eng.wait_tok(t)
        slot[1] = val + 16
        eng.h.dma_start(out=out, in_=in_, **kw).then_inc(sem, 16)
        tok = (sem, val + 16, None)
        for b in reads:
            b.r.append(tok)
        for b in writes:
            b.w = tok
            b.r = []
        if is_output:
            self.out_toks.append(tok)
        return tok

    def finish(self):
        for e in self.engs:
            e.flush()
        for q in range(2):
            for sem, val in self.dma_sems[q]:
                if val > 0:
                    self.sp.wait_tok((sem, val, None))
        self.es.close()


RMS_EPS = 1e-6


def build_stage_a(NTOK, SEQ, cast_cols=0):
    nc = bass.Bass("TRN2", target_bir_lowering=False)
    D = 2048
    KC = D // 128
    TT = 512
    xT = nc.dram_tensor("xT", [D, NTOK], F32, kind="ExternalInput").ap()
    gain = nc.dram_tensor("gain", [128, KC], F32, kind="ExternalInput").ap()
    wq = nc.dram_tensor("wq", [D, 256], F32, kind="ExternalInput").ap()
    wk = nc.dram_tensor("wk", [D, 256], F32, kind="ExternalInput").ap()
    wv = nc.dram_tensor("wv", [D, 512], F32, kind="ExternalInput").ap()
    wg = nc.dram_tensor("wg", [D, 512], F32, kind="ExternalInput").ap()
    cosT = nc.dram_tensor("cosT", [128, SEQ], F32, kind="ExternalInput").ap()
    sinT = nc.dram_tensor("sinT", [128, SEQ], F32, kind="ExternalInput").ap()
    decT = nc.dram_tensor("decT", [128, 128], F32, kind="ExternalInput").ap()
    qdec = nc.dram_tensor("qdec", [128, TT], F32, kind="ExternalInput").ap()
    kdec = nc.dram_tensor("kdec", [128, 1], F32, kind="ExternalInput").ap()
    cdec = nc.dram_tensor("cdec", [128, 1], F32, kind="ExternalInput").ap()
    gng = nc.dram_tensor("gng", [128, 512], F32, kind="ExternalInput").ap()
    ident = nc.dram_tensor("ident", [128, 128], F32, kind="ExternalInput").ap()
    y = nc.dram_tensor("y", [NTOK, 512], BF16, kind="ExternalOutput").ap()
    if cast_cols:
        csrc = nc.dram_tensor("csrc", [128, cast_cols], F32, kind="ExternalInput").ap()
        cdst = nc.dram_tensor("cdst", [128, cast_cols], BF16, kind="ExternalOutput").ap()

    fw = FW(nc)
    S = fw.sbuf
    w_sb = S("w_sb", [128, KC, 1536], BF16)
    gain_sb = S("gain_sb", [128, KC], F32)
    decT_sb = S("decT_sb", [128, 128], F32)
    qdec_sb = S("qdec_sb", [128, TT], F32)
    kdec_sb = S("kdec_sb", [128, 1], F32)
    cdec_sb = S("cdec_sb", [128, 1], F32)
    gng_sb = S("gng_sb", [128, 512], F32)
    ident_sb = S("ident_sb", [128, 128], BF16)
    ones_sb = S("ones_sb", [128, 128], BF16)
    eps_sb = S("eps_sb", [128, 1], F32)
    HT = 256
    x_sb = [S(f"x_sb{i}", [128, KC, HT], F32) for i in range(2)]
    cs_sb = [S(f"cs_sb{i}", [128, 2, TT], F32) for i in range(2)]
    sq_sb = S("sq_sb", [128, KC, HT], BF16)
    xn_sbs = [S(f"xn_sb{i}", [128, KC, TT], BF16) for i in range(2)]
    rstd_sb = S("rstd_sb", [128, HT], F32)
    qT_sb = S("qT_sb", [128, 2, TT], BF16)
    kT_sb = S("kT_sb", [128, 2, TT], BF16)
    qdT_sb = S("qdT_sb", [128, 2, TT], BF16)
    tmp_sb = [S(f"tmp_sb{i}", [128, TT], F32) for i in range(4)]
    v_sb = [S(f"v_sb{i}", [128, 512], BF16) for i in range(2)]
    gs_sb = [S(f"gs_sb{i}", [128, 512], F32) for i in range(2)]
    pT_sb = [S(f"pT_sb{i}", [128, 128], BF16) for i in range(2)]
    kd_sb = [S(f"kd_sb{i}", [128, 256], BF16) for i in range(2)]
    st_sb = S("st_sb", [128, 2, 512], F32)
    stb_sb = S("stb_sb", [128, 2, 512], BF16)
    stat_sb = [S(f"stat_sb{i}", [128, 6], F32) for i in range(2)]
    mv_sb = [S(f"mv_sb{i}", [128, 2], F32) for i in range(2)]
    rs_sb = [S(f"rs_sb{i}", [128, 1], F32) for i in range(2)]
    t_sb = [S(f"t_sb{i}", [128, 512], F32) for i in range(2)]
    y_sb = [S(f"y_sb{i}", [128, 512], BF16) for i in range(2)]

    P = fw.psum
    ps_qk = [P(f"ps_qk{i}", [128, 512]) for i in range(2)]
    ps_vg = [P(f"ps_vg{i}", [128, 512]) for i in range(2)]
    ps_sc = P("ps_sc", [128, 512])
    ps_kt = P("ps_kt", [128, 1024], BF16)
    ps_out = P("ps_out", [128, 512])
    ps_su = P("ps_su", [128, 512])

    B = Buf
    b_w, b_const = B("w"), B("const")
    b_x = [B("x0"), B("x1")]
    b_cs = [B("cs0"), B("cs1")]
    b_sq, b_rstd, b_qT, b_kT, b_qdT = B(), B(), B(), B(), B()
    b_xns = [B(), B()]
    b_psss = B()
    b_tmp = [B() for _ in range(4)]
    b_v, b_gs, b_pT, b_kd = [B(), B()], [B(), B()], [B(), B()], [B(), B()]
    b_st, b_stb = B(), B()
    b_stat, b_mv, b_rs, b_t, b_y = [B(), B()], [B(), B()], [B(), B()], [B(), B()], [B(), B()]
    b_psqk, b_psvg = [B(), B()], [B(), B()]
    b_pssc, b_pskt, b_psout, b_pssu = B(), B(), B(), B()

    fw.dma(gain_sb[:], gain, writes=[b_const], q=0)
    fw.dma(decT_sb[:], decT, writes=[b_const], q=0)
    fw.dma(qdec_sb[:], qdec, writes=[b_const], q=0)
    fw.dma(kdec_sb[:], kdec, writes=[b_const], q=0)
    fw.dma(cdec_sb[:], cdec, writes=[b_const], q=0)
    fw.dma(gng_sb[:], gng, writes=[b_const], q=0)
    fw.dma(ident_sb[:], ident, writes=[b_const], q=1)
    off = 0
    for wsrc, n in ((wq, 256), (wk, 256), (wv, 512), (wg, 512)):
        for c0 in range(0, KC, 4):
            fw.dma(w_sb[:, c0:c0 + 4, off:off + n],
                   wsrc.rearrange("(c p) m -> p c m", p=128)[:, c0:c0 + 4, :], writes=[b_w], q=1)
        off += n
    fw.op(fw.dve, lambda e: e.memset(ones_sb[:], 1.0), writes=[b_const])
    fw.op(fw.dve, lambda e: e.memset(eps_sb[:], RMS_EPS), writes=[b_const])

    xT_v = xT.rearrange("(c p) t -> p c t", p=128)
    ntiles = NTOK // TT
    nhalves = NTOK // HT

    def load_half(hi):
        t0 = hi * HT
        bi = hi % 2
        fw.dma(x_sb[bi][:, 0:8, :], xT_v[:, 0:8, t0:t0 + HT], writes=[b_x[bi]], q=0)
        fw.dma(x_sb[bi][:, 8:16, :], xT_v[:, 8:16, t0:t0 + HT], writes=[b_x[bi]], q=0)

    def load_cs(ti):
        s0 = (ti * TT) % SEQ
        bi = ti % 2
        fw.dma(cs_sb[bi][:, 0, :], cosT[:, s0:s0 + TT], writes=[b_cs[bi]], q=0)
        fw.dma(cs_sb[bi][:, 1, :], sinT[:, s0:s0 + TT], writes=[b_cs[bi]], q=0)

    def rms_half(hi):
        bi = hi % 2
        ti = hi // 2
        hs = slice((hi % 2) * HT, (hi % 2 + 1) * HT)
        xs = x_sb[bi]
        xn = xn_sbs[ti % 2]
        b_xn = b_xns[ti % 2]
        fw.op(fw.act, lambda e: e.activation(out=sq_sb[:], in_=xs[:], func=AF.Square), reads=[b_x[bi]], writes=[b_sq])
        for c in range(KC):
            fw.op(fw.pe, lambda e: e.matmul(ps_sc[:, 256:512], ones_sb[:], sq_sb[:, c, :], start=(c == 0), stop=(c == KC - 1)),
                  reads=[b_sq, b_const], writes=[b_psss], inc=(c == KC - 1))
        fw.op(fw.act, lambda e: e.activation(out=rstd_sb[:], in_=ps_sc[:, 256:512], func=AF.Sqrt, bias=eps_sb[:], scale=1.0 / D),
              reads=[b_psss, b_const], writes=[b_rstd])
        fw.op(fw.dve, lambda e: e.reciprocal(out=rstd_sb[:], in_=rstd_sb[:]), reads=[b_rstd], writes=[b_rstd])
        for c in range(KC):
            fw.op(fw.dve, lambda e: e.scalar_tensor_tensor(out=xn[:, c, hs], in0=xs[:, c, :], scalar=gain_sb[:, c:c + 1],
                                                           in1=rstd_sb[:], op0=ALU.mult, op1=ALU.mult),
                  reads=[b_x[bi], b_rstd, b_const], writes=[b_xn])
        if hi + 2 < nhalves:
            load_half(hi + 2)

    load_half(0)
    load_half(1)
    load_cs(0)
    if cast_cols:
        CCH = 2048
        cst = [S(f"cst{i}", [128, CCH], BF16) for i in range(2)]
        b_cst = [B(), B()]
        for i, c0 in enumerate(range(0, cast_cols, CCH)):
            c1 = min(cast_cols, c0 + CCH)
            fw.dma(cst[i % 2][:, 0:c1 - c0], csrc[:, c0:c1], writes=[b_cst[i % 2]], q=1, max_dma_last_dim=8192)
            fw.dma(cdst[:, c0:c1], cst[i % 2][:, 0:c1 - c0], reads=[b_cst[i % 2]], q=0, is_output=True)
    rms_half(0)
    rms_half(1)
    cidx = 0
    for ti in range(ntiles):
        bi = ti % 2
        t0 = ti * TT
        xn_sb = xn_sbs[ti % 2]
        b_xn = b_xns[ti % 2]
        if ti + 1 < ntiles:
            load_cs(ti + 1)
        for which, (dst, b_dst, woff, scl) in enumerate(((qT_sb, b_qT, 0, 1.0), (kT_sb, b_kT, 256, 1.0 / 16.0))):
            for m in range(2):
                for c in range(KC):
                    fw.op(fw.pe, lambda e: e.matmul(ps_qk[m][:], w_sb[:, c, woff + m * 128: woff + (m + 1) * 128], xn_sb[:, c, :],
                                                    start=(c == 0), stop=(c == KC - 1)),
                          reads=[b_w, b_xn], writes=[b_psqk[m]], inc=(c == KC - 1))
            cos_t, sin_t = cs_sb[bi][:, 0, :], cs_sb[bi][:, 1, :]
            D_ = fw.dve
            fw.op(D_, lambda e: e.scalar_tensor_tensor(out=tmp_sb[0][:], in0=ps_qk[0][:], scalar=scl, in1=cos_t, op0=ALU.mult, op1=ALU.mult),
                  reads=[b_psqk[0], b_cs[bi]], writes=[b_tmp[0]])
            fw.op(D_, lambda e: e.scalar_tensor_tensor(out=tmp_sb[1][:], in0=ps_qk[1][:], scalar=scl, in1=sin_t, op0=ALU.mult, op1=ALU.mult),
                  reads=[b_psqk[1], b_cs[bi]], writes=[b_tmp[1]])
            fw.op(D_, lambda e: e.scalar_tensor_tensor(out=tmp_sb[2][:], in0=ps_qk[0][:], scalar=scl, in1=sin_t, op0=ALU.mult, op1=ALU.mult),
                  reads=[b_psqk[0], b_cs[bi]], writes=[b_tmp[2]])
            fw.op(D_, lambda e: e.scalar_tensor_tensor(out=tmp_sb[3][:], in0=ps_qk[1][:], scalar=scl, in1=cos_t, op0=ALU.mult, op1=ALU.mult),
                  reads=[b_psqk[1], b_cs[bi]], writes=[b_tmp[3]])
            fw.op(fw.pool, lambda e: e.tensor_tensor(out=dst[:, 0, :], in0=tmp_sb[0][:], in1=tmp_sb[1][:], op=ALU.subtract),
                  reads=[b_tmp[0], b_tmp[1]], writes=[b_dst])
            fw.op(fw.pool, lambda e: e.tensor_tensor(out=dst[:, 1, :], in0=tmp_sb[2][:], in1=tmp_sb[3][:], op=ALU.add),
                  reads=[b_tmp[2], b_tmp[3]], writes=[b_dst])
        for m in range(2):
            fw.op(fw.pool, lambda e: e.tensor_tensor(out=qdT_sb[:, m, :], in0=qT_sb[:, m, :], in1=qdec_sb[:], op=ALU.mult),
                  reads=[b_qT, b_const], writes=[b_qdT])
        if ti + 1 < ntiles:
            rms_half(2 * (ti + 1))
            rms_half(2 * (ti + 1) + 1)
        for ch in range(TT // 128):
            tok0 = t0 + ch * 128
            cs = slice(ch * 128, (ch + 1) * 128)
            ci = cidx % 2
            cidx += 1
            if tok0 % SEQ == 0:
                fw.op(fw.dve, lambda e: e.memset(st_sb[:], 0.0), writes=[b_st])
                fw.op(fw.pool, lambda e: e.memset(stb_sb[:], 0.0), writes=[b_stb])
            for c in range(KC):
                fw.op(fw.pe, lambda e: e.matmul(ps_vg[0][:], xn_sb[:, c, cs], w_sb[:, c, 512:1024], start=(c == 0), stop=(c == KC - 1)),
                      reads=[b_w, b_xn], writes=[b_psvg[0]], inc=(c == KC - 1))
            fw.op(fw.act, lambda e: e.activation(out=v_sb[ci][:], in_=ps_vg[0][:], func=AF.Copy), reads=[b_psvg[0]], writes=[b_v[ci]])
            for c in range(KC):
                fw.op(fw.pe, lambda e: e.matmul(ps_vg[1][:], xn_sb[:, c, cs], w_sb[:, c, 1024:1536], start=(c == 0), stop=(c == KC - 1)),
                      reads=[b_w, b_xn], writes=[b_psvg[1]], inc=(c == KC - 1))
            fw.op(fw.act, lambda e: e.activation(out=gs_sb[ci][:], in_=ps_vg[1][:], func=AF.Silu), reads=[b_psvg[1]], writes=[b_gs[ci]])
            fw.op(fw.pool, lambda e: e.tensor_tensor(out=gs_sb[ci][:], in0=gs_sb[ci][:], in1=gng_sb[:], op=ALU.mult),
                  reads=[b_gs[ci], b_const], writes=[b_gs[ci]])
            for m in range(2):
                fw.op(fw.pe, lambda e: e.matmul(ps_sc[:, 0:128], kT_sb[:, m, cs], qT_sb[:, m, cs], start=(m == 0), stop=(m == 1)),
                      reads=[b_kT, b_qT], writes=[b_pssc], inc=(m == 1))
            fw.op(fw.dve, lambda e: e.tensor_tensor(out=pT_sb[ci][:], in0=ps_sc[:, 0:128], in1=decT_sb[:], op=ALU.mult),
                  reads=[b_pssc, b_const], writes=[b_pT[ci]])
            for m in range(2):
                fw.op(fw.pe, lambda e: e.transpose(ps_kt[:, m * 128:(m + 1) * 128], kT_sb[:, m, cs], ident_sb[:]),
                      reads=[b_kT, b_const], writes=[b_pskt], inc=(m == 1))
            fw.op(fw.dve, lambda e: e.tensor_scalar(out=kd_sb[ci][:], in0=ps_kt[:, 0:256], scalar1=kdec_sb[:], scalar2=None, op0=ALU.mult),
                  reads=[b_pskt, b_const], writes=[b_kd[ci]])
            fw.op(fw.pe, lambda e: e.matmul(ps_out[:], pT_sb[ci][:], v_sb[ci][:], start=True, stop=False),
                  reads=[b_pT[ci], b_v[ci]], writes=[b_psout], inc=False)
            for m in range(2):
                fw.op(fw.pe, lambda e: e.matmul(ps_out[:], qdT_sb[:, m, cs], stb_sb[:, m, :], start=False, stop=(m == 1)),
                      reads=[b_qdT, b_stb], writes=[b_psout], inc=(m == 1))
            for m in range(2):
                fw.op(fw.pe, lambda e: e.matmul(ps_su[:], kd_sb[ci][:, m * 128:(m + 1) * 128], v_sb[ci][:], start=True, stop=True),
                      reads=[b_kd[ci], b_v[ci]], writes=[b_pssu], inc=True)
                fw.op(fw.dve, lambda e: e.scalar_tensor_tensor(out=st_sb[:, m, :], in0=st_sb[:, m, :], scalar=cdec_sb[:], in1=ps_su[:], op0=ALU.mult, op1=ALU.add),
                      reads=[b_st, b_pssu, b_const], writes=[b_st])
            fw.op(fw.act, lambda e: e.activation(out=stb_sb[:], in_=st_sb[:], func=AF.Copy), reads=[b_st], writes=[b_stb])
            fw.op(fw.dve, lambda e: e.bn_stats(out=stat_sb[ci][:], in_=ps_out[:]), reads=[b_psout], writes=[b_stat[ci]])
            fw.op(fw.dve, lambda e: e.bn_aggr(out=mv_sb[ci][:], in_=stat_sb[ci][:]), reads=[b_stat[ci]], writes=[b_mv[ci]])
            fw.op(fw.act, lambda e: e.activation(out=rs_sb[ci][:], in_=mv_sb[ci][:, 1:2], func=AF.Sqrt, bias=eps_sb[:], scale=1.0),
                  reads=[b_mv[ci], b_const], writes=[b_rs[ci]])
            fw.op(fw.dve, lambda e: e.reciprocal(out=rs_sb[ci][:], in_=rs_sb[ci][:]), reads=[b_rs[ci]], writes=[b_rs[ci]])
            fw.op(fw.dve, lambda e: e.tensor_scalar(out=t_sb[ci][:], in0=ps_out[:], scalar1=mv_sb[ci][:, 0:1], scalar2=rs_sb[ci][:],
                                                    op0=ALU.subtract, op1=ALU.mult),
                  reads=[b_psout, b_mv[ci], b_rs[ci]], writes=[b_t[ci]])
            fw.op(fw.pool, lambda e: e.tensor_tensor(out=y_sb[ci][:], in0=t_sb[ci][:], in1=gs_sb[ci][:], op=ALU.mult),
                  reads=[b_t[ci], b_gs[ci]], writes=[b_y[ci]])
            fw.dma(y[tok0:tok0 + 128, :], y_sb[ci][:], reads=[b_y[ci]], q=0, is_output=True)
    fw.finish()
    return nc


def stage_a_consts(SEQ):
    H = 8
    half = 128
    freqs = (10000.0 ** (-np.arange(half, dtype=np.float32) / half)).astype(np.float32)
    pos = np.arange(SEQ, dtype=np.float32)
    ang = (freqs[:, None] * pos[None, :]).astype(np.float32)
    cosT = np.cos(ang).astype(np.float32)
    sinT = np.sin(ang).astype(np.float32)
    out = []
    idx = np.arange(128, dtype=np.float64)
    for h in range(H):
        lg = np.log(1.0 - 2.0 ** (-5.0 - h))
        diff = idx[None, :] - idx[:, None]
        decT = np.where(diff >= 0, np.exp(np.maximum(diff, 0) * lg), 0.0).astype(np.float32)
        qd = np.exp((idx + 1.0) * lg).astype(np.float32)
        kd = np.exp((127.0 - idx) * lg).astype(np.float32)
        cd = np.float32(np.exp(128.0 * lg))
        out.append(dict(
            cosT=cosT, sinT=sinT, decT=decT,
            qdec=np.ascontiguousarray(np.broadcast_to(np.tile(qd, 4)[None, :], (128, 512))).astype(np.float32),
            kdec=kd.reshape(128, 1).copy(), cdec=np.full((128, 1), cd, np.float32),
            ident=np.eye(128, dtype=np.float32)))
    return out


RMS_EPS = 1e-6
D = 2048
KC = 16
DFF = 4096


def build_stage_b(T, KA, variant, WDT=BF16):
    nc = bass.Bass("TRN2", target_bir_lowering=False)
    KAC = KA // 128
    TT = 512
    assert T % TT == 0
    di = lambda n, s, dt: nc.dram_tensor(n, s, dt, kind="ExternalInput").ap()
    do = lambda n, s, dt: nc.dram_tensor(n, s, dt, kind="ExternalOutput").ap()
    hT = di("hT", [D, 2 + T], F32)
    aT = di("aT", [KA, 2 + T], BF16)
    w_mix = di("w_mix", [16, 128, KAC * 128], WDT)
    w_in = di("w_in", [64, 128, KC * 128], WDT)
    w_out = di("w_out", [16, 128, 32 * 128], WDT)
    gains = di("gains", [128, 3, KC], F32)
    convw = di("convw", [128, 64, 4], F32)
    if variant == "mid":
        w_kv = di("w_kv", [24, 128, KC * 128], WDT)
        w_q = di("w_q", [17, 128, KC * 128], WDT)
        h2T = do("h2T", [D, T], F32)
        kvT = do("kvT", [24 * 128, T], BF16)
        qT = do("qT", [16 * 128, T], BF16)
        gT = do("gT", [128, T], F32)
    else:
        outT = do("outT", [D, T], F32)

    fw = FW(nc)
    S = fw.sbuf
    h_sb = S("h_sb", [128, KC, TT], F32)
    a_sb = S("a_sb", [128, 32, TT], BF16)
    xn_sb = S("xn_sb", [128, KC, TT], BF16)
    sq_sb = S("sq_sb", [128, KC, TT], BF16)
    act_sb = S("act_sb", [128, 32, TT], BF16)
    NW = 4
    w_sb = [S(f"w_sb{i}", [128, 32 * 128], WDT) for i in range(NW)]
    u_sb = [S(f"u_sb{i}", [128, 2 + TT], F32) for i in range(4)]
    c_sb = [S(f"c_sb{i}", [128, TT], F32) for i in range(4)]
    carry_sb = S("carry_sb", [128, 64, 2], F32)
    rstd_sb = S("rstd_sb", [128, TT], F32)
    gains_sb = S("gains_sb", [128, 3, KC], F32)
    convw_sb = S("convw_sb", [128, 64, 4], F32)
    ones_sb = S("ones_sb", [128, 128], BF16)
    eps_sb = S("eps_sb", [128, 1], F32)
    g_sb = S("g_sb", [128, TT], F32)

    P = fw.psum
    ps = [P(f"ps{i}", [128, 512]) for i in range(6)]
    ps_ss = P("ps_ss", [128, 512])
    B = Buf
    b_h, b_a, b_xn, b_sq, b_act, b_rstd, b_const, b_carry, b_g = B(), B(), B(), B(), B(), B(), B(), B(), B()
    b_w = [B() for _ in range(NW)]
    b_u = [B() for _ in range(4)]
    b_c = [B() for _ in range(4)]
    b_ps = [B() for _ in range(6)]
    b_psss = B()

    fw.dma(gains_sb[:], gains, writes=[b_const], q=0)
    fw.dma(convw_sb[:], convw, writes=[b_const], q=0)
    fw.op(fw.dve, lambda e: e.memset(ones_sb[:], 1.0), writes=[b_const])
    fw.op(fw.dve, lambda e: e.memset(eps_sb[:], RMS_EPS), writes=[b_const])

    hT_v = hT.rearrange("(c p) t -> p c t", p=128)
    aT_v = aT.rearrange("(c p) t -> p c t", p=128)
    wq_state = {"n": 0, "ps": 0}

    def load_w(src, m, kc):
        i = wq_state["n"] % NW
        wq_state["n"] += 1
        half = kc * 128 // 2
        fw.dma(w_sb[i][:, 0:half], src[m, :, 0:half], writes=[b_w[i]], q=0)
        fw.dma(w_sb[i][:, half:kc * 128], src[m, :, half:kc * 128], writes=[b_w[i]], q=0)
        return i

    def linear(src_w, M, kc, rhs_sb, b_rhs, n, epilogue, prefetch=3, mrows=None):
        loaded = []
        for m in range(min(prefetch, M)):
            loaded.append(load_w(src_w, m, kc))
        for m in range(M):
            if m + prefetch < M:
                loaded.append(load_w(src_w, m + prefetch, kc))
            wi = loaded[m]
            pi = wq_state["ps"] % 6
            wq_state["ps"] += 1
            for k in range(kc):
                fw.op(fw.pe, lambda e: e.matmul(ps[pi][:, 0:n], w_sb[wi][:, k * 128:(k + 1) * 128], rhs_sb[:, k, 0:n],
                                                start=(k == 0), stop=(k == kc - 1)),
                      reads=[b_w[wi], b_rhs], writes=[b_ps[pi]], inc=(k == kc - 1))
            epilogue(m, ps[pi], b_ps[pi])

    def rms(n, gi, dst_sb, b_dst, compute_rstd=True):
        if compute_rstd:
            fw.op(fw.act, lambda e: e.activation(out=sq_sb[:, :, 0:n], in_=h_sb[:, :, 0:n], func=AF.Square), reads=[b_h], writes=[b_sq])
            for c in range(KC):
                fw.op(fw.pe, lambda e: e.matmul(ps_ss[:, 0:n], ones_sb[:], sq_sb[:, c, 0:n], start=(c == 0), stop=(c == KC - 1)),
                      reads=[b_sq, b_const], writes=[b_psss], inc=(c == KC - 1))
            fw.op(fw.act, lambda e: e.activation(out=rstd_sb[:, 0:n], in_=ps_ss[:, 0:n], func=AF.Sqrt, bias=eps_sb[:], scale=1.0 / D),
                  reads=[b_psss, b_const], writes=[b_rstd])
            fw.op(fw.dve, lambda e: e.reciprocal(out=rstd_sb[:, 0:n], in_=rstd_sb[:, 0:n]), reads=[b_rstd], writes=[b_rstd])
        for c in range(KC):
            fw.op(fw.dve, lambda e: e.scalar_tensor_tensor(out=dst_sb[:, c, 0:n], in0=h_sb[:, c, 0:n], scalar=gains_sb[:, gi, c:c + 1],
                                                           in1=rstd_sb[:, 0:n], op0=ALU.mult, op1=ALU.mult),
                  reads=[b_h, b_rstd, b_const], writes=[b_dst])

    tiles = [(0, 2, True)] + [(2 + i * TT, TT, False) for i in range(T // TT)]
    for (c0, n, halo) in tiles:
        o0 = c0 - 2
        fw.dma(h_sb[:, 0:8, 0:n], hT_v[:, 0:8, c0:c0 + n], writes=[b_h], q=0)
        fw.dma(h_sb[:, 8:16, 0:n], hT_v[:, 8:16, c0:c0 + n], writes=[b_h], q=0)
        for k0 in range(0, KAC, 8):
            fw.dma(a_sb[:, k0:k0 + 8, 0:n], aT_v[:, k0:k0 + 8, c0:c0 + n], writes=[b_a], q=0)

        def ep1(m, p, bp):
            fw.op(fw.dve, lambda e: e.tensor_tensor(out=h_sb[:, m, 0:n], in0=h_sb[:, m, 0:n], in1=p[:, 0:n], op=ALU.add),
                  reads=[bp, b_h], writes=[b_h])
        linear(w_mix, 16, KAC, a_sb, b_a, n, ep1)
        rms(n, 0, xn_sb, b_xn)

        def ep3(mm, p, bp):
            j = mm // 2
            isb = mm % 2
            ch = j + 32 * isb
            ui = (mm % 4)
            U, bU = u_sb[ui], b_u[ui]
            C, bC = c_sb[ui], b_c[ui]
            fw.op(fw.pool, lambda e: e.tensor_copy(out=U[:, 0:2], in_=carry_sb[:, ch, :]), reads=[b_carry], writes=[bU])
            fw.op(fw.act, lambda e: e.activation(out=U[:, 2:2 + n], in_=p[:, 0:n], func=AF.Copy), reads=[bp], writes=[bU])
            fw.op(fw.pool, lambda e: e.tensor_copy(out=carry_sb[:, ch, :], in_=U[:, n:n + 2]), reads=[bU], writes=[b_carry])
            if halo:
                return
            cw = convw_sb
            fw.op(fw.dve, lambda e: e.tensor_scalar(out=C[:, 0:n], in0=U[:, 2:2 + n], scalar1=cw[:, ch, 2:3], scalar2=cw[:, ch, 3:4],
                                                    op0=ALU.mult, op1=ALU.add), reads=[bU, b_const], writes=[bC])
            fw.op(fw.dve, lambda e: e.scalar_tensor_tensor(out=C[:, 0:n], in0=U[:, 1:1 + n], scalar=cw[:, ch, 1:2], in1=C[:, 0:n],
                                                           op0=ALU.mult, op1=ALU.add), reads=[bU, bC, b_const], writes=[bC])
            fw.op(fw.dve, lambda e: e.scalar_tensor_tensor(out=C[:, 0:n], in0=U[:, 0:n], scalar=cw[:, ch, 0:1], in1=C[:, 0:n],
                                                           op0=ALU.mult, op1=ALU.add), reads=[bU, bC, b_const], writes=[bC])
            if isb == 0:
                fw.op(fw.act, lambda e: e.activation(out=C[:, 0:n], in_=C[:, 0:n], func=AF.Silu), reads=[bC], writes=[bC])
            else:
                Ca, bCa = c_sb[ui - 1], b_c[ui - 1]
                fw.op(fw.pool, lambda e: e.tensor_tensor(out=act_sb[:, j, 0:n], in0=Ca[:, 0:n], in1=C[:, 0:n], op=ALU.mult),
                      reads=[bC, bCa], writes=[b_act])

        class WIn:
            pass
        w_in_perm = w_in
        linear(w_in_perm, 64, KC, xn_sb, b_xn, n, ep3)
        if halo:
            continue

        linear(w_out, 16, 32, act_sb, b_act, n, ep1)

        if variant == "mid":
            fw.dma(h2T.rearrange("(c p) t -> p c t", p=128)[:, :, o0:o0 + n], h_sb[:, :, 0:n], reads=[b_h], q=0, is_output=True)
            rms(n, 1, xn_sb, b_xn)

            def ep_kv(m, p, bp):
                fw.op(fw.act, lambda e: e.activation(out=a_sb[:, m, 0:n], in_=p[:, 0:n], func=AF.Copy), reads=[bp], writes=[b_a])
            linear(w_kv, 24, KC, xn_sb, b_xn, n, ep_kv)
            fw.dma(kvT.rearrange("(c p) t -> p c t", p=128)[:, :, o0:o0 + n], a_sb[:, 0:24, 0:n], reads=[b_a], q=0, is_output=True)
            rms(n, 2, xn_sb, b_xn, compute_rstd=False)

            def ep_q(m, p, bp):
                if m < 16:
                    fw.op(fw.act, lambda e: e.activation(out=act_sb[:, m, 0:n], in_=p[:, 0:n], func=AF.Copy, scale=128.0 ** -0.5),
                          reads=[bp], writes=[b_act])
                else:
                    fw.op(fw.act, lambda e: e.activation(out=g_sb[:, 0:n], in_=p[:, 0:n], func=AF.Sigmoid), reads=[bp], writes=[b_g])
            linear(w_q, 17, KC, xn_sb, b_xn, n, ep_q)
            fw.dma(qT.rearrange("(c p) t -> p c t", p=128)[:, :, o0:o0 + n], act_sb[:, 0:16, 0:n], reads=[b_act], q=0, is_output=True)
            fw.dma(gT[:, o0:o0 + n], g_sb[:, 0:n], reads=[b_g], q=0, is_output=True)
        else:
            fw.op(fw.act, lambda e: e.activation(out=sq_sb[:, :, 0:n], in_=h_sb[:, :, 0:n], func=AF.Square), reads=[b_h], writes=[b_sq])
            for c in range(KC):
                fw.op(fw.pe, lambda e: e.matmul(ps_ss[:, 0:n], ones_sb[:], sq_sb[:, c, 0:n], start=(c == 0), stop=(c == KC - 1)),
                      reads=[b_sq, b_const], writes=[b_psss], inc=(c == KC - 1))
            fw.op(fw.act, lambda e: e.activation(out=rstd_sb[:, 0:n], in_=ps_ss[:, 0:n], func=AF.Sqrt, bias=eps_sb[:], scale=1.0 / D),
                  reads=[b_psss, b_const], writes=[b_rstd])
            fw.op(fw.dve, lambda e: e.reciprocal(out=rstd_sb[:, 0:n], in_=rstd_sb[:, 0:n]), reads=[b_rstd], writes=[b_rstd])
            for c in range(KC):
                fw.op(fw.dve, lambda e: e.scalar_tensor_tensor(out=h_sb[:, c, 0:n], in0=h_sb[:, c, 0:n], scalar=gains_sb[:, 1, c:c + 1],
                                                               in1=rstd_sb[:, 0:n], op0=ALU.mult, op1=ALU.mult),
                      reads=[b_h, b_rstd, b_const], writes=[b_h])
            fw.dma(outT.rearrange("(c p) t -> p c t", p=128)[:, :, o0:o0 + n], h_sb[:, :, 0:n], reads=[b_h], q=0, is_output=True)
    fw.finish()
    return nc


def lay_w(w, pad_to=None):
    K, M = w.shape
    if pad_to is not None and M < pad_to:
        w = np.concatenate([w, np.zeros((K, pad_to - M), w.dtype)], axis=1)
        M = pad_to
    return np.ascontiguousarray(w.reshape(K // 128, 128, M // 128, 128).transpose(2, 1, 0, 3)).reshape(M // 128, 128, K)


def lay_gain(g):
    return np.ascontiguousarray(g.reshape(16, 128).T)


NEGB = -30000.0


def build_stage_c(S):
    nc = bass.Bass("TRN2", target_bir_lowering=False)
    NQT = S // 128
    NC = S // 16
    NCT = (NC + 127) // 128
    NCP = NCT * 128
    NSEL = S // 64
    NBH = (NSEL + 127) // 128
    NBW = NBH * 128
    EW = min(NSEL, 128) * 64
    di = lambda n, s, dt: nc.dram_tensor(n, s, dt, kind="ExternalInput").ap()
    kcT = di("kcT", [128, S], BF16)
    vcT = di("vcT", [128, S], BF16)
    ksT = di("ksT", [128, S], BF16)
    kwT = di("kwT", [128, S], BF16)
    vs = di("vs", [S, 128], BF16)
    vw = di("vw", [S, 128], BF16)
    qT = di("qT", [NQT, 128, 512], BF16)
    gates = di("gates", [NQT, 128, 12], F32)
    w1 = [di("w1k", [128, 32, 256], F32), di("w1v", [128, 32, 256], F32)]
    peT = [di("peTk", [128, 32], F32), di("peTv", [128, 32], F32)]
    w2 = [di("w2k", [128, 2, 128], F32), di("w2v", [128, 2, 128], F32)]
    ident = di("ident", [128, 128], F32)
    ebig = di("ebig", [128, EW], F32)
    caus = di("caus", [128, 128], F32)
    low = di("low", [128, 128], F32)
    cposrow = di("cposrow", [128, NCP], F32)
    tcol = di("tcol", [128, NQT], F32)
    curcol = di("curcol", [128, NQT], F32)
    jrow = di("jrow", [128, NBW], F32)
    qrow = di("qrow", [128, 128], F32)
    cpq = di("cpq", [128, NCT, NQT], F32)
    o = nc.dram_tensor("o", [S, 512], BF16, kind="ExternalOutput").ap()

    fw = FW(nc)
    Sb = fw.sbuf
    B = Buf
    CBW = 16 * 512 + 32
    cbuf = Sb("cbuf", [128, CBW], BF16)
    w1_sb = Sb("w1_sb", [128, 32, 256], BF16)
    peT_sb = Sb("peT_sb", [128, 32], BF16)
    w2_sb = Sb("w2_sb", [128, 2, 128], BF16)
    hb_sb = Sb("hb_sb", [128, 2], F32)
    hx = [Sb(f"hx{i}", [128, 512], F32) for i in range(3)]
    hg_sb = Sb("hg_sb", [128, 2, 512], BF16)
    kcmpT = Sb("kcmpT", [128, NCP], BF16)
    vcmp = Sb("vcmp", [128, NCT, 129], BF16)
    ksT_sb = Sb("ksT_sb", [128, S], BF16)
    vs_sb = Sb("vs_sb", [128, NQT, 129], BF16)
    kw_sb = Sb("kw_sb", [128, 8, 128], BF16)
    vw_sb = Sb("vw_sb", [128, 8, 129], BF16)
    ident_sb = Sb("ident_sb", [128, 128], BF16)
    identf_sb = Sb("identf_sb", [128, 128], F32)
    ebig_sb = Sb("ebig_sb", [128, EW], BF16)
    caus_sb = Sb("caus_sb", [128, 128], BF16)
    low_sb = Sb("low_sb", [128, 128], BF16)
    cposrow_sb = Sb("cposrow_sb", [128, NCP], F32)
    tcol_sb = Sb("tcol_sb", [128, NQT], F32)
    curcol_sb = Sb("curcol_sb", [128, NQT], F32)
    jrow_sb = Sb("jrow_sb", [128, NBW], F32)
    qrow_sb = Sb("qrow_sb", [128, 128], F32)
    cpq_sb = Sb("cpq_sb", [128, NCT, NQT], F32)
    zero_sb = Sb("zero_sb", [128, 512], BF16)
    q_sb = [Sb(f"q_sb{i}", [128, 512], BF16) for i in range(2)]
    g_sb = [Sb(f"g_sb{i}", [128, 12], F32) for i in range(2)]
    bsel_sb = Sb("bsel_sb", [128, NCP], F32)
    s_sb = [Sb(f"s_sb{i}", [128, NCP], F32) for i in range(2)]
    e_sb = [Sb(f"e_sb{i}", [128, NCP], F32) for i in range(2)]
    sum_sb = [Sb(f"sum_sb{i}", [128, 1], F32) for i in range(2)]
    imp_sb = Sb("imp_sb", [128, NCP + 8], F32)
    psl_sb = Sb("psl_sb", [128, NBW], F32)
    d_sb = Sb("d_sb", [128, NBW], F32)
    val_sb = Sb("val_sb", [128, NBW], F32)
    fm_sb = Sb("fm_sb", [128, NBW], F32)
    sco_sb = Sb("sco_sb", [128, NBW], F32)
    wrk_sb = Sb("wrk_sb", [128, NBW], F32)
    m8_sb = Sb("m8_sb", [128, 16], F32)
    nsel_sb = Sb("nsel_sb", [128, NBW], F32)
    nselT_sbs = [Sb(f"nselT_sb{i}", [128, NBH, 128], BF16) for i in range(2)]
    cb_sb = [Sb(f"cb_sb{i}", [128, 128], BF16) for i in range(2)]
    pT_sb = [Sb(f"pT_sb{i}", [128, 512], BF16) for i in range(3)]
    coef_sb = [Sb(f"coef_sb{i}", [128, 2], F32) for i in range(4)]
    otmp_sb = [Sb(f"otmp_sb{i}", [128, 2, 128], F32) for i in range(2)]
    oacc_sb = [Sb(f"oacc_sb{i}", [128, 4, 128], F32) for i in range(2)]
    oout_sb = [Sb(f"oout_sb{i}", [128, 512], BF16) for i in range(2)]

    P = fw.psum
    ps_s = [P(f"ps_s{i}", [128, 512]) for i in range(3)]
    ps_o = [P(f"ps_o{i}", [128, 2, 256]) for i in range(4)]
    ps_sel = [P("ps_sel0", [128, 512])]
    ps_sel = [ps_sel[0], ps_sel[0]]
    b_pss = [B(), B(), B()]
    b_pso = [B() for _ in range(4)]
    b_pssel = [B()]
    b_pssel = [b_pssel[0], b_pssel[0]]
    b_const, b_cbuf, b_w1, b_w2, b_hb, b_hg, b_kcmp, b_vcmp, b_ks, b_vs = [B() for _ in range(10)]
    b_hx = [B(), B(), B()]
    b_kw = [B() for _ in range(8)]
    b_vw = [B() for _ in range(8)]
    b_q, b_g = [B(), B()], [B(), B()]
    b_bsel, b_imp, b_psl, b_d, b_val, b_fm, b_sco, b_wrk, b_m8, b_nsel = [B() for _ in range(10)]
    b_nselTs = [B(), B()]
    b_s, b_e, b_sum = [B(), B()], [B(), B()], [B(), B()]
    b_cb = [B(), B()]
    b_pT = [B(), B(), B()]
    b_coef = [B() for _ in range(4)]
    b_otmp = [B(), B()]
    b_oacc, b_oout = [B(), B()], [B(), B()]

    for dst, src in ((ident_sb, ident), (ebig_sb, ebig), (caus_sb, caus), (low_sb, low)):
        fw.dma(dst[:], src, writes=[b_const], q=1)
    for dst, src in ((identf_sb, ident), (cposrow_sb, cposrow), (tcol_sb, tcol), (curcol_sb, curcol), (jrow_sb, jrow),
                     (qrow_sb, qrow), (cpq_sb, cpq)):
        fw.dma(dst[:], src, writes=[b_const], q=0)
    fw.op(fw.dve, lambda e: e.memset(zero_sb[:], 0.0), writes=[b_const])
    fw.op(fw.dve, lambda e: e.memset(vs_sb[:, :, 128:129], 1.0), writes=[b_vs])
    fw.op(fw.dve, lambda e: e.memset(vw_sb[:, :, 128:129], 1.0), writes=[b_vw[0]])
    fw.op(fw.dve, lambda e: e.memset(vcmp[:, :, 128:129], 1.0), writes=[b_vcmp])
    fw.op(fw.dve, lambda e: e.memset(imp_sb[:], 0.0), writes=[b_imp])
    for i in range(1, 8):
        b_vw[i].w = b_vw[0].w
    fw.dma(ksT_sb[:], ksT, writes=[b_ks], q=0)
    vs_v = vs.rearrange("(t p) d -> p t d", p=128)
    for t0 in range(0, NQT, 16):
        t1 = min(NQT, t0 + 16)
        fw.dma(vs_sb[:, t0:t1, 0:128], vs_v[:, t0:t1, :], writes=[b_vs], q=0)

    for kv in range(2):
        src = kcT if kv == 0 else vcT
        fw.dma(w1_sb[:, 0:16, :], w1[kv][:, 0:16, :], writes=[b_w1], q=1)
        fw.dma(w1_sb[:, 16:32, :], w1[kv][:, 16:32, :], writes=[b_w1], q=1)
        fw.dma(peT_sb[:], peT[kv], writes=[b_w1], q=1)
        fw.dma(w2_sb[:], w2[kv], writes=[b_w2], q=1)
        for mc in range(2):
            for p in range(32):
                fw.op(fw.pe, lambda e: e.matmul(ps_sel[0][:, mc:mc + 1], w1_sb[:, p, mc * 128:(mc + 1) * 128], peT_sb[:, p:p + 1],
                                                start=(p == 0), stop=(p == 31)),
                      reads=[b_w1], writes=[b_pssel[0]], inc=(p == 31))
        fw.op(fw.dve, lambda e: e.tensor_copy(out=hb_sb[:], in_=ps_sel[0][:, 0:2]), reads=[b_pssel[0]], writes=[b_hb])
        for cbk in range((NC + 511) // 512):
            nb = min(512, NC - 512 * cbk)
            tok0 = 8192 * cbk
            need = 16 * nb + 16
            avail = min(need, S - tok0)
            if avail < need:
                fw.op(fw.dve, lambda e: e.memset(cbuf[:, avail:need], 0.0), writes=[b_cbuf])
            fw.dma(cbuf[:, 0:avail], src[:, tok0:tok0 + avail], writes=[b_cbuf], q=0)
            for mc in range(2):
                pb = ps_s[mc]
                for p in range(32):
                    fw.op(fw.pe, lambda e: e.matmul(pb[:, 0:nb], w1_sb[:, p, mc * 128:(mc + 1) * 128],
                                                    cbuf[:, p:p + 16 * nb].rearrange("d (i s) -> d i s", s=16)[:, :, 0],
                                                    start=(p == 0), stop=(p == 31)),
                          reads=[b_w1, b_cbuf], writes=[b_pss[mc]], inc=(p == 31))
                X, X2, T3 = hx[0], hx[1], hx[2]
                fw.op(fw.act, lambda e: e.activation(out=X[:, 0:nb], in_=pb[:, 0:nb], func=AF.Identity, bias=hb_sb[:, mc:mc + 1], scale=1.0),
                      reads=[b_pss[mc], b_hb], writes=[b_hx[0]])
                fw.op(fw.dve, lambda e: e.tensor_tensor(out=X2[:, 0:nb], in0=X[:, 0:nb], in1=X[:, 0:nb], op=ALU.mult),
                      reads=[b_hx[0]], writes=[b_hx[1]])
                fw.op(fw.dve, lambda e: e.tensor_scalar(out=X2[:, 0:nb], in0=X2[:, 0:nb], scalar1=0.044715, scalar2=1.0, op0=ALU.mult, op1=ALU.add),
                      reads=[b_hx[1]], writes=[b_hx[1]])
                fw.op(fw.dve, lambda e: e.tensor_tensor(out=X2[:, 0:nb], in0=X2[:, 0:nb], in1=X[:, 0:nb], op=ALU.mult),
                      reads=[b_hx[0], b_hx[1]], writes=[b_hx[1]])
                fw.op(fw.act, lambda e: e.activation(out=T3[:, 0:nb], in_=X2[:, 0:nb], func=AF.Sigmoid, scale=1.5957691216057308),
                      reads=[b_hx[1]], writes=[b_hx[2]])
                fw.op(fw.dve, lambda e: e.tensor_tensor(out=hg_sb[:, mc, 0:nb], in0=T3[:, 0:nb], in1=X[:, 0:nb], op=ALU.mult),
                      reads=[b_hx[0], b_hx[2]], writes=[b_hg])
            if kv == 0:
                for mc in range(2):
                    fw.op(fw.pe, lambda e: e.matmul(ps_sel[0][:, 0:nb], w2_sb[:, mc, :], hg_sb[:, mc, 0:nb], start=(mc == 0), stop=(mc == 1)),
                          reads=[b_w2, b_hg], writes=[b_pssel[0]], inc=(mc == 1))
                fw.op(fw.act, lambda e: e.activation(out=kcmpT[:, 512 * cbk:512 * cbk + nb], in_=ps_sel[0][:, 0:nb], func=AF.Copy),
                      reads=[b_pssel[0]], writes=[b_kcmp])
            else:
                for ct in range((nb + 127) // 128):
                    w_ = min(128, nb - ct * 128)
                    for mc in range(2):
                        fw.op(fw.pe, lambda e: e.matmul(ps_sel[0][0:w_, ct * 128:(ct + 1) * 128], hg_sb[:, mc, ct * 128:ct * 128 + w_], w2_sb[:, mc, :],
                                                        start=(mc == 0), stop=(mc == 1)),
                              reads=[b_w2, b_hg], writes=[b_pssel[0]], inc=(mc == 1))
                    fw.op(fw.act, lambda e: e.activation(out=vcmp[0:w_, 4 * cbk + ct, 0:128], in_=ps_sel[0][0:w_, ct * 128:(ct + 1) * 128], func=AF.Copy),
                          reads=[b_pssel[0]], writes=[b_vcmp])

    cnt = {"s": 0, "p": 0, "o": 0, "cb": 0}

    def load_q(qt):
        bi = qt % 2
        fw.dma(q_sb[bi][:], qT[qt], writes=[b_q[bi]], q=0)
        fw.dma(g_sb[bi][:], gates[qt], writes=[b_g[bi]], q=0)
        sl = qt % 8
        fw.dma(kw_sb[:, sl, :], kwT[:, qt * 128:(qt + 1) * 128], writes=[b_kw[sl]], q=0)
        fw.dma(vw_sb[:, sl, 0:128], vw[qt * 128:(qt + 1) * 128, :], writes=[b_vw[sl]], q=0)

    def attend(qt, tiles, br, first_branch, between=None):
        bi = qt % 2
        oset = cnt["o"] % 2
        cnt["o"] += 1
        po = [ps_o[2 * oset], ps_o[2 * oset + 1]]
        bpo = [b_pso[2 * oset], b_pso[2 * oset + 1]]
        for hb in range(2):
            fw.op(fw.pe, lambda e: e.matmul(po[hb][:, :, 0:129], zero_sb[:, 0:128], zero_sb[:, 0:258].rearrange("p (a b) -> p a b", a=2),
                                            start=True, stop=False, skip_group_check=True),
                  reads=[b_const], writes=[bpo[hb]], inc=False)
        nt = len(tiles)
        slots = []

        def emit_s(ti):
            kT_ap, b_k, va_ap, b_v, biases = tiles[ti]
            si = cnt["s"] % 3
            cnt["s"] += 1
            fw.op(fw.pe, lambda e: e.matmul(ps_s[si][:], kT_ap, q_sb[bi][:], start=True, stop=(len(biases) == 0)),
                  reads=[b_k, b_q[bi]], writes=[b_pss[si]], inc=(len(biases) == 0))
            for bj, (l_ap, r_ap, bufs) in enumerate(biases):
                last = bj == len(biases) - 1
                fw.op(fw.pe, lambda e: e.matmul(ps_s[si][:], l_ap, r_ap, start=False, stop=last),
                      reads=bufs, writes=[b_pss[si]], inc=last)
            slots.append(si)

        emit_s(0)
        if nt > 1:
            emit_s(1)
        for ti in range(nt):
            kT_ap, b_k, va_ap, b_v, biases = tiles[ti]
            si = slots[ti]
            pi = cnt["p"] % 3
            cnt["p"] += 1
            fw.op(fw.act, lambda e: e.activation(out=pT_sb[pi][:], in_=ps_s[si][:], func=AF.Exp), reads=[b_pss[si]], writes=[b_pT[pi]])
            if ti + 2 < nt:
                emit_s(ti + 2)
            for g in range(4):
                lastmm = (ti == nt - 1)
                fw.op(fw.pe, lambda e: e.matmul(po[g // 2][:, g % 2, 0:129], pT_sb[pi][:, g * 128:(g + 1) * 128], va_ap,
                                                start=False, stop=lastmm, skip_group_check=True),
                      reads=[b_pT[pi], b_v], writes=[bpo[g // 2]], inc=(lastmm and g % 2 == 1) or (g == 3))
            if between is not None:
                between(ti)
        for hb in range(2):
            ci = (cnt["o"] * 2 + hb) % 4
            cf, bcf = coef_sb[ci], b_coef[ci]
            fw.op(fw.dve, lambda e: e.tensor_scalar(out=cf[:], in0=po[hb][:, :, 128], scalar1=1e-30, scalar2=None, op0=ALU.max),
                  reads=[bpo[hb]], writes=[bcf])
            fw.op(fw.dve, lambda e: e.reciprocal(out=cf[:], in_=cf[:]), reads=[bcf], writes=[bcf])
            fw.op(fw.dve, lambda e: e.tensor_tensor(out=cf[:], in0=cf[:], in1=g_sb[bi][:, br * 4 + 2 * hb: br * 4 + 2 * hb + 2], op=ALU.mult),
                  reads=[bcf, b_g[bi]], writes=[bcf])
            dst = oacc_sb[bi][:, 2 * hb:2 * hb + 2, :]
            if first_branch:
                fw.op(fw.dve, lambda e: e.tensor_tensor(out=dst, in0=po[hb][:, :, 0:128], in1=cf[:].unsqueeze(2).to_broadcast([128, 2, 128]), op=ALU.mult),
                      reads=[bpo[hb], bcf], writes=[b_oacc[bi]])
            else:
                ot, bot = otmp_sb[hb], b_otmp[hb]
                fw.op(fw.dve, lambda e: e.tensor_tensor(out=ot[:], in0=po[hb][:, :, 0:128], in1=cf[:].unsqueeze(2).to_broadcast([128, 2, 128]), op=ALU.mult),
                      reads=[bpo[hb], bcf], writes=[bot])
                fw.op(fw.pool, lambda e: e.tensor_tensor(out=dst, in0=dst, in1=ot[:], op=ALU.add), reads=[bot, b_oacc[bi]], writes=[b_oacc[bi]])

    def sel_gen(qt):
        bi = qt % 2
        nselT_sb = nselT_sbs[qt % 2]
        b_nselT = b_nselTs[qt % 2]
        cmax = 8 * qt + 6
        ncw = min(NCP, ((cmax + 1 + 7) // 8) * 8)
        ncw = max(ncw, 8)
        nbw = min(NBW, max(16, ((2 * qt + 2 + 7) // 8) * 8))
        fw.op(fw.dve, lambda e: e.tensor_scalar(out=bsel_sb[:, 0:ncw], in0=cposrow_sb[:, 0:ncw], scalar1=tcol_sb[:, qt:qt + 1], scalar2=NEGB,
                                                op0=ALU.is_gt, op1=ALU.mult), reads=[b_const], writes=[b_bsel])
        for g in range(4):
            gi = g % 2
            yield
            for c0 in range(0, ncw, 512):
                if c0 > 0:
                    yield
                c1 = min(ncw, c0 + 512)
                pb = ps_sel[c0 // 512]
                fw.op(fw.pe, lambda e: e.matmul(pb[:, 0:c1 - c0], q_sb[bi][:, g * 128:(g + 1) * 128], kcmpT[:, c0:c1], start=True, stop=True),
                      reads=[b_q[bi], b_kcmp], writes=[b_pssel[c0 // 512]], inc=True)
                fw.op(fw.dve, lambda e: e.tensor_tensor(out=s_sb[gi][:, c0:c1], in0=pb[:, 0:c1 - c0], in1=bsel_sb[:, c0:c1], op=ALU.add),
                      reads=[b_pssel[c0 // 512], b_bsel], writes=[b_s[gi]])
            fw.op(fw.act, lambda e: e.activation(out=e_sb[gi][:, 0:ncw], in_=s_sb[gi][:, 0:ncw], func=AF.Exp, accum_out=sum_sb[gi][:]),
                  reads=[b_s[gi]], writes=[b_e[gi], b_sum[gi]])
            fw.op(fw.dve, lambda e: e.tensor_scalar(out=sum_sb[gi][:], in0=sum_sb[gi][:], scalar1=1e-30, scalar2=None, op0=ALU.max),
                  reads=[b_sum[gi]], writes=[b_sum[gi]])
            fw.op(fw.dve, lambda e: e.reciprocal(out=sum_sb[gi][:], in_=sum_sb[gi][:]), reads=[b_sum[gi]], writes=[b_sum[gi]])
            if g == 0:
                fw.op(fw.dve, lambda e: e.tensor_scalar(out=imp_sb[:, 1:1 + ncw], in0=e_sb[gi][:, 0:ncw], scalar1=sum_sb[gi][:], scalar2=None, op0=ALU.mult),
                      reads=[b_e[gi], b_sum[gi]], writes=[b_imp])
            else:
                fw.op(fw.dve, lambda e: e.scalar_tensor_tensor(out=imp_sb[:, 1:1 + ncw], in0=e_sb[gi][:, 0:ncw], scalar=sum_sb[gi][:], in1=imp_sb[:, 1:1 + ncw],
                                                               op0=ALU.mult, op1=ALU.add), reads=[b_e[gi], b_sum[gi], b_imp], writes=[b_imp])
        if ncw < NCP:
            fw.op(fw.dve, lambda e: e.memset(imp_sb[:, 1 + ncw:min(NCP + 8, 1 + ncw + 8)], 0.0), writes=[b_imp])
        nbe = min(nbw, (ncw + 3) // 4)
        fw.op(fw.dve, lambda e: e.memset(psl_sb[:, 0:nbw], 0.0), writes=[b_psl])
        fw.op(fw.dve, lambda e: e.tensor_reduce(out=psl_sb[:, 0:nbe], in_=imp_sb[:, 0:4 * nbe].rearrange("p (a b) -> p a b", b=4), axis=AX.X, op=ALU.add),
              reads=[b_imp], writes=[b_psl])
        fw.op(fw.dve, lambda e: e.tensor_tensor(out=psl_sb[:, 0:nbe], in0=psl_sb[:, 0:nbe],
                                                in1=imp_sb[:, 4:4 + 4 * nbe].rearrange("p (a b) -> p a b", b=4)[:, :, 0], op=ALU.add),
              reads=[b_imp, b_psl], writes=[b_psl])
        fw.op(fw.dve, lambda e: e.tensor_scalar(out=d_sb[:, 0:nbw], in0=jrow_sb[:, 0:nbw], scalar1=curcol_sb[:, qt:qt + 1], scalar2=None, op0=ALU.subtract),
              reads=[b_const], writes=[b_d])
        fw.op(fw.dve, lambda e: e.tensor_scalar(out=val_sb[:, 0:nbw], in0=d_sb[:, 0:nbw], scalar1=0.0, scalar2=None, op0=ALU.is_le),
              reads=[b_d], writes=[b_val])
        fw.op(fw.dve, lambda e: e.scalar_tensor_tensor(out=fm_sb[:, 0:nbw], in0=d_sb[:, 0:nbw], scalar=-1.0, in1=val_sb[:, 0:nbw], op0=ALU.is_ge, op1=ALU.mult),
              reads=[b_d, b_val], writes=[b_fm])
        fw.op(fw.dve, lambda e: e.scalar_tensor_tensor(out=sco_sb[:, 0:nbw], in0=fm_sb[:, 0:nbw], scalar=100.0, in1=psl_sb[:, 0:nbw], op0=ALU.mult, op1=ALU.add),
              reads=[b_fm, b_psl], writes=[b_sco])
        fw.op(fw.dve, lambda e: e.tensor_scalar(out=sco_sb[:, 0:1], in0=sco_sb[:, 0:1], scalar1=100.0, scalar2=None, op0=ALU.add),
              reads=[b_sco], writes=[b_sco])
        fw.op(fw.dve, lambda e: e.max(out=m8_sb[:, 0:8], in_=sco_sb[:, 0:nbw]), reads=[b_sco], writes=[b_m8])
        fw.op(fw.dve, lambda e: e.match_replace(out=wrk_sb[:, 0:nbw], in_to_replace=m8_sb[:, 0:8], in_values=sco_sb[:, 0:nbw], imm_value=-1.0),
              reads=[b_sco, b_m8], writes=[b_wrk])
        fw.op(fw.dve, lambda e: e.max(out=m8_sb[:, 8:16], in_=wrk_sb[:, 0:nbw]), reads=[b_wrk, b_m8], writes=[b_m8])
        fw.op(fw.dve, lambda e: e.scalar_tensor_tensor(out=wrk_sb[:, 0:nbw], in0=sco_sb[:, 0:nbw], scalar=m8_sb[:, 15:16], in1=val_sb[:, 0:nbw],
                                                       op0=ALU.is_ge, op1=ALU.mult), reads=[b_sco, b_m8, b_val], writes=[b_wrk])
        if nbw < NBW:
            fw.op(fw.dve, lambda e: e.memset(nsel_sb[:, nbw:NBW], NEGB), writes=[b_nsel])
        fw.op(fw.dve, lambda e: e.tensor_scalar(out=nsel_sb[:, 0:nbw], in0=wrk_sb[:, 0:nbw], scalar1=1.0, scalar2=-NEGB, op0=ALU.subtract, op1=ALU.mult),
              reads=[b_wrk], writes=[b_nsel])
        for _ in range(6):
            yield
        nbh_used = (min(NSEL, 2 * qt + 2) + 127) // 128
        for hb in range(nbh_used):
            if hb > 0:
                yield
            fw.op(fw.pe, lambda e: e.transpose(ps_sel[hb][:, 0:128], nsel_sb[:, hb * 128:(hb + 1) * 128], identf_sb[:]),
                  reads=[b_nsel, b_const], writes=[b_pssel[hb]], inc=True)
            fw.op(fw.act, lambda e: e.activation(out=nselT_sb[:, hb, :], in_=ps_sel[hb][:, 0:128], func=AF.Copy), reads=[b_pssel[hb]], writes=[b_nselT])


    load_q(0)
    for _ in sel_gen(0):
        pass
    for qt in range(NQT):
        bi = qt % 2
        if qt + 1 < NQT:
            load_q(qt + 1)
        tiles = []
        cmax = 8 * qt + 6
        nct = cmax // 128 + 1
        for ct in range(min(nct, NCT)):
            biases = []
            if 16 * (128 * ct + 127) + 31 > 128 * qt or (ct == NCT - 1):
                ci = cnt["cb"] % 2
                cnt["cb"] += 1
                fw.op(fw.dve, lambda e: e.tensor_scalar(out=cb_sb[ci][:], in0=qrow_sb[:], scalar1=cpq_sb[:, ct, qt:qt + 1], scalar2=NEGB,
                                                        op0=ALU.is_lt, op1=ALU.mult), reads=[b_const], writes=[b_cb[ci]])
                biases.append((ident_sb[:], cb_sb[ci][:].unsqueeze(1).to_broadcast([128, 4, 128]), [b_const, b_cb[ci]]))
            tiles.append((kcmpT[:, ct * 128:(ct + 1) * 128], b_kcmp, vcmp[:, ct, :], b_vcmp, biases))
        attend(qt, tiles, 0, True)
        tiles = []
        for kt in range(qt + 1):
            hb = kt // 64
            ktp = kt % 64
            biases = [(ebig_sb[:, ktp * 128:(ktp + 1) * 128], nselT_sbs[qt % 2][:, hb, :].unsqueeze(1).to_broadcast([128, 4, 128]), [b_const, b_nselTs[qt % 2]])]
            if kt == qt:
                biases.append((ident_sb[:], caus_sb[:].unsqueeze(1).to_broadcast([128, 4, 128]), [b_const]))
            tiles.append((ksT_sb[:, kt * 128:(kt + 1) * 128], b_ks, vs_sb[:, kt, :], b_vs, biases))
        gen = sel_gen(qt + 1) if qt + 1 < NQT else iter(())

        def between(ti, gen=gen):
            if ti % 2 == 1:
                next(gen, None)
        attend(qt, tiles, 1, False, between=between)
        for _ in gen:
            pass
        tiles = []
        for kt in range(max(0, qt - 4), qt + 1):
            sl = kt % 8
            biases = []
            if kt == qt:
                biases.append((ident_sb[:], caus_sb[:].unsqueeze(1).to_broadcast([128, 4, 128]), [b_const]))
            if kt == qt - 4:
                biases.append((ident_sb[:], low_sb[:].unsqueeze(1).to_broadcast([128, 4, 128]), [b_const]))
            tiles.append((kw_sb[:, sl, :], b_kw[sl], vw_sb[:, sl, :], b_vw[sl], biases))
        attend(qt, tiles, 2, False)
        fw.op(fw.act, lambda e: e.activation(out=oout_sb[bi][:], in_=oacc_sb[bi][:].rearrange("p a b -> p (a b)"), func=AF.Copy),
              reads=[b_oacc[bi]], writes=[b_oout[bi]])
        fw.dma(o[qt * 128:(qt + 1) * 128, :], oout_sb[bi][:], reads=[b_oout[bi]], q=0, is_output=True)
    fw.finish()
    return nc


def stage_c_consts(S):
    NQT = S // 128
    NC = S // 16
    NCT = (NC + 127) // 128
    NCP = NCT * 128
    NSEL = S // 64
    NBH = (NSEL + 127) // 128
    NBW = NBH * 128
    EW = min(NSEL, 128) * 64
    p = np.arange(128)
    c = np.arange(NCP)
    cpos = (16.0 * c + 31.0).astype(np.float32)
    cpos[NC - 1:] = 1e9
    kl = p[:, None]
    ql = p[None, :]
    cpq = np.zeros((128, NCT, NQT), np.float32)
    for ct in range(NCT):
        cpq[:, ct, :] = cpos[ct * 128 + p][:, None] - 128.0 * np.arange(NQT)[None, :]
    return dict(
        ident=np.eye(128, dtype=np.float32),
        ebig=(np.arange(EW)[None, :] // 64 == p[:, None]).astype(np.float32),
        caus=np.where(kl > ql, NEGB, 0.0).astype(np.float32),
        low=np.where(kl <= ql, NEGB, 0.0).astype(np.float32),
        cposrow=np.ascontiguousarray(np.broadcast_to(cpos[None, :], (128, NCP))),
        tcol=(128.0 * np.arange(NQT)[None, :] + p[:, None]).astype(np.float32),
        curcol=((128 * np.arange(NQT)[None, :] + p[:, None]) // 64).astype(np.float32),
        jrow=np.ascontiguousarray(np.broadcast_to(np.arange(NBW, dtype=np.float32)[None, :], (128, NBW))),
        qrow=np.ascontiguousarray(np.broadcast_to(np.arange(128, dtype=np.float32)[None, :], (128, 128))),
        cpq=cpq,
    )

import ml_dtypes

BF_NP = ml_dtypes.bfloat16
NCORES = 8
BATCH, SEQ, DM = 2, 16384, 2048
_CACHE = {}


def _prog(key, fn):
    if key not in _CACHE:
        _CACHE[key] = fn()
    return _CACHE[key]


def build_cast(ncols):
    nc = bass.Bass("TRN2", target_bir_lowering=False)
    src = nc.dram_tensor("src", [128, ncols], F32, kind="ExternalInput").ap()
    dst = nc.dram_tensor("dst", [128, ncols], BF16, kind="ExternalOutput").ap()
    fw = FW(nc)
    CH = 8192
    bufs = [fw.sbuf(f"cb{i}", [128, CH], BF16) for i in range(2)]
    bb = [Buf(), Buf()]
    for i, c0 in enumerate(range(0, ncols, CH)):
        c1 = min(ncols, c0 + CH)
        fw.dma(bufs[i % 2][:, 0:c1 - c0], src[:, c0:c1], writes=[bb[i % 2]], q=1, max_dma_last_dim=8192)
        fw.dma(dst[:, c0:c1], bufs[i % 2][:, 0:c1 - c0], reads=[bb[i % 2]], q=0, is_output=True)
    fw.finish()
    return nc


def _run(nc, in_maps):
    res = run_bass_kernel_spmd(nc, in_maps, core_ids=list(range(NCORES)))
    return res.results


def kernel(x, norm_mix_gain, norm_ffn_gain, ret_w_in, ret_gn_gain, ret_w_out, nsa_kv_norm_gain, nsa_w_kv,
           cmp_pe_k, cmp_w1_k, cmp_w2_k, cmp_pe_v, cmp_w1_v, cmp_w2_v, nsa_w_q, nsa_w_o,
           ffn_w_in, ffn_conv_w, ffn_conv_b, ffn_w_out, final_norm_gain):
    f32 = np.float32
    x = np.asarray(x, f32)
    NTOK = BATCH * SEQ
    T = NTOK // NCORES

    ncB = _prog("Bmid", lambda: build_stage_b(T, 4096, "mid"))
    ncC = _prog("C", lambda: build_stage_c(SEQ))
    ncD = _prog("Bfin", lambda: build_stage_b(T, 2048, "final"))
    wlist = [("ret_w_out", np.asarray(ret_w_out[0], f32)), ("ffn_w_in0", np.asarray(ffn_w_in[0], f32)),
             ("ffn_w_in1", np.asarray(ffn_w_in[1], f32)), ("ffn_w_out0", np.asarray(ffn_w_out[0], f32)),
             ("ffn_w_out1", np.asarray(ffn_w_out[1], f32)), ("nsa_w_kv", np.asarray(nsa_w_kv, f32)),
             ("nsa_w_q", np.asarray(nsa_w_q[0], f32)), ("nsa_w_o", np.asarray(nsa_w_o[0], f32))]
    flat = np.concatenate([w.reshape(-1) for _, w in wlist])
    per = -(-flat.size // (NCORES * 128))
    per = -(-per // 64) * 64
    tot = per * NCORES * 128
    flat_p = np.zeros(tot, f32)
    flat_p[:flat.size] = flat
    flat_p = flat_p.reshape(NCORES, 128, per)

    xT = np.ascontiguousarray(x.reshape(NTOK, DM).T)
    ncA = _prog(("A", per), lambda: build_stage_a(NTOK, SEQ, cast_cols=per))
    cA = stage_a_consts(SEQ)
    w_in = np.asarray(ret_w_in[0], f32)
    H, dk, dv = 8, 256, 512
    gn = np.asarray(ret_gn_gain[0], f32)
    g_mix0 = lay_gain(np.asarray(norm_mix_gain[0], f32))
    maps = []
    for h in range(H):
        m = dict(cA[h])
        m["xT"] = xT
        m["gain"] = g_mix0
        m["wq"] = np.ascontiguousarray(w_in[:, h * dk:(h + 1) * dk])
        m["wk"] = np.ascontiguousarray(w_in[:, H * dk + h * dk: H * dk + (h + 1) * dk])
        m["wv"] = np.ascontiguousarray(w_in[:, 2 * H * dk + h * dv: 2 * H * dk + (h + 1) * dv])
        m["wg"] = np.ascontiguousarray(w_in[:, 2 * H * dk + H * dv + h * dv: 2 * H * dk + H * dv + (h + 1) * dv])
        m["gng"] = np.ascontiguousarray(np.broadcast_to(gn[h * dv:(h + 1) * dv][None, :], (128, 512)))
        m["csrc"] = flat_p[h]
        maps.append(m)
    rA = _run(ncA, maps)
    flat_b = np.concatenate([np.asarray(r["cdst"]).reshape(-1) for r in rA])[:flat.size]
    wb = {}
    off = 0
    for name, w in wlist:
        wb[name] = flat_b[off:off + w.size].reshape(w.shape)
        off += w.size
    yT = np.concatenate([np.asarray(r["y"]).T for r in rA], axis=0)
    del rA

    perm = np.stack([np.arange(32), np.arange(32) + 32], axis=1).reshape(-1)

    def permcols(w):
        return np.ascontiguousarray(w.reshape(w.shape[0], 64, 128)[:, perm, :].reshape(w.shape[0], 8192))

    def convw_of(layer):
        cw = np.asarray(ffn_conv_w[layer], f32)
        cb = np.asarray(ffn_conv_b[layer], f32)
        out = np.zeros((128, 64, 4), f32)
        out[:, :, 0:3] = cw.reshape(3, 64, 128).transpose(2, 1, 0)
        out[:, :, 3] = cb.reshape(64, 128).T
        return out

    def with_halo(fullT, c, dt):
        b, j = divmod(c, NCORES // BATCH)
        lo = b * SEQ + j * T
        out = np.zeros((fullT.shape[0], 2 + T), dt)
        out[:, 2:] = fullT[:, lo:lo + T]
        if j > 0:
            out[:, 0:2] = fullT[:, lo - 2:lo]
        return out

    ncB = _prog("Bmid", lambda: build_stage_b(T, 4096, "mid"))
    gains_mid = np.ascontiguousarray(np.stack([lay_gain(np.asarray(norm_ffn_gain[0], f32)), lay_gain(np.asarray(nsa_kv_norm_gain, f32)),
                                               lay_gain(np.asarray(norm_mix_gain[1], f32))], axis=1))
    common = dict(w_mix=lay_w(wb["ret_w_out"]), w_in=lay_w(permcols(wb["ffn_w_in0"])), w_out=lay_w(wb["ffn_w_out0"]),
                  gains=gains_mid, convw=convw_of(0), w_kv=lay_w(wb["nsa_w_kv"]), w_q=lay_w(wb["nsa_w_q"], pad_to=17 * 128))
    maps = []
    for c in range(NCORES):
        m = dict(common)
        m["hT"] = with_halo(xT, c, f32)
        m["aT"] = with_halo(yT, c, BF_NP)
        maps.append(m)
    rB = _run(ncB, maps)
    del yT, xT
    h2T = np.concatenate([r["h2T"] for r in rB], axis=1)
    kvT = np.concatenate([np.asarray(r["kvT"]) for r in rB], axis=1)
    qTf = np.concatenate([np.asarray(r["qT"]) for r in rB], axis=1)
    gTf = np.concatenate([r["gT"] for r in rB], axis=1)[:48]
    del rB

    ncC = _prog("C", lambda: build_stage_c(SEQ))
    cC = stage_c_consts(SEQ)
    NQT = SEQ // 128

    def w1lay(w):
        return np.ascontiguousarray(np.asarray(w, f32).reshape(32, 128, 256).transpose(1, 0, 2))

    def w2lay(w):
        return np.ascontiguousarray(np.asarray(w, f32).reshape(2, 128, 128).transpose(1, 0, 2))
    cw = dict(w1k=w1lay(cmp_w1_k), w1v=w1lay(cmp_w1_v), w2k=w2lay(cmp_w2_k), w2v=w2lay(cmp_w2_v),
              peTk=np.ascontiguousarray(np.asarray(cmp_pe_k, f32).T), peTv=np.ascontiguousarray(np.asarray(cmp_pe_v, f32).T))
    maps = []
    for b in range(BATCH):
        ts = slice(b * SEQ, (b + 1) * SEQ)
        for h in range(4):
            m = dict(cC)
            m.update(cw)
            row = lambda i: slice((i * 4 + h) * 128, (i * 4 + h + 1) * 128)
            m["kcT"] = np.ascontiguousarray(kvT[row(0), ts])
            m["vcT"] = np.ascontiguousarray(kvT[row(1), ts])
            m["ksT"] = np.ascontiguousarray(kvT[row(2), ts])
            m["vs"] = np.ascontiguousarray(kvT[row(3), ts].T)
            m["kwT"] = np.ascontiguousarray(kvT[row(4), ts])
            m["vw"] = np.ascontiguousarray(kvT[row(5), ts].T)
            qq = qTf[h * 512:(h + 1) * 512, ts].reshape(4, 128, NQT, 128)
            m["qT"] = np.ascontiguousarray(qq.transpose(2, 1, 0, 3)).reshape(NQT, 128, 512)
            gg = gTf[h * 12:(h + 1) * 12, ts].reshape(4, 3, NQT, 128)
            m["gates"] = np.ascontiguousarray(gg.transpose(2, 3, 1, 0)).reshape(NQT, 128, 12)
            maps.append(m)
    rC = _run(ncC, maps)
    del kvT, qTf, gTf
    oT = np.zeros((2048, NTOK), BF_NP)
    for b in range(BATCH):
        for h in range(4):
            oT[h * 512:(h + 1) * 512, b * SEQ:(b + 1) * SEQ] = np.asarray(rC[b * 4 + h]["o"]).T
    del rC

    ncD = _prog("Bfin", lambda: build_stage_b(T, 2048, "final"))
    gains_fin = np.ascontiguousarray(np.stack([lay_gain(np.asarray(norm_ffn_gain[1], f32)), lay_gain(np.asarray(final_norm_gain, f32)),
                                               lay_gain(np.asarray(final_norm_gain, f32))], axis=1))
    common = dict(w_mix=lay_w(wb["nsa_w_o"]), w_in=lay_w(permcols(wb["ffn_w_in1"])), w_out=lay_w(wb["ffn_w_out1"]),
                  gains=gains_fin, convw=convw_of(1))
    maps = []
    for c in range(NCORES):
        m = dict(common)
        m["hT"] = with_halo(h2T, c, f32)
        m["aT"] = with_halo(oT, c, BF_NP)
        maps.append(m)
    rD = _run(ncD, maps)
    outT = np.concatenate([r["outT"] for r in rD], axis=1)
    return np.ascontiguousarray(outT.T).reshape(BATCH, SEQ, DM).astype(f32)
```
